# Optimizing a Trainium2 kernel written in Bass

```python
import math
import jax, jax.numpy as jnp
from jax import lax
import numpy as np

D_MODEL = 4096
BATCH = 1
SEQ = 8192
DEPTH = 2

CTX_LEN = 256
GRID_W = 64
N_EVEN = (DEPTH + 1) // 2
N_ODD = DEPTH // 2
RMS_EPS = 1e-6
LN_EPS = 1e-5
A_WIDTH = D_MODEL // 2
A_HEAD_DIM = 64
A_HEADS = A_WIDTH // (2 * A_HEAD_DIM)
B_WIDTH = D_MODEL - A_WIDTH
CONV_WIDTH = 31
Q_BLOCK = 128
ROPE_BASE = 10000.0
ROPE_AXIS_DIM = A_HEAD_DIM // 2
EV_V0 = A_WIDTH
EV_Q0 = 2 * A_WIDTH
EV_B0 = 3 * A_WIDTH
EV_IN = 3 * A_WIDTH + 2 * B_WIDTH
S5_WIDTH = D_MODEL // 4
S5_GROUP = 16
S5_GROUPS = S5_WIDTH // S5_GROUP
S5_STATE = 64
M2_INNER = D_MODEL - S5_WIDTH
M2_HEAD_DIM = 64
M2_HEADS = M2_INNER // M2_HEAD_DIM
M2_GROUPS = 8
M2_STATE = 128
M2_CONV = 5
M2_CONV_DIM = M2_INNER + 2 * M2_GROUPS * M2_STATE
SSD_CHUNK = 128
OD_DT0 = S5_WIDTH + M2_CONV_DIM
OD_Z0 = OD_DT0 + 2 * M2_HEADS
OD_IN = OD_Z0 + M2_INNER
FFN_HIDDEN = -(-8 * D_MODEL // (3 * 256)) * 256

kernel_name = 'hybrid_diffattn_conformer_s5_ssd_trunk'


def _rmsnorm(x, w, eps=RMS_EPS):
    xf = x.astype(jnp.float32)
    y = xf * lax.rsqrt(jnp.mean(xf * xf, axis=-1, keepdims=True) + eps)
    return (y * w.astype(jnp.float32)).astype(x.dtype)


def _layernorm(x, w, b, eps=LN_EPS):
    xf = x.astype(jnp.float32)
    mu = jnp.mean(xf, axis=-1, keepdims=True)
    var = jnp.mean(jnp.square(xf - mu), axis=-1, keepdims=True)
    y = (xf - mu) * lax.rsqrt(var + eps)
    return (y * w.astype(jnp.float32) + b.astype(jnp.float32)).astype(x.dtype)


def _modulate(x, shift, scale):
    return x * (1 + scale) + shift


def _swiglu(x, w1, w3, w2):
    return (jax.nn.silu(x @ w1) * (x @ w3)) @ w2


def _dwconv(u, w, bias):
    k = w.shape[0]
    pad = (k - 1) // 2
    y = lax.conv_general_dilated(u, w[:, None, :], window_strides=(1,), padding=[(pad, pad)],
                                 dimension_numbers=('NWC', 'WIO', 'NWC'),
                                 feature_group_count=u.shape[-1])
    return y + bias


def _axial_rope_tables(n_rows):
    rows = jnp.repeat(jnp.arange(n_rows), GRID_W).astype(jnp.float32)
    cols = jnp.tile(jnp.arange(GRID_W), n_rows).astype(jnp.float32)
    inv = jnp.power(ROPE_BASE, -jnp.arange(0, ROPE_AXIS_DIM, 2, dtype=jnp.float32) / ROPE_AXIS_DIM)
    ang_r = rows[:, None, None] * inv
    ang_c = cols[:, None, None] * inv
    return (jnp.cos(ang_r), jnp.sin(ang_r), jnp.cos(ang_c), jnp.sin(ang_c))


def _rotate_half(x, cos, sin):
    x1, x2 = jnp.split(x, 2, axis=-1)
    return jnp.concatenate([x1 * cos - x2 * sin, x2 * cos + x1 * sin], axis=-1)


def _rope_2d(x, rope):
    cos_r, sin_r, cos_c, sin_c = [r.astype(x.dtype) for r in rope]
    x_row, x_col = jnp.split(x, 2, axis=-1)
    return jnp.concatenate([_rotate_half(x_row, cos_r, sin_r), _rotate_half(x_col, cos_c, sin_c)], axis=-1)


def _diff_softmax_mix(q, k, v, lam_full):
    s = jnp.einsum('bmqd,bmkd->bmqk', q, k).astype(jnp.float32)
    p = jax.nn.softmax(s, axis=-1)
    bsz, maps, nq, nk = p.shape
    p = p.reshape(bsz, maps // 2, 2, nq, nk)
    w = p[:, :, 0] - lam_full * p[:, :, 1]
    return jnp.einsum('bhqk,bhkv->bhqv', w.astype(v.dtype), v)


def _diff_attention_blocked(q, k, v, lam_full):
    bsz, maps, t, d = q.shape
    nblk = t // Q_BLOCK
    qb = jnp.moveaxis(q.reshape(bsz, maps, nblk, Q_BLOCK, d), 2, 0)
    out = lax.map(lambda qq: _diff_softmax_mix(qq, k, v, lam_full), qb)
    return jnp.moveaxis(out, 0, 2).reshape(bsz, maps // 2, t, v.shape[-1])


def _diff_heads_merge(o, subln_w, lam_init):
    o = _rmsnorm(o, subln_w, eps=1e-5) * (1 - lam_init)
    bsz, h, t, e = o.shape
    return o.transpose(0, 2, 1, 3).reshape(bsz, t, h * e)


def _conformer_conv(u, conv_w, conv_b, ln_w, ln_b):
    a, g = jnp.split(u, 2, axis=-1)
    y = _dwconv(a * jax.nn.sigmoid(g), conv_w, conv_b)
    return jax.nn.silu(_layernorm(y, ln_w, ln_b))


def _even_mixer(a_lat, a_ctx, w_in, w_out, lam, subln_w, conv_w, conv_b, ln_w, ln_b, rope, layer_idx, need_ctx):
    bsz, n_lat, _ = a_lat.shape
    n_ctx = a_ctx.shape[1]
    p_lat = a_lat @ w_in
    p_ctx = a_ctx @ (w_in if need_ctx else w_in[:, :EV_Q0])
    q_scale = A_HEAD_DIM ** -0.5
    k_lat = _rope_2d(p_lat[..., :EV_V0].reshape(bsz, n_lat, 2 * A_HEADS, A_HEAD_DIM), rope)
    q_lat = _rope_2d(p_lat[..., EV_Q0:EV_B0].reshape(bsz, n_lat, 2 * A_HEADS, A_HEAD_DIM), rope) * q_scale
    v_lat = p_lat[..., EV_V0:EV_Q0].reshape(bsz, n_lat, A_HEADS, 2 * A_HEAD_DIM)
    k_ctx = p_ctx[..., :EV_V0].reshape(bsz, n_ctx, 2 * A_HEADS, A_HEAD_DIM)
    v_ctx = p_ctx[..., EV_V0:EV_Q0].reshape(bsz, n_ctx, A_HEADS, 2 * A_HEAD_DIM)
    k_all = jnp.concatenate([k_ctx, k_lat], axis=1).transpose(0, 2, 1, 3)
    v_all = jnp.concatenate([v_ctx, v_lat], axis=1).transpose(0, 2, 1, 3)
    lam_f = lam.astype(jnp.float32)
    lam_init = 0.8 - 0.6 * math.exp(-0.3 * layer_idx)
    lam_full = jnp.exp(jnp.sum(lam_f[0] * lam_f[1])) - jnp.exp(jnp.sum(lam_f[2] * lam_f[3])) + lam_init
    o_lat = _diff_attention_blocked(q_lat.transpose(0, 2, 1, 3), k_all, v_all, lam_full)
    y_lat = jnp.concatenate([_diff_heads_merge(o_lat, subln_w, lam_init),
                             _conformer_conv(p_lat[..., EV_B0:], conv_w, conv_b, ln_w, ln_b)], axis=-1) @ w_out
    if not need_ctx:
        return y_lat, None
    q_ctx = p_ctx[..., EV_Q0:EV_B0].reshape(bsz, n_ctx, 2 * A_HEADS, A_HEAD_DIM) * q_scale
    o_ctx = _diff_softmax_mix(q_ctx.transpose(0, 2, 1, 3), k_ctx.transpose(0, 2, 1, 3),
                              v_ctx.transpose(0, 2, 1, 3), lam_full)
    y_ctx = jnp.concatenate([_diff_heads_merge(o_ctx, subln_w, lam_init),
                             _conformer_conv(p_ctx[..., EV_B0:], conv_w, conv_b, ln_w, ln_b)], axis=-1) @ w_out
    return y_lat, y_ctx


def _linear_recurrence(e1, e2):
    a1, b1 = e1
    a2, b2 = e2
    return a1 * a2, a2 * b1 + b2


def _s5_states(u_c, lam_bar, b_bar, h0):
    bu = jnp.einsum('btgh,gph->btgp', u_c, b_bar)
    if h0 is not None:
        bu = bu.at[:, 0].add(lam_bar * h0)
    a = jnp.broadcast_to(lam_bar, bu.shape)
    _, s = lax.associative_scan(_linear_recurrence, (a, bu), axis=1)
    return s


def _s5_branch(u_lat, u_ctx, lam_re, lam_im, log_step, b_re, b_im, c_re, c_im, d_skip, glu_w, glu_b, need_ctx):
    f32 = jnp.float32
    lam = lax.complex(lam_re.astype(f32), lam_im.astype(f32))
    lam_bar = jnp.exp(lam * jnp.exp(log_step.astype(f32))[..., None])
    b_mat = lax.complex(b_re.astype(f32), b_im.astype(f32))
    b_bar = ((lam_bar - 1.0) / lam)[..., None] * b_mat
    c_mat = lax.complex(c_re.astype(f32), c_im.astype(f32))

    def grouped(u):
        return u.astype(f32).reshape(u.shape[0], u.shape[1], S5_GROUPS, S5_GROUP).astype(jnp.complex64)

    def rev(u):
        return u[:, ::-1]

    uc_ctx, uc_lat = grouped(u_ctx), grouped(u_lat)
    s_cf = _s5_states(uc_ctx, lam_bar[0], b_bar[0], None)
    s_cb = _s5_states(rev(uc_ctx), lam_bar[1], b_bar[1], None)
    s_lf = _s5_states(uc_lat, lam_bar[0], b_bar[0], s_cf[:, -1])
    s_lb = _s5_states(rev(uc_lat), lam_bar[1], b_bar[1], s_cb[:, -1])

    def finish(s_f, s_b_rev, u):
        bsz, n = u.shape[:2]
        y = (jnp.einsum('btgp,ghp->btgh', s_f, c_mat[0])
             + rev(jnp.einsum('btgp,ghp->btgh', s_b_rev, c_mat[1]))).real
        y = y.reshape(bsz, n, S5_WIDTH) + d_skip.astype(f32) * u.astype(f32)
        g = jax.nn.gelu(y).astype(u.dtype)
        return g * jax.nn.sigmoid(g @ glu_w + glu_b)

    y_lat = finish(s_lf, s_lb, u_lat)
    y_ctx = finish(s_cf, s_cb, u_ctx) if need_ctx else None
    return y_lat, y_ctx


def _segsum(a):
    cs = jnp.cumsum(a, axis=-1)
    T = a.shape[-1]
    diff = cs[..., :, None] - cs[..., None, :]
    return jnp.where(jnp.tril(jnp.ones((T, T), dtype=bool)), diff, -jnp.inf)


def _ssd_chunked(x, dt, a, bm, cm, h0, with_output):
    f32 = jnp.float32
    bsz, t, h, p = x.shape
    g, n = bm.shape[2], bm.shape[3]
    r = h // g
    nc, l = t // SSD_CHUNK, SSD_CHUNK
    xs = (x.astype(f32) * dt[..., None]).reshape(bsz, nc, l, g, r, p)
    da = (dt * a).reshape(bsz, nc, l, g, r).transpose(0, 3, 4, 1, 2)
    bs = bm.astype(f32).reshape(bsz, nc, l, g, n)
    cs = cm.astype(f32).reshape(bsz, nc, l, g, n)
    da_cum = jnp.cumsum(da, axis=-1)
    decay_states = jnp.exp(da_cum[..., -1:] - da_cum)
    states = jnp.einsum('bclgn,bgrcl,bclgrp->bcgrpn', bs, decay_states, xs)
    states = jnp.concatenate([h0.astype(f32).reshape(bsz, 1, g, r, p, n), states], axis=1)
    chunk_tot = jnp.pad(da_cum[..., -1], ((0, 0), (0, 0), (0, 0), (1, 0)))
    decay_chunk = jnp.exp(_segsum(chunk_tot))
    new_states = jnp.einsum('bgrzc,bcgrpn->bzgrpn', decay_chunk, states)
    final = new_states[:, -1].reshape(bsz, h, p, n)
    if not with_output:
        return None, final
    L = jnp.exp(_segsum(da))
    y_diag = jnp.einsum('bclgn,bcsgn,bgrcls,bcsgrp->bclgrp', cs, bs, L, xs)
    y_off = jnp.einsum('bclgn,bcgrpn,bgrcl->bclgrp', cs, new_states[:, :-1], jnp.exp(da_cum))
    return (y_diag + y_off).reshape(bsz, t, h, p), final


def _ssd_branch(xbc_lat, xbc_ctx, dt_lat, dt_ctx, z_lat, z_ctx, conv_w, conv_b, a_log, dt_bias, d_skip, norm_w,
                need_ctx):
    f32 = jnp.float32
    a = -jnp.exp(a_log.astype(f32))

    def prep(xbc, dt_raw):
        bsz, n, _ = xbc.shape
        xbc = jax.nn.silu(_dwconv(xbc, conv_w, conv_b))
        xs = xbc[..., :M2_INNER].reshape(bsz, n, M2_HEADS, M2_HEAD_DIM)
        bm = xbc[..., M2_INNER:M2_INNER + M2_GROUPS * M2_STATE].reshape(bsz, n, M2_GROUPS, M2_STATE)
        cm = xbc[..., M2_INNER + M2_GROUPS * M2_STATE:].reshape(bsz, n, M2_GROUPS, M2_STATE)
        dt = jax.nn.softplus(dt_raw.astype(f32).reshape(bsz, n, 2, M2_HEADS) + dt_bias.astype(f32))
        return xs, bm, cm, dt

    def rev(u):
        return u[:, ::-1]

    xs_c, b_c, c_c, dt_c = prep(xbc_ctx, dt_ctx)
    xs_l, b_l, c_l, dt_l = prep(xbc_lat, dt_lat)
    h0 = jnp.zeros((xs_c.shape[0], M2_HEADS, M2_HEAD_DIM, M2_STATE), f32)
    y_cf, h_f = _ssd_chunked(xs_c, dt_c[:, :, 0], a[0], b_c, c_c, h0, need_ctx)
    y_cb, h_b = _ssd_chunked(rev(xs_c), rev(dt_c[:, :, 1]), a[1], rev(b_c), rev(c_c), h0, need_ctx)
    y_lf, _ = _ssd_chunked(xs_l, dt_l[:, :, 0], a[0], b_l, c_l, h_f, True)
    y_lb, _ = _ssd_chunked(rev(xs_l), rev(dt_l[:, :, 1]), a[1], rev(b_l), rev(c_l), h_b, True)

    def finish(y_f, y_b_rev, xs, z):
        bsz, n = z.shape[:2]
        y = y_f + rev(y_b_rev) + d_skip.astype(f32)[:, None] * xs.astype(f32)
        y = y.reshape(bsz, n, M2_INNER) * jax.nn.silu(z.astype(f32))
        y = y.reshape(bsz, n, M2_GROUPS, M2_INNER // M2_GROUPS)
        y = y * lax.rsqrt(jnp.mean(y * y, axis=-1, keepdims=True) + RMS_EPS)
        return (y.reshape(bsz, n, M2_INNER) * norm_w.astype(f32)).astype(z.dtype)

    y_lat = finish(y_lf, y_lb, xs_l, z_lat)
    y_ctx = finish(y_cf, y_cb, xs_c, z_ctx) if need_ctx else None
    return y_lat, y_ctx


def _odd_mixer(a_lat, a_ctx, w_in, w_out, lam_re, lam_im, log_step, b_re, b_im, c_re, c_im, s5_d, glu_w, glu_b,
               conv_w, conv_b, a_log, dt_bias, m2_d, m2_norm_w, need_ctx):
    p_lat = a_lat @ w_in
    p_ctx = a_ctx @ (w_in if need_ctx else w_in[:, :OD_Z0])
    s5_lat, s5_ctx = _s5_branch(p_lat[..., :S5_WIDTH], p_ctx[..., :S5_WIDTH], lam_re, lam_im, log_step,
                                b_re, b_im, c_re, c_im, s5_d, glu_w, glu_b, need_ctx)
    ssd_lat, ssd_ctx = _ssd_branch(p_lat[..., S5_WIDTH:OD_DT0], p_ctx[..., S5_WIDTH:OD_DT0],
                                   p_lat[..., OD_DT0:OD_Z0], p_ctx[..., OD_DT0:OD_Z0],
                                   p_lat[..., OD_Z0:], p_ctx[..., OD_Z0:] if need_ctx else None,
                                   conv_w, conv_b, a_log, dt_bias, m2_d, m2_norm_w, need_ctx)
    y_lat = jnp.concatenate([s5_lat, ssd_lat], axis=-1) @ w_out
    y_ctx = (jnp.concatenate([s5_ctx, ssd_ctx], axis=-1) @ w_out) if need_ctx else None
    return y_lat, y_ctx


def setup_inputs(seed: int = 0) -> dict:
    key = jax.random.key(seed)
    ks = iter(jax.random.split(key, 48))
    f32 = jnp.float32
    D = D_MODEL

    def nrm(shape, scale):
        return scale * jax.random.normal(next(ks), shape, f32)

    def unif(shape, lo, hi):
        return jax.random.uniform(next(ks), shape, f32, lo, hi)

    x = nrm((BATCH, SEQ, D), 1.0)
    c = nrm((BATCH, D), 1.0)
    ctx = nrm((BATCH, CTX_LEN, D), 1.0)
    c_ctx = nrm((D,), 1.0)
    ada_w = nrm((DEPTH, D, 6 * D), 0.5 * D ** -0.5)
    ada_b = nrm((DEPTH, 6 * D), 0.02)
    norm_w = 1.0 + nrm((DEPTH, 2, D), 0.02)
    ffn_w1 = nrm((DEPTH, D, FFN_HIDDEN), D ** -0.5)
    ffn_w3 = nrm((DEPTH, D, FFN_HIDDEN), D ** -0.5)
    ffn_w2 = nrm((DEPTH, FFN_HIDDEN, D), FFN_HIDDEN ** -0.5)
    ev_w_in = nrm((N_EVEN, D, EV_IN), D ** -0.5)
    ev_w_out = nrm((N_EVEN, A_WIDTH + B_WIDTH, D), (A_WIDTH + B_WIDTH) ** -0.5)
    ev_lambda = nrm((N_EVEN, 4, A_HEAD_DIM), 0.1)
    ev_subln_w = 1.0 + nrm((N_EVEN, 2 * A_HEAD_DIM), 0.02)
    ev_conv_w = nrm((N_EVEN, CONV_WIDTH, B_WIDTH), CONV_WIDTH ** -0.5)
    ev_conv_b = nrm((N_EVEN, B_WIDTH), 0.02)
    ev_ln_w = 1.0 + nrm((N_EVEN, B_WIDTH), 0.02)
    ev_ln_b = nrm((N_EVEN, B_WIDTH), 0.02)
    od_w_in = nrm((N_ODD, D, OD_IN), D ** -0.5)
    od_w_out = nrm((N_ODD, S5_WIDTH + M2_INNER, D), (S5_WIDTH + M2_INNER) ** -0.5)
    s5_lam_re = -0.5 + nrm((N_ODD, 2, S5_GROUPS, S5_STATE), 0.01)
    s5_lam_im = math.pi * jnp.arange(S5_STATE, dtype=f32) + nrm((N_ODD, 2, S5_GROUPS, S5_STATE), 0.01)
    s5_log_step = unif((N_ODD, 2, S5_GROUPS), math.log(1e-3), math.log(1e-1))
    s5_b_re = nrm((N_ODD, S5_GROUPS, S5_STATE, S5_GROUP), (2 * S5_GROUP) ** -0.5)
    s5_b_im = nrm((N_ODD, S5_GROUPS, S5_STATE, S5_GROUP), (2 * S5_GROUP) ** -0.5)
    s5_c_re = nrm((N_ODD, 2, S5_GROUPS, S5_GROUP, S5_STATE), (2 * S5_STATE) ** -0.5)
    s5_c_im = nrm((N_ODD, 2, S5_GROUPS, S5_GROUP, S5_STATE), (2 * S5_STATE) ** -0.5)
    s5_d = nrm((N_ODD, S5_WIDTH), 0.5)
    s5_glu_w = nrm((N_ODD, S5_WIDTH, S5_WIDTH), S5_WIDTH ** -0.5)
    s5_glu_b = nrm((N_ODD, S5_WIDTH), 0.02)
    m2_conv_w = nrm((N_ODD, M2_CONV, M2_CONV_DIM), M2_CONV ** -0.5)
    m2_conv_b = nrm((N_ODD, M2_CONV_DIM), 0.02)
    m2_a_log = jnp.log(unif((N_ODD, 2, M2_HEADS), 1.0, 16.0))
    dt0 = jnp.exp(unif((N_ODD, 2, M2_HEADS), math.log(1e-3), math.log(1e-1)))
    m2_dt_bias = dt0 + jnp.log(-jnp.expm1(-dt0))
    m2_d = 1.0 + nrm((N_ODD, M2_HEADS), 0.02)
    m2_norm_w = 1.0 + nrm((N_ODD, M2_INNER), 0.02)
    final_norm_w = 1.0 + nrm((D,), 0.02)
    return {'x': x, 'c': c, 'ctx': ctx, 'c_ctx': c_ctx, 'ada_w': ada_w, 'ada_b': ada_b, 'norm_w': norm_w,
            'ffn_w1': ffn_w1, 'ffn_w3': ffn_w3, 'ffn_w2': ffn_w2,
            'ev_w_in': ev_w_in, 'ev_w_out': ev_w_out, 'ev_lambda': ev_lambda, 'ev_subln_w': ev_subln_w,
            'ev_conv_w': ev_conv_w, 'ev_conv_b': ev_conv_b, 'ev_ln_w': ev_ln_w, 'ev_ln_b': ev_ln_b,
            'od_w_in': od_w_in, 'od_w_out': od_w_out, 's5_lam_re': s5_lam_re, 's5_lam_im': s5_lam_im,
            's5_log_step': s5_log_step, 's5_b_re': s5_b_re, 's5_b_im': s5_b_im, 's5_c_re': s5_c_re,
            's5_c_im': s5_c_im, 's5_d': s5_d, 's5_glu_w': s5_glu_w, 's5_glu_b': s5_glu_b,
            'm2_conv_w': m2_conv_w, 'm2_conv_b': m2_conv_b, 'm2_a_log': m2_a_log, 'm2_dt_bias': m2_dt_bias,
            'm2_d': m2_d, 'm2_norm_w': m2_norm_w, 'final_norm_w': final_norm_w}


def reference(x, c, ctx, c_ctx, ada_w, ada_b, norm_w, ffn_w1, ffn_w3, ffn_w2,
              ev_w_in, ev_w_out, ev_lambda, ev_subln_w, ev_conv_w, ev_conv_b, ev_ln_w, ev_ln_b,
              od_w_in, od_w_out, s5_lam_re, s5_lam_im, s5_log_step, s5_b_re, s5_b_im, s5_c_re, s5_c_im,
              s5_d, s5_glu_w, s5_glu_b, m2_conv_w, m2_conv_b, m2_a_log, m2_dt_bias, m2_d, m2_norm_w,
              final_norm_w):
    n_rows = x.shape[1] // GRID_W
    rope = _axial_rope_tables(n_rows)
    silu_c = jax.nn.silu(c)[:, None, :]
    silu_cc = jax.nn.silu(c_ctx)
    h_lat, h_ctx = x, ctx
    for i in range(DEPTH):
        need_ctx = i < DEPTH - 1
        mod_lat = jnp.split(silu_c @ ada_w[i] + ada_b[i], 6, axis=-1)
        mod_ctx = jnp.split(silu_cc @ ada_w[i] + ada_b[i], 6, axis=-1)
        a_lat = _modulate(_rmsnorm(h_lat, norm_w[i, 0]), mod_lat[0], mod_lat[1])
        a_ctx = _modulate(_rmsnorm(h_ctx, norm_w[i, 0]), mod_ctx[0], mod_ctx[1])
        j = i // 2
        if i % 2 == 0:
            m_lat, m_ctx = _even_mixer(a_lat, a_ctx, ev_w_in[j], ev_w_out[j], ev_lambda[j], ev_subln_w[j],
                                       ev_conv_w[j], ev_conv_b[j], ev_ln_w[j], ev_ln_b[j], rope, i, need_ctx)
        else:
            m_lat, m_ctx = _odd_mixer(a_lat, a_ctx, od_w_in[j], od_w_out[j], s5_lam_re[j], s5_lam_im[j],
                                      s5_log_step[j], s5_b_re[j], s5_b_im[j], s5_c_re[j], s5_c_im[j], s5_d[j],
                                      s5_glu_w[j], s5_glu_b[j], m2_conv_w[j], m2_conv_b[j], m2_a_log[j],
                                      m2_dt_bias[j], m2_d[j], m2_norm_w[j], need_ctx)
        h_lat = h_lat + mod_lat[2] * m_lat
        h_lat = h_lat + mod_lat[5] * _swiglu(_modulate(_rmsnorm(h_lat, norm_w[i, 1]), mod_lat[3], mod_lat[4]),
                                             ffn_w1[i], ffn_w3[i], ffn_w2[i])
        if need_ctx:
            h_ctx = h_ctx + mod_ctx[2] * m_ctx
            h_ctx = h_ctx + mod_ctx[5] * _swiglu(_modulate(_rmsnorm(h_ctx, norm_w[i, 1]), mod_ctx[3], mod_ctx[4]),
                                                 ffn_w1[i], ffn_w3[i], ffn_w2[i])
    return _rmsnorm(h_lat, final_norm_w)
```

```python
import contextlib
import math
import numpy as np
import ml_dtypes
import concourse.bass as bass
import concourse.mybir as mybir
from concourse.bass_utils import run_bass_kernel_spmd

F32 = mybir.dt.float32
BF16 = mybir.dt.bfloat16
I32 = mybir.dt.int32
AF = mybir.ActivationFunctionType
ALU = mybir.AluOpType
AX = mybir.AxisListType

NCORES = 8
D = 4096
DC = 32
SEQ = 8192
CTX = 256
NTOK = SEQ + CTX
LAT_PC = SEQ // NCORES
CTX_PC = CTX // NCORES
FFN_H = 11008
HC = FFN_H // 128
RMS_EPS = 1e-6
LN_EPS = 1e-5

ENGS = ("pe", "act", "dve", "pool", "sp")
EPOCH = 30000
DMA_K = 8


class Sched:
    def __init__(self, nc):
        self.nc = nc
        self.ops = {e: [] for e in ENGS}
        self.cnt = {e: 0 for e in ENGS}
        self.dcnt = {e: 0 for e in ENGS}
        self.last_w = {}
        self.readers = {}
        self.waited = {e: {} for e in ENGS}
        self.dma_tokens = []
        self.bar_deps = []
        self.bar_pending = set()
        self.last_dma = {}

    def barrier(self):
        deps = list(self.last_dma.values())
        for e in ENGS:
            if self.cnt[e] > 0:
                idx = self.cnt[e] - 1
                deps.append((("c", e, idx // EPOCH), idx % EPOCH + 1, e))
        self.bar_deps = deps
        self.bar_pending = set(ENGS)

    def _deps(self, reads, writes):
        deps = []
        for k in reads:
            t = self.last_w.get(k)
            if t is not None:
                deps.append(t)
        for k in writes:
            t = self.last_w.get(k)
            if t is not None:
                deps.append(t)
            deps.extend(self.readers.get(k, ()))
        return deps

    def _commit(self, tok, reads, writes):
        for k in reads:
            if k in writes:
                continue
            self.readers.setdefault(k, []).append(tok)
        for k in writes:
            self.last_w[k] = tok
            self.readers[k] = []

    def _waits(self, eng, deps):
        best = {}
        for (sk, val, src) in deps:
            if best.get(sk, 0) < val:
                best[sk] = val
        out = []
        for sk, val in best.items():
            if sk[0] == "c":
                lin = sk[2] * EPOCH + val
                key = ("clin", sk[1])
                if self.waited[eng].get(key, 0) >= lin:
                    continue
                self.waited[eng][key] = lin
                out.append((sk, val))
            else:
                if self.waited[eng].get(sk, 0) >= val:
                    continue
                self.waited[eng][sk] = val
                out.append((sk, val))
        return out

    def op(self, eng, fn, reads=(), writes=()):
        reads = tuple(reads)
        writes = tuple(writes)
        deps = self._deps(reads, writes)
        if eng in self.bar_pending:
            deps = deps + self.bar_deps
            self.bar_pending.discard(eng)
        waits = self._waits(eng, deps)
        idx = self.cnt[eng]
        self.cnt[eng] += 1
        sk = ("c", eng, idx // EPOCH)
        tok = (sk, idx % EPOCH + 1, eng)
        self.ops[eng].append((fn, waits, (sk, 1)))
        self._commit(tok, reads, writes)
        return tok

    def dma(self, eng, fn, reads=(), writes=()):
        reads = tuple(reads)
        writes = tuple(writes)
        deps = self._deps(reads, writes)
        n = self.dcnt[eng]
        self.dcnt[eng] += 1
        slot = n % DMA_K
        rnd = n // DMA_K
        sk = ("d", eng, slot)
        if rnd > 0:
            deps.append((sk, 16 * rnd, eng))
        if eng in self.bar_pending:
            deps = deps + self.bar_deps
            self.bar_pending.discard(eng)
        waits = self._waits(eng, deps)
        tok = (sk, 16 * (rnd + 1), eng)
        self.last_dma[sk] = tok
        self.ops[eng].append((fn, waits, (sk, 16)))
        self._commit(tok, reads, writes)
        self.dma_tokens.append(tok)
        return tok

    def emit(self):
        nc = self.nc
        final_deps = list(self.dma_tokens)
        for e in ENGS:
            if self.cnt[e] > 0:
                idx = self.cnt[e] - 1
                final_deps.append((("c", e, idx // EPOCH), idx % EPOCH + 1, e))
        fw = self._waits("sp", final_deps)
        self.ops["sp"].append((None, fw, None))
        semkeys = set()
        for e in ENGS:
            for (fn, waits, inc) in self.ops[e]:
                for (sk, v) in waits:
                    semkeys.add(sk)
                if inc is not None:
                    semkeys.add(inc[0])
        semkeys = sorted(semkeys)
        with contextlib.ExitStack() as st:
            sems = {}
            for sk in semkeys:
                sems[sk] = st.enter_context(nc.semaphore("s_%s_%s_%d" % sk))
            block = st.enter_context(nc.Block())
            handles = {"pe": block.tensor, "act": block.scalar, "dve": block.vector,
                       "pool": block.gpsimd, "sp": block.sync}
            for e in ENGS:
                ops = self.ops[e]
                if not ops:
                    continue

                def body(eng, ops=ops):
                    for (fn, waits, inc) in ops:
                        for (sk, v) in waits:
                            eng.wait_ge(sems[sk], v)
                        if fn is not None:
                            ins = fn(eng)
                            ins.then_inc(sems[inc[0]], inc[1])
                handles[e](body)


class B:
    def __init__(self):
        self.nc = bass.Bass("TRN2", target_bir_lowering=False)
        self.S = Sched(self.nc)
        self.st = contextlib.ExitStack()
        self.nps = 0

    def din(self, name, shape, dt=F32):
        return self.nc.dram_tensor(name, list(shape), dt, kind="ExternalInput").ap()

    def dout(self, name, shape, dt=F32):
        return self.nc.dram_tensor(name, list(shape), dt, kind="ExternalOutput").ap()

    def dint(self, name, shape, dt=F32):
        return self.nc.dram_tensor(name, list(shape), dt, kind="Internal").ap()

    def sb(self, name, shape, dt=F32):
        return self.st.enter_context(self.nc.sbuf_tensor(name, list(shape), dt))

    def ps(self, name, shape=(128, 512), dt=F32):
        self.nps += 1
        return self.st.enter_context(self.nc.psum_tensor(name, list(shape), dt))

    def finish(self):
        self.S.emit()
        self.st.close()
        return self.nc


TRACE = False
DEBUG = False
DBG = {}
LAST_NS = []


RUN_CORES = NCORES


def run(nc, in_maps):
    if RUN_CORES < NCORES:
        res = run_bass_kernel_spmd(nc, in_maps[:RUN_CORES], core_ids=list(range(RUN_CORES)), trace=TRACE)
        if TRACE:
            print("exec_time_ns", res.exec_time_ns, flush=True)
        return list(res.results) + [res.results[0]] * (NCORES - RUN_CORES)
    if TRACE:
        res = run_bass_kernel_spmd(nc, in_maps, core_ids=list(range(NCORES)), trace=True)
        LAST_NS.append(res.exec_time_ns)
        print("exec_time_ns", res.exec_time_ns, flush=True)
    else:
        res = run_bass_kernel_spmd(nc, in_maps, core_ids=list(range(NCORES)))
    return res.results


def tab(v):
    v = np.asarray(v, np.float32).reshape(-1, 128)
    return np.ascontiguousarray(v.T)


A_COLS = 6 * D // NCORES


def build_A():
    b = B()
    S = b.S
    cT = b.din("cT", [128, DC * 2])
    W = b.din("W", [2, D, A_COLS])
    bias = b.din("bias", [2, 2 * A_COLS])
    out = b.dout("out", [2, 2 * A_COLS])
    ct = b.sb("ct", [128, DC * 2])
    s = b.sb("s", [128, DC * 2])
    bt = b.sb("bt", [2, 2 * A_COLS])
    ot = b.sb("ot", [2, 2 * A_COLS])
    wts = [b.sb("wt%d" % i, [128, DC * 512]) for i in range(2)]
    pss = [b.ps("ps%d" % i) for i in range(2)]
    S.dma("sp", lambda e: e.dma_start(out=ct[:], in_=cT), writes=["ct"])
    S.dma("sp", lambda e: e.dma_start(out=bt[:], in_=bias), writes=["bt"])
    S.op("act", lambda e: e.activation(out=s[:], in_=ct[:], func=AF.Silu), reads=["ct"], writes=["s"])
    items = [(i, n) for i in range(2) for n in range(A_COLS // 512)]
    for it, (i, n) in enumerate(items):
        wt = wts[it % 2]
        ps = pss[it % 2]
        wk = "wt%d" % (it % 2)
        pk = "ps%d" % (it % 2)
        src = W[i].rearrange("(kc p) n -> p kc n", p=128)[:, :, n * 512:(n + 1) * 512]
        S.dma("sp", lambda e, wt=wt, src=src: e.dma_start(
            out=wt[:].rearrange("p (kc n) -> p kc n", n=512), in_=src), writes=[wk])

        def mm(e, wt=wt, ps=ps):
            ins = None
            for kc in range(DC):
                ins = e.matmul(ps[0:2, :], s[:, kc * 2:(kc + 1) * 2], wt[:, kc * 512:(kc + 1) * 512],
                               start=(kc == 0), stop=(kc == DC - 1))
            return ins
        S.op("pe", mm, reads=[wk, "s"], writes=[pk])
        c0 = i * A_COLS + n * 512
        S.op("dve", lambda e, ps=ps, c0=c0: e.tensor_tensor(out=ot[:, c0:c0 + 512], in0=ps[0:2, :],
                                                            in1=bt[:, c0:c0 + 512], op=ALU.add),
             reads=[pk, "bt"], writes=["ot"])
    S.dma("sp", lambda e: e.dma_start(out=out, in_=ot[:]), reads=["ot"])
    return b.finish()


def stage_A(inp):
    c2 = np.concatenate([inp["c"].reshape(1, D), inp["c_ctx"].reshape(1, D)], 0)
    cT = np.ascontiguousarray(c2.reshape(2, DC, 128).transpose(2, 1, 0)).reshape(128, DC * 2)
    maps = []
    for j in range(NCORES):
        sl = slice(j * A_COLS, (j + 1) * A_COLS)
        Wj = np.ascontiguousarray(inp["ada_w"][:, :, sl])
        bj = np.ascontiguousarray(inp["ada_b"][:, sl]).reshape(1, 2 * A_COLS)
        maps.append({"cT": cT, "W": Wj, "bias": np.ascontiguousarray(np.repeat(bj, 2, 0))})
    res = run(build_A(), maps)
    mod = np.zeros((2, 2, 6 * D), np.float32)
    for j in range(NCORES):
        o = res[j]["out"].reshape(2, 2, A_COLS)
        for i in range(2):
            mod[i, :, j * A_COLS:(j + 1) * A_COLS] = o[:, i, :]
    return mod


def mod_tables(mod_i):
    return np.ascontiguousarray(np.concatenate([tab(mod_i[r, v * D:(v + 1) * D]) for r in range(2) for v in range(6)], 1))


def mcol(modt, r, v, c0=0, c1=DC):
    base = (r * 6 + v) * DC
    return modt[:, base + c0:base + c1]


def emit_weff(b, modt, nwt, mk, nk, v_scale, name):
    S = b.S
    weff = b.sb(name, [128, 2 * DC])
    for r in range(2):
        S.op("dve", lambda e, r=r: e.scalar_tensor_tensor(
            out=weff[:, r * DC:(r + 1) * DC], in0=mcol(modt, r, v_scale), scalar=1.0, in1=nwt[:],
            op0=ALU.add, op1=ALU.mult), reads=[mk, nk], writes=[name])
    return weff


def emit_rstd(b, src, srck, C, N, ones, sq, sqk, ps, psk, rstd, rstdk, eps, ncols_feat):
    S = b.S
    S.op("act", lambda e: e.activation(out=sq[:, 0:C * N], in_=src, func=AF.Square) if False else
         e.activation(out=sq[:, 0:C * N].rearrange("p (c n) -> p c n", n=N), in_=src, func=AF.Square),
         reads=[srck], writes=[sqk])

    def mm(e):
        ins = None
        for c in range(C):
            ins = e.matmul(ps[:, 0:N], ones[:], sq[:, c * N:(c + 1) * N], start=(c == 0), stop=(c == C - 1))
        return ins
    S.op("pe", mm, reads=[sqk], writes=[psk])
    S.op("act", lambda e: e.activation(out=rstd[:, 0:N], in_=ps[:, 0:N], func=AF.Sqrt, scale=1.0 / ncols_feat, bias=eps),
         reads=[psk], writes=[rstdk])
    S.op("dve", lambda e: e.reciprocal(out=rstd[:, 0:N], in_=rstd[:, 0:N]), reads=[rstdk], writes=[rstdk])


TOK_PC = LAT_PC + CTX_PC


def build_B():
    b = B()
    S = b.S
    xt = b.din("xt", [TOK_PC, D])
    modT = b.din("modT", [128, 12 * DC])
    nwT = b.din("nwT", [128, DC])
    identD = b.din("ident", [128, 128])
    epsD = b.din("epsv", [128, 1])
    hT = b.dout("hT", [D, TOK_PC])
    aT = b.dout("aT", [D, TOK_PC], BF16)
    modt = b.sb("modt", [128, 12 * DC])
    nwt = b.sb("nwt", [128, DC])
    ident = b.sb("identt", [128, 128])
    epst = b.sb("epst", [128, 1])
    ones = b.sb("ones", [128, 128], BF16)
    S.dma("sp", lambda e: e.dma_start(out=modt[:], in_=modT), writes=["modt"])
    S.dma("sp", lambda e: e.dma_start(out=nwt[:], in_=nwT), writes=["nwt"])
    S.dma("sp", lambda e: e.dma_start(out=ident[:], in_=identD), writes=["ident"])
    S.dma("sp", lambda e: e.dma_start(out=epst[:], in_=epsD), writes=["epst"])
    S.op("dve", lambda e: e.memset(ones[:], 1.0), writes=["ones"])
    weff = emit_weff(b, modt, nwt, "modt", "nwt", 1, "weff")
    xin = [b.sb("xin%d" % i, [128, D]) for i in range(2)]
    hTt = [b.sb("hTt%d" % i, [128, DC * 128]) for i in range(2)]
    xn = b.sb("xn", [128, DC * 128])
    aTt = [b.sb("aTt%d" % i, [128, DC * 128], BF16) for i in range(2)]
    sq = b.sb("sq", [128, DC * 128], BF16)
    rstd = b.sb("rstd", [128, 128])
    pst = [b.ps("pst%d" % i) for i in range(4)]
    pss = b.ps("pss")
    tiles = [(i * 128, 128, 0) for i in range(8)] + [(LAT_PC, CTX_PC, 1)]
    for ti, (t0, T, r) in enumerate(tiles):
        xi = xin[ti % 2]; xk = "xin%d" % (ti % 2)
        ht = hTt[ti % 2]; hk = "hTt%d" % (ti % 2)
        at = aTt[ti % 2]; ak = "aTt%d" % (ti % 2)
        S.dma("sp", lambda e, xi=xi, t0=t0, T=T: e.dma_start(out=xi[0:T, :], in_=xt[t0:t0 + T, :]), writes=[xk])
        for cg in range(8):
            ps = pst[cg % 4]; pk = "pst%d" % (cg % 4)

            def tr(e, ps=ps, xi=xi, cg=cg, T=T):
                ins = None
                for k in range(4):
                    c = cg * 4 + k
                    ins = e.transpose(ps[:, k * T:(k + 1) * T], xi[0:T, c * 128:(c + 1) * 128], ident[0:T, 0:T])
                return ins
            S.op("pe", tr, reads=[xk, "ident"], writes=[pk])
            eng = "dve" if cg % 2 == 0 else "act"
            if eng == "dve":
                S.op("dve", lambda e, ps=ps, ht=ht, cg=cg, T=T: e.tensor_copy(ht[:, cg * 4 * T:(cg + 1) * 4 * T], ps[:, 0:4 * T]),
                     reads=[pk], writes=[hk + "_%d" % cg])
            else:
                S.op("act", lambda e, ps=ps, ht=ht, cg=cg, T=T: e.activation(out=ht[:, cg * 4 * T:(cg + 1) * 4 * T], in_=ps[:, 0:4 * T], func=AF.Copy),
                     reads=[pk], writes=[hk + "_%d" % cg])
        hks = [hk + "_%d" % cg for cg in range(8)]
        hv = ht[:, 0:DC * T].rearrange("p (c n) -> p c n", n=T)
        S.dma("sp", lambda e, hv=hv, t0=t0, T=T: e.dma_start(
            out=hT.rearrange("(c p) t -> p c t", p=128)[:, :, t0:t0 + T], in_=hv), reads=hks)
        S.op("act", lambda e, hv=hv, T=T: e.activation(out=sq[:, 0:DC * T].rearrange("p (c n) -> p c n", n=T), in_=hv, func=AF.Square),
             reads=hks, writes=["sq"])

        def mm(e, T=T):
            ins = None
            for c in range(DC):
                ins = e.matmul(pss[:, 0:T], ones[:], sq[:, c * T:(c + 1) * T], start=(c == 0), stop=(c == DC - 1))
            return ins
        S.op("pe", mm, reads=["sq", "ones"], writes=["pss"])
        S.op("act", lambda e, T=T: e.activation(out=rstd[:, 0:T], in_=pss[:, 0:T], func=AF.Sqrt, scale=1.0 / D, bias=epst[:, 0:1]),
             reads=["pss", "epst"], writes=["rstd"])
        S.op("dve", lambda e, T=T: e.reciprocal(out=rstd[:, 0:T], in_=rstd[:, 0:T]), reads=["rstd"], writes=["rstd"])
        xnv = xn[:, 0:DC * T].rearrange("p (c n) -> p c n", n=T)
        S.op("dve", lambda e, hv=hv, xnv=xnv, T=T: e.tensor_tensor(
            out=xnv, in0=hv, in1=rstd[:, 0:T].unsqueeze(1).to_broadcast([128, DC, T]), op=ALU.mult),
            reads=hks + ["rstd"], writes=["xn"])
        S.op("dve", lambda e, xnv=xnv, T=T, r=r: e.tensor_tensor(
            out=xnv, in0=xnv, in1=weff[:, r * DC:(r + 1) * DC].unsqueeze(2).to_broadcast([128, DC, T]), op=ALU.mult),
            reads=["xn", "weff"], writes=["xn"])
        atv = at[:, 0:DC * T].rearrange("p (c n) -> p c n", n=T)
        S.op("dve", lambda e, xnv=xnv, atv=atv, T=T, r=r: e.tensor_tensor(
            out=atv, in0=xnv, in1=mcol(modt, r, 0).unsqueeze(2).to_broadcast([128, DC, T]), op=ALU.add),
            reads=["xn", "modt"], writes=[ak])
        S.dma("sp", lambda e, atv=atv, t0=t0, T=T: e.dma_start(
            out=aT.rearrange("(c p) t -> p c t", p=128)[:, :, t0:t0 + T], in_=atv), reads=[ak])
    return b.finish()


def tok_shard(lat, ctx, j):
    return np.ascontiguousarray(np.concatenate([lat[j * LAT_PC:(j + 1) * LAT_PC], ctx[j * CTX_PC:(j + 1) * CTX_PC]], 0))


def gather_T(parts):
    return np.ascontiguousarray(np.concatenate([p[:, LAT_PC:] for p in parts] + [p[:, :LAT_PC] for p in parts], 1))


def stage_B(inp, mod):
    modt = mod_tables(mod[0])
    nwt = tab(inp["norm_w"][0, 0])
    ident = np.eye(128, dtype=np.float32)
    epsv = np.full((128, 1), RMS_EPS, np.float32)
    x = inp["x"].reshape(SEQ, D)
    ctx = inp["ctx"].reshape(CTX, D)
    maps = [{"xt": tok_shard(x, ctx, j), "modT": modt, "nwT": nwt, "ident": ident, "epsv": epsv} for j in range(NCORES)]
    res = run(build_B(), maps)
    hT = [r["hT"] for r in res]
    aT = [r["aT"] for r in res]
    return hT, aT


EV_A = 2048
CW = 31
CPAD = 15
U_CTX0 = CPAD
U_LAT0 = CPAD + CTX + CPAD
U_LEN = U_LAT0 + SEQ + CPAD
NKB = NTOK // 128
LAM_INIT0 = 0.8 - 0.6 * math.exp(-0.3 * 0)


def supertiles():
    return [(0, CTX)] + [(CTX + 512 * i, 512) for i in range(SEQ // 512)]


def build_C(phases=(1, 2, 3)):
    b = B()
    S = b.S
    aTd = b.din("aT", [D, NTOK], BF16)
    wd = b.din("w", [D, 1280])
    ropeD = b.din("rope", [NTOK, 512])
    lamD = b.din("lam", [256])
    swD = b.din("subw", [128])
    cwD = b.din("cw", [128, 2 * CW])
    cbD = b.din("cb", [128, 2])
    identD = b.din("ident", [128, 128])
    attnT = b.dout("attnT", [256, NTOK], BF16)
    convT = b.dout("convT", [256, NTOK])
    KTd = b.dint("KTd", [2, 128, NTOK], BF16)
    QTd = b.dint("QTd", [2, 128, NTOK], BF16)
    Vd = b.dint("Vd", [NTOK, 256], BF16)
    UTd = b.dint("UTd", [2, 128, NTOK])

    ident = b.sb("identt", [128, 128])
    identb = b.sb("identb", [128, 128], BF16)
    S.dma("sp", lambda e: e.dma_start(out=ident[:], in_=identD), writes=["ident"])
    S.op("dve", lambda e: e.tensor_copy(identb[:], ident[:]), reads=["ident"], writes=["identb"])

    st1 = contextlib.ExitStack()
    Wt = st1.enter_context(b.nc.sbuf_tensor("Wt", [128, DC * 1280], BF16))
    Wv = Wt[:].rearrange("p (kc n) -> p kc n", n=1280)
    for g in range(8):
        S.dma("pool", lambda e, g=g: e.dma_start(
            out=Wv[:, g * 4:(g + 1) * 4, :], in_=wd.rearrange("(kc p) n -> p kc n", p=128)[:, g * 4:(g + 1) * 4, :]),
            writes=["W%d" % g])
    Wk = ["W%d" % g for g in range(8)]
    aTs = [st1.enter_context(b.nc.sbuf_tensor("aTs%d" % i, [128, DC * 512], BF16)) for i in range(2)]
    cs = [st1.enter_context(b.nc.sbuf_tensor("cs%d" % i, [128, 512], F32)) for i in range(2)]
    t1 = st1.enter_context(b.nc.sbuf_tensor("t1", [128, 256], F32))
    t2 = st1.enter_context(b.nc.sbuf_tensor("t2", [128, 256], F32))
    rot = [st1.enter_context(b.nc.sbuf_tensor("rot%d" % i, [128, 512], BF16)) for i in range(2)]
    vt = [st1.enter_context(b.nc.sbuf_tensor("vt%d" % i, [128, 256], BF16)) for i in range(2)]
    kqT = [st1.enter_context(b.nc.sbuf_tensor("kqT%d" % i, [128, 4 * 512], BF16)) for i in range(2)]
    sig = st1.enter_context(b.nc.sbuf_tensor("sig", [128, 512], F32))
    ut = [st1.enter_context(b.nc.sbuf_tensor("ut%d" % i, [128, 512], F32)) for i in range(2)]
    ps_kv = [st1.enter_context(b.nc.psum_tensor("ps_kv%d" % i, [128, 512], F32)) for i in range(2)]
    ps_q = [st1.enter_context(b.nc.psum_tensor("ps_q%d" % i, [128, 512], F32)) for i in range(2)]
    ps_T = st1.enter_context(b.nc.psum_tensor("ps_T", [128, 1024], BF16))
    ps_c = [st1.enter_context(b.nc.psum_tensor("ps_c%d" % i, [128, 512], F32)) for i in range(2)]
    sts = supertiles()
    aTv = aTd.rearrange("(c p) t -> p c t", p=128)

    def load_aT(si):
        t0, N = sts[si]
        buf = aTs[si % 2]
        S.dma("sp", lambda e: e.dma_start(out=buf[:, 0:DC * N].rearrange("p (c n) -> p c n", n=N), in_=aTv[:, :, t0:t0 + N]),
              writes=["aTs%d" % (si % 2)])
    if 1 in phases:
        load_aT(0)
    nsub_total = 0
    for si, (t0, N) in enumerate(sts if 1 in phases else []):
        if si + 1 < len(sts):
            load_aT(si + 1)
        a = aTs[si % 2][:, 0:DC * N].rearrange("p (c n) -> p c n", n=N)
        ak = "aTs%d" % (si % 2)
        kq = kqT[si % 2]; kqk = "kqT%d" % (si % 2)
        kqv = kq[:, 0:4 * N].rearrange("p (a n) -> p a n", n=N)
        for sub in range(N // 128):
            i2 = nsub_total % 2
            nsub_total += 1
            tt0 = t0 + sub * 128
            pk = ps_kv[i2]; pq = ps_q[i2]
            pkk = "ps_kv%d" % i2; pqk = "ps_q%d" % i2
            S.dma("sp", lambda e, i2=i2, tt0=tt0: e.dma_start(out=cs[i2][:], in_=ropeD[tt0:tt0 + 128, :]), writes=["cs%d" % i2])

            def mmkv(e, a=a, sub=sub, pk=pk):
                ins = None
                for kc in range(DC):
                    ins = e.matmul(pk[:, 0:512], a[:, kc, sub * 128:(sub + 1) * 128], Wv[:, kc, 0:512], start=(kc == 0), stop=(kc == DC - 1))
                return ins
            S.op("pe", mmkv, reads=[ak] + Wk, writes=[pkk])

            def mmq(e, a=a, sub=sub, pq=pq):
                ins = None
                for kc in range(DC):
                    ins = e.matmul(pq[:, 0:256], a[:, kc, sub * 128:(sub + 1) * 128], Wv[:, kc, 512:768], start=(kc == 0), stop=(kc == DC - 1))
                return ins
            S.op("pe", mmq, reads=[ak] + Wk, writes=[pqk])
            csk = "cs%d" % i2
            rt = rot[i2]; rk = "rot%d" % i2
            for which, (src, srck, off) in enumerate(((pk[:, 0:256], pkk, 0), (pq[:, 0:256], pqk, 256))):
                swp = src.rearrange("p (g s i) -> p g s i", s=2, i=16)[:, :, ::-1, :]
                S.op("dve", lambda e, src=src, i2=i2: e.tensor_tensor(out=t1[:], in0=src, in1=cs[i2][:, 0:256], op=ALU.mult),
                     reads=[srck, csk], writes=["t1"])
                S.op("dve", lambda e, swp=swp, i2=i2: e.tensor_tensor(
                    out=t2[:].rearrange("p (g s i) -> p g s i", s=2, i=16), in0=swp,
                    in1=cs[i2][:, 256:512].rearrange("p (g s i) -> p g s i", s=2, i=16), op=ALU.mult),
                    reads=[srck, csk], writes=["t2"])
                S.op("dve", lambda e, rt=rt, off=off: e.tensor_tensor(out=rt[:, off:off + 256], in0=t1[:], in1=t2[:], op=ALU.add),
                     reads=["t1", "t2"], writes=[rk + "_%d" % which])
            S.op("act", lambda e, i2=i2, pk=pk: e.activation(out=vt[i2][:], in_=pk[:, 256:512], func=AF.Copy),
                 reads=[pkk], writes=["vt%d" % i2, pkk])
            S.dma("sp", lambda e, i2=i2, tt0=tt0: e.dma_start(out=Vd[tt0:tt0 + 128, :], in_=vt[i2][:]), reads=["vt%d" % i2], writes=["Vd"])

            def trs(e, rt=rt):
                ins = None
                for blk in range(4):
                    ins = e.transpose(ps_T[:, blk * 128:(blk + 1) * 128], rt[:, blk * 128:(blk + 1) * 128], identb[:])
                return ins
            S.op("pe", trs, reads=[rk + "_0", rk + "_1", "identb"], writes=["ps_T"])
            S.op("act", lambda e, kqv=kqv, sub=sub: e.activation(
                out=kqv[:, :, sub * 128:(sub + 1) * 128], in_=ps_T[:, 0:512].rearrange("p (a n) -> p a n", n=128), func=AF.Copy),
                reads=["ps_T"], writes=[kqk])
        S.dma("sp", lambda e, kqv=kqv, t0=t0, N=N: e.dma_start(out=KTd[:, :, t0:t0 + N].rearrange("a p n -> p a n"), in_=kqv[:, 0:2, :]),
              reads=[kqk], writes=["KTd"])
        S.dma("sp", lambda e, kqv=kqv, t0=t0, N=N: e.dma_start(out=QTd[:, :, t0:t0 + N].rearrange("a p n -> p a n"), in_=kqv[:, 2:4, :]),
              reads=[kqk], writes=["QTd"])
        for blk in range(2):
            for w2 in range(2):
                col = 768 + w2 * 256 + blk * 128

                def mmc(e, a=a, col=col, w2=w2, N=N):
                    ins = None
                    for kc in range(DC):
                        ins = e.matmul(ps_c[w2][:, 0:N], Wv[:, kc, col:col + 128], a[:, kc, :], start=(kc == 0), stop=(kc == DC - 1))
                    return ins
                S.op("pe", mmc, reads=[ak] + Wk, writes=["ps_c%d" % w2])
            S.op("act", lambda e, N=N: e.activation(out=sig[:, 0:N], in_=ps_c[1][:, 0:N], func=AF.Sigmoid), reads=["ps_c1"], writes=["sig"])
            S.op("dve", lambda e, N=N, blk=blk: e.tensor_tensor(out=ut[blk][:, 0:N], in0=ps_c[0][:, 0:N], in1=sig[:, 0:N], op=ALU.mult),
                 reads=["ps_c0", "sig"], writes=["ut%d" % blk])
            S.dma("sp", lambda e, N=N, blk=blk, t0=t0: e.dma_start(out=UTd[blk, :, t0:t0 + N], in_=ut[blk][:, 0:N]),
                  reads=["ut%d" % blk], writes=["UTd"])
    st1.close()
    S.barrier()

    st2 = contextlib.ExitStack()
    uT = st2.enter_context(b.nc.sbuf_tensor("uT", [128, 2 * U_LEN], F32))
    cwt = st2.enter_context(b.nc.sbuf_tensor("cwt", [128, 2 * CW], F32))
    cbt = st2.enter_context(b.nc.sbuf_tensor("cbt", [128, 2], F32))
    acc = [st2.enter_context(b.nc.sbuf_tensor("acc%d" % i, [128, 512], F32)) for i in range(2)]
    uv = uT[:].rearrange("p (b n) -> p b n", n=U_LEN)
    S.dma("sp", lambda e: e.dma_start(out=cwt[:], in_=cwD), writes=["cwt"])
    S.dma("sp", lambda e: e.dma_start(out=cbt[:], in_=cbD), writes=["cbt"])
    S.op("dve", lambda e: e.memset(uT[:], 0.0), writes=["uT"])
    S.dma("sp", lambda e: e.dma_start(out=uv[:, :, U_CTX0:U_CTX0 + CTX], in_=UTd[:, :, 0:CTX].rearrange("b p n -> p b n")),
          reads=["UTd"], writes=["uT"])
    S.dma("sp", lambda e: e.dma_start(out=uv[:, :, U_LAT0:U_LAT0 + SEQ], in_=UTd[:, :, CTX:NTOK].rearrange("b p n -> p b n")),
          reads=["UTd"], writes=["uT"])
    it = 0
    for (ubase, slen, tok0) in (((U_CTX0, CTX, 0), (U_LAT0, SEQ, CTX)) if 2 in phases else ()):
        for c0 in range(0, slen, 512):
            n = min(512, slen - c0)
            for blk in range(2):
                ac = acc[it % 2]; ack = "acc%d" % (it % 2)
                it += 1
                u0 = ubase + c0 - CPAD

                def conv(e, ac=ac, blk=blk, u0=u0, n=n):
                    ins = e.tensor_scalar(out=ac[:, 0:n], in0=uv[:, blk, u0:u0 + n], scalar1=cwt[:, blk * CW:blk * CW + 1],
                                          scalar2=cbt[:, blk:blk + 1], op0=ALU.mult, op1=ALU.add)
                    for k in range(1, CW):
                        ins = e.scalar_tensor_tensor(out=ac[:, 0:n], in0=uv[:, blk, u0 + k:u0 + k + n],
                                                     scalar=cwt[:, blk * CW + k:blk * CW + k + 1], in1=ac[:, 0:n],
                                                     op0=ALU.mult, op1=ALU.add)
                    return ins
                S.op("dve", conv, reads=["uT", "cwt", "cbt"], writes=[ack])
                S.dma("sp", lambda e, ac=ac, blk=blk, tok0=tok0, c0=c0, n=n: e.dma_start(
                    out=convT[blk * 128:(blk + 1) * 128, tok0 + c0:tok0 + c0 + n], in_=ac[:, 0:n]), reads=[ack])
    st2.close()
    S.barrier()

    st3 = contextlib.ExitStack()
    KT = st3.enter_context(b.nc.sbuf_tensor("KT", [128, 2 * NTOK], BF16))
    KTv = KT[:].rearrange("p (a n) -> p a n", n=NTOK)
    Va = st3.enter_context(b.nc.sbuf_tensor("Va", [128, NKB * 2 * 129], BF16))
    Vav = Va[:].rearrange("p (k h d) -> p k h d", h=2, d=129)
    QTs = [st3.enter_context(b.nc.sbuf_tensor("QTs%d" % i, [128, 2 * 512], BF16)) for i in range(2)]
    PT = [st3.enter_context(b.nc.sbuf_tensor("PT%d" % i, [128, 512], BF16)) for i in range(3)]
    Osb = [st3.enter_context(b.nc.sbuf_tensor("Osb%d" % i, [128, 4 * 129], F32)) for i in range(2)]
    lamt = st3.enter_context(b.nc.sbuf_tensor("lamt", [128, 256], F32))
    lp = st3.enter_context(b.nc.sbuf_tensor("lp", [128, 128], F32))
    lsum = st3.enter_context(b.nc.sbuf_tensor("lsum", [128, 4], F32))
    nlam = st3.enter_context(b.nc.sbuf_tensor("nlam", [128, 1], F32))
    swe = st3.enter_context(b.nc.sbuf_tensor("swe", [128, 128], F32))
    rr = st3.enter_context(b.nc.sbuf_tensor("rr", [128, 8], F32))
    ot = st3.enter_context(b.nc.sbuf_tensor("ot", [128, 128], F32))
    osq = st3.enter_context(b.nc.sbuf_tensor("osq", [128, 128], F32))
    ob = st3.enter_context(b.nc.sbuf_tensor("ob", [128, 128], BF16))
    oT = [st3.enter_context(b.nc.sbuf_tensor("oT%d" % i, [128, 512], BF16)) for i in range(2)]
    psS = [st3.enter_context(b.nc.psum_tensor("psS%d" % i, [128, 512], F32)) for i in range(2)]
    psO = [st3.enter_context(b.nc.psum_tensor("psO%d" % i, [128, 512], F32)) for i in range(4)]
    psX = st3.enter_context(b.nc.psum_tensor("psX", [128, 1024], BF16))
    S.dma("sp", lambda e: e.dma_start(out=KTv, in_=KTd.rearrange("a p n -> p a n")), reads=["KTd"], writes=["KT"])
    S.op("dve", lambda e: e.memset(Va[:], 1.0), writes=["Va"])
    for h in range(2):
        S.dma("sp", lambda e, h=h: e.dma_start(out=Vav[:, :, h, 0:128], in_=Vd[:, h * 128:(h + 1) * 128].rearrange("(k p) d -> p k d", p=128)),
              reads=["Vd"], writes=["Va"])
    S.dma("sp", lambda e: e.dma_start(out=lamt[:], in_=lamD.partition_broadcast(128)), writes=["lamt"])
    S.dma("sp", lambda e: e.dma_start(out=swe[:], in_=swD.partition_broadcast(128)), writes=["swe"])
    S.op("dve", lambda e: e.tensor_tensor(out=lp[:].rearrange("p (a n) -> p a n", n=64),
                                          in0=lamt[:].rearrange("p (a b n) -> p a b n", b=2, n=64)[:, :, 0, :],
                                          in1=lamt[:].rearrange("p (a b n) -> p a b n", b=2, n=64)[:, :, 1, :], op=ALU.mult),
         reads=["lamt"], writes=["lp"])
    S.op("dve", lambda e: e.reduce_sum(out=lsum[:, 0:2], in_=lp[:].rearrange("p (a n) -> p a n", n=64), axis=AX.X), reads=["lp"], writes=["lsum"])
    S.op("act", lambda e: e.activation(out=lsum[:, 2:4], in_=lsum[:, 0:2], func=AF.Exp), reads=["lsum"], writes=["lsum"])
    S.op("dve", lambda e: e.tensor_tensor(out=nlam[:], in0=lsum[:, 3:4], in1=lsum[:, 2:3], op=ALU.subtract), reads=["lsum"], writes=["nlam"])
    S.op("dve", lambda e: e.tensor_scalar_add(nlam[:], nlam[:], -LAM_INIT0), reads=["nlam"], writes=["nlam"])
    S.op("dve", lambda e: e.tensor_scalar_mul(swe[:], swe[:], 1.0 - LAM_INIT0), reads=["swe"], writes=["swe"])

    qsts = [(0, CTX, 0, 2)] + [(CTX + 512 * i, 512, 0, NKB) for i in range(SEQ // 512)]
    QTdv = QTd.rearrange("a p n -> p a n")

    def load_q(qi):
        q0, NQ, _, _ = qsts[qi]
        S.dma("sp", lambda e: e.dma_start(out=QTs[qi % 2][:, 0:2 * NQ].rearrange("p (a n) -> p a n", n=NQ), in_=QTdv[:, :, q0:q0 + NQ]),
              reads=["QTd"], writes=["QTs%d" % (qi % 2)])
    if 3 not in phases:
        qsts = []
    else:
        load_q(0)
    nS = 0
    nOT = 0
    for qi, (q0, NQ, kb0, kb1) in enumerate(qsts):
        if qi + 1 < len(qsts):
            load_q(qi + 1)
        Q = QTs[qi % 2][:, 0:2 * NQ].rearrange("p (a n) -> p a n", n=NQ)
        Qk = "QTs%d" % (qi % 2)
        nqs = NQ // 128
        for hh in range(2):
            for mi in range(2):
                p0 = mi * 64
                for kb in range(kb0, kb1):
                    sS = psS[nS % 2]; sSk = "psS%d" % (nS % 2)
                    pt = PT[nS % 3]; ptk = "PT%d" % (nS % 3)
                    nS += 1
                    S.op("pe", lambda e, sS=sS, p0=p0, hh=hh, kb=kb, Q=Q, NQ=NQ: e.matmul(
                        sS[:, 0:NQ], KTv[p0:p0 + 64, hh, kb * 128:(kb + 1) * 128], Q[p0:p0 + 64, hh, :], start=True, stop=True),
                        reads=["KT", Qk], writes=[sSk])
                    S.op("act", lambda e, sS=sS, pt=pt, NQ=NQ: e.activation(out=pt[:, 0:NQ], in_=sS[:, 0:NQ], func=AF.Exp, scale=0.125),
                         reads=[sSk], writes=[ptk])

                    def pv(e, pt=pt, kb=kb, hh=hh, nqs=nqs, kb0=kb0, kb1=kb1):
                        ins = None
                        for qs in range(nqs):
                            ins = e.matmul(psO[qs][:, 0:129], pt[:, qs * 128:(qs + 1) * 128], Vav[:, kb, hh, :],
                                           start=(kb == kb0), stop=(kb == kb1 - 1))
                        return ins
                    S.op("pe", pv, reads=[ptk, "Va"], writes=["psO"])
                for qs in range(nqs):
                    S.op("dve", lambda e, mi=mi, qs=qs: e.tensor_copy(Osb[mi][:, qs * 129:(qs + 1) * 129], psO[qs][:, 0:129]),
                         reads=["psO"], writes=["Osb%d_%d" % (mi, qs)])
            o_T = oT[nOT % 2]; oTk = "oT%d" % (nOT % 2)
            nOT += 1
            for qs in range(nqs):
                O1 = Osb[0][:, qs * 129:(qs + 1) * 129]
                O2 = Osb[1][:, qs * 129:(qs + 1) * 129]
                k1 = "Osb0_%d" % qs; k2 = "Osb1_%d" % qs
                S.op("dve", lambda e, O1=O1: e.reciprocal(out=rr[:, 0:1], in_=O1[:, 128:129]), reads=[k1], writes=["rr0"])
                S.op("dve", lambda e, O2=O2: e.reciprocal(out=rr[:, 1:2], in_=O2[:, 128:129]), reads=[k2], writes=["rr1"])
                S.op("dve", lambda e: e.tensor_tensor(out=rr[:, 2:3], in0=rr[:, 1:2], in1=nlam[:], op=ALU.mult), reads=["rr1", "nlam"], writes=["rr2"])
                S.op("dve", lambda e, O1=O1: e.tensor_scalar(out=ot[:], in0=O1[:, 0:128], scalar1=rr[:, 0:1], scalar2=None, op0=ALU.mult),
                     reads=[k1, "rr0"], writes=["ot"])
                S.op("dve", lambda e, O2=O2: e.scalar_tensor_tensor(out=ot[:], in0=O2[:, 0:128], scalar=rr[:, 2:3], in1=ot[:], op0=ALU.mult, op1=ALU.add),
                     reads=[k2, "rr2", "ot"], writes=["ot"])
                S.op("dve", lambda e: e.tensor_tensor(out=osq[:], in0=ot[:], in1=ot[:], op=ALU.mult), reads=["ot"], writes=["osq"])
                S.op("dve", lambda e: e.reduce_sum(out=rr[:, 3:4], in_=osq[:], axis=AX.X), reads=["osq"], writes=["rr3"])
                S.op("dve", lambda e: e.tensor_scalar(out=rr[:, 4:5], in0=rr[:, 3:4], scalar1=1.0 / 128, scalar2=1e-5, op0=ALU.mult, op1=ALU.add),
                     reads=["rr3"], writes=["rr4"])
                S.op("act", lambda e: e.activation(out=rr[:, 5:6], in_=rr[:, 4:5], func=AF.Sqrt), reads=["rr4"], writes=["rr5"])
                S.op("dve", lambda e: e.reciprocal(out=rr[:, 6:7], in_=rr[:, 5:6]), reads=["rr5"], writes=["rr6"])
                S.op("dve", lambda e: e.scalar_tensor_tensor(out=ob[:], in0=ot[:], scalar=rr[:, 6:7], in1=swe[:], op0=ALU.mult, op1=ALU.mult),
                     reads=["ot", "rr6", "swe"], writes=["ob"])
                S.op("pe", lambda e: e.transpose(psX[:, 0:128], ob[:], identb[:]), reads=["ob", "identb"], writes=["psX"])
                S.op("act", lambda e, o_T=o_T, qs=qs: e.activation(out=o_T[:, qs * 128:(qs + 1) * 128], in_=psX[:, 0:128], func=AF.Copy),
                     reads=["psX"], writes=[oTk])
            S.dma("sp", lambda e, o_T=o_T, hh=hh, q0=q0, NQ=NQ: e.dma_start(out=attnT[hh * 128:(hh + 1) * 128, q0:q0 + NQ], in_=o_T[:, 0:NQ]),
                  reads=[oTk])
    st3.close()
    S.barrier()
    return b.finish()


def rope_tables():
    n_rows = SEQ // 64
    rows = np.repeat(np.arange(n_rows), 64).astype(np.float32)
    cols = np.tile(np.arange(64), n_rows).astype(np.float32)
    inv = np.power(np.float32(10000.0), -np.arange(0, 32, 2, dtype=np.float32) / np.float32(32)).astype(np.float32)
    ang = [rows[:, None] * inv, cols[:, None] * inv]
    cosf = np.zeros((NTOK, 4, 2, 2, 16), np.float32)
    sinf = np.zeros((NTOK, 4, 2, 2, 16), np.float32)
    cosf[:CTX] = 1.0
    for a in range(2):
        c = np.cos(ang[a]).astype(np.float32)
        s = np.sin(ang[a]).astype(np.float32)
        cosf[CTX:, :, a, 0, :] = c[:, None, :]
        cosf[CTX:, :, a, 1, :] = c[:, None, :]
        sinf[CTX:, :, a, 0, :] = -s[:, None, :]
        sinf[CTX:, :, a, 1, :] = s[:, None, :]
    return np.ascontiguousarray(np.concatenate([cosf.reshape(NTOK, 256), sinf.reshape(NTOK, 256)], 1))


def stage_C(inp, aT_all, phases=(1, 2, 3)):
    w = inp["ev_w_in"][0]
    rope = rope_tables()
    ident = np.eye(128, dtype=np.float32)
    maps = []
    for j in range(NCORES):
        cols = np.concatenate([np.arange(j * 256, (j + 1) * 256), EV_A + np.arange(j * 256, (j + 1) * 256),
                               2 * EV_A + np.arange(j * 256, (j + 1) * 256), 3 * EV_A + np.arange(j * 256, (j + 1) * 256),
                               3 * EV_A + 2048 + np.arange(j * 256, (j + 1) * 256)])
        cwj = inp["ev_conv_w"][0][:, j * 256:(j + 1) * 256]
        cw = np.ascontiguousarray(cwj.reshape(CW, 2, 128).transpose(2, 1, 0)).reshape(128, 2 * CW)
        cb = np.ascontiguousarray(inp["ev_conv_b"][0][j * 256:(j + 1) * 256].reshape(2, 128).T)
        maps.append({"aT": aT_all, "w": np.ascontiguousarray(w[:, cols]), "rope": rope,
                     "lam": np.ascontiguousarray(inp["ev_lambda"][0].reshape(256)), "subw": np.ascontiguousarray(inp["ev_subln_w"][0]),
                     "cw": cw, "cb": cb, "ident": ident})
    res = run(build_C(phases), maps)
    attnT = np.concatenate([r["attnT"] for r in res], 0)
    convT = np.concatenate([r["convT"] for r in res], 0)
    return attnT, convT


def group_cols(layer):
    if layer == 0:
        return np.concatenate([np.arange(0, 512), LAT_PC + np.arange(0, 16), np.arange(512, 1024), LAT_PC + np.arange(16, 32)])
    return np.arange(0, LAT_PC)


def build_T2(layer):
    b = B()
    S = b.S
    nc = b.nc
    NT = 528 if layer == 0 else 512
    NTOT = 2 * NT
    H = NT // 2
    halves = [(0, H), (H, NT)]
    if layer == 0:
        rngs = [(0, H, 0), (H, 512, 0), (512, NT, 1)]
    else:
        rngs = [(0, H, 0), (H, NT, 0)]
    hTin = b.din("hTin", [D, NTOT])
    modT = b.din("modT", [128, 12 * DC])
    nw2T = b.din("nw2T", [128, DC])
    w_out = b.din("w_out", [D, D])
    w1 = b.din("w1", [D, FFN_H])
    w3 = b.din("w3", [D, FFN_H])
    w2 = b.din("w2", [FFN_H, D])
    epsD = b.din("epsv", [128, 2])
    if layer == 0:
        attnTin = b.din("attnTin", [2048, NTOT], BF16)
        convTin = b.din("convTin", [2048, NTOT])
        lnT = b.din("lnT", [128, 32])
        modN = b.din("modN", [128, 12 * DC])
        nwN = b.din("nwN", [128, DC])
        aTout = b.dout("aTout", [D, NTOT], BF16)
        hTout = b.dout("hTout", [D, NTOT])
    else:
        s5T = b.din("s5T", [1024, NTOT], BF16)
        ssdT = b.din("ssdT", [3072, NTOT], BF16)
        gluW = b.din("gluW", [1024, 1024])
        glubT = b.din("glubT", [128, 8])
        nwN = b.din("nwN", [128, DC])
        identD = b.din("ident", [128, 128])
        outD = b.dout("out", [NTOT, D])
    hT1 = b.dout("hT1", [D, NTOT]) if DEBUG else b.dint("hT1", [D, NTOT])
    hT2 = hTout if layer == 0 else b.dint("hT2", [D, NTOT])

    modt = b.sb("modt", [128, 12 * DC])
    nw2t = b.sb("nw2t", [128, DC])
    nwNt = b.sb("nwNt", [128, DC])
    epst = b.sb("epst", [128, 2])
    ones = b.sb("ones", [128, 128], BF16)
    S.dma("sp", lambda e: e.dma_start(out=modt[:], in_=modT), writes=["modt"])
    S.dma("sp", lambda e: e.dma_start(out=nw2t[:], in_=nw2T), writes=["nw2t"])
    S.dma("sp", lambda e: e.dma_start(out=nwNt[:], in_=nwN), writes=["nwNt"])
    S.dma("sp", lambda e: e.dma_start(out=epst[:], in_=epsD), writes=["epst"])
    S.op("dve", lambda e: e.memset(ones[:], 1.0), writes=["ones"])
    weff2 = emit_weff(b, modt, nw2t, "modt", "nw2t", 4, "weff2")
    if layer == 0:
        lnt = b.sb("lnt", [128, 32])
        modn = b.sb("modn", [128, 12 * DC])
        onesf = b.sb("onesf", [128, 128])
        S.dma("sp", lambda e: e.dma_start(out=lnt[:], in_=lnT), writes=["lnt"])
        S.dma("sp", lambda e: e.dma_start(out=modn[:], in_=modN), writes=["modn"])
        S.op("dve", lambda e: e.memset(onesf[:], 1.0), writes=["onesf"])
        weffN = emit_weff(b, modn, nwNt, "modn", "nwNt", 1, "weffN")
    else:
        glubt = b.sb("glubt", [128, 8])
        ident = b.sb("identt", [128, 128])
        S.dma("sp", lambda e: e.dma_start(out=glubt[:], in_=glubT), writes=["glubt"])
        S.dma("sp", lambda e: e.dma_start(out=ident[:], in_=identD), writes=["ident"])

    NWB = 3
    wb = [b.sb("wb%d" % i, [128, 43 * 256], BF16) for i in range(NWB)]
    actT = b.sb("actT", [128, DC * NT], BF16)
    actv = actT[:].rearrange("p (c n) -> p c n", n=NT)
    hch = [b.sb("hch%d" % i, [128, NT]) for i in range(2)]
    hnew = [b.sb("hnew%d" % i, [128, NT]) for i in range(2)]
    sqb = [b.sb("sqb%d" % i, [128, NT], BF16) for i in range(2)]
    tmpf = [b.sb("tmpf%d" % i, [128, NT]) for i in range(2)]
    rstd = b.sb("rstd", [128, NT])
    pb = [b.ps("pb%d" % i) for i in range(8)]
    cnt = {"w": 0, "h": 0, "hn": 0, "sq": 0, "tmp": 0}

    def wload(src_ap, kcn, ncols):
        i = cnt["w"] % NWB
        cnt["w"] += 1
        buf = wb[i]
        view = buf[:, 0:kcn * ncols].rearrange("p (k n) -> p k n", n=ncols)
        S.dma("pool", lambda e: e.dma_start(out=view, in_=src_ap.rearrange("(k p) n -> p k n", p=128)), writes=["wb%d" % i])
        return view, "wb%d" % i

    def stats_begin():
        return {"first": True}

    def stats_add(st, src, srck, last):
        i = cnt["sq"] % 2
        cnt["sq"] += 1
        sq = sqb[i]
        S.op("act", lambda e: e.activation(out=sq[:], in_=src, func=AF.Square), reads=[srck], writes=["sqb%d" % i])
        first = st["first"]
        st["first"] = False

        def mm(e):
            ins = None
            for hi, (a0, a1) in enumerate(halves):
                ins = e.matmul(pb[4 + hi][:, 0:a1 - a0], ones[:], sq[:, a0:a1], start=first, stop=last)
            return ins
        S.op("pe", mm, reads=["sqb%d" % i, "ones"], writes=["pb4", "pb5"])

    def stats_finish(eps_col, nfeat):
        for hi, (a0, a1) in enumerate(halves):
            S.op("act", lambda e, hi=hi, a0=a0, a1=a1: e.activation(out=rstd[:, a0:a1], in_=pb[4 + hi][:, 0:a1 - a0], func=AF.Sqrt,
                                                                    scale=1.0 / nfeat, bias=epst[:, eps_col:eps_col + 1]),
                 reads=["pb4", "pb5", "epst"], writes=["rstd", "pb4", "pb5"])
        S.op("dve", lambda e: e.reciprocal(out=rstd[:], in_=rstd[:]), reads=["rstd"], writes=["rstd"])

    def load_h(src, c, g0):
        i = cnt["h"] % 2
        cnt["h"] += 1
        S.dma("sp", lambda e: e.dma_start(out=hch[i][:], in_=src[c * 128:(c + 1) * 128, g0:g0 + NT]), reads=[id(src)], writes=["hch%d" % i])
        return hch[i], "hch%d" % i

    def gemm_resid(Wd, KC, xk, gate_v, src_h, dst_h, g0, st):
        npan = D // 256
        segs = [(k0, min(k0 + 43, KC)) for k0 in range(0, KC, 43)]
        for pn in range(npan):
            wviews = []
            for (k0, k1) in segs:
                wviews.append(wload(Wd[k0 * 128:k1 * 128, pn * 256:(pn + 1) * 256], k1 - k0, 256))
            for o2 in range(2):
                oc = pn * 2 + o2
                bks = [pb[o2 * 2], pb[o2 * 2 + 1]]
                bkk = ["pb%d" % (o2 * 2), "pb%d" % (o2 * 2 + 1)]

                def mm(e, o2=o2, bks=bks, wviews=wviews, xv=gemm_resid.xv):
                    ins = None
                    for si, (k0, k1) in enumerate(segs):
                        wv = wviews[si][0]
                        for k in range(k0, k1):
                            for hi, (a0, a1) in enumerate(halves):
                                ins = e.matmul(bks[hi][:, 0:a1 - a0], wv[:, k - k0, o2 * 128:(o2 + 1) * 128], xv[:, k, a0:a1],
                                               start=(k == 0), stop=(k == KC - 1))
                    return ins
                xv = gemm_resid.xv
                S.op("pe", mm, reads=[wk for (_, wk) in wviews] + list(xk), writes=bkk)
                ht, hk = load_h(src_h, oc, g0)
                j = cnt["hn"] % 2
                cnt["hn"] += 1
                hn = hnew[j]; hnk = "hnew%d" % j
                for (c0, c1, r) in rngs:
                    hi = 0 if c1 <= H else 1
                    a0 = halves[hi][0]
                    S.op("dve", lambda e, hi=hi, a0=a0, c0=c0, c1=c1, r=r, oc=oc, ht=ht, hn=hn, bks=bks: e.scalar_tensor_tensor(
                        out=hn[:, c0:c1], in0=bks[hi][:, c0 - a0:c1 - a0], scalar=mcol(modt, r, gate_v, oc, oc + 1), in1=ht[:, c0:c1],
                        op0=ALU.mult, op1=ALU.add), reads=[bkk[hi], hk, "modt"], writes=[hnk, bkk[hi]])
                S.dma("sp", lambda e, hn=hn, oc=oc: e.dma_start(out=dst_h[oc * 128:(oc + 1) * 128, g0:g0 + NT], in_=hn[:]),
                      reads=[hnk], writes=[id(dst_h)])
                stats_add(st, hn[:], hnk, oc == DC - 1)

    def norm_mod(src_h, g0, weff, shift_tab, shift_v, dst_view_fn, dstk_fn, after=None):
        for c in range(DC):
            ht, hk = load_h(src_h, c, g0)
            i = cnt["tmp"] % 2
            cnt["tmp"] += 1
            tf = tmpf[i]; tk = "tmpf%d" % i
            S.op("dve", lambda e, ht=ht, tf=tf: e.tensor_tensor(out=tf[:], in0=ht[:], in1=rstd[:], op=ALU.mult), reads=[hk, "rstd"], writes=[tk])
            dv = dst_view_fn(c)
            for (c0, c1, r) in rngs:
                if weff is not None and shift_tab is not None:
                    S.op("act", lambda e, tf=tf, dv=dv, c0=c0, c1=c1, r=r, c=c: e.activation(
                        out=dv[:, c0:c1], in_=tf[:, c0:c1], func=AF.Identity,
                        scale=weff[:, r * DC + c:r * DC + c + 1], bias=mcol(shift_tab, r, shift_v, c, c + 1)),
                        reads=[tk], writes=[dstk_fn(c)])
                else:
                    S.op("act", lambda e, tf=tf, dv=dv, c0=c0, c1=c1, c=c: e.activation(
                        out=dv[:, c0:c1], in_=tf[:, c0:c1], func=AF.Copy, scale=weff[:, c:c + 1]) if False else
                        e.activation(out=dv[:, c0:c1], in_=tf[:, c0:c1], func=AF.Identity, scale=nwNt[:, c:c + 1], bias=0.0),
                        reads=[tk], writes=[dstk_fn(c)])
            if after is not None:
                after(c)

    def do_group(g):
        g0 = g * NT
        if layer == 0:
            st1 = contextlib.ExitStack()
            cv = st1.enter_context(nc.sbuf_tensor("cv" + "_g%d" % g, [128, 16 * NT], F32))
            cvv = cv[:].rearrange("p (c n) -> p c n", n=NT)
            S.dma("sp", lambda e: e.dma_start(out=cvv, in_=convTin.rearrange("(c p) t -> p c t", p=128)[:, :, g0:g0 + NT]), writes=["cv"])
            S.dma("sp", lambda e: e.dma_start(out=actv[:, 0:16, :], in_=attnTin.rearrange("(c p) t -> p c t", p=128)[:, :, g0:g0 + NT]),
                  writes=["act_%d" % c for c in range(16)])
            for c in range(16):
                i = cnt["tmp"] % 2
                cnt["tmp"] += 1
                tf = tmpf[i]; tk = "tmpf%d" % i
                S.op("act", lambda e, c=c, tf=tf: e.activation(out=tf[:], in_=cvv[:, c, :], func=AF.Square), reads=["cv"], writes=[tk])

                def mm(e, c=c, tf=tf):
                    ins = None
                    for hi, (a0, a1) in enumerate(halves):
                        ins = e.matmul(pb[hi][:, 0:a1 - a0], onesf[:], cvv[:, c, a0:a1], start=(c == 0), stop=(c == 15))
                        ins = e.matmul(pb[2 + hi][:, 0:a1 - a0], onesf[:], tf[:, a0:a1], start=(c == 0), stop=(c == 15))
                    return ins
                S.op("pe", mm, reads=["cv", tk, "onesf"], writes=["pb0", "pb1", "pb2", "pb3"])
            mean = st1.enter_context(nc.sbuf_tensor("lnmean" + "_g%d" % g, [128, NT], F32))
            msq = st1.enter_context(nc.sbuf_tensor("lnmsq" + "_g%d" % g, [128, NT], F32))
            nmr = st1.enter_context(nc.sbuf_tensor("lnnmr" + "_g%d" % g, [128, NT], F32))
            for hi, (a0, a1) in enumerate(halves):
                S.op("dve", lambda e, hi=hi, a0=a0, a1=a1: e.tensor_scalar(out=mean[:, a0:a1], in0=pb[hi][:, 0:a1 - a0], scalar1=1.0 / 2048, scalar2=None, op0=ALU.mult),
                     reads=["pb%d" % hi], writes=["lnmean", "pb%d" % hi])
            S.op("dve", lambda e: e.tensor_tensor(out=msq[:], in0=mean[:], in1=mean[:], op=ALU.mult), reads=["lnmean"], writes=["lnmsq"])
            for hi, (a0, a1) in enumerate(halves):
                S.op("dve", lambda e, hi=hi, a0=a0, a1=a1: e.scalar_tensor_tensor(
                    out=rstd[:, a0:a1], in0=pb[2 + hi][:, 0:a1 - a0], scalar=1.0 / 2048, in1=msq[:, a0:a1], op0=ALU.mult, op1=ALU.subtract),
                    reads=["pb%d" % (2 + hi), "lnmsq"], writes=["rstd", "pb%d" % (2 + hi)])
            S.op("act", lambda e: e.activation(out=rstd[:], in_=rstd[:], func=AF.Sqrt, bias=epst[:, 1:2]), reads=["rstd", "epst"], writes=["rstd"])
            S.op("dve", lambda e: e.reciprocal(out=rstd[:], in_=rstd[:]), reads=["rstd"], writes=["rstd"])
            S.op("dve", lambda e: e.scalar_tensor_tensor(out=nmr[:], in0=mean[:], scalar=-1.0, in1=rstd[:], op0=ALU.mult, op1=ALU.mult),
                 reads=["lnmean", "rstd"], writes=["lnnmr"])
            for c in range(16):
                i = cnt["tmp"] % 2
                cnt["tmp"] += 1
                tf = tmpf[i]; tk = "tmpf%d" % i
                S.op("dve", lambda e, c=c, tf=tf: e.tensor_tensor(out=tf[:], in0=cvv[:, c, :], in1=rstd[:], op=ALU.mult), reads=["cv", "rstd"], writes=[tk])
                S.op("dve", lambda e, tf=tf: e.tensor_tensor(out=tf[:], in0=tf[:], in1=nmr[:], op=ALU.add), reads=[tk, "lnnmr"], writes=[tk])
                S.op("act", lambda e, c=c, tf=tf: e.activation(out=actv[:, 16 + c, :], in_=tf[:], func=AF.Silu,
                                                               scale=lnt[:, c:c + 1], bias=lnt[:, 16 + c:17 + c]),
                     reads=[tk, "lnt"], writes=["act_%d" % (16 + c)])
            st1.close()
            S.barrier()
        else:
            st1 = contextlib.ExitStack()
            gT = st1.enter_context(nc.sbuf_tensor("gT5" + "_g%d" % g, [128, 8 * NT], BF16))
            gv = gT[:].rearrange("p (c n) -> p c n", n=NT)
            S.dma("sp", lambda e: e.dma_start(out=gv, in_=s5T.rearrange("(c p) t -> p c t", p=128)[:, :, g0:g0 + NT]), writes=["gT5"])
            S.dma("sp", lambda e: e.dma_start(out=actv[:, 8:32, :], in_=ssdT.rearrange("(c p) t -> p c t", p=128)[:, :, g0:g0 + NT]),
                  writes=["act_%d" % c for c in range(8, 32)])
            for pn in range(4):
                wv, wk = wload(gluW[:, pn * 256:(pn + 1) * 256], 8, 256)
                for o2 in range(2):
                    oc = pn * 2 + o2
                    bks = [pb[o2 * 2], pb[o2 * 2 + 1]]
                    bkk = ["pb%d" % (o2 * 2), "pb%d" % (o2 * 2 + 1)]

                    def mm(e, wv=wv, o2=o2, bks=bks):
                        ins = None
                        for k in range(8):
                            for hi, (a0, a1) in enumerate(halves):
                                ins = e.matmul(bks[hi][:, 0:a1 - a0], wv[:, k, o2 * 128:(o2 + 1) * 128], gv[:, k, a0:a1], start=(k == 0), stop=(k == 7))
                        return ins
                    S.op("pe", mm, reads=[wk, "gT5"], writes=bkk)
                    i = cnt["tmp"] % 2
                    cnt["tmp"] += 1
                    tf = tmpf[i]; tk = "tmpf%d" % i
                    for hi, (a0, a1) in enumerate(halves):
                        S.op("act", lambda e, hi=hi, a0=a0, a1=a1, tf=tf, oc=oc, bks=bks: e.activation(
                            out=tf[:, a0:a1], in_=bks[hi][:, 0:a1 - a0], func=AF.Sigmoid, bias=glubt[:, oc:oc + 1]),
                            reads=[bkk[hi], "glubt"], writes=[tk, bkk[hi]])
                    S.op("dve", lambda e, tf=tf, oc=oc: e.tensor_tensor(out=actv[:, oc, :], in0=gv[:, oc, :], in1=tf[:], op=ALU.mult),
                         reads=[tk, "gT5"], writes=["act_%d" % oc])
            st1.close()
            S.barrier()
        allact = ["act_%d" % c for c in range(DC)]
        gemm_resid.xv = actv
        st = stats_begin()
        gemm_resid(w_out, DC, allact, 2, hTin, hT1, g0, st)
        stats_finish(0, D)
        norm_mod(hT1, g0, weff2, modt, 3, lambda c: actv[:, c, :], lambda c: "act_%d" % c)
        st4 = contextlib.ExitStack()
        gT = st4.enter_context(nc.sbuf_tensor("gT" + "_g%d" % g, [128, HC * NT], BF16))
        gTv = gT[:].rearrange("p (c n) -> p c n", n=NT)
        for pn in range(HC // 2):
            w1v, w1k = wload(w1[:, pn * 256:(pn + 1) * 256], DC, 256)
            w3v, w3k = wload(w3[:, pn * 256:(pn + 1) * 256], DC, 256)
            for o2 in range(2):
                hc = pn * 2 + o2
                bks = [pb[o2 * 4 + i] for i in range(4)]
                bkk = ["pb%d" % (o2 * 4 + i) for i in range(4)]

                def mm(e, w1v=w1v, w3v=w3v, o2=o2, bks=bks):
                    ins = None
                    for wi, wv in enumerate((w1v, w3v)):
                        for k in range(DC):
                            for hi, (a0, a1) in enumerate(halves):
                                ins = e.matmul(bks[wi * 2 + hi][:, 0:a1 - a0], wv[:, k, o2 * 128:(o2 + 1) * 128], actv[:, k, a0:a1],
                                               start=(k == 0), stop=(k == DC - 1))
                    return ins
                S.op("pe", mm, reads=[w1k, w3k] + allact, writes=bkk)
                i = cnt["tmp"] % 2
                cnt["tmp"] += 1
                tf = tmpf[i]; tk = "tmpf%d" % i
                for hi, (a0, a1) in enumerate(halves):
                    S.op("act", lambda e, hi=hi, a0=a0, a1=a1, tf=tf, bks=bks: e.activation(out=tf[:, a0:a1], in_=bks[hi][:, 0:a1 - a0], func=AF.Silu),
                         reads=[bkk[hi]], writes=[tk + "_%d" % hi, bkk[hi]])
                    S.op("dve", lambda e, hi=hi, a0=a0, a1=a1, tf=tf, hc=hc, bks=bks: e.tensor_tensor(
                        out=gTv[:, hc, a0:a1], in0=bks[2 + hi][:, 0:a1 - a0], in1=tf[:, a0:a1], op=ALU.mult),
                        reads=[bkk[2 + hi], tk + "_%d" % hi], writes=["gT_%d" % hc, bkk[2 + hi]])
        gemm_resid.xv = gTv
        st = stats_begin()
        gemm_resid(w2, HC, ["gT_%d" % c for c in range(HC)], 5, hT1, hT2, g0, st)
        stats_finish(0, D)
        st4.close()
        S.barrier()
        if layer == 0:
            def store_a(c):
                pass
            norm_mod(hT2, g0, weffN, modn, 0, lambda c: actv[:, c, :], lambda c: "act_%d" % c)
            S.dma("sp", lambda e: e.dma_start(out=aTout.rearrange("(c p) t -> p c t", p=128)[:, :, g0:g0 + NT], in_=actv), reads=allact)
        else:
            st6 = contextlib.ExitStack()
            yT = st6.enter_context(nc.sbuf_tensor("yT" + "_g%d" % g, [128, DC * NT], F32))
            yv = yT[:].rearrange("p (c n) -> p c n", n=NT)
            norm_mod(hT2, g0, None, None, 0, lambda c: yv[:, c, :], lambda c: "yT_%d" % c)
            ot = [st6.enter_context(nc.sbuf_tensor("otk%d_g%d" % (i, g), [128, 2048], F32)) for i in range(2)]
            for tt in range(NT // 128):
                for hf in range(2):
                    o = ot[hf]; ok = "otk%d" % hf
                    for cq in range(4):
                        cg = hf * 4 + cq
                        bk = pb[cg % 4]; bkn = "pb%d" % (cg % 4)

                        def tr(e, cg=cg, bk=bk, tt=tt):
                            ins = None
                            for k in range(4):
                                c = cg * 4 + k
                                ins = e.transpose(bk[:, k * 128:(k + 1) * 128], yv[:, c, tt * 128:(tt + 1) * 128], ident[:])
                            return ins
                        S.op("pe", tr, reads=["yT_%d" % (cg * 4 + k) for k in range(4)] + ["ident"], writes=[bkn])
                        if cg % 2 == 0:
                            S.op("dve", lambda e, o=o, cq=cq, bk=bk: e.tensor_copy(o[:, cq * 512:(cq + 1) * 512], bk[:, 0:512]), reads=[bkn], writes=[ok, bkn])
                        else:
                            S.op("act", lambda e, o=o, cq=cq, bk=bk: e.activation(out=o[:, cq * 512:(cq + 1) * 512], in_=bk[:, 0:512], func=AF.Copy),
                                 reads=[bkn], writes=[ok, bkn])
                    S.dma("sp", lambda e, o=o, tt=tt, hf=hf: e.dma_start(out=outD[g0 + tt * 128:g0 + (tt + 1) * 128, hf * 2048:(hf + 1) * 2048], in_=o[:]), reads=[ok])
            st6.close()
            S.barrier()
    for g in range(2):
        do_group(g)
    return b.finish()


def core_cols_all(j, layer):
    gc = group_cols(layer)
    loc = np.where(gc < LAT_PC, CTX + j * LAT_PC + gc, j * CTX_PC + (gc - LAT_PC))
    return loc


def stage_D(inp, mod, hT_parts, attnT, convT):
    modt = mod_tables(mod[0])
    modn = mod_tables(mod[1])
    lnT = np.ascontiguousarray(np.concatenate([tab(inp["ev_ln_w"][0]), tab(inp["ev_ln_b"][0])], 1))
    epsv = np.ascontiguousarray(np.tile(np.array([[RMS_EPS, LN_EPS]], np.float32), (128, 1)))
    gc = group_cols(0)
    maps = []
    for j in range(NCORES):
        cols = core_cols_all(j, 0)
        maps.append({"hTin": np.ascontiguousarray(hT_parts[j][:, gc]), "modT": modt, "nw2T": tab(inp["norm_w"][0, 1]),
                     "w_out": inp["ev_w_out"][0], "w1": inp["ffn_w1"][0], "w3": inp["ffn_w3"][0], "w2": inp["ffn_w2"][0],
                     "epsv": epsv, "attnTin": np.ascontiguousarray(attnT[:, cols]), "convTin": np.ascontiguousarray(convT[:, cols]),
                     "lnT": lnT, "modN": modn, "nwN": tab(inp["norm_w"][1, 0])})
    res = run(build_T2(0), maps)
    inv = np.argsort(gc)
    if DEBUG:
        DBG["hT1"] = [np.ascontiguousarray(r["hT1"][:, inv]) for r in res]
    aT = [np.ascontiguousarray(r["aTout"][:, inv]) for r in res]
    hT = [np.ascontiguousarray(r["hTout"][:, inv]) for r in res]
    return hT, aT


E_COLS = 1164
TWO_PI = 2.0 * math.pi


def emit_sincos(b, ang, angk, n, sin_out, cos_out, outk, tag, scratch=None):
    S = b.S
    if scratch is None:
        scratch = (b.sb("sc_yi_" + tag, [128, n], I32), b.sb("sc_yf_" + tag, [128, n]), b.sb("sc_r_" + tag, [128, n]), b.sb("sc_mk_" + tag, [128, n]))
    yi, yf, r, mk = scratch
    tag = "x" if len(tag) > 1 else tag
    S.op("dve", lambda e: e.tensor_scalar(out=yf[:], in0=ang, scalar1=1.0 / TWO_PI, scalar2=None, op0=ALU.mult), reads=[angk], writes=["sc_yf" + tag])
    S.op("dve", lambda e: e.tensor_copy(yi[:], yf[:]), reads=["sc_yf" + tag], writes=["sc_yi" + tag])
    S.op("dve", lambda e: e.tensor_copy(yf[:], yi[:]), reads=["sc_yi" + tag], writes=["sc_yf" + tag])
    S.op("dve", lambda e: e.scalar_tensor_tensor(out=r[:], in0=yf[:], scalar=-TWO_PI, in1=ang, op0=ALU.mult, op1=ALU.add),
         reads=["sc_yf" + tag, angk], writes=["sc_r" + tag])
    S.op("dve", lambda e: e.tensor_single_scalar(out=mk[:], in_=r[:], scalar=math.pi, op=ALU.is_gt), reads=["sc_r" + tag], writes=["sc_mk" + tag])
    S.op("dve", lambda e: e.scalar_tensor_tensor(out=r[:], in0=mk[:], scalar=-TWO_PI, in1=r[:], op0=ALU.mult, op1=ALU.add),
         reads=["sc_mk" + tag, "sc_r" + tag], writes=["sc_r" + tag])
    S.op("dve", lambda e: e.tensor_single_scalar(out=mk[:], in_=r[:], scalar=-math.pi, op=ALU.is_lt), reads=["sc_r" + tag], writes=["sc_mk" + tag])
    S.op("dve", lambda e: e.scalar_tensor_tensor(out=r[:], in0=mk[:], scalar=TWO_PI, in1=r[:], op0=ALU.mult, op1=ALU.add),
         reads=["sc_mk" + tag, "sc_r" + tag], writes=["sc_r" + tag])
    S.op("dve", lambda e: e.tensor_scalar(out=r[:], in0=r[:], scalar1=math.pi, scalar2=-math.pi, op0=ALU.min, op1=ALU.max),
         reads=["sc_r" + tag], writes=["sc_r" + tag])
    S.op("act", lambda e: e.activation(out=sin_out, in_=r[:], func=AF.Sin), reads=["sc_r" + tag], writes=[outk + "_s"])
    S.op("act", lambda e: e.activation(out=yf[:], in_=r[:], func=AF.Sin, scale=0.5), reads=["sc_r" + tag], writes=["sc_yf" + tag])
    S.op("dve", lambda e: e.tensor_tensor(out=yf[:], in0=yf[:], in1=yf[:], op=ALU.mult), reads=["sc_yf" + tag], writes=["sc_yf" + tag])
    S.op("dve", lambda e: e.tensor_scalar(out=cos_out, in0=yf[:], scalar1=-2.0, scalar2=1.0, op0=ALU.mult, op1=ALU.add),
         reads=["sc_yf" + tag], writes=[outk + "_c"])
    return r, "sc_r" + tag


def build_E(phases=(1, 2, 3)):
    b = B()
    S = b.S
    nc = b.nc
    aTd = b.din("aT", [D, NTOK], BF16)
    wd = b.din("w", [D, E_COLS])
    identD = b.din("ident", [128, 128])
    tvecD = b.din("tvec", [128, 512])
    s5pD = b.din("s5p", [128, 3 * 8])
    s5bD = b.din("s5b", [128, 2 * 4 * 16])
    s5cD = b.din("s5c", [128, 2 * 8 * 16])
    s5dD = b.din("s5d", [128, 1])
    s5T = b.dout("s5T", [128, SEQ], BF16)
    uTd = b.dint("uTd", [128, NTOK])
    xbcTd = b.dint("xbcTd", [5, 128, NTOK])
    zdtd = b.dint("zdtd", [NTOK, 396])
    ssd_io = build_E_ssd_decl(b)

    ident = b.sb("identt", [128, 128])
    S.dma("sp", lambda e: e.dma_start(out=ident[:], in_=identD), writes=["ident"])

    if 1 in phases:
        st1 = contextlib.ExitStack()
        Wt = st1.enter_context(nc.sbuf_tensor("Wt", [128, DC * E_COLS], BF16))
        Wv = Wt[:].rearrange("p (kc n) -> p kc n", n=E_COLS)
        for g in range(8):
            S.dma("pool", lambda e, g=g: e.dma_start(out=Wv[:, g * 4:(g + 1) * 4, :],
                                                     in_=wd.rearrange("(kc p) n -> p kc n", p=128)[:, g * 4:(g + 1) * 4, :]), writes=["W%d" % g])
        Wk = ["W%d" % g for g in range(8)]
        aTs = [st1.enter_context(nc.sbuf_tensor("aTs%d" % i, [128, DC * 512], BF16)) for i in range(2)]
        ob = [st1.enter_context(nc.sbuf_tensor("ob%d" % i, [128, 512], F32)) for i in range(3)]
        zb = [st1.enter_context(nc.sbuf_tensor("zb%d" % i, [128, 396], F32)) for i in range(2)]
        psA = [st1.enter_context(nc.psum_tensor("psA%d" % i, [128, 512], F32)) for i in range(3)]
        psZ = [st1.enter_context(nc.psum_tensor("psZ%d" % i, [128, 512], F32)) for i in range(2)]
        sts = supertiles()
        aTv = aTd.rearrange("(c p) t -> p c t", p=128)

        def load_aT(si):
            t0, N = sts[si]
            buf = aTs[si % 2]
            S.dma("sp", lambda e: e.dma_start(out=buf[:, 0:DC * N].rearrange("p (c n) -> p c n", n=N), in_=aTv[:, :, t0:t0 + N]),
                  writes=["aTs%d" % (si % 2)])
        load_aT(0)
        nA = 0
        nZ = 0
        for si, (t0, N) in enumerate(sts):
            if si + 1 < len(sts):
                load_aT(si + 1)
            a = aTs[si % 2][:, 0:DC * N].rearrange("p (c n) -> p c n", n=N)
            ak = "aTs%d" % (si % 2)
            for blk in range(6):
                i = nA % 3
                nA += 1

                def mm(e, a=a, blk=blk, i=i, N=N):
                    ins = None
                    for kc in range(DC):
                        ins = e.matmul(psA[i][:, 0:N], Wv[:, kc, blk * 128:(blk + 1) * 128], a[:, kc, :], start=(kc == 0), stop=(kc == DC - 1))
                    return ins
                S.op("pe", mm, reads=[ak] + Wk, writes=["psA%d" % i])
                if blk % 2 == 0:
                    S.op("dve", lambda e, i=i, N=N: e.tensor_copy(ob[i][:, 0:N], psA[i][:, 0:N]), reads=["psA%d" % i], writes=["ob%d" % i, "psA%d" % i])
                else:
                    S.op("act", lambda e, i=i, N=N: e.activation(out=ob[i][:, 0:N], in_=psA[i][:, 0:N], func=AF.Copy), reads=["psA%d" % i], writes=["ob%d" % i, "psA%d" % i])
                dst = uTd[:, t0:t0 + N] if blk == 0 else xbcTd[blk - 1, :, t0:t0 + N]
                S.dma("sp", lambda e, i=i, N=N, dst=dst: e.dma_start(out=dst, in_=ob[i][:, 0:N]), reads=["ob%d" % i], writes=["uTd" if blk == 0 else "xbcTd"])
            for sub in range(N // 128):
                i = nZ % 2
                nZ += 1

                def mmz(e, a=a, sub=sub, i=i):
                    ins = None
                    for kc in range(DC):
                        ins = e.matmul(psZ[i][:, 0:396], a[:, kc, sub * 128:(sub + 1) * 128], Wv[:, kc, 768:1164], start=(kc == 0), stop=(kc == DC - 1))
                    return ins
                S.op("pe", mmz, reads=[ak] + Wk, writes=["psZ%d" % i])
                S.op("dve", lambda e, i=i: e.tensor_copy(zb[i][:], psZ[i][:, 0:396]), reads=["psZ%d" % i], writes=["zb%d" % i, "psZ%d" % i])
                S.dma("sp", lambda e, i=i, t0=t0, sub=sub: e.dma_start(out=zdtd[t0 + sub * 128:t0 + (sub + 1) * 128, :], in_=zb[i][:]),
                      reads=["zb%d" % i], writes=["zdtd"])
        st1.close()
        S.barrier()

    if 2 in phases:
        st2 = contextlib.ExitStack()

        def T(name, shape, dt=F32):
            return st2.enter_context(nc.sbuf_tensor(name + "_s2", list(shape), dt))
        bsave = b.sb
        b.sb = T
        uT = T("uT", [128, NTOK])
        yf = T("yf", [128, SEQ])
        tvec = T("tvec", [128, 512])
        s5p = T("s5p", [128, 24])
        s5b = T("s5b", [128, 128])
        s5c = T("s5c", [128, 256])
        s5d = T("s5d", [128, 1])
        S.dma("sp", lambda e: e.dma_start(out=uT[:], in_=uTd), reads=["uTd"], writes=["uT"])
        S.dma("sp", lambda e: e.dma_start(out=tvec[:], in_=tvecD), writes=["tvec"])
        S.dma("sp", lambda e: e.dma_start(out=s5p[:], in_=s5pD), writes=["s5p"])
        S.dma("sp", lambda e: e.dma_start(out=s5b[:], in_=s5bD), writes=["s5b"])
        S.dma("sp", lambda e: e.dma_start(out=s5c[:], in_=s5cD), writes=["s5c"])
        S.dma("sp", lambda e: e.dma_start(out=s5d[:], in_=s5dD), writes=["s5d"])
        lre = s5p[:, 0:8]; lim = s5p[:, 8:16]
        dl = T("dl", [128, 8]); th = T("th", [128, 8]); mg = T("mg", [128, 8])
        sn = T("sn", [128, 8]); cs = T("cs", [128, 8])
        S.op("act", lambda e: e.activation(out=dl[:], in_=s5p[:, 16:24], func=AF.Exp), reads=["s5p"], writes=["dl"])
        S.op("dve", lambda e: e.tensor_tensor(out=th[:], in0=lim, in1=dl[:], op=ALU.mult), reads=["s5p", "dl"], writes=["th"])
        S.op("dve", lambda e: e.tensor_tensor(out=mg[:], in0=lre, in1=dl[:], op=ALU.mult), reads=["s5p", "dl"], writes=["mg"])
        S.op("act", lambda e: e.activation(out=mg[:], in_=mg[:], func=AF.Exp), reads=["mg"], writes=["mg"])
        thred, thredk = emit_sincos(b, th[:], "th", 8, sn[:], cs[:], "sc8", "p")
        ar = T("ar", [128, 8]); ai = T("ai", [128, 8]); den = T("den", [128, 8]); kr = T("kr", [128, 8]); ki = T("ki", [128, 8]); tq = T("tq", [128, 8])
        S.op("dve", lambda e: e.tensor_tensor(out=ar[:], in0=mg[:], in1=cs[:], op=ALU.mult), reads=["mg", "sc8_c"], writes=["ar"])
        S.op("dve", lambda e: e.tensor_scalar_add(ar[:], ar[:], -1.0), reads=["ar"], writes=["ar"])
        S.op("dve", lambda e: e.tensor_tensor(out=ai[:], in0=mg[:], in1=sn[:], op=ALU.mult), reads=["mg", "sc8_s"], writes=["ai"])
        S.op("dve", lambda e: e.tensor_tensor(out=den[:], in0=lre, in1=lre, op=ALU.mult), reads=["s5p"], writes=["den"])
        S.op("dve", lambda e: e.tensor_tensor(out=tq[:], in0=lim, in1=lim, op=ALU.mult), reads=["s5p"], writes=["tq"])
        S.op("dve", lambda e: e.tensor_tensor(out=den[:], in0=den[:], in1=tq[:], op=ALU.add), reads=["den", "tq"], writes=["den"])
        S.op("dve", lambda e: e.reciprocal(out=den[:], in_=den[:]), reads=["den"], writes=["den"])
        S.op("dve", lambda e: e.tensor_tensor(out=kr[:], in0=ar[:], in1=lre, op=ALU.mult), reads=["ar", "s5p"], writes=["kr"])
        S.op("dve", lambda e: e.tensor_tensor(out=tq[:], in0=ai[:], in1=lim, op=ALU.mult), reads=["ai", "s5p"], writes=["tq"])
        S.op("dve", lambda e: e.tensor_tensor(out=kr[:], in0=kr[:], in1=tq[:], op=ALU.add), reads=["kr", "tq"], writes=["kr"])
        S.op("dve", lambda e: e.tensor_tensor(out=kr[:], in0=kr[:], in1=den[:], op=ALU.mult), reads=["kr", "den"], writes=["kr"])
        S.op("dve", lambda e: e.tensor_tensor(out=ki[:], in0=ai[:], in1=lre, op=ALU.mult), reads=["ai", "s5p"], writes=["ki"])
        S.op("dve", lambda e: e.tensor_tensor(out=tq[:], in0=ar[:], in1=lim, op=ALU.mult), reads=["ar", "s5p"], writes=["tq"])
        S.op("dve", lambda e: e.tensor_tensor(out=ki[:], in0=ki[:], in1=tq[:], op=ALU.subtract), reads=["ki", "tq"], writes=["ki"])
        S.op("dve", lambda e: e.tensor_tensor(out=ki[:], in0=ki[:], in1=den[:], op=ALU.mult), reads=["ki", "den"], writes=["ki"])
        bbr = T("bbr", [128, 128]); bbi = T("bbi", [128, 128]); tb = T("tb", [128, 128])
        bre = s5b[:, 0:64].rearrange("p (s h) -> p s h", h=16)
        bim = s5b[:, 64:128].rearrange("p (s h) -> p s h", h=16)
        for d in range(2):
            krb = kr[:, d * 4:(d + 1) * 4].unsqueeze(2).to_broadcast([128, 4, 16])
            kib = ki[:, d * 4:(d + 1) * 4].unsqueeze(2).to_broadcast([128, 4, 16])
            ovr = bbr[:, d * 64:(d + 1) * 64].rearrange("p (s h) -> p s h", h=16)
            ovi = bbi[:, d * 64:(d + 1) * 64].rearrange("p (s h) -> p s h", h=16)
            tv = tb[:, d * 64:(d + 1) * 64].rearrange("p (s h) -> p s h", h=16)
            S.op("dve", lambda e, ovr=ovr, krb=krb: e.tensor_tensor(out=ovr, in0=bre, in1=krb, op=ALU.mult), reads=["s5b", "kr"], writes=["bbr"])
            S.op("dve", lambda e, tv=tv, kib=kib: e.tensor_tensor(out=tv, in0=bim, in1=kib, op=ALU.mult), reads=["s5b", "ki"], writes=["tb"])
            S.op("dve", lambda e, ovr=ovr, tv=tv: e.tensor_tensor(out=ovr, in0=ovr, in1=tv, op=ALU.subtract), reads=["bbr", "tb"], writes=["bbr"])
            S.op("dve", lambda e, ovi=ovi, krb=krb: e.tensor_tensor(out=ovi, in0=bim, in1=krb, op=ALU.mult), reads=["s5b", "kr"], writes=["bbi"])
            S.op("dve", lambda e, tv=tv, kib=kib: e.tensor_tensor(out=tv, in0=bre, in1=kib, op=ALU.mult), reads=["s5b", "ki"], writes=["tb"])
            S.op("dve", lambda e, ovi=ovi, tv=tv: e.tensor_tensor(out=ovi, in0=ovi, in1=tv, op=ALU.add), reads=["bbi", "tb"], writes=["bbi"])
        BD = T("BD", [128, 128])
        BL = T("BL", [128, 16 * 128])
        CD = T("CD", [128, 16 * 128])
        psB = st2.enter_context(nc.psum_tensor("psB5", [128, 512], F32))
        S.op("dve", lambda e: e.memset(CD[:], 0.0), writes=["CD"])
        for ri in range(2):
            src = bbr if ri == 0 else bbi
            srck = "bbr" if ri == 0 else "bbi"
            for k in range(8):
                sb_ = k % 4
                S.op("dve", lambda e: e.memset(BD[:], 0.0), writes=["BD"])
                for gl in range(2):
                    g8 = 2 * sb_ + gl
                    S.op("dve", lambda e, src=src, k=k, gl=gl, g8=g8: e.tensor_copy(
                        BD[gl * 64:(gl + 1) * 64, g8 * 16:(g8 + 1) * 16], src[gl * 64:(gl + 1) * 64, k * 16:(k + 1) * 16]),
                        reads=[srck], writes=["BD"])
                    col = (ri * 8 + k) * 128 + g8 * 16
                    sgn = 1.0 if ri == 0 else -1.0
                    S.op("dve", lambda e, ri=ri, k=k, gl=gl, col=col, sgn=sgn: e.tensor_scalar(
                        out=CD[gl * 64:(gl + 1) * 64, col:col + 16], in0=s5c[gl * 64:(gl + 1) * 64, (ri * 8 + k) * 16:(ri * 8 + k + 1) * 16],
                        scalar1=sgn, scalar2=None, op0=ALU.mult), reads=["s5c"], writes=["CD"])
                S.op("pe", lambda e: e.transpose(psB[:, 0:128], BD[:], ident[:]), reads=["BD", "ident"], writes=["psB5"])
                S.op("dve", lambda e, ri=ri, k=k: e.tensor_copy(BL[:, (ri * 8 + k) * 128:(ri * 8 + k + 1) * 128], psB[:, 0:128]),
                     reads=["psB5"], writes=["BL", "psB5"])
        thr = T("thr", [128, 8])
        S.op("dve", lambda e: e.tensor_copy(thr[:], thred[:]), reads=[thredk], writes=["thr"])
        COS = T("COS", [128, 8 * 512]); SIN = T("SIN", [128, 8 * 512])
        ang = T("ang", [128, 512])
        scr = (T("scyi", [128, 512], I32), T("scyf", [128, 512]), T("scr", [128, 512]), T("scmk", [128, 512]))
        for k in range(8):
            S.op("dve", lambda e, k=k: e.tensor_scalar(out=ang[:], in0=tvec[:], scalar1=thr[:, k:k + 1], scalar2=None, op0=ALU.mult),
                 reads=["tvec", "thr"], writes=["ang"])
            emit_sincos(b, ang[:], "ang", 512, SIN[:, k * 512:(k + 1) * 512], COS[:, k * 512:(k + 1) * 512], "tab%d" % k, "t%d" % k, scr)
        NW = 2
        wr = [T("wr%d" % i, [128, 512]) for i in range(NW)]; wi = [T("wi%d" % i, [128, 512]) for i in range(NW)]
        zr = [T("zr%d" % i, [128, 512]) for i in range(NW)]; zi = [T("zi%d" % i, [128, 512]) for i in range(NW)]
        sr = [T("sr%d" % i, [128, 512]) for i in range(NW)]; si_ = [T("si%d" % i, [128, 512]) for i in range(NW)]
        t1 = [T("tt1%d" % i, [128, 512]) for i in range(NW)]
        carry = T("carry", [128, 2 * 8])
        gy = T("gy", [128, 512]); g2 = T("g2", [128, 512]); gob = [T("gob%d" % i, [128, 512], BF16) for i in range(2)]
        psU = [st2.enter_context(nc.psum_tensor("psU%d" % i, [128, 512], F32)) for i in range(4)]
        psY = [st2.enter_context(nc.psum_tensor("psY%d" % i, [128, 512], F32)) for i in range(2)]
        S.op("dve", lambda e: e.memset(carry[:], 0.0), writes=["carry"])
        chunks = [(0, CTX, True)] + [(CTX + 512 * i, 512, False) for i in range(SEQ // 512)]
        nit = 0
        ny = 0
        for d in range(2):
            order = chunks if d == 0 else [chunks[0]] + chunks[:0:-1]
            for (t0, L, is_ctx) in order:
                yps = psY[ny % 2]; ypk = "psY%d" % (ny % 2)
                ny += 1
                for sb_ in range(4):
                    k = d * 4 + sb_
                    i = nit % NW
                    pu = [psU[(nit % 2) * 2], psU[(nit % 2) * 2 + 1]]
                    puk = ["psU%d" % ((nit % 2) * 2), "psU%d" % ((nit % 2) * 2 + 1)]
                    nit += 1

                    def mmb(e, k=k, t0=t0, L=L, pu=pu):
                        e.matmul(pu[0][:, 0:L], BL[:, k * 128:(k + 1) * 128], uT[:, t0:t0 + L], start=True, stop=True)
                        return e.matmul(pu[1][:, 0:L], BL[:, (8 + k) * 128:(9 + k) * 128], uT[:, t0:t0 + L], start=True, stop=True)
                    S.op("pe", mmb, reads=["BL", "uT"], writes=puk)
                    co = COS[:, k * 512:k * 512 + L]; so = SIN[:, k * 512:k * 512 + L]
                    if d == 0:
                        bur = pu[0][:, 0:L]; bui = pu[1][:, 0:L]
                        srv = sr[i][:, 0:L]; siv = si_[i][:, 0:L]
                    else:
                        bur = pu[0][:, 0:L][:, ::-1]; bui = pu[1][:, 0:L][:, ::-1]
                        srv = sr[i][:, 0:L][:, ::-1]; siv = si_[i][:, 0:L][:, ::-1]
                    W_r = wr[i][:, 0:L]; W_i = wi[i][:, 0:L]; Z_r = zr[i][:, 0:L]; Z_i = zi[i][:, 0:L]; T1 = t1[i][:, 0:L]
                    ks = ["wr%d" % i, "wi%d" % i, "zr%d" % i, "zi%d" % i, "sr%d" % i, "si%d" % i, "tt1%d" % i]
                    S.op("dve", lambda e, T1=T1, so=so, bui=bui: e.tensor_tensor(out=T1, in0=bui, in1=so, op=ALU.mult),
                         reads=[puk[1], "tab%d_s" % k, "tab%d_c" % k], writes=[ks[6], puk[1]])
                    S.op("dve", lambda e, W_r=W_r, co=co, bur=bur: e.tensor_tensor(out=W_r, in0=bur, in1=co, op=ALU.mult), reads=[puk[0]], writes=[ks[0], puk[0]])
                    S.op("dve", lambda e, W_r=W_r, T1=T1: e.tensor_tensor(out=W_r, in0=W_r, in1=T1, op=ALU.add), reads=[ks[0], ks[6]], writes=[ks[0]])
                    S.op("dve", lambda e, T1=T1, so=so, bur=bur: e.tensor_tensor(out=T1, in0=bur, in1=so, op=ALU.mult), reads=[puk[0], ks[0]], writes=[ks[6], puk[0]])
                    S.op("dve", lambda e, W_i=W_i, co=co, bui=bui: e.tensor_tensor(out=W_i, in0=bui, in1=co, op=ALU.mult), reads=[puk[1]], writes=[ks[1], puk[1]])
                    S.op("dve", lambda e, W_i=W_i, T1=T1: e.tensor_tensor(out=W_i, in0=W_i, in1=T1, op=ALU.subtract), reads=[ks[1], ks[6]], writes=[ks[1]])
                    S.op("dve", lambda e, Z_r=Z_r, W_r=W_r, k=k, L=L: e.tensor_tensor_scan(
                        out=Z_r, data0=mg[:, k:k + 1].to_broadcast([128, L]), data1=W_r, initial=carry[:, k:k + 1], op0=ALU.mult, op1=ALU.add),
                        reads=[ks[0], "mg", "carry"], writes=[ks[2]])
                    S.op("dve", lambda e, Z_i=Z_i, W_i=W_i, k=k, L=L: e.tensor_tensor_scan(
                        out=Z_i, data0=mg[:, k:k + 1].to_broadcast([128, L]), data1=W_i, initial=carry[:, 8 + k:9 + k], op0=ALU.mult, op1=ALU.add),
                        reads=[ks[1], "mg", "carry"], writes=[ks[3]])
                    S.op("dve", lambda e, T1=T1, so=so, Z_i=Z_i: e.tensor_tensor(out=T1, in0=Z_i, in1=so, op=ALU.mult), reads=[ks[3], ks[1]], writes=[ks[6]])
                    S.op("dve", lambda e, srv=srv, co=co, Z_r=Z_r: e.tensor_tensor(out=srv, in0=Z_r, in1=co, op=ALU.mult), reads=[ks[2]], writes=[ks[4]])
                    S.op("dve", lambda e, srv=srv, T1=T1: e.tensor_tensor(out=srv, in0=srv, in1=T1, op=ALU.subtract), reads=[ks[4], ks[6]], writes=[ks[4]])
                    S.op("dve", lambda e, T1=T1, so=so, Z_r=Z_r: e.tensor_tensor(out=T1, in0=Z_r, in1=so, op=ALU.mult), reads=[ks[2], ks[4]], writes=[ks[6]])
                    S.op("dve", lambda e, siv=siv, co=co, Z_i=Z_i: e.tensor_tensor(out=siv, in0=Z_i, in1=co, op=ALU.mult), reads=[ks[3]], writes=[ks[5]])
                    S.op("dve", lambda e, siv=siv, T1=T1: e.tensor_tensor(out=siv, in0=siv, in1=T1, op=ALU.add), reads=[ks[5], ks[6]], writes=[ks[5]])
                    last = (L - 1) if d == 0 else 0
                    S.op("act", lambda e, i=i, k=k, last=last: e.activation(out=carry[:, k:k + 1], in_=sr[i][:, last:last + 1], func=AF.Copy),
                         reads=[ks[4]], writes=["carry"])
                    S.op("act", lambda e, i=i, k=k, last=last: e.activation(out=carry[:, 8 + k:9 + k], in_=si_[i][:, last:last + 1], func=AF.Copy),
                         reads=[ks[5]], writes=["carry"])
                    if not is_ctx:
                        def mmy(e, i=i, k=k, L=L, sb_=sb_, yps=yps):
                            e.matmul(yps[:, 0:L], CD[:, k * 128:(k + 1) * 128], sr[i][:, 0:L], start=(sb_ == 0), stop=False)
                            return e.matmul(yps[:, 0:L], CD[:, (8 + k) * 128:(9 + k) * 128], si_[i][:, 0:L], start=False, stop=(sb_ == 3))
                        S.op("pe", mmy, reads=["CD", ks[4], ks[5]], writes=[ypk])
                if is_ctx:
                    continue
                l0 = t0 - CTX
                if d == 0:
                    S.op("act", lambda e, l0=l0, L=L, yps=yps: e.activation(out=yf[:, l0:l0 + L], in_=yps[:, 0:L], func=AF.Copy),
                         reads=[ypk], writes=["yf", ypk])
                else:
                    go = gob[ny % 2]; gok = "gob%d" % (ny % 2)
                    S.op("dve", lambda e, l0=l0, L=L, yps=yps: e.tensor_tensor(out=gy[:, 0:L], in0=yps[:, 0:L], in1=yf[:, l0:l0 + L], op=ALU.add),
                         reads=[ypk, "yf"], writes=["gy", ypk])
                    S.op("dve", lambda e, t0=t0, L=L: e.scalar_tensor_tensor(out=gy[:, 0:L], in0=uT[:, t0:t0 + L], scalar=s5d[:, 0:1], in1=gy[:, 0:L],
                                                                             op0=ALU.mult, op1=ALU.add), reads=["uT", "s5d", "gy"], writes=["gy"])
                    S.op("dve", lambda e, L=L: e.tensor_tensor(out=g2[:, 0:L], in0=gy[:, 0:L], in1=gy[:, 0:L], op=ALU.mult), reads=["gy"], writes=["g2"])
                    S.op("dve", lambda e, L=L: e.tensor_scalar(out=g2[:, 0:L], in0=g2[:, 0:L], scalar1=0.044715, scalar2=1.0, op0=ALU.mult, op1=ALU.add),
                         reads=["g2"], writes=["g2"])
                    S.op("dve", lambda e, L=L: e.tensor_tensor(out=g2[:, 0:L], in0=g2[:, 0:L], in1=gy[:, 0:L], op=ALU.mult), reads=["g2", "gy"], writes=["g2"])
                    S.op("act", lambda e, L=L: e.activation(out=g2[:, 0:L], in_=g2[:, 0:L], func=AF.Sigmoid, scale=2.0 * math.sqrt(2.0 / math.pi)),
                         reads=["g2"], writes=["g2"])
                    S.op("dve", lambda e, L=L, go=go: e.tensor_tensor(out=go[:, 0:L], in0=g2[:, 0:L], in1=gy[:, 0:L], op=ALU.mult), reads=["g2", "gy"], writes=[gok])
                    S.dma("sp", lambda e, l0=l0, L=L, go=go: e.dma_start(out=s5T[:, l0:l0 + L], in_=go[:, 0:L]), reads=[gok])
        b.sb = bsave
        st2.close()
        S.barrier()
    if 3 in phases:
        build_E_ssd(b, ssd_io, xbcTd, zdtd, ident)
    return b.finish()


def build_E_ssd_decl(b):
    io = {}
    io["cw"] = b.din("m2cw", [128, 25])
    io["cb"] = b.din("m2cb", [128, 5])
    io["alog"] = b.din("m2alog", [12])
    io["dtb"] = b.din("m2dtb", [12])
    io["dsk"] = b.din("m2d", [6])
    io["nw"] = b.din("m2nw", [384])
    io["tri"] = b.din("tri", [128, 256])
    io["ssdT"] = b.dout("ssdT", [384, SEQ], BF16)
    mk = b.dout if DEBUG else b.dint
    io["xtm"] = mk("xtm", [NTOK, 384])
    io["Btm"] = mk("Btm", [NTOK, 128], BF16)
    io["BTd"] = mk("BTd", [128, NTOK], BF16)
    io["CTd"] = mk("CTd", [128, NTOK], BF16)
    io["dtsp"] = mk("dtsp", [NTOK, 12])
    io["Yf"] = mk("Yf", [SEQ, 384])
    return io


def build_E_ssd(b, io, xbcTd, zdtd, ident):
    S = b.S
    nc = b.nc
    st = contextlib.ExitStack()

    def T(name, shape, dt=F32):
        return st.enter_context(nc.sbuf_tensor(name + "_s3", list(shape), dt))

    def PS(name, dt=F32):
        return st.enter_context(nc.psum_tensor(name + "_s3", [128, 512 if dt == F32 else 1024], dt))
    cw = T("cw", [128, 25]); cb = T("cb", [128, 5]); abc = T("abc", [128, 12]); dtb = T("dtb", [128, 12])
    dsk = T("dsk", [128, 6]); nw = T("nw", [128, 384]); tri = T("tri", [128, 256]); onesf = T("onesf", [128, 128])
    identb = T("identb", [128, 128], BF16)
    S.dma("sp", lambda e: e.dma_start(out=cw[:], in_=io["cw"]), writes=["cw"])
    S.dma("sp", lambda e: e.dma_start(out=cb[:], in_=io["cb"]), writes=["cb"])
    S.dma("sp", lambda e: e.dma_start(out=abc[:], in_=io["alog"].partition_broadcast(128)), writes=["abc"])
    S.dma("sp", lambda e: e.dma_start(out=dtb[:], in_=io["dtb"].partition_broadcast(128)), writes=["dtb"])
    S.dma("sp", lambda e: e.dma_start(out=dsk[:], in_=io["dsk"].partition_broadcast(128)), writes=["dsk"])
    S.dma("sp", lambda e: e.dma_start(out=nw[:], in_=io["nw"].partition_broadcast(128)), writes=["nw"])
    S.dma("sp", lambda e: e.dma_start(out=tri[:], in_=io["tri"]), writes=["tri"])
    S.op("dve", lambda e: e.memset(onesf[:], 1.0), writes=["onesf"])
    S.op("dve", lambda e: e.tensor_copy(identb[:], ident[:]), reads=["ident"], writes=["identb3"])
    S.op("act", lambda e: e.activation(out=abc[:], in_=abc[:], func=AF.Exp), reads=["abc"], writes=["abc"])
    S.op("dve", lambda e: e.tensor_scalar_mul(abc[:], abc[:], -1.0), reads=["abc"], writes=["abc"])

    psTb = PS("psTb", BF16)
    st_outer = st
    st = contextlib.ExitStack()
    xin = [T("xin%d" % i, [128, 5 * 516]) for i in range(2)]
    acc = [T("cacc%d" % i, [128, 512]) for i in range(2)]
    xcb = [T("xcb%d" % i, [128, 512], BF16) for i in range(2)]
    xtt = [T("xtt%d" % i, [128, 384]) for i in range(4)]
    btt = [T("btt%d" % i, [128, 128], BF16) for i in range(2)]
    dtt = [T("dtt%d" % i, [128, 4 * 12]) for i in range(2)]
    psT = [PS("psT%d" % i) for i in range(2)]
    xbv = xbcTd.rearrange("b p t -> p b t")
    sts = supertiles()
    ntr = 0
    for si, (t0, N) in enumerate(sts):
        seq0, seq1 = (0, CTX) if si == 0 else (CTX, NTOK)
        lo = max(seq0, t0 - 2); hi = min(seq1, t0 + N + 2)
        xi = xin[si % 2]; xk = "xin%d" % (si % 2)
        xiv = xi[:].rearrange("p (b n) -> p b n", n=516)
        S.op("dve", lambda e, xi=xi: e.memset(xi[:], 0.0), writes=[xk])
        S.dma("sp", lambda e, xiv=xiv, lo=lo, hi=hi, t0=t0: e.dma_start(out=xiv[:, :, lo - (t0 - 2):hi - (t0 - 2)], in_=xbv[:, :, lo:hi]),
              reads=["xbcTd"], writes=[xk])
        nsub = N // 128
        dt_ = dtt[si % 2]; dk = "dtt%d" % (si % 2)
        dv = dt_[:, 0:nsub * 12].rearrange("p (s c) -> p s c", c=12)
        S.dma("sp", lambda e, dv=dv, t0=t0, N=N: e.dma_start(out=dv, in_=zdtd[t0:t0 + N, 384:396].rearrange("(s p) c -> p s c", p=128)),
              reads=["zdtd"], writes=[dk])
        S.op("dve", lambda e, dv=dv, nsub=nsub: e.tensor_tensor(out=dv, in0=dv, in1=dtb[:].unsqueeze(1).to_broadcast([128, nsub, 12]), op=ALU.add),
             reads=[dk, "dtb"], writes=[dk])
        S.op("act", lambda e, dv=dv: e.activation(out=dv, in_=dv, func=AF.Exp), reads=[dk], writes=[dk])
        S.op("act", lambda e, dv=dv: e.activation(out=dv, in_=dv, func=AF.Ln, bias=1.0), reads=[dk], writes=[dk])
        S.dma("sp", lambda e, dv=dv, t0=t0, N=N: e.dma_start(out=io["dtsp"][t0:t0 + N, :].rearrange("(s p) c -> p s c", p=128), in_=dv),
              reads=[dk], writes=["dtsp"])
        for blk in range(5):
            ai = (si * 5 + blk) % 2
            ac = acc[ai]; ack = "cacc%d" % ai
            xc = xcb[ai]; xck = "xcb%d" % ai

            def conv(e, ac=ac, xiv=xiv, blk=blk, N=N):
                ins = e.tensor_scalar(out=ac[:, 0:N], in0=xiv[:, blk, 0:N], scalar1=cw[:, blk * 5:blk * 5 + 1], scalar2=cb[:, blk:blk + 1],
                                      op0=ALU.mult, op1=ALU.add)
                for k in range(1, 5):
                    ins = e.scalar_tensor_tensor(out=ac[:, 0:N], in0=xiv[:, blk, k:k + N], scalar=cw[:, blk * 5 + k:blk * 5 + k + 1],
                                                 in1=ac[:, 0:N], op0=ALU.mult, op1=ALU.add)
                return ins
            S.op("dve", conv, reads=[xk, "cw", "cb"], writes=[ack])
            if blk < 3:
                S.op("act", lambda e, ac=ac, N=N: e.activation(out=ac[:, 0:N], in_=ac[:, 0:N], func=AF.Silu), reads=[ack], writes=[ack])
            else:
                S.op("act", lambda e, ac=ac, xc=xc, N=N: e.activation(out=xc[:, 0:N], in_=ac[:, 0:N], func=AF.Silu), reads=[ack], writes=[xck])
                dstT = io["BTd"] if blk == 3 else io["CTd"]
                S.dma("sp", lambda e, xc=xc, N=N, t0=t0, dstT=dstT: e.dma_start(out=dstT[:, t0:t0 + N], in_=xc[:, 0:N]), reads=[xck],
                      writes=["BTd" if blk == 3 else "CTd"])
            for sub in range(nsub):
                if blk < 3:
                    pt = psT[ntr % 2]; ptk = "psT%d" % (ntr % 2)
                    xo = xtt[sub]; xok = "xtt%d" % sub
                    ntr += 1
                    S.op("pe", lambda e, pt=pt, ac=ac, sub=sub: e.transpose(pt[:, 0:128], ac[:, sub * 128:(sub + 1) * 128], ident[:]),
                         reads=[ack, "ident"], writes=[ptk])
                    S.op("act", lambda e, pt=pt, xo=xo, blk=blk: e.activation(out=xo[:, blk * 128:(blk + 1) * 128], in_=pt[:, 0:128], func=AF.Copy),
                         reads=[ptk], writes=[xok + "_%d" % blk, ptk])
                    if blk == 2:
                        S.dma("sp", lambda e, xo=xo, t0=t0, sub=sub: e.dma_start(out=io["xtm"][t0 + sub * 128:t0 + (sub + 1) * 128, :], in_=xo[:]),
                              reads=[xok + "_0", xok + "_1", xok + "_2"], writes=["xtm"])
                elif blk == 3:
                    bo = btt[sub % 2]; bok = "btt%d" % (sub % 2)
                    S.op("pe", lambda e, xc=xc, sub=sub: e.transpose(psTb[:, 0:128], xc[:, sub * 128:(sub + 1) * 128], identb[:]),
                         reads=[xck, "identb3"], writes=["psTb"])
                    S.op("act", lambda e, bo=bo: e.activation(out=bo[:], in_=psTb[:, 0:128], func=AF.Copy), reads=["psTb"], writes=[bok, "psTb"])
                    S.dma("sp", lambda e, bo=bo, t0=t0, sub=sub: e.dma_start(out=io["Btm"][t0 + sub * 128:t0 + (sub + 1) * 128, :], in_=bo[:]),
                          reads=[bok], writes=["Btm"])

    st.close()
    S.barrier()
    st = st_outer
    xt = [T("xt%d" % i, [128, 384]) for i in range(2)]
    Bt = [T("Bt%d" % i, [128, 128], BF16) for i in range(2)]
    BTc = [T("BTc%d" % i, [128, 128], BF16) for i in range(2)]
    CTc = [T("CTc%d" % i, [128, 128], BF16) for i in range(2)]
    dtc = [T("dtc%d" % i, [128, 12]) for i in range(2)]
    zt = [T("zt%d" % i, [128, 384]) for i in range(2)]
    yft = [T("yft%d" % i, [128, 384]) for i in range(2)]
    da = T("da", [128, 6]); cumc = T("cumc", [128, 16]); ncum = T("ncum", [128, 6]); ecum = T("ecum", [128, 6])
    dsc = T("dsc", [128, 6]); dch = T("dch", [128, 6])
    GmT = T("GmT", [128, 128]); Ex = [T("Ex%d" % i, [128, 128]) for i in range(2)]
    MT = [T("MT%d" % i, [128, 128], BF16) for i in range(6)]
    xdt = T("xdt", [128, 384], BF16); xw = T("xw", [128, 384], BF16)
    yo = T("yo", [128, 384]); ydir = [T("ydir%d" % i, [128, 384]) for i in range(2)]
    Sst = T("Sst", [128, 384]); Sb = T("Sb", [128, 384], BF16)
    ysq = T("ysq", [128, 384]); rs = T("rs", [128, 4]); ynb = T("ynb", [128, 384], BF16)
    oT = [T("oTs%d" % i, [128, 3 * 128], BF16) for i in range(2)]
    psC = PS("psC"); psCB = [PS("psCB0"), PS("psCB1")]; psG = PS("psG"); psYd = PS("psYd"); psYo = PS("psYo"); psSt = PS("psSt")
    chunks = list(range(NKB))
    nld = 0
    nyo = 0
    for d in range(2):
        order = chunks if d == 0 else [1, 0] + chunks[:1:-1]
        trd = tri[:, d * 128:(d + 1) * 128]
        S.op("dve", lambda e: e.memset(Sst[:], 0.0), writes=["Sst"])
        S.op("dve", lambda e: e.memset(Sb[:], 0.0), writes=["Sb"])
        for c in order:
            t0 = c * 128
            is_ctx = c < 2
            i = nld % 2
            nld += 1
            lk = ["xt%d" % i, "Bt%d" % i, "BTc%d" % i, "CTc%d" % i, "dtc%d" % i]
            S.dma("sp", lambda e, i=i, t0=t0: e.dma_start(out=xt[i][:], in_=io["xtm"][t0:t0 + 128, :]), reads=["xtm"], writes=[lk[0]])
            S.dma("sp", lambda e, i=i, t0=t0: e.dma_start(out=Bt[i][:], in_=io["Btm"][t0:t0 + 128, :]), reads=["Btm"], writes=[lk[1]])
            S.dma("sp", lambda e, i=i, t0=t0: e.dma_start(out=dtc[i][:], in_=io["dtsp"][t0:t0 + 128, :]), reads=["dtsp"], writes=[lk[4]])
            if not is_ctx:
                S.dma("sp", lambda e, i=i, t0=t0: e.dma_start(out=BTc[i][:], in_=io["BTd"][:, t0:t0 + 128]), reads=["BTd"], writes=[lk[2]])
                S.dma("sp", lambda e, i=i, t0=t0: e.dma_start(out=CTc[i][:], in_=io["CTd"][:, t0:t0 + 128]), reads=["CTd"], writes=[lk[3]])
            dth = dtc[i][:, d * 6:(d + 1) * 6]
            S.op("dve", lambda e, dth=dth, d=d: e.tensor_tensor(out=da[:], in0=dth, in1=abc[:, d * 6:(d + 1) * 6], op=ALU.mult),
                 reads=[lk[4], "abc"], writes=["da"])

            def mmc(e, trd=trd):
                e.matmul(psC[:, 0:6], trd, da[:], start=True, stop=True)
                return e.matmul(psC[:, 8:14], onesf[:], da[:], start=True, stop=True)
            S.op("pe", mmc, reads=["da", "tri", "onesf"], writes=["psC"])
            S.op("dve", lambda e: e.tensor_copy(cumc[:, 0:14], psC[:, 0:14]), reads=["psC"], writes=["cumc", "psC"])
            S.op("dve", lambda e: e.tensor_tensor(out=dsc[:], in0=cumc[:, 8:14], in1=cumc[:, 0:6], op=ALU.subtract), reads=["cumc"], writes=["dsc"])
            S.op("act", lambda e: e.activation(out=dsc[:], in_=dsc[:], func=AF.Exp), reads=["dsc"], writes=["dsc"])
            S.op("act", lambda e: e.activation(out=dch[:], in_=cumc[:, 8:14], func=AF.Exp), reads=["cumc"], writes=["dch"])
            xv6 = xt[i][:].rearrange("p (h q) -> p h q", q=64)
            S.op("dve", lambda e, xv6=xv6, dth=dth: e.tensor_tensor(out=xdt[:].rearrange("p (h q) -> p h q", q=64), in0=xv6,
                                                                     in1=dth.unsqueeze(2).to_broadcast([128, 6, 64]), op=ALU.mult),
                 reads=[lk[0], lk[4]], writes=["xdt"])
            if not is_ctx:
                S.op("dve", lambda e: e.tensor_scalar_mul(ncum[:], cumc[:, 0:6], -1.0), reads=["cumc"], writes=["ncum"])
                S.op("act", lambda e: e.activation(out=ecum[:], in_=cumc[:, 0:6], func=AF.Exp), reads=["cumc"], writes=["ecum"])

                def mmcb(e, trd=trd):
                    ins = None
                    for h in range(6):
                        ins = e.matmul(psCB[h // 4][:, (h % 4) * 128:(h % 4 + 1) * 128], da[:, h:h + 1].to_broadcast([128, 128]), trd,
                                       start=True, stop=True)
                    return ins
                S.op("pe", mmcb, reads=["da", "tri"], writes=["psCB0", "psCB1"])
                S.op("pe", lambda e, i=i: e.matmul(psG[:, 0:128], BTc[i][:], CTc[i][:], start=True, stop=True), reads=[lk[2], lk[3]], writes=["psG"])
                S.op("dve", lambda e, trd=trd: e.tensor_tensor(out=GmT[:], in0=psG[:, 0:128], in1=trd, op=ALU.mult), reads=["psG", "tri"], writes=["GmT", "psG"])
                S.op("pe", lambda e, i=i: e.matmul(psYo[:, 0:384], CTc[i][:], Sb[:], start=True, stop=True), reads=[lk[3], "Sb"], writes=["psYo"])
                S.op("act", lambda e: e.activation(out=yo[:], in_=psYo[:, 0:384], func=AF.Copy), reads=["psYo"], writes=["yo", "psYo"])
                for h in range(6):
                    ex = Ex[h % 2]; exk = "Ex%d" % (h % 2)
                    cbk = "psCB%d" % (h // 4)
                    S.op("act", lambda e, h=h, ex=ex: e.activation(out=ex[:], in_=psCB[h // 4][:, (h % 4) * 128:(h % 4 + 1) * 128], func=AF.Exp,
                                                                   bias=ncum[:, h:h + 1]), reads=[cbk, "ncum"], writes=[exk, cbk])
                    S.op("dve", lambda e, h=h, ex=ex: e.scalar_tensor_tensor(out=MT[h][:], in0=ex[:], scalar=1.0, in1=GmT[:], op0=ALU.min, op1=ALU.mult),
                         reads=[exk, "GmT"], writes=["MT%d" % h])
                    S.op("pe", lambda e, h=h: e.matmul(psYd[:, h * 64:(h + 1) * 64], MT[h][:], xdt[:, h * 64:(h + 1) * 64], start=True, stop=True),
                         reads=["MT%d" % h, "xdt"], writes=["psYd"])
                yd = ydir[nyo % 2]; ydk = "ydir%d" % (nyo % 2)
                nyo += 1
                S.op("dve", lambda e: e.tensor_tensor(out=yo[:].rearrange("p (h q) -> p h q", q=64), in0=yo[:].rearrange("p (h q) -> p h q", q=64),
                                                      in1=ecum[:].unsqueeze(2).to_broadcast([128, 6, 64]), op=ALU.mult), reads=["yo", "ecum"], writes=["yo"])
                S.op("dve", lambda e, yd=yd: e.tensor_tensor(out=yd[:], in0=psYd[:, 0:384], in1=yo[:], op=ALU.add), reads=["psYd", "yo"], writes=[ydk, "psYd"])
            S.op("dve", lambda e: e.tensor_tensor(out=xw[:].rearrange("p (h q) -> p h q", q=64), in0=xdt[:].rearrange("p (h q) -> p h q", q=64),
                                                  in1=dsc[:].unsqueeze(2).to_broadcast([128, 6, 64]), op=ALU.mult), reads=["xdt", "dsc"], writes=["xw"])
            S.op("pe", lambda e, i=i: e.matmul(psSt[:, 0:384], Bt[i][:], xw[:], start=True, stop=True), reads=[lk[1], "xw"], writes=["psSt"])
            S.op("dve", lambda e: e.tensor_tensor(out=Sst[:].rearrange("p (h q) -> p h q", q=64), in0=Sst[:].rearrange("p (h q) -> p h q", q=64),
                                                  in1=dch[:].unsqueeze(2).to_broadcast([128, 6, 64]), op=ALU.mult), reads=["Sst", "dch"], writes=["Sst"])
            S.op("dve", lambda e: e.tensor_tensor(out=Sst[:], in0=Sst[:], in1=psSt[:, 0:384], op=ALU.add), reads=["Sst", "psSt"], writes=["Sst", "psSt"])
            S.op("act", lambda e: e.activation(out=Sb[:], in_=Sst[:], func=AF.Copy), reads=["Sst"], writes=["Sb"])
            if is_ctx:
                continue
            l0 = t0 - CTX
            if d == 0:
                S.dma("sp", lambda e, yd=yd, l0=l0: e.dma_start(out=io["Yf"][l0:l0 + 128, :], in_=yd[:]), reads=[ydk], writes=["Yf"])
            else:
                j = nyo % 2
                S.dma("sp", lambda e, j=j, l0=l0: e.dma_start(out=yft[j][:], in_=io["Yf"][l0:l0 + 128, :]), reads=["Yf"], writes=["yft%d" % j])
                S.dma("sp", lambda e, j=j, t0=t0: e.dma_start(out=zt[j][:], in_=zdtd[t0:t0 + 128, 0:384]), reads=["zdtd"], writes=["zt%d" % j])
                S.op("dve", lambda e, yd=yd, j=j: e.tensor_tensor(out=yd[:], in0=yd[:], in1=yft[j][:], op=ALU.add), reads=[ydk, "yft%d" % j], writes=[ydk])
                S.op("dve", lambda e, xv6=xv6: e.tensor_tensor(out=ysq[:].rearrange("p (h q) -> p h q", q=64), in0=xv6,
                                                                in1=dsk[:].unsqueeze(2).to_broadcast([128, 6, 64]), op=ALU.mult), reads=[lk[0], "dsk"], writes=["ysq"])
                S.op("dve", lambda e, yd=yd: e.tensor_tensor(out=yd[:], in0=yd[:], in1=ysq[:], op=ALU.add), reads=[ydk, "ysq"], writes=[ydk])
                S.op("act", lambda e, j=j: e.activation(out=zt[j][:], in_=zt[j][:], func=AF.Silu), reads=["zt%d" % j], writes=["zt%d" % j])
                S.op("dve", lambda e, yd=yd, j=j: e.tensor_tensor(out=yd[:], in0=yd[:], in1=zt[j][:], op=ALU.mult), reads=[ydk, "zt%d" % j], writes=[ydk])
                S.op("dve", lambda e, yd=yd: e.tensor_tensor(out=ysq[:], in0=yd[:], in1=yd[:], op=ALU.mult), reads=[ydk, "ysq"], writes=["ysq"])
                S.op("dve", lambda e: e.reduce_sum(out=rs[:, 0:1], in_=ysq[:], axis=AX.X), reads=["ysq"], writes=["rs"])
                S.op("dve", lambda e: e.tensor_scalar(out=rs[:, 1:2], in0=rs[:, 0:1], scalar1=1.0 / 384, scalar2=RMS_EPS, op0=ALU.mult, op1=ALU.add),
                     reads=["rs"], writes=["rs"])
                S.op("act", lambda e: e.activation(out=rs[:, 2:3], in_=rs[:, 1:2], func=AF.Sqrt), reads=["rs"], writes=["rs"])
                S.op("dve", lambda e: e.reciprocal(out=rs[:, 3:4], in_=rs[:, 2:3]), reads=["rs"], writes=["rs"])
                S.op("dve", lambda e, yd=yd: e.scalar_tensor_tensor(out=ynb[:], in0=yd[:], scalar=rs[:, 3:4], in1=nw[:], op0=ALU.mult, op1=ALU.mult),
                     reads=[ydk, "rs", "nw"], writes=["ynb"])

                def tr3(e):
                    ins = None
                    for k in range(3):
                        ins = e.transpose(psTb[:, k * 128:(k + 1) * 128], ynb[:, k * 128:(k + 1) * 128], identb[:])
                    return ins
                S.op("pe", tr3, reads=["ynb", "identb3"], writes=["psTb"])
                o = oT[j]; ok = "oTs%d" % j
                S.op("act", lambda e, o=o: e.activation(out=o[:], in_=psTb[:, 0:384], func=AF.Copy), reads=["psTb"], writes=[ok, "psTb"])
                S.dma("sp", lambda e, o=o, l0=l0: e.dma_start(out=io["ssdT"].rearrange("(k p) t -> p k t", p=128)[:, :, l0:l0 + 128],
                                                              in_=o[:].rearrange("p (k n) -> p k n", n=128)), reads=[ok])
    st.close()


def ssd_inputs(inp, j):
    ch = np.concatenate([j * 384 + np.arange(384), 3072 + j * 128 + np.arange(128), 4096 + j * 128 + np.arange(128)])
    cwj = inp["m2_conv_w"][0][:, ch]
    cw = np.ascontiguousarray(cwj.reshape(5, 5, 128).transpose(2, 1, 0)).reshape(128, 25)
    cb = np.ascontiguousarray(inp["m2_conv_b"][0][ch].reshape(5, 128).T)
    hs = 6 * j + np.arange(6)
    tri = np.ascontiguousarray(np.concatenate([np.triu(np.ones((128, 128), np.float32)), np.tril(np.ones((128, 128), np.float32))], 1))
    return {"m2cw": cw, "m2cb": cb, "m2alog": np.ascontiguousarray(inp["m2_a_log"][0][:, hs].reshape(12)),
            "m2dtb": np.ascontiguousarray(inp["m2_dt_bias"][0][:, hs].reshape(12)), "m2d": np.ascontiguousarray(inp["m2_d"][0][hs]),
            "m2nw": np.ascontiguousarray(inp["m2_norm_w"][0][j * 384:(j + 1) * 384]), "tri": tri}


def s5_tables(inp, j):
    gs = np.arange(8 * j, 8 * j + 8)
    def st(a):
        a = a.reshape(2, 4, 2, 64)
        return np.ascontiguousarray(a.transpose(2, 3, 0, 1).reshape(128, 8))
    lre = st(inp["s5_lam_re"][0][:, gs]); lim = st(inp["s5_lam_im"][0][:, gs])
    ls = st(np.repeat(inp["s5_log_step"][0][:, gs][:, :, None], 64, 2))
    s5p = np.ascontiguousarray(np.concatenate([lre, lim, ls], 1))
    def bt(a):
        a = a.reshape(4, 2, 64, 16)
        return a.transpose(1, 2, 0, 3).reshape(128, 64)
    s5b = np.ascontiguousarray(np.concatenate([bt(inp["s5_b_re"][0][gs]), bt(inp["s5_b_im"][0][gs])], 1))
    def ct(a):
        a = a.reshape(2, 4, 2, 16, 64)
        return a.transpose(2, 4, 0, 1, 3).reshape(128, 128)
    s5c = np.ascontiguousarray(np.concatenate([ct(inp["s5_c_re"][0][:, gs]), ct(inp["s5_c_im"][0][:, gs])], 1))
    s5d = np.ascontiguousarray(inp["s5_d"][0][j * 128:(j + 1) * 128].reshape(128, 1))
    return s5p, s5b, s5c, s5d


def e_cols(j):
    return np.concatenate([np.arange(j * 128, (j + 1) * 128), 1024 + np.arange(j * 384, (j + 1) * 384),
                           1024 + 3072 + np.arange(j * 128, (j + 1) * 128), 1024 + 4096 + np.arange(j * 128, (j + 1) * 128),
                           6240 + np.arange(j * 384, (j + 1) * 384), 6144 + 6 * j + np.arange(6), 6144 + 48 + 6 * j + np.arange(6)])


def stage_E(inp, aT_all, phases=(1, 2, 3)):
    w = inp["od_w_in"][0]
    ident = np.eye(128, dtype=np.float32)
    tvec = np.ascontiguousarray(np.tile(np.arange(1, 513, dtype=np.float32)[None, :], (128, 1)))
    maps = []
    for j in range(NCORES):
        s5p, s5b, s5c, s5d = s5_tables(inp, j)
        m = {"aT": aT_all, "w": np.ascontiguousarray(w[:, e_cols(j)]), "ident": ident, "tvec": tvec,
             "s5p": s5p, "s5b": s5b, "s5c": s5c, "s5d": s5d}
        m.update(ssd_inputs(inp, j))
        maps.append(m)
    res = run(build_E(phases), maps)
    return res


def stage_F(inp, mod, hT_lat_parts, s5T, ssdT):
    modt = mod_tables(mod[1])
    epsv = np.ascontiguousarray(np.tile(np.array([[RMS_EPS, LN_EPS]], np.float32), (128, 1)))
    ident = np.eye(128, dtype=np.float32)
    maps = []
    for j in range(NCORES):
        sl = slice(j * LAT_PC, (j + 1) * LAT_PC)
        maps.append({"hTin": np.ascontiguousarray(hT_lat_parts[j]), "modT": modt, "nw2T": tab(inp["norm_w"][1, 1]),
                     "w_out": inp["od_w_out"][0], "w1": inp["ffn_w1"][1], "w3": inp["ffn_w3"][1], "w2": inp["ffn_w2"][1],
                     "epsv": epsv, "s5T": np.ascontiguousarray(s5T[:, sl]), "ssdT": np.ascontiguousarray(ssdT[:, sl]),
                     "gluW": inp["s5_glu_w"][0], "glubT": tab(inp["s5_glu_b"][0]), "nwN": tab(inp["final_norm_w"]), "ident": ident})
    res = run(build_T2(1), maps)
    return np.concatenate([r["out"] for r in res], 0)


def kernel(**inp):
    inp = {k: np.asarray(v) for k, v in inp.items()}
    mod = stage_A(inp)
    hT0, aT0 = stage_B(inp, mod)
    aT0_all = gather_T(aT0)
    attnT, convT = stage_C(inp, aT0_all)
    hT1, aT1 = stage_D(inp, mod, hT0, attnT, convT)
    aT1_all = gather_T(aT1)
    resE = stage_E(inp, aT1_all)
    s5T = np.concatenate([r["s5T"] for r in resE], 0)
    ssdT = np.concatenate([r["ssdT"] for r in resE], 0)
    out = stage_F(inp, mod, [h[:, :LAT_PC] for h in hT1], s5T, ssdT)
    return out.reshape(1, SEQ, D).astype(np.float32)
```

```python
import contextlib
import math
import numpy as np
import ml_dtypes
import concourse.bass as bass
import concourse.mybir as mybir
from concourse.bass_utils import run_bass_kernel_spmd

F32 = mybir.dt.float32
BF16 = mybir.dt.bfloat16
I32 = mybir.dt.int32
AF = mybir.ActivationFunctionType
ALU = mybir.AluOpType
AX = mybir.AxisListType

NCORES = 8
D = 4096
DC = 32
SEQ = 8192
CTX = 256
NTOK = SEQ + CTX
LAT_PC = SEQ // NCORES
CTX_PC = CTX // NCORES
FFN_H = 11008
HC = FFN_H // 128
RMS_EPS = 1e-6
LN_EPS = 1e-5

ENGS = ("pe", "act", "dve", "pool", "sp")
EPOCH = 30000
DMA_K = 8


class Sched:
    def __init__(self, nc):
        self.nc = nc
        self.ops = {e: [] for e in ENGS}
        self.cnt = {e: 0 for e in ENGS}
        self.dcnt = {e: 0 for e in ENGS}
        self.last_w = {}
        self.readers = {}
        self.waited = {e: {} for e in ENGS}
        self.dma_tokens = []
        self.bar_deps = []
        self.bar_pending = set()
        self.last_dma = {}

    def barrier(self):
        deps = list(self.last_dma.values())
        for e in ENGS:
            if self.cnt[e] > 0:
                idx = self.cnt[e] - 1
                deps.append((("c", e, idx // EPOCH), idx % EPOCH + 1, e))
        self.bar_deps = deps
        self.bar_pending = set(ENGS)

    def _deps(self, reads, writes):
        deps = []
        for k in reads:
            t = self.last_w.get(k)
            if t is not None:
                deps.append(t)
        for k in writes:
            t = self.last_w.get(k)
            if t is not None:
                deps.append(t)
            deps.extend(self.readers.get(k, ()))
        return deps

    def _commit(self, tok, reads, writes):
        for k in reads:
            if k in writes:
                continue
            self.readers.setdefault(k, []).append(tok)
        for k in writes:
            self.last_w[k] = tok
            self.readers[k] = []

    def _waits(self, eng, deps):
        best = {}
        for (sk, val, src) in deps:
            if best.get(sk, 0) < val:
                best[sk] = val
        out = []
        for sk, val in best.items():
            if sk[0] == "c":
                lin = sk[2] * EPOCH + val
                key = ("clin", sk[1])
                if self.waited[eng].get(key, 0) >= lin:
                    continue
                self.waited[eng][key] = lin
                out.append((sk, val))
            else:
                if self.waited[eng].get(sk, 0) >= val:
                    continue
                self.waited[eng][sk] = val
                out.append((sk, val))
        return out

    def op(self, eng, fn, reads=(), writes=()):
        reads = tuple(reads)
        writes = tuple(writes)
        deps = self._deps(reads, writes)
        if eng in self.bar_pending:
            deps = deps + self.bar_deps
            self.bar_pending.discard(eng)
        waits = self._waits(eng, deps)
        idx = self.cnt[eng]
        self.cnt[eng] += 1
        sk = ("c", eng, idx // EPOCH)
        tok = (sk, idx % EPOCH + 1, eng)
        self.ops[eng].append((fn, waits, (sk, 1)))
        self._commit(tok, reads, writes)
        return tok

    def dma(self, eng, fn, reads=(), writes=()):
        reads = tuple(reads)
        writes = tuple(writes)
        deps = self._deps(reads, writes)
        n = self.dcnt[eng]
        self.dcnt[eng] += 1
        slot = n % DMA_K
        rnd = n // DMA_K
        sk = ("d", eng, slot)
        if rnd > 0:
            deps.append((sk, 16 * rnd, eng))
        if eng in self.bar_pending:
            deps = deps + self.bar_deps
            self.bar_pending.discard(eng)
        waits = self._waits(eng, deps)
        tok = (sk, 16 * (rnd + 1), eng)
        self.last_dma[sk] = tok
        self.ops[eng].append((fn, waits, (sk, 16)))
        self._commit(tok, reads, writes)
        self.dma_tokens.append(tok)
        return tok

    def emit(self):
        nc = self.nc
        final_deps = list(self.dma_tokens)
        for e in ENGS:
            if self.cnt[e] > 0:
                idx = self.cnt[e] - 1
                final_deps.append((("c", e, idx // EPOCH), idx % EPOCH + 1, e))
        fw = self._waits("sp", final_deps)
        self.ops["sp"].append((None, fw, None))
        semkeys = set()
        for e in ENGS:
            for (fn, waits, inc) in self.ops[e]:
                for (sk, v) in waits:
                    semkeys.add(sk)
                if inc is not None:
                    semkeys.add(inc[0])
        semkeys = sorted(semkeys)
        with contextlib.ExitStack() as st:
            sems = {}
            for sk in semkeys:
                sems[sk] = st.enter_context(nc.semaphore("s_%s_%s_%d" % sk))
            block = st.enter_context(nc.Block())
            handles = {"pe": block.tensor, "act": block.scalar, "dve": block.vector,
                       "pool": block.gpsimd, "sp": block.sync}
            for e in ENGS:
                ops = self.ops[e]
                if not ops:
                    continue

                def body(eng, ops=ops):
                    for (fn, waits, inc) in ops:
                        for (sk, v) in waits:
                            eng.wait_ge(sems[sk], v)
                        if fn is not None:
                            ins = fn(eng)
                            ins.then_inc(sems[inc[0]], inc[1])
                handles[e](body)


class B:
    def __init__(self):
        self.nc = bass.Bass("TRN2", target_bir_lowering=False)
        self.S = Sched(self.nc)
        self.st = contextlib.ExitStack()
        self.nps = 0

    def din(self, name, shape, dt=F32):
        return self.nc.dram_tensor(name, list(shape), dt, kind="ExternalInput").ap()

    def dout(self, name, shape, dt=F32):
        return self.nc.dram_tensor(name, list(shape), dt, kind="ExternalOutput").ap()

    def dint(self, name, shape, dt=F32):
        return self.nc.dram_tensor(name, list(shape), dt, kind="Internal").ap()

    def sb(self, name, shape, dt=F32):
        return self.st.enter_context(self.nc.sbuf_tensor(name, list(shape), dt))

    def ps(self, name, shape=(128, 512), dt=F32):
        self.nps += 1
        return self.st.enter_context(self.nc.psum_tensor(name, list(shape), dt))

    def finish(self):
        self.S.emit()
        self.st.close()
        return self.nc


TRACE = False
DEBUG = False
DBG = {}
LAST_NS = []


RUN_CORES = NCORES


def run(nc, in_maps):
    if RUN_CORES < NCORES:
        res = run_bass_kernel_spmd(nc, in_maps[:RUN_CORES], core_ids=list(range(RUN_CORES)), trace=TRACE)
        if TRACE:
            print("exec_time_ns", res.exec_time_ns, flush=True)
        return list(res.results) + [res.results[0]] * (NCORES - RUN_CORES)
    if TRACE:
        res = run_bass_kernel_spmd(nc, in_maps, core_ids=list(range(NCORES)), trace=True)
        LAST_NS.append(res.exec_time_ns)
        print("exec_time_ns", res.exec_time_ns, flush=True)
    else:
        res = run_bass_kernel_spmd(nc, in_maps, core_ids=list(range(NCORES)))
    return res.results


def tab(v):
    v = np.asarray(v, np.float32).reshape(-1, 128)
    return np.ascontiguousarray(v.T)


A_COLS = 6 * D // NCORES


def build_A():
    b = B()
    S = b.S
    cT = b.din("cT", [128, DC * 2])
    W = b.din("W", [2, D, A_COLS])
    bias = b.din("bias", [2, 2 * A_COLS])
    out = b.dout("out", [2, 2 * A_COLS])
    ct = b.sb("ct", [128, DC * 2])
    s = b.sb("s", [128, DC * 2])
    bt = b.sb("bt", [2, 2 * A_COLS])
    ot = b.sb("ot", [2, 2 * A_COLS])
    wts = [b.sb("wt%d" % i, [128, DC * 512]) for i in range(2)]
    pss = [b.ps("ps%d" % i) for i in range(2)]
    S.dma("sp", lambda e: e.dma_start(out=ct[:], in_=cT), writes=["ct"])
    S.dma("sp", lambda e: e.dma_start(out=bt[:], in_=bias), writes=["bt"])
    S.op("act", lambda e: e.activation(out=s[:], in_=ct[:], func=AF.Silu), reads=["ct"], writes=["s"])
    items = [(i, n) for i in range(2) for n in range(A_COLS // 512)]
    for it, (i, n) in enumerate(items):
        wt = wts[it % 2]
        ps = pss[it % 2]
        wk = "wt%d" % (it % 2)
        pk = "ps%d" % (it % 2)
        src = W[i].rearrange("(kc p) n -> p kc n", p=128)[:, :, n * 512:(n + 1) * 512]
        S.dma("sp", lambda e, wt=wt, src=src: e.dma_start(
            out=wt[:].rearrange("p (kc n) -> p kc n", n=512), in_=src), writes=[wk])

        def mm(e, wt=wt, ps=ps):
            ins = None
            for kc in range(DC):
                ins = e.matmul(ps[0:2, :], s[:, kc * 2:(kc + 1) * 2], wt[:, kc * 512:(kc + 1) * 512],
                               start=(kc == 0), stop=(kc == DC - 1))
            return ins
        S.op("pe", mm, reads=[wk, "s"], writes=[pk])
        c0 = i * A_COLS + n * 512
        S.op("dve", lambda e, ps=ps, c0=c0: e.tensor_tensor(out=ot[:, c0:c0 + 512], in0=ps[0:2, :],
                                                            in1=bt[:, c0:c0 + 512], op=ALU.add),
             reads=[pk, "bt"], writes=["ot"])
    S.dma("sp", lambda e: e.dma_start(out=out, in_=ot[:]), reads=["ot"])
    return b.finish()


def stage_A(inp):
    c2 = np.concatenate([inp["c"].reshape(1, D), inp["c_ctx"].reshape(1, D)], 0)
    cT = np.ascontiguousarray(c2.reshape(2, DC, 128).transpose(2, 1, 0)).reshape(128, DC * 2)
    maps = []
    for j in range(NCORES):
        sl = slice(j * A_COLS, (j + 1) * A_COLS)
        Wj = np.ascontiguousarray(inp["ada_w"][:, :, sl])
        bj = np.ascontiguousarray(inp["ada_b"][:, sl]).reshape(1, 2 * A_COLS)
        maps.append({"cT": cT, "W": Wj, "bias": np.ascontiguousarray(np.repeat(bj, 2, 0))})
    res = run(build_A(), maps)
    mod = np.zeros((2, 2, 6 * D), np.float32)
    for j in range(NCORES):
        o = res[j]["out"].reshape(2, 2, A_COLS)
        for i in range(2):
            mod[i, :, j * A_COLS:(j + 1) * A_COLS] = o[:, i, :]
    return mod


def mod_tables(mod_i):
    return np.ascontiguousarray(np.concatenate([tab(mod_i[r, v * D:(v + 1) * D]) for r in range(2) for v in range(6)], 1))


def mcol(modt, r, v, c0=0, c1=DC):
    base = (r * 6 + v) * DC
    return modt[:, base + c0:base + c1]


def emit_weff(b, modt, nwt, mk, nk, v_scale, name):
    S = b.S
    weff = b.sb(name, [128, 2 * DC])
    for r in range(2):
        S.op("dve", lambda e, r=r: e.scalar_tensor_tensor(
            out=weff[:, r * DC:(r + 1) * DC], in0=mcol(modt, r, v_scale), scalar=1.0, in1=nwt[:],
            op0=ALU.add, op1=ALU.mult), reads=[mk, nk], writes=[name])
    return weff


def emit_rstd(b, src, srck, C, N, ones, sq, sqk, ps, psk, rstd, rstdk, eps, ncols_feat):
    S = b.S
    S.op("act", lambda e: e.activation(out=sq[:, 0:C * N], in_=src, func=AF.Square) if False else
         e.activation(out=sq[:, 0:C * N].rearrange("p (c n) -> p c n", n=N), in_=src, func=AF.Square),
         reads=[srck], writes=[sqk])

    def mm(e):
        ins = None
        for c in range(C):
            ins = e.matmul(ps[:, 0:N], ones[:], sq[:, c * N:(c + 1) * N], start=(c == 0), stop=(c == C - 1))
        return ins
    S.op("pe", mm, reads=[sqk], writes=[psk])
    S.op("act", lambda e: e.activation(out=rstd[:, 0:N], in_=ps[:, 0:N], func=AF.Sqrt, scale=1.0 / ncols_feat, bias=eps),
         reads=[psk], writes=[rstdk])
    S.op("dve", lambda e: e.reciprocal(out=rstd[:, 0:N], in_=rstd[:, 0:N]), reads=[rstdk], writes=[rstdk])


TOK_PC = LAT_PC + CTX_PC


def build_B():
    b = B()
    S = b.S
    xt = b.din("xt", [TOK_PC, D])
    modT = b.din("modT", [128, 12 * DC])
    nwT = b.din("nwT", [128, DC])
    identD = b.din("ident", [128, 128])
    epsD = b.din("epsv", [128, 1])
    hT = b.dout("hT", [D, TOK_PC])
    aT = b.dout("aT", [D, TOK_PC], BF16)
    modt = b.sb("modt", [128, 12 * DC])
    nwt = b.sb("nwt", [128, DC])
    ident = b.sb("identt", [128, 128])
    epst = b.sb("epst", [128, 1])
    ones = b.sb("ones", [128, 128], BF16)
    S.dma("sp", lambda e: e.dma_start(out=modt[:], in_=modT), writes=["modt"])
    S.dma("sp", lambda e: e.dma_start(out=nwt[:], in_=nwT), writes=["nwt"])
    S.dma("sp", lambda e: e.dma_start(out=ident[:], in_=identD), writes=["ident"])
    S.dma("sp", lambda e: e.dma_start(out=epst[:], in_=epsD), writes=["epst"])
    S.op("dve", lambda e: e.memset(ones[:], 1.0), writes=["ones"])
    weff = emit_weff(b, modt, nwt, "modt", "nwt", 1, "weff")
    xin = [b.sb("xin%d" % i, [128, D]) for i in range(2)]
    hTt = [b.sb("hTt%d" % i, [128, DC * 128]) for i in range(2)]
    xn = b.sb("xn", [128, DC * 128])
    aTt = [b.sb("aTt%d" % i, [128, DC * 128], BF16) for i in range(2)]
    sq = b.sb("sq", [128, DC * 128], BF16)
    rstd = b.sb("rstd", [128, 128])
    pst = [b.ps("pst%d" % i) for i in range(4)]
    pss = b.ps("pss")
    tiles = [(i * 128, 128, 0) for i in range(8)] + [(LAT_PC, CTX_PC, 1)]
    for ti, (t0, T, r) in enumerate(tiles):
        xi = xin[ti % 2]; xk = "xin%d" % (ti % 2)
        ht = hTt[ti % 2]; hk = "hTt%d" % (ti % 2)
        at = aTt[ti % 2]; ak = "aTt%d" % (ti % 2)
        S.dma("sp", lambda e, xi=xi, t0=t0, T=T: e.dma_start(out=xi[0:T, :], in_=xt[t0:t0 + T, :]), writes=[xk])
        for cg in range(8):
            ps = pst[cg % 4]; pk = "pst%d" % (cg % 4)

            def tr(e, ps=ps, xi=xi, cg=cg, T=T):
                ins = None
                for k in range(4):
                    c = cg * 4 + k
                    ins = e.transpose(ps[:, k * T:(k + 1) * T], xi[0:T, c * 128:(c + 1) * 128], ident[0:T, 0:T])
                return ins
            S.op("pe", tr, reads=[xk, "ident"], writes=[pk])
            eng = "dve" if cg % 2 == 0 else "act"
            if eng == "dve":
                S.op("dve", lambda e, ps=ps, ht=ht, cg=cg, T=T: e.tensor_copy(ht[:, cg * 4 * T:(cg + 1) * 4 * T], ps[:, 0:4 * T]),
                     reads=[pk], writes=[hk + "_%d" % cg])
            else:
                S.op("act", lambda e, ps=ps, ht=ht, cg=cg, T=T: e.activation(out=ht[:, cg * 4 * T:(cg + 1) * 4 * T], in_=ps[:, 0:4 * T], func=AF.Copy),
                     reads=[pk], writes=[hk + "_%d" % cg])
        hks = [hk + "_%d" % cg for cg in range(8)]
        hv = ht[:, 0:DC * T].rearrange("p (c n) -> p c n", n=T)
        S.dma("sp", lambda e, hv=hv, t0=t0, T=T: e.dma_start(
            out=hT.rearrange("(c p) t -> p c t", p=128)[:, :, t0:t0 + T], in_=hv), reads=hks)
        S.op("act", lambda e, hv=hv, T=T: e.activation(out=sq[:, 0:DC * T].rearrange("p (c n) -> p c n", n=T), in_=hv, func=AF.Square),
             reads=hks, writes=["sq"])

        def mm(e, T=T):
            ins = None
            for c in range(DC):
                ins = e.matmul(pss[:, 0:T], ones[:], sq[:, c * T:(c + 1) * T], start=(c == 0), stop=(c == DC - 1))
            return ins
        S.op("pe", mm, reads=["sq", "ones"], writes=["pss"])
        S.op("act", lambda e, T=T: e.activation(out=rstd[:, 0:T], in_=pss[:, 0:T], func=AF.Sqrt, scale=1.0 / D, bias=epst[:, 0:1]),
             reads=["pss", "epst"], writes=["rstd"])
        S.op("dve", lambda e, T=T: e.reciprocal(out=rstd[:, 0:T], in_=rstd[:, 0:T]), reads=["rstd"], writes=["rstd"])
        xnv = xn[:, 0:DC * T].rearrange("p (c n) -> p c n", n=T)
        S.op("dve", lambda e, hv=hv, xnv=xnv, T=T: e.tensor_tensor(
            out=xnv, in0=hv, in1=rstd[:, 0:T].unsqueeze(1).to_broadcast([128, DC, T]), op=ALU.mult),
            reads=hks + ["rstd"], writes=["xn"])
        S.op("dve", lambda e, xnv=xnv, T=T, r=r: e.tensor_tensor(
            out=xnv, in0=xnv, in1=weff[:, r * DC:(r + 1) * DC].unsqueeze(2).to_broadcast([128, DC, T]), op=ALU.mult),
            reads=["xn", "weff"], writes=["xn"])
        atv = at[:, 0:DC * T].rearrange("p (c n) -> p c n", n=T)
        S.op("dve", lambda e, xnv=xnv, atv=atv, T=T, r=r: e.tensor_tensor(
            out=atv, in0=xnv, in1=mcol(modt, r, 0).unsqueeze(2).to_broadcast([128, DC, T]), op=ALU.add),
            reads=["xn", "modt"], writes=[ak])
        S.dma("sp", lambda e, atv=atv, t0=t0, T=T: e.dma_start(
            out=aT.rearrange("(c p) t -> p c t", p=128)[:, :, t0:t0 + T], in_=atv), reads=[ak])
    return b.finish()


def tok_shard(lat, ctx, j):
    return np.ascontiguousarray(np.concatenate([lat[j * LAT_PC:(j + 1) * LAT_PC], ctx[j * CTX_PC:(j + 1) * CTX_PC]], 0))


def gather_T(parts):
    return np.ascontiguousarray(np.concatenate([p[:, LAT_PC:] for p in parts] + [p[:, :LAT_PC] for p in parts], 1))


def stage_B(inp, mod):
    modt = mod_tables(mod[0])
    nwt = tab(inp["norm_w"][0, 0])
    ident = np.eye(128, dtype=np.float32)
    epsv = np.full((128, 1), RMS_EPS, np.float32)
    x = inp["x"].reshape(SEQ, D)
    ctx = inp["ctx"].reshape(CTX, D)
    maps = [{"xt": tok_shard(x, ctx, j), "modT": modt, "nwT": nwt, "ident": ident, "epsv": epsv} for j in range(NCORES)]
    res = run(build_B(), maps)
    hT = [r["hT"] for r in res]
    aT = [r["aT"] for r in res]
    return hT, aT


EV_A = 2048
CW = 31
CPAD = 15
U_CTX0 = CPAD
U_LAT0 = CPAD + CTX + CPAD
U_LEN = U_LAT0 + SEQ + CPAD
NKB = NTOK // 128
LAM_INIT0 = 0.8 - 0.6 * math.exp(-0.3 * 0)


def supertiles():
    return [(0, CTX)] + [(CTX + 512 * i, 512) for i in range(SEQ // 512)]


def build_C(phases=(1, 2, 3)):
    b = B()
    S = b.S
    aTd = b.din("aT", [D, NTOK], BF16)
    wd = b.din("w", [D, 1280])
    ropeD = b.din("rope", [NTOK, 512])
    lamD = b.din("lam", [256])
    swD = b.din("subw", [128])
    cwD = b.din("cw", [128, 2 * CW])
    cbD = b.din("cb", [128, 2])
    identD = b.din("ident", [128, 128])
    attnT = b.dout("attnT", [256, NTOK], BF16)
    convT = b.dout("convT", [256, NTOK])
    KTd = b.dint("KTd", [2, 128, NTOK], BF16)
    QTd = b.dint("QTd", [2, 128, NTOK], BF16)
    Vd = b.dint("Vd", [NTOK, 256], BF16)
    UTd = b.dint("UTd", [2, 128, NTOK])

    ident = b.sb("identt", [128, 128])
    identb = b.sb("identb", [128, 128], BF16)
    S.dma("sp", lambda e: e.dma_start(out=ident[:], in_=identD), writes=["ident"])
    S.op("dve", lambda e: e.tensor_copy(identb[:], ident[:]), reads=["ident"], writes=["identb"])

    st1 = contextlib.ExitStack()
    Wt = st1.enter_context(b.nc.sbuf_tensor("Wt", [128, DC * 1280], BF16))
    Wv = Wt[:].rearrange("p (kc n) -> p kc n", n=1280)
    for g in range(8):
        S.dma("pool", lambda e, g=g: e.dma_start(
            out=Wv[:, g * 4:(g + 1) * 4, :], in_=wd.rearrange("(kc p) n -> p kc n", p=128)[:, g * 4:(g + 1) * 4, :]),
            writes=["W%d" % g])
    Wk = ["W%d" % g for g in range(8)]
    aTs = [st1.enter_context(b.nc.sbuf_tensor("aTs%d" % i, [128, DC * 512], BF16)) for i in range(2)]
    cs = [st1.enter_context(b.nc.sbuf_tensor("cs%d" % i, [128, 512], F32)) for i in range(2)]
    t1 = st1.enter_context(b.nc.sbuf_tensor("t1", [128, 256], F32))
    t2 = st1.enter_context(b.nc.sbuf_tensor("t2", [128, 256], F32))
    rot = [st1.enter_context(b.nc.sbuf_tensor("rot%d" % i, [128, 512], BF16)) for i in range(2)]
    vt = [st1.enter_context(b.nc.sbuf_tensor("vt%d" % i, [128, 256], BF16)) for i in range(2)]
    kqT = [st1.enter_context(b.nc.sbuf_tensor("kqT%d" % i, [128, 4 * 512], BF16)) for i in range(2)]
    sig = st1.enter_context(b.nc.sbuf_tensor("sig", [128, 512], F32))
    ut = [st1.enter_context(b.nc.sbuf_tensor("ut%d" % i, [128, 512], F32)) for i in range(2)]
    ps_kv = [st1.enter_context(b.nc.psum_tensor("ps_kv%d" % i, [128, 512], F32)) for i in range(2)]
    ps_q = [st1.enter_context(b.nc.psum_tensor("ps_q%d" % i, [128, 512], F32)) for i in range(2)]
    ps_T = st1.enter_context(b.nc.psum_tensor("ps_T", [128, 1024], BF16))
    ps_c = [st1.enter_context(b.nc.psum_tensor("ps_c%d" % i, [128, 512], F32)) for i in range(2)]
    sts = supertiles()
    aTv = aTd.rearrange("(c p) t -> p c t", p=128)

    def load_aT(si):
        t0, N = sts[si]
        buf = aTs[si % 2]
        S.dma("sp", lambda e: e.dma_start(out=buf[:, 0:DC * N].rearrange("p (c n) -> p c n", n=N), in_=aTv[:, :, t0:t0 + N]),
              writes=["aTs%d" % (si % 2)])
    if 1 in phases:
        load_aT(0)
    nsub_total = 0
    for si, (t0, N) in enumerate(sts if 1 in phases else []):
        if si + 1 < len(sts):
            load_aT(si + 1)
        a = aTs[si % 2][:, 0:DC * N].rearrange("p (c n) -> p c n", n=N)
        ak = "aTs%d" % (si % 2)
        kq = kqT[si % 2]; kqk = "kqT%d" % (si % 2)
        kqv = kq[:, 0:4 * N].rearrange("p (a n) -> p a n", n=N)
        for sub in range(N // 128):
            i2 = nsub_total % 2
            nsub_total += 1
            tt0 = t0 + sub * 128
            pk = ps_kv[i2]; pq = ps_q[i2]
            pkk = "ps_kv%d" % i2; pqk = "ps_q%d" % i2
            S.dma("sp", lambda e, i2=i2, tt0=tt0: e.dma_start(out=cs[i2][:], in_=ropeD[tt0:tt0 + 128, :]), writes=["cs%d" % i2])

            def mmkv(e, a=a, sub=sub, pk=pk):
                ins = None
                for kc in range(DC):
                    ins = e.matmul(pk[:, 0:512], a[:, kc, sub * 128:(sub + 1) * 128], Wv[:, kc, 0:512], start=(kc == 0), stop=(kc == DC - 1))
                return ins
            S.op("pe", mmkv, reads=[ak] + Wk, writes=[pkk])

            def mmq(e, a=a, sub=sub, pq=pq):
                ins = None
                for kc in range(DC):
                    ins = e.matmul(pq[:, 0:256], a[:, kc, sub * 128:(sub + 1) * 128], Wv[:, kc, 512:768], start=(kc == 0), stop=(kc == DC - 1))
                return ins
            S.op("pe", mmq, reads=[ak] + Wk, writes=[pqk])
            csk = "cs%d" % i2
            rt = rot[i2]; rk = "rot%d" % i2
            for which, (src, srck, off) in enumerate(((pk[:, 0:256], pkk, 0), (pq[:, 0:256], pqk, 256))):
                swp = src.rearrange("p (g s i) -> p g s i", s=2, i=16)[:, :, ::-1, :]
                S.op("dve", lambda e, src=src, i2=i2: e.tensor_tensor(out=t1[:], in0=src, in1=cs[i2][:, 0:256], op=ALU.mult),
                     reads=[srck, csk], writes=["t1"])
                S.op("dve", lambda e, swp=swp, i2=i2: e.tensor_tensor(
                    out=t2[:].rearrange("p (g s i) -> p g s i", s=2, i=16), in0=swp,
                    in1=cs[i2][:, 256:512].rearrange("p (g s i) -> p g s i", s=2, i=16), op=ALU.mult),
                    reads=[srck, csk], writes=["t2"])
                S.op("dve", lambda e, rt=rt, off=off: e.tensor_tensor(out=rt[:, off:off + 256], in0=t1[:], in1=t2[:], op=ALU.add),
                     reads=["t1", "t2"], writes=[rk + "_%d" % which])
            S.op("act", lambda e, i2=i2, pk=pk: e.activation(out=vt[i2][:], in_=pk[:, 256:512], func=AF.Copy),
                 reads=[pkk], writes=["vt%d" % i2, pkk])
            S.dma("sp", lambda e, i2=i2, tt0=tt0: e.dma_start(out=Vd[tt0:tt0 + 128, :], in_=vt[i2][:]), reads=["vt%d" % i2], writes=["Vd"])

            def trs(e, rt=rt):
                ins = None
                for blk in range(4):
                    ins = e.transpose(ps_T[:, blk * 128:(blk + 1) * 128], rt[:, blk * 128:(blk + 1) * 128], identb[:])
                return ins
            S.op("pe", trs, reads=[rk + "_0", rk + "_1", "identb"], writes=["ps_T"])
            S.op("act", lambda e, kqv=kqv, sub=sub: e.activation(
                out=kqv[:, :, sub * 128:(sub + 1) * 128], in_=ps_T[:, 0:512].rearrange("p (a n) -> p a n", n=128), func=AF.Copy),
                reads=["ps_T"], writes=[kqk])
        S.dma("sp", lambda e, kqv=kqv, t0=t0, N=N: e.dma_start(out=KTd[:, :, t0:t0 + N].rearrange("a p n -> p a n"), in_=kqv[:, 0:2, :]),
              reads=[kqk], writes=["KTd"])
        S.dma("sp", lambda e, kqv=kqv, t0=t0, N=N: e.dma_start(out=QTd[:, :, t0:t0 + N].rearrange("a p n -> p a n"), in_=kqv[:, 2:4, :]),
              reads=[kqk], writes=["QTd"])
        for blk in range(2):
            for w2 in range(2):
                col = 768 + w2 * 256 + blk * 128

                def mmc(e, a=a, col=col, w2=w2, N=N):
                    ins = None
                    for kc in range(DC):
                        ins = e.matmul(ps_c[w2][:, 0:N], Wv[:, kc, col:col + 128], a[:, kc, :], start=(kc == 0), stop=(kc == DC - 1))
                    return ins
                S.op("pe", mmc, reads=[ak] + Wk, writes=["ps_c%d" % w2])
            S.op("act", lambda e, N=N: e.activation(out=sig[:, 0:N], in_=ps_c[1][:, 0:N], func=AF.Sigmoid), reads=["ps_c1"], writes=["sig"])
            S.op("dve", lambda e, N=N, blk=blk: e.tensor_tensor(out=ut[blk][:, 0:N], in0=ps_c[0][:, 0:N], in1=sig[:, 0:N], op=ALU.mult),
                 reads=["ps_c0", "sig"], writes=["ut%d" % blk])
            S.dma("sp", lambda e, N=N, blk=blk, t0=t0: e.dma_start(out=UTd[blk, :, t0:t0 + N], in_=ut[blk][:, 0:N]),
                  reads=["ut%d" % blk], writes=["UTd"])
    st1.close()
    S.barrier()

    st2 = contextlib.ExitStack()
    uT = st2.enter_context(b.nc.sbuf_tensor("uT", [128, 2 * U_LEN], F32))
    cwt = st2.enter_context(b.nc.sbuf_tensor("cwt", [128, 2 * CW], F32))
    cbt = st2.enter_context(b.nc.sbuf_tensor("cbt", [128, 2], F32))
    acc = [st2.enter_context(b.nc.sbuf_tensor("acc%d" % i, [128, 512], F32)) for i in range(2)]
    uv = uT[:].rearrange("p (b n) -> p b n", n=U_LEN)
    S.dma("sp", lambda e: e.dma_start(out=cwt[:], in_=cwD), writes=["cwt"])
    S.dma("sp", lambda e: e.dma_start(out=cbt[:], in_=cbD), writes=["cbt"])
    S.op("dve", lambda e: e.memset(uT[:], 0.0), writes=["uT"])
    S.dma("sp", lambda e: e.dma_start(out=uv[:, :, U_CTX0:U_CTX0 + CTX], in_=UTd[:, :, 0:CTX].rearrange("b p n -> p b n")),
          reads=["UTd"], writes=["uT"])
    S.dma("sp", lambda e: e.dma_start(out=uv[:, :, U_LAT0:U_LAT0 + SEQ], in_=UTd[:, :, CTX:NTOK].rearrange("b p n -> p b n")),
          reads=["UTd"], writes=["uT"])
    it = 0
    for (ubase, slen, tok0) in (((U_CTX0, CTX, 0), (U_LAT0, SEQ, CTX)) if 2 in phases else ()):
        for c0 in range(0, slen, 512):
            n = min(512, slen - c0)
            for blk in range(2):
                ac = acc[it % 2]; ack = "acc%d" % (it % 2)
                it += 1
                u0 = ubase + c0 - CPAD

                def conv(e, ac=ac, blk=blk, u0=u0, n=n):
                    ins = e.tensor_scalar(out=ac[:, 0:n], in0=uv[:, blk, u0:u0 + n], scalar1=cwt[:, blk * CW:blk * CW + 1],
                                          scalar2=cbt[:, blk:blk + 1], op0=ALU.mult, op1=ALU.add)
                    for k in range(1, CW):
                        ins = e.scalar_tensor_tensor(out=ac[:, 0:n], in0=uv[:, blk, u0 + k:u0 + k + n],
                                                     scalar=cwt[:, blk * CW + k:blk * CW + k + 1], in1=ac[:, 0:n],
                                                     op0=ALU.mult, op1=ALU.add)
                    return ins
                S.op("dve", conv, reads=["uT", "cwt", "cbt"], writes=[ack])
                S.dma("sp", lambda e, ac=ac, blk=blk, tok0=tok0, c0=c0, n=n: e.dma_start(
                    out=convT[blk * 128:(blk + 1) * 128, tok0 + c0:tok0 + c0 + n], in_=ac[:, 0:n]), reads=[ack])
    st2.close()
    S.barrier()

    st3 = contextlib.ExitStack()

    def T3(name, shape, dt=F32):
        return st3.enter_context(b.nc.sbuf_tensor(name + "_p3", list(shape), dt))
    KT = T3("KT", [128, 2 * NTOK], BF16)
    KTv = KT[:].rearrange("p (a n) -> p a n", n=NTOK)
    Vt = T3("Vt", [128, NKB * 256], BF16)
    Vtv = Vt[:].rearrange("p (k d) -> p k d", d=256)
    QTs = [T3("QTs%d" % i, [128, 2 * 512], BF16) for i in range(2)]
    PT = [T3("PT%d" % i, [128, 512], BF16) for i in range(3)]
    pacc = [T3("pacc%d" % i, [128, 512]) for i in range(4)]
    lamt = T3("lamt", [128, 256]); lp = T3("lp", [128, 128]); lsum = T3("lsum", [128, 4]); nlam = T3("nlam", [128, 1])
    swc = T3("swc", [128, 1]); eps5 = T3("eps5", [128, 1]); onesf = T3("onesf", [128, 128])
    r1 = T3("r1", [128, 512]); r2 = T3("r2", [128, 512]); o1 = T3("o1", [128, 512]); o2 = T3("o2", [128, 512])
    obf = [T3("obf%d" % i, [128, 512], BF16) for i in range(2)]
    psS = [st3.enter_context(b.nc.psum_tensor("psS%d" % i, [128, 512], F32)) for i in range(2)]
    psO = [st3.enter_context(b.nc.psum_tensor("psO%d" % i, [128, 512], F32)) for i in range(4)]
    psD = [st3.enter_context(b.nc.psum_tensor("psD%d" % i, [128, 512], F32)) for i in range(2)]
    S.dma("sp", lambda e: e.dma_start(out=KTv, in_=KTd.rearrange("a p n -> p a n")), reads=["KTd"], writes=["KT"])
    S.dma("sp", lambda e: e.dma_start(out=Vtv, in_=Vd.rearrange("(k p) d -> p k d", p=128)), reads=["Vd"], writes=["Vt"])
    S.dma("sp", lambda e: e.dma_start(out=lamt[:], in_=lamD.partition_broadcast(128)), writes=["lamt"])
    S.dma("sp", lambda e: e.dma_start(out=swc[:], in_=swD.rearrange("(p o) -> p o", o=1)), writes=["swc"])
    S.op("dve", lambda e: e.memset(eps5[:], 1e-5), writes=["eps5"])
    S.op("dve", lambda e: e.memset(onesf[:], 1.0), writes=["onesf"])
    S.op("dve", lambda e: e.tensor_tensor(out=lp[:].rearrange("p (a n) -> p a n", n=64),
                                          in0=lamt[:].rearrange("p (a b n) -> p a b n", b=2, n=64)[:, :, 0, :],
                                          in1=lamt[:].rearrange("p (a b n) -> p a b n", b=2, n=64)[:, :, 1, :], op=ALU.mult),
         reads=["lamt"], writes=["lp"])
    S.op("dve", lambda e: e.reduce_sum(out=lsum[:, 0:2], in_=lp[:].rearrange("p (a n) -> p a n", n=64), axis=AX.X), reads=["lp"], writes=["lsum"])
    S.op("act", lambda e: e.activation(out=lsum[:, 2:4], in_=lsum[:, 0:2], func=AF.Exp), reads=["lsum"], writes=["lsum"])
    S.op("dve", lambda e: e.tensor_tensor(out=nlam[:], in0=lsum[:, 3:4], in1=lsum[:, 2:3], op=ALU.subtract), reads=["lsum"], writes=["nlam"])
    S.op("dve", lambda e: e.tensor_scalar_add(nlam[:], nlam[:], -LAM_INIT0), reads=["nlam"], writes=["nlam"])
    S.op("dve", lambda e: e.tensor_scalar_mul(swc[:], swc[:], 1.0 - LAM_INIT0), reads=["swc"], writes=["swc"])

    qsts = [(0, CTX, 0, 2)] + [(CTX + 512 * i, 512, 0, NKB) for i in range(SEQ // 512)]
    QTdv = QTd.rearrange("a p n -> p a n")

    def load_q(qi):
        q0, NQ, _, _ = qsts[qi]
        S.dma("sp", lambda e: e.dma_start(out=QTs[qi % 2][:, 0:2 * NQ].rearrange("p (a n) -> p a n", n=NQ), in_=QTdv[:, :, q0:q0 + NQ]),
              reads=["QTd"], writes=["QTs%d" % (qi % 2)])
    if 3 not in phases:
        qsts = []
    else:
        load_q(0)
    nS = 0
    nH = 0
    for qi, (q0, NQ, kb0, kb1) in enumerate(qsts):
        if qi + 1 < len(qsts):
            load_q(qi + 1)
        Q = QTs[qi % 2][:, 0:2 * NQ].rearrange("p (a n) -> p a n", n=NQ)
        Qk = "QTs%d" % (qi % 2)
        for hh in range(2):
            par = nH % 2
            nH += 1
            for mi in range(2):
                p0 = mi * 64
                oi = par * 2 + mi
                pO = psO[oi]; pOk = "psO%d" % oi
                pa = pacc[oi]; pak = "pacc%d" % oi

                def emit_S(kb, n, p0=p0):
                    sS = psS[n % 2]; sSk = "psS%d" % (n % 2)
                    S.op("pe", lambda e, sS=sS, p0=p0, hh=hh, kb=kb, Q=Q, NQ=NQ: e.matmul(
                        sS[:, 0:NQ], KTv[p0:p0 + 64, hh, kb * 128:(kb + 1) * 128], Q[p0:p0 + 64, hh, :], start=True, stop=True),
                        reads=["KT", Qk], writes=[sSk])
                emit_S(kb0, nS)
                for kb in range(kb0, kb1):
                    sS = psS[nS % 2]; sSk = "psS%d" % (nS % 2)
                    pt = PT[nS % 3]; ptk = "PT%d" % (nS % 3)
                    nS += 1
                    S.op("act", lambda e, sS=sS, pt=pt, NQ=NQ: e.activation(out=pt[:, 0:NQ], in_=sS[:, 0:NQ], func=AF.Exp, scale=0.125),
                         reads=[sSk], writes=[ptk, sSk])
                    if kb + 1 < kb1:
                        emit_S(kb + 1, nS)
                    S.op("pe", lambda e, pO=pO, pt=pt, kb=kb, hh=hh, NQ=NQ, kb0=kb0, kb1=kb1: e.matmul(
                        pO[:, 0:NQ], Vtv[:, kb, hh * 128:(hh + 1) * 128], pt[:, 0:NQ], start=(kb == kb0), stop=(kb == kb1 - 1)),
                        reads=[ptk, "Vt"], writes=[pOk])
                    if kb == kb0:
                        S.op("dve", lambda e, pa=pa, pt=pt, NQ=NQ: e.tensor_copy(pa[:, 0:NQ], pt[:, 0:NQ]), reads=[ptk], writes=[pak])
                    else:
                        S.op("dve", lambda e, pa=pa, pt=pt, NQ=NQ: e.tensor_tensor(out=pa[:, 0:NQ], in0=pt[:, 0:NQ], in1=pa[:, 0:NQ], op=ALU.add),
                             reads=[ptk, pak], writes=[pak])
            oa = par * 2
            for mi in range(2):
                S.op("pe", lambda e, mi=mi, oa=oa, NQ=NQ: e.matmul(psD[mi][:, 0:NQ], onesf[:], pacc[oa + mi][:, 0:NQ], start=True, stop=True),
                     reads=["pacc%d" % (oa + mi), "onesf"], writes=["psD%d" % mi])
            S.op("dve", lambda e, NQ=NQ: e.reciprocal(out=r1[:, 0:NQ], in_=psD[0][:, 0:NQ]), reads=["psD0"], writes=["r1", "psD0"])
            S.op("dve", lambda e, NQ=NQ: e.reciprocal(out=r2[:, 0:NQ], in_=psD[1][:, 0:NQ]), reads=["psD1"], writes=["r2", "psD1"])
            S.op("dve", lambda e, NQ=NQ, oa=oa: e.tensor_tensor(out=o1[:, 0:NQ], in0=psO[oa][:, 0:NQ], in1=r1[:, 0:NQ], op=ALU.mult),
                 reads=["psO%d" % oa, "r1"], writes=["o1", "psO%d" % oa])
            S.op("dve", lambda e, NQ=NQ, oa=oa: e.tensor_tensor(out=o2[:, 0:NQ], in0=psO[oa + 1][:, 0:NQ], in1=r2[:, 0:NQ], op=ALU.mult),
                 reads=["psO%d" % (oa + 1), "r2"], writes=["o2", "psO%d" % (oa + 1)])
            S.op("dve", lambda e, NQ=NQ: e.scalar_tensor_tensor(out=o1[:, 0:NQ], in0=o2[:, 0:NQ], scalar=nlam[:, 0:1], in1=o1[:, 0:NQ],
                                                                op0=ALU.mult, op1=ALU.add), reads=["o1", "o2", "nlam"], writes=["o1"])
            S.op("dve", lambda e, NQ=NQ: e.tensor_tensor(out=o2[:, 0:NQ], in0=o1[:, 0:NQ], in1=o1[:, 0:NQ], op=ALU.mult), reads=["o1", "o2"], writes=["o2"])
            S.op("pe", lambda e, NQ=NQ: e.matmul(psD[0][:, 0:NQ], onesf[:], o2[:, 0:NQ], start=True, stop=True), reads=["o2", "onesf"], writes=["psD0"])
            S.op("act", lambda e, NQ=NQ: e.activation(out=r1[:, 0:NQ], in_=psD[0][:, 0:NQ], func=AF.Sqrt, scale=1.0 / 128, bias=eps5[:, 0:1]),
                 reads=["psD0", "eps5"], writes=["r1", "psD0"])
            S.op("dve", lambda e, NQ=NQ: e.reciprocal(out=r1[:, 0:NQ], in_=r1[:, 0:NQ]), reads=["r1"], writes=["r1"])
            ob_ = obf[par]; obk = "obf%d" % par
            S.op("dve", lambda e, NQ=NQ, ob_=ob_: e.scalar_tensor_tensor(out=ob_[:, 0:NQ], in0=o1[:, 0:NQ], scalar=swc[:, 0:1], in1=r1[:, 0:NQ],
                                                                         op0=ALU.mult, op1=ALU.mult), reads=["o1", "swc", "r1"], writes=[obk])
            S.dma("sp", lambda e, ob_=ob_, hh=hh, q0=q0, NQ=NQ: e.dma_start(out=attnT[hh * 128:(hh + 1) * 128, q0:q0 + NQ], in_=ob_[:, 0:NQ]),
                  reads=[obk])
    st3.close()
    S.barrier()
    return b.finish()


def rope_tables():
    n_rows = SEQ // 64
    rows = np.repeat(np.arange(n_rows), 64).astype(np.float32)
    cols = np.tile(np.arange(64), n_rows).astype(np.float32)
    inv = np.power(np.float32(10000.0), -np.arange(0, 32, 2, dtype=np.float32) / np.float32(32)).astype(np.float32)
    ang = [rows[:, None] * inv, cols[:, None] * inv]
    cosf = np.zeros((NTOK, 4, 2, 2, 16), np.float32)
    sinf = np.zeros((NTOK, 4, 2, 2, 16), np.float32)
    cosf[:CTX] = 1.0
    for a in range(2):
        c = np.cos(ang[a]).astype(np.float32)
        s = np.sin(ang[a]).astype(np.float32)
        cosf[CTX:, :, a, 0, :] = c[:, None, :]
        cosf[CTX:, :, a, 1, :] = c[:, None, :]
        sinf[CTX:, :, a, 0, :] = -s[:, None, :]
        sinf[CTX:, :, a, 1, :] = s[:, None, :]
    return np.ascontiguousarray(np.concatenate([cosf.reshape(NTOK, 256), sinf.reshape(NTOK, 256)], 1))


def stage_C(inp, aT_all, phases=(1, 2, 3)):
    w = inp["ev_w_in"][0]
    rope = rope_tables()
    ident = np.eye(128, dtype=np.float32)
    maps = []
    for j in range(NCORES):
        cols = np.concatenate([np.arange(j * 256, (j + 1) * 256), EV_A + np.arange(j * 256, (j + 1) * 256),
                               2 * EV_A + np.arange(j * 256, (j + 1) * 256), 3 * EV_A + np.arange(j * 256, (j + 1) * 256),
                               3 * EV_A + 2048 + np.arange(j * 256, (j + 1) * 256)])
        cwj = inp["ev_conv_w"][0][:, j * 256:(j + 1) * 256]
        cw = np.ascontiguousarray(cwj.reshape(CW, 2, 128).transpose(2, 1, 0)).reshape(128, 2 * CW)
        cb = np.ascontiguousarray(inp["ev_conv_b"][0][j * 256:(j + 1) * 256].reshape(2, 128).T)
        maps.append({"aT": aT_all, "w": np.ascontiguousarray(w[:, cols]), "rope": rope,
                     "lam": np.ascontiguousarray(inp["ev_lambda"][0].reshape(256)), "subw": np.ascontiguousarray(inp["ev_subln_w"][0]),
                     "cw": cw, "cb": cb, "ident": ident})
    res = run(build_C(phases), maps)
    attnT = np.concatenate([r["attnT"] for r in res], 0)
    convT = np.concatenate([r["convT"] for r in res], 0)
    return attnT, convT


def group_cols(layer):
    if layer == 0:
        return np.concatenate([np.arange(0, 512), LAT_PC + np.arange(0, 16), np.arange(512, 1024), LAT_PC + np.arange(16, 32)])
    return np.arange(0, LAT_PC)


def build_T2(layer):
    b = B()
    S = b.S
    nc = b.nc
    NT = 528 if layer == 0 else 512
    NTOT = 2 * NT
    H = NT // 2
    halves = [(0, H), (H, NT)]
    if layer == 0:
        rngs = [(0, H, 0), (H, 512, 0), (512, NT, 1)]
    else:
        rngs = [(0, H, 0), (H, NT, 0)]
    hTin = b.din("hTin", [D, NTOT])
    modT = b.din("modT", [128, 12 * DC])
    nw2T = b.din("nw2T", [128, DC])
    w_out = b.din("w_out", [D, D])
    w1 = b.din("w1", [D, FFN_H])
    w3 = b.din("w3", [D, FFN_H])
    w2 = b.din("w2", [FFN_H, D])
    epsD = b.din("epsv", [128, 2])
    if layer == 0:
        attnTin = b.din("attnTin", [2048, NTOT], BF16)
        convTin = b.din("convTin", [2048, NTOT])
        lnT = b.din("lnT", [128, 32])
        modN = b.din("modN", [128, 12 * DC])
        nwN = b.din("nwN", [128, DC])
        aTout = b.dout("aTout", [D, NTOT], BF16)
        hTout = b.dout("hTout", [D, NTOT])
    else:
        s5T = b.din("s5T", [1024, NTOT], BF16)
        ssdT = b.din("ssdT", [3072, NTOT], BF16)
        gluW = b.din("gluW", [1024, 1024])
        glubT = b.din("glubT", [128, 8])
        nwN = b.din("nwN", [128, DC])
        identD = b.din("ident", [128, 128])
        outD = b.dout("out", [NTOT, D])
    hT1 = b.dout("hT1", [D, NTOT]) if DEBUG else b.dint("hT1", [D, NTOT])
    hT2 = hTout if layer == 0 else b.dint("hT2", [D, NTOT])

    modt = b.sb("modt", [128, 12 * DC])
    nw2t = b.sb("nw2t", [128, DC])
    nwNt = b.sb("nwNt", [128, DC])
    epst = b.sb("epst", [128, 2])
    ones = b.sb("ones", [128, 128], BF16)
    S.dma("sp", lambda e: e.dma_start(out=modt[:], in_=modT), writes=["modt"])
    S.dma("sp", lambda e: e.dma_start(out=nw2t[:], in_=nw2T), writes=["nw2t"])
    S.dma("sp", lambda e: e.dma_start(out=nwNt[:], in_=nwN), writes=["nwNt"])
    S.dma("sp", lambda e: e.dma_start(out=epst[:], in_=epsD), writes=["epst"])
    S.op("dve", lambda e: e.memset(ones[:], 1.0), writes=["ones"])
    weff2 = emit_weff(b, modt, nw2t, "modt", "nw2t", 4, "weff2")
    if layer == 0:
        lnt = b.sb("lnt", [128, 32])
        modn = b.sb("modn", [128, 12 * DC])
        onesf = b.sb("onesf", [128, 128])
        S.dma("sp", lambda e: e.dma_start(out=lnt[:], in_=lnT), writes=["lnt"])
        S.dma("sp", lambda e: e.dma_start(out=modn[:], in_=modN), writes=["modn"])
        S.op("dve", lambda e: e.memset(onesf[:], 1.0), writes=["onesf"])
        weffN = emit_weff(b, modn, nwNt, "modn", "nwNt", 1, "weffN")
    else:
        glubt = b.sb("glubt", [128, 8])
        ident = b.sb("identt", [128, 128])
        S.dma("sp", lambda e: e.dma_start(out=glubt[:], in_=glubT), writes=["glubt"])
        S.dma("sp", lambda e: e.dma_start(out=ident[:], in_=identD), writes=["ident"])

    NWB = 4
    wb = [b.sb("wb%d" % i, [128, 32 * 256], BF16) for i in range(NWB)]
    actT = b.sb("actT", [128, DC * NT], BF16)
    actv = actT[:].rearrange("p (c n) -> p c n", n=NT)
    hch = [b.sb("hch%d" % i, [128, NT]) for i in range(2)]
    hnew = [b.sb("hnew%d" % i, [128, NT]) for i in range(2)]
    sqb = [b.sb("sqb%d" % i, [128, NT], BF16) for i in range(2)]
    tmpf = [b.sb("tmpf%d" % i, [128, NT]) for i in range(2)]
    rstd = b.sb("rstd", [128, NT])
    pb = [b.ps("pb%d" % i) for i in range(8)]
    cnt = {"w": 0, "h": 0, "hn": 0, "sq": 0, "tmp": 0}

    def wload(src_ap, kcn, ncols):
        i = cnt["w"] % NWB
        cnt["w"] += 1
        buf = wb[i]
        view = buf[:, 0:kcn * ncols].rearrange("p (k n) -> p k n", n=ncols)
        S.dma("pool", lambda e: e.dma_start(out=view, in_=src_ap.rearrange("(k p) n -> p k n", p=128)), writes=["wb%d" % i])
        return view, "wb%d" % i

    def stats_begin():
        return {"first": True}

    def stats_add(st, src, srck, last):
        i = cnt["sq"] % 2
        cnt["sq"] += 1
        sq = sqb[i]
        S.op("act", lambda e: e.activation(out=sq[:], in_=src, func=AF.Square), reads=[srck], writes=["sqb%d" % i])
        first = st["first"]
        st["first"] = False

        def mm(e):
            ins = None
            for hi, (a0, a1) in enumerate(halves):
                ins = e.matmul(pb[4 + hi][:, 0:a1 - a0], ones[:], sq[:, a0:a1], start=first, stop=last)
            return ins
        S.op("pe", mm, reads=["sqb%d" % i, "ones"], writes=["pb4", "pb5"])

    def stats_finish(eps_col, nfeat):
        for hi, (a0, a1) in enumerate(halves):
            S.op("act", lambda e, hi=hi, a0=a0, a1=a1: e.activation(out=rstd[:, a0:a1], in_=pb[4 + hi][:, 0:a1 - a0], func=AF.Sqrt,
                                                                    scale=1.0 / nfeat, bias=epst[:, eps_col:eps_col + 1]),
                 reads=["pb4", "pb5", "epst"], writes=["rstd", "pb4", "pb5"])
        S.op("dve", lambda e: e.reciprocal(out=rstd[:], in_=rstd[:]), reads=["rstd"], writes=["rstd"])

    def load_h(src, c, g0):
        i = cnt["h"] % 2
        cnt["h"] += 1
        S.dma("sp", lambda e: e.dma_start(out=hch[i][:], in_=src[c * 128:(c + 1) * 128, g0:g0 + NT]), reads=[id(src)], writes=["hch%d" % i])
        return hch[i], "hch%d" % i

    def gemm_resid(Wd, KC, xk, gate_v, src_h, dst_h, g0, st):
        npan = D // 256
        segs = [(k0, min(k0 + 32, KC)) for k0 in range(0, KC, 32)]
        xv = gemm_resid.xv
        for pn in range(npan):
            for si, (k0, k1) in enumerate(segs):
                wv, wk = wload(Wd[k0 * 128:k1 * 128, pn * 256:(pn + 1) * 256], k1 - k0, 256)
                for o2 in range(2):
                    bi = (o2 * 2) if (o2 == 1 or pn % 2 == 0) else 6
                    bks = [pb[bi], pb[bi + 1]]
                    bkk = ["pb%d" % bi, "pb%d" % (bi + 1)]

                    def mm(e, o2=o2, bks=bks, wv=wv, k0=k0, k1=k1, xv=xv):
                        ins = None
                        for k in range(k0, k1):
                            for hi, (a0, a1) in enumerate(halves):
                                ins = e.matmul(bks[hi][:, 0:a1 - a0], wv[:, k - k0, o2 * 128:(o2 + 1) * 128], xv[:, k, a0:a1],
                                               start=(k == 0), stop=(k == KC - 1))
                        return ins
                    S.op("pe", mm, reads=[wk] + list(xk[k0:k1]), writes=bkk)
            for o2 in range(2):
                oc = pn * 2 + o2
                bi = (o2 * 2) if (o2 == 1 or pn % 2 == 0) else 6
                bks = [pb[bi], pb[bi + 1]]
                bkk = ["pb%d" % bi, "pb%d" % (bi + 1)]
                ht, hk = load_h(src_h, oc, g0)
                j = cnt["hn"] % 2
                cnt["hn"] += 1
                hn = hnew[j]; hnk = "hnew%d" % j
                for (c0, c1, r) in rngs:
                    hi = 0 if c1 <= H else 1
                    a0 = halves[hi][0]
                    S.op("dve", lambda e, hi=hi, a0=a0, c0=c0, c1=c1, r=r, oc=oc, ht=ht, hn=hn, bks=bks: e.scalar_tensor_tensor(
                        out=hn[:, c0:c1], in0=bks[hi][:, c0 - a0:c1 - a0], scalar=mcol(modt, r, gate_v, oc, oc + 1), in1=ht[:, c0:c1],
                        op0=ALU.mult, op1=ALU.add), reads=[bkk[hi], hk, "modt"], writes=[hnk, bkk[hi]])
                S.dma("sp", lambda e, hn=hn, oc=oc: e.dma_start(out=dst_h[oc * 128:(oc + 1) * 128, g0:g0 + NT], in_=hn[:]),
                      reads=[hnk], writes=[id(dst_h)])
                stats_add(st, hn[:], hnk, oc == DC - 1)

    def norm_mod(src_h, g0, weff, shift_tab, shift_v, dst_view_fn, dstk_fn, after=None):
        for c in range(DC):
            ht, hk = load_h(src_h, c, g0)
            i = cnt["tmp"] % 2
            cnt["tmp"] += 1
            tf = tmpf[i]; tk = "tmpf%d" % i
            S.op("dve", lambda e, ht=ht, tf=tf: e.tensor_tensor(out=tf[:], in0=ht[:], in1=rstd[:], op=ALU.mult), reads=[hk, "rstd"], writes=[tk])
            dv = dst_view_fn(c)
            for (c0, c1, r) in rngs:
                if weff is not None and shift_tab is not None:
                    S.op("act", lambda e, tf=tf, dv=dv, c0=c0, c1=c1, r=r, c=c: e.activation(
                        out=dv[:, c0:c1], in_=tf[:, c0:c1], func=AF.Identity,
                        scale=weff[:, r * DC + c:r * DC + c + 1], bias=mcol(shift_tab, r, shift_v, c, c + 1)),
                        reads=[tk], writes=[dstk_fn(c)])
                else:
                    S.op("act", lambda e, tf=tf, dv=dv, c0=c0, c1=c1, c=c: e.activation(
                        out=dv[:, c0:c1], in_=tf[:, c0:c1], func=AF.Copy, scale=weff[:, c:c + 1]) if False else
                        e.activation(out=dv[:, c0:c1], in_=tf[:, c0:c1], func=AF.Identity, scale=nwNt[:, c:c + 1], bias=0.0),
                        reads=[tk], writes=[dstk_fn(c)])
            if after is not None:
                after(c)

    def do_group(g):
        g0 = g * NT
        if layer == 0:
            st1 = contextlib.ExitStack()
            cv = st1.enter_context(nc.sbuf_tensor("cv" + "_g%d" % g, [128, 16 * NT], F32))
            cvv = cv[:].rearrange("p (c n) -> p c n", n=NT)
            S.dma("sp", lambda e: e.dma_start(out=cvv, in_=convTin.rearrange("(c p) t -> p c t", p=128)[:, :, g0:g0 + NT]), writes=["cv"])
            S.dma("sp", lambda e: e.dma_start(out=actv[:, 0:16, :], in_=attnTin.rearrange("(c p) t -> p c t", p=128)[:, :, g0:g0 + NT]),
                  writes=["act_%d" % c for c in range(16)])
            for c in range(16):
                i = cnt["tmp"] % 2
                cnt["tmp"] += 1
                tf = tmpf[i]; tk = "tmpf%d" % i
                S.op("act", lambda e, c=c, tf=tf: e.activation(out=tf[:], in_=cvv[:, c, :], func=AF.Square), reads=["cv"], writes=[tk])

                def mm(e, c=c, tf=tf):
                    ins = None
                    for hi, (a0, a1) in enumerate(halves):
                        ins = e.matmul(pb[hi][:, 0:a1 - a0], onesf[:], cvv[:, c, a0:a1], start=(c == 0), stop=(c == 15))
                        ins = e.matmul(pb[2 + hi][:, 0:a1 - a0], onesf[:], tf[:, a0:a1], start=(c == 0), stop=(c == 15))
                    return ins
                S.op("pe", mm, reads=["cv", tk, "onesf"], writes=["pb0", "pb1", "pb2", "pb3"])
            mean = st1.enter_context(nc.sbuf_tensor("lnmean" + "_g%d" % g, [128, NT], F32))
            msq = st1.enter_context(nc.sbuf_tensor("lnmsq" + "_g%d" % g, [128, NT], F32))
            nmr = st1.enter_context(nc.sbuf_tensor("lnnmr" + "_g%d" % g, [128, NT], F32))
            for hi, (a0, a1) in enumerate(halves):
                S.op("dve", lambda e, hi=hi, a0=a0, a1=a1: e.tensor_scalar(out=mean[:, a0:a1], in0=pb[hi][:, 0:a1 - a0], scalar1=1.0 / 2048, scalar2=None, op0=ALU.mult),
                     reads=["pb%d" % hi], writes=["lnmean", "pb%d" % hi])
            S.op("dve", lambda e: e.tensor_tensor(out=msq[:], in0=mean[:], in1=mean[:], op=ALU.mult), reads=["lnmean"], writes=["lnmsq"])
            for hi, (a0, a1) in enumerate(halves):
                S.op("dve", lambda e, hi=hi, a0=a0, a1=a1: e.scalar_tensor_tensor(
                    out=rstd[:, a0:a1], in0=pb[2 + hi][:, 0:a1 - a0], scalar=1.0 / 2048, in1=msq[:, a0:a1], op0=ALU.mult, op1=ALU.subtract),
                    reads=["pb%d" % (2 + hi), "lnmsq"], writes=["rstd", "pb%d" % (2 + hi)])
            S.op("act", lambda e: e.activation(out=rstd[:], in_=rstd[:], func=AF.Sqrt, bias=epst[:, 1:2]), reads=["rstd", "epst"], writes=["rstd"])
            S.op("dve", lambda e: e.reciprocal(out=rstd[:], in_=rstd[:]), reads=["rstd"], writes=["rstd"])
            S.op("dve", lambda e: e.scalar_tensor_tensor(out=nmr[:], in0=mean[:], scalar=-1.0, in1=rstd[:], op0=ALU.mult, op1=ALU.mult),
                 reads=["lnmean", "rstd"], writes=["lnnmr"])
            for c in range(16):
                i = cnt["tmp"] % 2
                cnt["tmp"] += 1
                tf = tmpf[i]; tk = "tmpf%d" % i
                S.op("dve", lambda e, c=c, tf=tf: e.tensor_tensor(out=tf[:], in0=cvv[:, c, :], in1=rstd[:], op=ALU.mult), reads=["cv", "rstd"], writes=[tk])
                S.op("dve", lambda e, tf=tf: e.tensor_tensor(out=tf[:], in0=tf[:], in1=nmr[:], op=ALU.add), reads=[tk, "lnnmr"], writes=[tk])
                S.op("act", lambda e, c=c, tf=tf: e.activation(out=actv[:, 16 + c, :], in_=tf[:], func=AF.Silu,
                                                               scale=lnt[:, c:c + 1], bias=lnt[:, 16 + c:17 + c]),
                     reads=[tk, "lnt"], writes=["act_%d" % (16 + c)])
            st1.close()
            S.barrier()
        else:
            st1 = contextlib.ExitStack()
            gT = st1.enter_context(nc.sbuf_tensor("gT5" + "_g%d" % g, [128, 8 * NT], BF16))
            gv = gT[:].rearrange("p (c n) -> p c n", n=NT)
            S.dma("sp", lambda e: e.dma_start(out=gv, in_=s5T.rearrange("(c p) t -> p c t", p=128)[:, :, g0:g0 + NT]), writes=["gT5"])
            S.dma("sp", lambda e: e.dma_start(out=actv[:, 8:32, :], in_=ssdT.rearrange("(c p) t -> p c t", p=128)[:, :, g0:g0 + NT]),
                  writes=["act_%d" % c for c in range(8, 32)])
            for pn in range(4):
                wv, wk = wload(gluW[:, pn * 256:(pn + 1) * 256], 8, 256)
                for o2 in range(2):
                    oc = pn * 2 + o2
                    bks = [pb[o2 * 2], pb[o2 * 2 + 1]]
                    bkk = ["pb%d" % (o2 * 2), "pb%d" % (o2 * 2 + 1)]

                    def mm(e, wv=wv, o2=o2, bks=bks):
                        ins = None
                        for k in range(8):
                            for hi, (a0, a1) in enumerate(halves):
                                ins = e.matmul(bks[hi][:, 0:a1 - a0], wv[:, k, o2 * 128:(o2 + 1) * 128], gv[:, k, a0:a1], start=(k == 0), stop=(k == 7))
                        return ins
                    S.op("pe", mm, reads=[wk, "gT5"], writes=bkk)
                    i = cnt["tmp"] % 2
                    cnt["tmp"] += 1
                    tf = tmpf[i]; tk = "tmpf%d" % i
                    for hi, (a0, a1) in enumerate(halves):
                        S.op("act", lambda e, hi=hi, a0=a0, a1=a1, tf=tf, oc=oc, bks=bks: e.activation(
                            out=tf[:, a0:a1], in_=bks[hi][:, 0:a1 - a0], func=AF.Sigmoid, bias=glubt[:, oc:oc + 1]),
                            reads=[bkk[hi], "glubt"], writes=[tk, bkk[hi]])
                    S.op("dve", lambda e, tf=tf, oc=oc: e.tensor_tensor(out=actv[:, oc, :], in0=gv[:, oc, :], in1=tf[:], op=ALU.mult),
                         reads=[tk, "gT5"], writes=["act_%d" % oc])
            st1.close()
            S.barrier()
        allact = ["act_%d" % c for c in range(DC)]
        gemm_resid.xv = actv
        st = stats_begin()
        gemm_resid(w_out, DC, allact, 2, hTin, hT1, g0, st)
        stats_finish(0, D)
        norm_mod(hT1, g0, weff2, modt, 3, lambda c: actv[:, c, :], lambda c: "act_%d" % c)
        st4 = contextlib.ExitStack()
        gT = st4.enter_context(nc.sbuf_tensor("gT" + "_g%d" % g, [128, HC * NT], BF16))
        gTv = gT[:].rearrange("p (c n) -> p c n", n=NT)
        for pn in range(HC // 2):
            w1v, w1k = wload(w1[:, pn * 256:(pn + 1) * 256], DC, 256)
            w3v, w3k = wload(w3[:, pn * 256:(pn + 1) * 256], DC, 256)
            for o2 in range(2):
                hc = pn * 2 + o2
                bks = [pb[o2 * 4 + i] for i in range(4)]
                bkk = ["pb%d" % (o2 * 4 + i) for i in range(4)]

                def mm(e, w1v=w1v, w3v=w3v, o2=o2, bks=bks):
                    ins = None
                    for wi, wv in enumerate((w1v, w3v)):
                        for k in range(DC):
                            for hi, (a0, a1) in enumerate(halves):
                                ins = e.matmul(bks[wi * 2 + hi][:, 0:a1 - a0], wv[:, k, o2 * 128:(o2 + 1) * 128], actv[:, k, a0:a1],
                                               start=(k == 0), stop=(k == DC - 1))
                    return ins
                S.op("pe", mm, reads=[w1k, w3k] + allact, writes=bkk)
                i = cnt["tmp"] % 2
                cnt["tmp"] += 1
                tf = tmpf[i]; tk = "tmpf%d" % i
                for hi, (a0, a1) in enumerate(halves):
                    S.op("act", lambda e, hi=hi, a0=a0, a1=a1, tf=tf, bks=bks: e.activation(out=tf[:, a0:a1], in_=bks[hi][:, 0:a1 - a0], func=AF.Silu),
                         reads=[bkk[hi]], writes=[tk + "_%d" % hi, bkk[hi]])
                    S.op("dve", lambda e, hi=hi, a0=a0, a1=a1, tf=tf, hc=hc, bks=bks: e.tensor_tensor(
                        out=gTv[:, hc, a0:a1], in0=bks[2 + hi][:, 0:a1 - a0], in1=tf[:, a0:a1], op=ALU.mult),
                        reads=[bkk[2 + hi], tk + "_%d" % hi], writes=["gT_%d" % hc, bkk[2 + hi]])
        gemm_resid.xv = gTv
        st = stats_begin()
        gemm_resid(w2, HC, ["gT_%d" % c for c in range(HC)], 5, hT1, hT2, g0, st)
        stats_finish(0, D)
        st4.close()
        S.barrier()
        if layer == 0:
            def store_a(c):
                pass
            norm_mod(hT2, g0, weffN, modn, 0, lambda c: actv[:, c, :], lambda c: "act_%d" % c)
            S.dma("sp", lambda e: e.dma_start(out=aTout.rearrange("(c p) t -> p c t", p=128)[:, :, g0:g0 + NT], in_=actv), reads=allact)
        else:
            st6 = contextlib.ExitStack()
            yT = st6.enter_context(nc.sbuf_tensor("yT" + "_g%d" % g, [128, DC * NT], F32))
            yv = yT[:].rearrange("p (c n) -> p c n", n=NT)
            norm_mod(hT2, g0, None, None, 0, lambda c: yv[:, c, :], lambda c: "yT_%d" % c)
            ot = [st6.enter_context(nc.sbuf_tensor("otk%d_g%d" % (i, g), [128, 2048], F32)) for i in range(2)]
            for tt in range(NT // 128):
                for hf in range(2):
                    o = ot[hf]; ok = "otk%d" % hf
                    for cq in range(4):
                        cg = hf * 4 + cq
                        bk = pb[cg % 4]; bkn = "pb%d" % (cg % 4)

                        def tr(e, cg=cg, bk=bk, tt=tt):
                            ins = None
                            for k in range(4):
                                c = cg * 4 + k
                                ins = e.transpose(bk[:, k * 128:(k + 1) * 128], yv[:, c, tt * 128:(tt + 1) * 128], ident[:])
                            return ins
                        S.op("pe", tr, reads=["yT_%d" % (cg * 4 + k) for k in range(4)] + ["ident"], writes=[bkn])
                        if cg % 2 == 0:
                            S.op("dve", lambda e, o=o, cq=cq, bk=bk: e.tensor_copy(o[:, cq * 512:(cq + 1) * 512], bk[:, 0:512]), reads=[bkn], writes=[ok, bkn])
                        else:
                            S.op("act", lambda e, o=o, cq=cq, bk=bk: e.activation(out=o[:, cq * 512:(cq + 1) * 512], in_=bk[:, 0:512], func=AF.Copy),
                                 reads=[bkn], writes=[ok, bkn])
                    S.dma("sp", lambda e, o=o, tt=tt, hf=hf: e.dma_start(out=outD[g0 + tt * 128:g0 + (tt + 1) * 128, hf * 2048:(hf + 1) * 2048], in_=o[:]), reads=[ok])
            st6.close()
            S.barrier()
    for g in range(2):
        do_group(g)
    return b.finish()


def core_cols_all(j, layer):
    gc = group_cols(layer)
    loc = np.where(gc < LAT_PC, CTX + j * LAT_PC + gc, j * CTX_PC + (gc - LAT_PC))
    return loc


def stage_D(inp, mod, hT_parts, attnT, convT):
    modt = mod_tables(mod[0])
    modn = mod_tables(mod[1])
    lnT = np.ascontiguousarray(np.concatenate([tab(inp["ev_ln_w"][0]), tab(inp["ev_ln_b"][0])], 1))
    epsv = np.ascontiguousarray(np.tile(np.array([[RMS_EPS, LN_EPS]], np.float32), (128, 1)))
    gc = group_cols(0)
    maps = []
    for j in range(NCORES):
        cols = core_cols_all(j, 0)
        maps.append({"hTin": np.ascontiguousarray(hT_parts[j][:, gc]), "modT": modt, "nw2T": tab(inp["norm_w"][0, 1]),
                     "w_out": inp["ev_w_out"][0], "w1": inp["ffn_w1"][0], "w3": inp["ffn_w3"][0], "w2": inp["ffn_w2"][0],
                     "epsv": epsv, "attnTin": np.ascontiguousarray(attnT[:, cols]), "convTin": np.ascontiguousarray(convT[:, cols]),
                     "lnT": lnT, "modN": modn, "nwN": tab(inp["norm_w"][1, 0])})
    res = run(build_T2(0), maps)
    inv = np.argsort(gc)
    if DEBUG:
        DBG["hT1"] = [np.ascontiguousarray(r["hT1"][:, inv]) for r in res]
    aT = [np.ascontiguousarray(r["aTout"][:, inv]) for r in res]
    hT = [np.ascontiguousarray(r["hTout"][:, inv]) for r in res]
    return hT, aT


E_COLS = 1164
TWO_PI = 2.0 * math.pi


def emit_sincos(b, ang, angk, n, sin_out, cos_out, outk, tag, scratch=None):
    S = b.S
    if scratch is None:
        scratch = (b.sb("sc_yi_" + tag, [128, n], I32), b.sb("sc_yf_" + tag, [128, n]), b.sb("sc_r_" + tag, [128, n]), b.sb("sc_mk_" + tag, [128, n]))
    yi, yf, r, mk = scratch
    tag = "x" if len(tag) > 1 else tag
    S.op("dve", lambda e: e.tensor_scalar(out=yf[:], in0=ang, scalar1=1.0 / TWO_PI, scalar2=None, op0=ALU.mult), reads=[angk], writes=["sc_yf" + tag])
    S.op("dve", lambda e: e.tensor_copy(yi[:], yf[:]), reads=["sc_yf" + tag], writes=["sc_yi" + tag])
    S.op("dve", lambda e: e.tensor_copy(yf[:], yi[:]), reads=["sc_yi" + tag], writes=["sc_yf" + tag])
    S.op("dve", lambda e: e.scalar_tensor_tensor(out=r[:], in0=yf[:], scalar=-TWO_PI, in1=ang, op0=ALU.mult, op1=ALU.add),
         reads=["sc_yf" + tag, angk], writes=["sc_r" + tag])
    S.op("dve", lambda e: e.tensor_single_scalar(out=mk[:], in_=r[:], scalar=math.pi, op=ALU.is_gt), reads=["sc_r" + tag], writes=["sc_mk" + tag])
    S.op("dve", lambda e: e.scalar_tensor_tensor(out=r[:], in0=mk[:], scalar=-TWO_PI, in1=r[:], op0=ALU.mult, op1=ALU.add),
         reads=["sc_mk" + tag, "sc_r" + tag], writes=["sc_r" + tag])
    S.op("dve", lambda e: e.tensor_single_scalar(out=mk[:], in_=r[:], scalar=-math.pi, op=ALU.is_lt), reads=["sc_r" + tag], writes=["sc_mk" + tag])
    S.op("dve", lambda e: e.scalar_tensor_tensor(out=r[:], in0=mk[:], scalar=TWO_PI, in1=r[:], op0=ALU.mult, op1=ALU.add),
         reads=["sc_mk" + tag, "sc_r" + tag], writes=["sc_r" + tag])
    S.op("dve", lambda e: e.tensor_scalar(out=r[:], in0=r[:], scalar1=math.pi, scalar2=-math.pi, op0=ALU.min, op1=ALU.max),
         reads=["sc_r" + tag], writes=["sc_r" + tag])
    S.op("act", lambda e: e.activation(out=sin_out, in_=r[:], func=AF.Sin), reads=["sc_r" + tag], writes=[outk + "_s"])
    S.op("act", lambda e: e.activation(out=yf[:], in_=r[:], func=AF.Sin, scale=0.5), reads=["sc_r" + tag], writes=["sc_yf" + tag])
    S.op("dve", lambda e: e.tensor_tensor(out=yf[:], in0=yf[:], in1=yf[:], op=ALU.mult), reads=["sc_yf" + tag], writes=["sc_yf" + tag])
    S.op("dve", lambda e: e.tensor_scalar(out=cos_out, in0=yf[:], scalar1=-2.0, scalar2=1.0, op0=ALU.mult, op1=ALU.add),
         reads=["sc_yf" + tag], writes=[outk + "_c"])
    return r, "sc_r" + tag


def build_E(phases=(1, 2, 3)):
    b = B()
    S = b.S
    nc = b.nc
    aTd = b.din("aT", [D, NTOK], BF16)
    wd = b.din("w", [D, E_COLS])
    identD = b.din("ident", [128, 128])
    tvecD = b.din("tvec", [128, 512])
    s5pD = b.din("s5p", [128, 3 * 8])
    s5bD = b.din("s5b", [128, 2 * 4 * 16])
    s5cD = b.din("s5c", [128, 2 * 8 * 16])
    s5dD = b.din("s5d", [128, 1])
    s5T = b.dout("s5T", [128, SEQ], BF16)
    uTd = b.dint("uTd", [128, NTOK])
    xbcTd = b.dint("xbcTd", [5, 128, NTOK])
    zdtd = b.dint("zdtd", [NTOK, 396])
    ssd_io = build_E_ssd_decl(b)

    ident = b.sb("identt", [128, 128])
    S.dma("sp", lambda e: e.dma_start(out=ident[:], in_=identD), writes=["ident"])

    if 1 in phases:
        st1 = contextlib.ExitStack()
        Wt = st1.enter_context(nc.sbuf_tensor("Wt", [128, DC * E_COLS], BF16))
        Wv = Wt[:].rearrange("p (kc n) -> p kc n", n=E_COLS)
        for g in range(8):
            S.dma("pool", lambda e, g=g: e.dma_start(out=Wv[:, g * 4:(g + 1) * 4, :],
                                                     in_=wd.rearrange("(kc p) n -> p kc n", p=128)[:, g * 4:(g + 1) * 4, :]), writes=["W%d" % g])
        Wk = ["W%d" % g for g in range(8)]
        aTs = [st1.enter_context(nc.sbuf_tensor("aTs%d" % i, [128, DC * 512], BF16)) for i in range(2)]
        ob = [st1.enter_context(nc.sbuf_tensor("ob%d" % i, [128, 512], F32)) for i in range(3)]
        zb = [st1.enter_context(nc.sbuf_tensor("zb%d" % i, [128, 396], F32)) for i in range(2)]
        psA = [st1.enter_context(nc.psum_tensor("psA%d" % i, [128, 512], F32)) for i in range(3)]
        psZ = [st1.enter_context(nc.psum_tensor("psZ%d" % i, [128, 512], F32)) for i in range(2)]
        sts = supertiles()
        aTv = aTd.rearrange("(c p) t -> p c t", p=128)

        def load_aT(si):
            t0, N = sts[si]
            buf = aTs[si % 2]
            S.dma("sp", lambda e: e.dma_start(out=buf[:, 0:DC * N].rearrange("p (c n) -> p c n", n=N), in_=aTv[:, :, t0:t0 + N]),
                  writes=["aTs%d" % (si % 2)])
        load_aT(0)
        nA = 0
        nZ = 0
        for si, (t0, N) in enumerate(sts):
            if si + 1 < len(sts):
                load_aT(si + 1)
            a = aTs[si % 2][:, 0:DC * N].rearrange("p (c n) -> p c n", n=N)
            ak = "aTs%d" % (si % 2)
            for blk in range(6):
                i = nA % 3
                nA += 1

                def mm(e, a=a, blk=blk, i=i, N=N):
                    ins = None
                    for kc in range(DC):
                        ins = e.matmul(psA[i][:, 0:N], Wv[:, kc, blk * 128:(blk + 1) * 128], a[:, kc, :], start=(kc == 0), stop=(kc == DC - 1))
                    return ins
                S.op("pe", mm, reads=[ak] + Wk, writes=["psA%d" % i])
                if blk % 2 == 0:
                    S.op("dve", lambda e, i=i, N=N: e.tensor_copy(ob[i][:, 0:N], psA[i][:, 0:N]), reads=["psA%d" % i], writes=["ob%d" % i, "psA%d" % i])
                else:
                    S.op("act", lambda e, i=i, N=N: e.activation(out=ob[i][:, 0:N], in_=psA[i][:, 0:N], func=AF.Copy), reads=["psA%d" % i], writes=["ob%d" % i, "psA%d" % i])
                dst = uTd[:, t0:t0 + N] if blk == 0 else xbcTd[blk - 1, :, t0:t0 + N]
                S.dma("sp", lambda e, i=i, N=N, dst=dst: e.dma_start(out=dst, in_=ob[i][:, 0:N]), reads=["ob%d" % i], writes=["uTd" if blk == 0 else "xbcTd"])
            for sub in range(N // 128):
                i = nZ % 2
                nZ += 1

                def mmz(e, a=a, sub=sub, i=i):
                    ins = None
                    for kc in range(DC):
                        ins = e.matmul(psZ[i][:, 0:396], a[:, kc, sub * 128:(sub + 1) * 128], Wv[:, kc, 768:1164], start=(kc == 0), stop=(kc == DC - 1))
                    return ins
                S.op("pe", mmz, reads=[ak] + Wk, writes=["psZ%d" % i])
                S.op("dve", lambda e, i=i: e.tensor_copy(zb[i][:], psZ[i][:, 0:396]), reads=["psZ%d" % i], writes=["zb%d" % i, "psZ%d" % i])
                S.dma("sp", lambda e, i=i, t0=t0, sub=sub: e.dma_start(out=zdtd[t0 + sub * 128:t0 + (sub + 1) * 128, :], in_=zb[i][:]),
                      reads=["zb%d" % i], writes=["zdtd"])
        st1.close()
        S.barrier()

    if 2 in phases:
        st2 = contextlib.ExitStack()

        def T(name, shape, dt=F32):
            return st2.enter_context(nc.sbuf_tensor(name + "_s2", list(shape), dt))
        bsave = b.sb
        b.sb = T
        uT = T("uT", [128, NTOK])
        yf = T("yf", [128, SEQ])
        tvec = T("tvec", [128, 512])
        s5p = T("s5p", [128, 24])
        s5b = T("s5b", [128, 128])
        s5c = T("s5c", [128, 256])
        s5d = T("s5d", [128, 1])
        S.dma("sp", lambda e: e.dma_start(out=uT[:], in_=uTd), reads=["uTd"], writes=["uT"])
        S.dma("sp", lambda e: e.dma_start(out=tvec[:], in_=tvecD), writes=["tvec"])
        S.dma("sp", lambda e: e.dma_start(out=s5p[:], in_=s5pD), writes=["s5p"])
        S.dma("sp", lambda e: e.dma_start(out=s5b[:], in_=s5bD), writes=["s5b"])
        S.dma("sp", lambda e: e.dma_start(out=s5c[:], in_=s5cD), writes=["s5c"])
        S.dma("sp", lambda e: e.dma_start(out=s5d[:], in_=s5dD), writes=["s5d"])
        lre = s5p[:, 0:8]; lim = s5p[:, 8:16]
        dl = T("dl", [128, 8]); th = T("th", [128, 8]); mg = T("mg", [128, 8])
        sn = T("sn", [128, 8]); cs = T("cs", [128, 8])
        S.op("act", lambda e: e.activation(out=dl[:], in_=s5p[:, 16:24], func=AF.Exp), reads=["s5p"], writes=["dl"])
        S.op("dve", lambda e: e.tensor_tensor(out=th[:], in0=lim, in1=dl[:], op=ALU.mult), reads=["s5p", "dl"], writes=["th"])
        S.op("dve", lambda e: e.tensor_tensor(out=mg[:], in0=lre, in1=dl[:], op=ALU.mult), reads=["s5p", "dl"], writes=["mg"])
        S.op("act", lambda e: e.activation(out=mg[:], in_=mg[:], func=AF.Exp), reads=["mg"], writes=["mg"])
        thred, thredk = emit_sincos(b, th[:], "th", 8, sn[:], cs[:], "sc8", "p")
        ar = T("ar", [128, 8]); ai = T("ai", [128, 8]); den = T("den", [128, 8]); kr = T("kr", [128, 8]); ki = T("ki", [128, 8]); tq = T("tq", [128, 8])
        S.op("dve", lambda e: e.tensor_tensor(out=ar[:], in0=mg[:], in1=cs[:], op=ALU.mult), reads=["mg", "sc8_c"], writes=["ar"])
        S.op("dve", lambda e: e.tensor_scalar_add(ar[:], ar[:], -1.0), reads=["ar"], writes=["ar"])
        S.op("dve", lambda e: e.tensor_tensor(out=ai[:], in0=mg[:], in1=sn[:], op=ALU.mult), reads=["mg", "sc8_s"], writes=["ai"])
        S.op("dve", lambda e: e.tensor_tensor(out=den[:], in0=lre, in1=lre, op=ALU.mult), reads=["s5p"], writes=["den"])
        S.op("dve", lambda e: e.tensor_tensor(out=tq[:], in0=lim, in1=lim, op=ALU.mult), reads=["s5p"], writes=["tq"])
        S.op("dve", lambda e: e.tensor_tensor(out=den[:], in0=den[:], in1=tq[:], op=ALU.add), reads=["den", "tq"], writes=["den"])
        S.op("dve", lambda e: e.reciprocal(out=den[:], in_=den[:]), reads=["den"], writes=["den"])
        S.op("dve", lambda e: e.tensor_tensor(out=kr[:], in0=ar[:], in1=lre, op=ALU.mult), reads=["ar", "s5p"], writes=["kr"])
        S.op("dve", lambda e: e.tensor_tensor(out=tq[:], in0=ai[:], in1=lim, op=ALU.mult), reads=["ai", "s5p"], writes=["tq"])
        S.op("dve", lambda e: e.tensor_tensor(out=kr[:], in0=kr[:], in1=tq[:], op=ALU.add), reads=["kr", "tq"], writes=["kr"])
        S.op("dve", lambda e: e.tensor_tensor(out=kr[:], in0=kr[:], in1=den[:], op=ALU.mult), reads=["kr", "den"], writes=["kr"])
        S.op("dve", lambda e: e.tensor_tensor(out=ki[:], in0=ai[:], in1=lre, op=ALU.mult), reads=["ai", "s5p"], writes=["ki"])
        S.op("dve", lambda e: e.tensor_tensor(out=tq[:], in0=ar[:], in1=lim, op=ALU.mult), reads=["ar", "s5p"], writes=["tq"])
        S.op("dve", lambda e: e.tensor_tensor(out=ki[:], in0=ki[:], in1=tq[:], op=ALU.subtract), reads=["ki", "tq"], writes=["ki"])
        S.op("dve", lambda e: e.tensor_tensor(out=ki[:], in0=ki[:], in1=den[:], op=ALU.mult), reads=["ki", "den"], writes=["ki"])
        bbr = T("bbr", [128, 128]); bbi = T("bbi", [128, 128]); tb = T("tb", [128, 128])
        bre = s5b[:, 0:64].rearrange("p (s h) -> p s h", h=16)
        bim = s5b[:, 64:128].rearrange("p (s h) -> p s h", h=16)
        for d in range(2):
            krb = kr[:, d * 4:(d + 1) * 4].unsqueeze(2).to_broadcast([128, 4, 16])
            kib = ki[:, d * 4:(d + 1) * 4].unsqueeze(2).to_broadcast([128, 4, 16])
            ovr = bbr[:, d * 64:(d + 1) * 64].rearrange("p (s h) -> p s h", h=16)
            ovi = bbi[:, d * 64:(d + 1) * 64].rearrange("p (s h) -> p s h", h=16)
            tv = tb[:, d * 64:(d + 1) * 64].rearrange("p (s h) -> p s h", h=16)
            S.op("dve", lambda e, ovr=ovr, krb=krb: e.tensor_tensor(out=ovr, in0=bre, in1=krb, op=ALU.mult), reads=["s5b", "kr"], writes=["bbr"])
            S.op("dve", lambda e, tv=tv, kib=kib: e.tensor_tensor(out=tv, in0=bim, in1=kib, op=ALU.mult), reads=["s5b", "ki"], writes=["tb"])
            S.op("dve", lambda e, ovr=ovr, tv=tv: e.tensor_tensor(out=ovr, in0=ovr, in1=tv, op=ALU.subtract), reads=["bbr", "tb"], writes=["bbr"])
            S.op("dve", lambda e, ovi=ovi, krb=krb: e.tensor_tensor(out=ovi, in0=bim, in1=krb, op=ALU.mult), reads=["s5b", "kr"], writes=["bbi"])
            S.op("dve", lambda e, tv=tv, kib=kib: e.tensor_tensor(out=tv, in0=bre, in1=kib, op=ALU.mult), reads=["s5b", "ki"], writes=["tb"])
            S.op("dve", lambda e, ovi=ovi, tv=tv: e.tensor_tensor(out=ovi, in0=ovi, in1=tv, op=ALU.add), reads=["bbi", "tb"], writes=["bbi"])
        BD = T("BD", [128, 128])
        BL = T("BL", [128, 16 * 128])
        CD = T("CD", [128, 16 * 128])
        psB = st2.enter_context(nc.psum_tensor("psB5", [128, 512], F32))
        S.op("dve", lambda e: e.memset(CD[:], 0.0), writes=["CD"])
        for ri in range(2):
            src = bbr if ri == 0 else bbi
            srck = "bbr" if ri == 0 else "bbi"
            for k in range(8):
                sb_ = k % 4
                S.op("dve", lambda e: e.memset(BD[:], 0.0), writes=["BD"])
                for gl in range(2):
                    g8 = 2 * sb_ + gl
                    S.op("dve", lambda e, src=src, k=k, gl=gl, g8=g8: e.tensor_copy(
                        BD[gl * 64:(gl + 1) * 64, g8 * 16:(g8 + 1) * 16], src[gl * 64:(gl + 1) * 64, k * 16:(k + 1) * 16]),
                        reads=[srck], writes=["BD"])
                    col = (ri * 8 + k) * 128 + g8 * 16
                    sgn = 1.0 if ri == 0 else -1.0
                    S.op("dve", lambda e, ri=ri, k=k, gl=gl, col=col, sgn=sgn: e.tensor_scalar(
                        out=CD[gl * 64:(gl + 1) * 64, col:col + 16], in0=s5c[gl * 64:(gl + 1) * 64, (ri * 8 + k) * 16:(ri * 8 + k + 1) * 16],
                        scalar1=sgn, scalar2=None, op0=ALU.mult), reads=["s5c"], writes=["CD"])
                S.op("pe", lambda e: e.transpose(psB[:, 0:128], BD[:], ident[:]), reads=["BD", "ident"], writes=["psB5"])
                S.op("dve", lambda e, ri=ri, k=k: e.tensor_copy(BL[:, (ri * 8 + k) * 128:(ri * 8 + k + 1) * 128], psB[:, 0:128]),
                     reads=["psB5"], writes=["BL", "psB5"])
        thr = T("thr", [128, 8])
        S.op("dve", lambda e: e.tensor_copy(thr[:], thred[:]), reads=[thredk], writes=["thr"])
        COS = T("COS", [128, 8 * 512]); SIN = T("SIN", [128, 8 * 512])
        ang = T("ang", [128, 512])
        scr = (T("scyi", [128, 512], I32), T("scyf", [128, 512]), T("scr", [128, 512]), T("scmk", [128, 512]))
        for k in range(8):
            S.op("dve", lambda e, k=k: e.tensor_scalar(out=ang[:], in0=tvec[:], scalar1=thr[:, k:k + 1], scalar2=None, op0=ALU.mult),
                 reads=["tvec", "thr"], writes=["ang"])
            emit_sincos(b, ang[:], "ang", 512, SIN[:, k * 512:(k + 1) * 512], COS[:, k * 512:(k + 1) * 512], "tab%d" % k, "t%d" % k, scr)
        NW = 2
        wr = [T("wr%d" % i, [128, 512]) for i in range(NW)]; wi = [T("wi%d" % i, [128, 512]) for i in range(NW)]
        zr = [T("zr%d" % i, [128, 512]) for i in range(NW)]; zi = [T("zi%d" % i, [128, 512]) for i in range(NW)]
        sr = [T("sr%d" % i, [128, 512]) for i in range(NW)]; si_ = [T("si%d" % i, [128, 512]) for i in range(NW)]
        t1 = [T("tt1%d" % i, [128, 512]) for i in range(NW)]
        carry = T("carry", [128, 2 * 8])
        gy = T("gy", [128, 512]); g2 = T("g2", [128, 512]); gob = [T("gob%d" % i, [128, 512], BF16) for i in range(2)]
        psU = [st2.enter_context(nc.psum_tensor("psU%d" % i, [128, 512], F32)) for i in range(4)]
        psY = [st2.enter_context(nc.psum_tensor("psY%d" % i, [128, 512], F32)) for i in range(2)]
        S.op("dve", lambda e: e.memset(carry[:], 0.0), writes=["carry"])
        chunks = [(0, CTX, True)] + [(CTX + 512 * i, 512, False) for i in range(SEQ // 512)]
        nit = 0
        ny = 0
        for d in range(2):
            order = chunks if d == 0 else [chunks[0]] + chunks[:0:-1]
            for (t0, L, is_ctx) in order:
                yps = psY[ny % 2]; ypk = "psY%d" % (ny % 2)
                ny += 1
                for sb_ in range(4):
                    k = d * 4 + sb_
                    i = nit % NW
                    pu = [psU[(nit % 2) * 2], psU[(nit % 2) * 2 + 1]]
                    puk = ["psU%d" % ((nit % 2) * 2), "psU%d" % ((nit % 2) * 2 + 1)]
                    nit += 1

                    def mmb(e, k=k, t0=t0, L=L, pu=pu):
                        e.matmul(pu[0][:, 0:L], BL[:, k * 128:(k + 1) * 128], uT[:, t0:t0 + L], start=True, stop=True)
                        return e.matmul(pu[1][:, 0:L], BL[:, (8 + k) * 128:(9 + k) * 128], uT[:, t0:t0 + L], start=True, stop=True)
                    S.op("pe", mmb, reads=["BL", "uT"], writes=puk)
                    co = COS[:, k * 512:k * 512 + L]; so = SIN[:, k * 512:k * 512 + L]
                    if d == 0:
                        bur = pu[0][:, 0:L]; bui = pu[1][:, 0:L]
                        srv = sr[i][:, 0:L]; siv = si_[i][:, 0:L]
                    else:
                        bur = pu[0][:, 0:L][:, ::-1]; bui = pu[1][:, 0:L][:, ::-1]
                        srv = sr[i][:, 0:L][:, ::-1]; siv = si_[i][:, 0:L][:, ::-1]
                    W_r = wr[i][:, 0:L]; W_i = wi[i][:, 0:L]; Z_r = zr[i][:, 0:L]; Z_i = zi[i][:, 0:L]; T1 = t1[i][:, 0:L]
                    ks = ["wr%d" % i, "wi%d" % i, "zr%d" % i, "zi%d" % i, "sr%d" % i, "si%d" % i, "tt1%d" % i]
                    S.op("dve", lambda e, T1=T1, so=so, bui=bui: e.tensor_tensor(out=T1, in0=bui, in1=so, op=ALU.mult),
                         reads=[puk[1], "tab%d_s" % k, "tab%d_c" % k], writes=[ks[6], puk[1]])
                    S.op("dve", lambda e, W_r=W_r, co=co, bur=bur: e.tensor_tensor(out=W_r, in0=bur, in1=co, op=ALU.mult), reads=[puk[0]], writes=[ks[0], puk[0]])
                    S.op("dve", lambda e, W_r=W_r, T1=T1: e.tensor_tensor(out=W_r, in0=W_r, in1=T1, op=ALU.add), reads=[ks[0], ks[6]], writes=[ks[0]])
                    S.op("dve", lambda e, T1=T1, so=so, bur=bur: e.tensor_tensor(out=T1, in0=bur, in1=so, op=ALU.mult), reads=[puk[0], ks[0]], writes=[ks[6], puk[0]])
                    S.op("dve", lambda e, W_i=W_i, co=co, bui=bui: e.tensor_tensor(out=W_i, in0=bui, in1=co, op=ALU.mult), reads=[puk[1]], writes=[ks[1], puk[1]])
                    S.op("dve", lambda e, W_i=W_i, T1=T1: e.tensor_tensor(out=W_i, in0=W_i, in1=T1, op=ALU.subtract), reads=[ks[1], ks[6]], writes=[ks[1]])
                    S.op("dve", lambda e, Z_r=Z_r, W_r=W_r, k=k, L=L: e.tensor_tensor_scan(
                        out=Z_r, data0=mg[:, k:k + 1].to_broadcast([128, L]), data1=W_r, initial=carry[:, k:k + 1], op0=ALU.mult, op1=ALU.add),
                        reads=[ks[0], "mg", "carry"], writes=[ks[2]])
                    S.op("dve", lambda e, Z_i=Z_i, W_i=W_i, k=k, L=L: e.tensor_tensor_scan(
                        out=Z_i, data0=mg[:, k:k + 1].to_broadcast([128, L]), data1=W_i, initial=carry[:, 8 + k:9 + k], op0=ALU.mult, op1=ALU.add),
                        reads=[ks[1], "mg", "carry"], writes=[ks[3]])
                    S.op("dve", lambda e, T1=T1, so=so, Z_i=Z_i: e.tensor_tensor(out=T1, in0=Z_i, in1=so, op=ALU.mult), reads=[ks[3], ks[1]], writes=[ks[6]])
                    S.op("dve", lambda e, srv=srv, co=co, Z_r=Z_r: e.tensor_tensor(out=srv, in0=Z_r, in1=co, op=ALU.mult), reads=[ks[2]], writes=[ks[4]])
                    S.op("dve", lambda e, srv=srv, T1=T1: e.tensor_tensor(out=srv, in0=srv, in1=T1, op=ALU.subtract), reads=[ks[4], ks[6]], writes=[ks[4]])
                    S.op("dve", lambda e, T1=T1, so=so, Z_r=Z_r: e.tensor_tensor(out=T1, in0=Z_r, in1=so, op=ALU.mult), reads=[ks[2], ks[4]], writes=[ks[6]])
                    S.op("dve", lambda e, siv=siv, co=co, Z_i=Z_i: e.tensor_tensor(out=siv, in0=Z_i, in1=co, op=ALU.mult), reads=[ks[3]], writes=[ks[5]])
                    S.op("dve", lambda e, siv=siv, T1=T1: e.tensor_tensor(out=siv, in0=siv, in1=T1, op=ALU.add), reads=[ks[5], ks[6]], writes=[ks[5]])
                    last = (L - 1) if d == 0 else 0
                    S.op("act", lambda e, i=i, k=k, last=last: e.activation(out=carry[:, k:k + 1], in_=sr[i][:, last:last + 1], func=AF.Copy),
                         reads=[ks[4]], writes=["carry"])
                    S.op("act", lambda e, i=i, k=k, last=last: e.activation(out=carry[:, 8 + k:9 + k], in_=si_[i][:, last:last + 1], func=AF.Copy),
                         reads=[ks[5]], writes=["carry"])
                    if not is_ctx:
                        def mmy(e, i=i, k=k, L=L, sb_=sb_, yps=yps):
                            e.matmul(yps[:, 0:L], CD[:, k * 128:(k + 1) * 128], sr[i][:, 0:L], start=(sb_ == 0), stop=False)
                            return e.matmul(yps[:, 0:L], CD[:, (8 + k) * 128:(9 + k) * 128], si_[i][:, 0:L], start=False, stop=(sb_ == 3))
                        S.op("pe", mmy, reads=["CD", ks[4], ks[5]], writes=[ypk])
                if is_ctx:
                    continue
                l0 = t0 - CTX
                if d == 0:
                    S.op("act", lambda e, l0=l0, L=L, yps=yps: e.activation(out=yf[:, l0:l0 + L], in_=yps[:, 0:L], func=AF.Copy),
                         reads=[ypk], writes=["yf", ypk])
                else:
                    go = gob[ny % 2]; gok = "gob%d" % (ny % 2)
                    S.op("dve", lambda e, l0=l0, L=L, yps=yps: e.tensor_tensor(out=gy[:, 0:L], in0=yps[:, 0:L], in1=yf[:, l0:l0 + L], op=ALU.add),
                         reads=[ypk, "yf"], writes=["gy", ypk])
                    S.op("dve", lambda e, t0=t0, L=L: e.scalar_tensor_tensor(out=gy[:, 0:L], in0=uT[:, t0:t0 + L], scalar=s5d[:, 0:1], in1=gy[:, 0:L],
                                                                             op0=ALU.mult, op1=ALU.add), reads=["uT", "s5d", "gy"], writes=["gy"])
                    S.op("dve", lambda e, L=L: e.tensor_tensor(out=g2[:, 0:L], in0=gy[:, 0:L], in1=gy[:, 0:L], op=ALU.mult), reads=["gy"], writes=["g2"])
                    S.op("dve", lambda e, L=L: e.tensor_scalar(out=g2[:, 0:L], in0=g2[:, 0:L], scalar1=0.044715, scalar2=1.0, op0=ALU.mult, op1=ALU.add),
                         reads=["g2"], writes=["g2"])
                    S.op("dve", lambda e, L=L: e.tensor_tensor(out=g2[:, 0:L], in0=g2[:, 0:L], in1=gy[:, 0:L], op=ALU.mult), reads=["g2", "gy"], writes=["g2"])
                    S.op("act", lambda e, L=L: e.activation(out=g2[:, 0:L], in_=g2[:, 0:L], func=AF.Sigmoid, scale=2.0 * math.sqrt(2.0 / math.pi)),
                         reads=["g2"], writes=["g2"])
                    S.op("dve", lambda e, L=L, go=go: e.tensor_tensor(out=go[:, 0:L], in0=g2[:, 0:L], in1=gy[:, 0:L], op=ALU.mult), reads=["g2", "gy"], writes=[gok])
                    S.dma("sp", lambda e, l0=l0, L=L, go=go: e.dma_start(out=s5T[:, l0:l0 + L], in_=go[:, 0:L]), reads=[gok])
        b.sb = bsave
        st2.close()
        S.barrier()
    if 3 in phases:
        build_E_ssd(b, ssd_io, xbcTd, zdtd, ident)
    return b.finish()


def build_E_ssd_decl(b):
    io = {}
    io["cw"] = b.din("m2cw", [128, 25])
    io["cb"] = b.din("m2cb", [128, 5])
    io["alog"] = b.din("m2alog", [12])
    io["dtb"] = b.din("m2dtb", [12])
    io["dsk"] = b.din("m2d", [6])
    io["nw"] = b.din("m2nw", [384])
    io["tri"] = b.din("tri", [128, 256])
    io["ssdT"] = b.dout("ssdT", [384, SEQ], BF16)
    mk = b.dout if DEBUG else b.dint
    io["xtm"] = mk("xtm", [NTOK, 384])
    io["Btm"] = mk("Btm", [NTOK, 128], BF16)
    io["BTd"] = mk("BTd", [128, NTOK], BF16)
    io["CTd"] = mk("CTd", [128, NTOK], BF16)
    io["dtsp"] = mk("dtsp", [NTOK, 12])
    io["Yf"] = mk("Yf", [SEQ, 384])
    return io


def build_E_ssd(b, io, xbcTd, zdtd, ident):
    S = b.S
    nc = b.nc
    st = contextlib.ExitStack()

    def T(name, shape, dt=F32):
        return st.enter_context(nc.sbuf_tensor(name + "_s3", list(shape), dt))

    def PS(name, dt=F32):
        return st.enter_context(nc.psum_tensor(name + "_s3", [128, 512 if dt == F32 else 1024], dt))
    cw = T("cw", [128, 25]); cb = T("cb", [128, 5]); abc = T("abc", [128, 12]); dtb = T("dtb", [128, 12])
    dsk = T("dsk", [128, 6]); nw = T("nw", [128, 384]); tri = T("tri", [128, 256]); onesf = T("onesf", [128, 128])
    identb = T("identb", [128, 128], BF16)
    S.dma("sp", lambda e: e.dma_start(out=cw[:], in_=io["cw"]), writes=["cw"])
    S.dma("sp", lambda e: e.dma_start(out=cb[:], in_=io["cb"]), writes=["cb"])
    S.dma("sp", lambda e: e.dma_start(out=abc[:], in_=io["alog"].partition_broadcast(128)), writes=["abc"])
    S.dma("sp", lambda e: e.dma_start(out=dtb[:], in_=io["dtb"].partition_broadcast(128)), writes=["dtb"])
    S.dma("sp", lambda e: e.dma_start(out=dsk[:], in_=io["dsk"].partition_broadcast(128)), writes=["dsk"])
    S.dma("sp", lambda e: e.dma_start(out=nw[:], in_=io["nw"].partition_broadcast(128)), writes=["nw"])
    S.dma("sp", lambda e: e.dma_start(out=tri[:], in_=io["tri"]), writes=["tri"])
    S.op("dve", lambda e: e.memset(onesf[:], 1.0), writes=["onesf"])
    S.op("dve", lambda e: e.tensor_copy(identb[:], ident[:]), reads=["ident"], writes=["identb3"])
    S.op("act", lambda e: e.activation(out=abc[:], in_=abc[:], func=AF.Exp), reads=["abc"], writes=["abc"])
    S.op("dve", lambda e: e.tensor_scalar_mul(abc[:], abc[:], -1.0), reads=["abc"], writes=["abc"])

    psTb = PS("psTb", BF16)
    st_outer = st
    st = contextlib.ExitStack()
    xin = [T("xin%d" % i, [128, 5 * 516]) for i in range(2)]
    acc = [T("cacc%d" % i, [128, 512]) for i in range(2)]
    xcb = [T("xcb%d" % i, [128, 512], BF16) for i in range(2)]
    xtt = [T("xtt%d" % i, [128, 384]) for i in range(4)]
    btt = [T("btt%d" % i, [128, 128], BF16) for i in range(2)]
    dtt = [T("dtt%d" % i, [128, 4 * 12]) for i in range(2)]
    psT = [PS("psT%d" % i) for i in range(2)]
    xbv = xbcTd.rearrange("b p t -> p b t")
    sts = supertiles()
    ntr = 0
    for si, (t0, N) in enumerate(sts):
        seq0, seq1 = (0, CTX) if si == 0 else (CTX, NTOK)
        lo = max(seq0, t0 - 2); hi = min(seq1, t0 + N + 2)
        xi = xin[si % 2]; xk = "xin%d" % (si % 2)
        xiv = xi[:].rearrange("p (b n) -> p b n", n=516)
        S.op("dve", lambda e, xi=xi: e.memset(xi[:], 0.0), writes=[xk])
        S.dma("sp", lambda e, xiv=xiv, lo=lo, hi=hi, t0=t0: e.dma_start(out=xiv[:, :, lo - (t0 - 2):hi - (t0 - 2)], in_=xbv[:, :, lo:hi]),
              reads=["xbcTd"], writes=[xk])
        nsub = N // 128
        dt_ = dtt[si % 2]; dk = "dtt%d" % (si % 2)
        dv = dt_[:, 0:nsub * 12].rearrange("p (s c) -> p s c", c=12)
        S.dma("sp", lambda e, dv=dv, t0=t0, N=N: e.dma_start(out=dv, in_=zdtd[t0:t0 + N, 384:396].rearrange("(s p) c -> p s c", p=128)),
              reads=["zdtd"], writes=[dk])
        S.op("dve", lambda e, dv=dv, nsub=nsub: e.tensor_tensor(out=dv, in0=dv, in1=dtb[:].unsqueeze(1).to_broadcast([128, nsub, 12]), op=ALU.add),
             reads=[dk, "dtb"], writes=[dk])
        S.op("act", lambda e, dv=dv: e.activation(out=dv, in_=dv, func=AF.Exp), reads=[dk], writes=[dk])
        S.op("act", lambda e, dv=dv: e.activation(out=dv, in_=dv, func=AF.Ln, bias=1.0), reads=[dk], writes=[dk])
        S.dma("sp", lambda e, dv=dv, t0=t0, N=N: e.dma_start(out=io["dtsp"][t0:t0 + N, :].rearrange("(s p) c -> p s c", p=128), in_=dv),
              reads=[dk], writes=["dtsp"])
        for blk in range(5):
            ai = (si * 5 + blk) % 2
            ac = acc[ai]; ack = "cacc%d" % ai
            xc = xcb[ai]; xck = "xcb%d" % ai

            def conv(e, ac=ac, xiv=xiv, blk=blk, N=N):
                ins = e.tensor_scalar(out=ac[:, 0:N], in0=xiv[:, blk, 0:N], scalar1=cw[:, blk * 5:blk * 5 + 1], scalar2=cb[:, blk:blk + 1],
                                      op0=ALU.mult, op1=ALU.add)
                for k in range(1, 5):
                    ins = e.scalar_tensor_tensor(out=ac[:, 0:N], in0=xiv[:, blk, k:k + N], scalar=cw[:, blk * 5 + k:blk * 5 + k + 1],
                                                 in1=ac[:, 0:N], op0=ALU.mult, op1=ALU.add)
                return ins
            S.op("dve", conv, reads=[xk, "cw", "cb"], writes=[ack])
            if blk < 3:
                S.op("act", lambda e, ac=ac, N=N: e.activation(out=ac[:, 0:N], in_=ac[:, 0:N], func=AF.Silu), reads=[ack], writes=[ack])
            else:
                S.op("act", lambda e, ac=ac, xc=xc, N=N: e.activation(out=xc[:, 0:N], in_=ac[:, 0:N], func=AF.Silu), reads=[ack], writes=[xck])
                dstT = io["BTd"] if blk == 3 else io["CTd"]
                S.dma("sp", lambda e, xc=xc, N=N, t0=t0, dstT=dstT: e.dma_start(out=dstT[:, t0:t0 + N], in_=xc[:, 0:N]), reads=[xck],
                      writes=["BTd" if blk == 3 else "CTd"])
            for sub in range(nsub):
                if blk < 3:
                    pt = psT[ntr % 2]; ptk = "psT%d" % (ntr % 2)
                    xo = xtt[sub]; xok = "xtt%d" % sub
                    ntr += 1
                    S.op("pe", lambda e, pt=pt, ac=ac, sub=sub: e.transpose(pt[:, 0:128], ac[:, sub * 128:(sub + 1) * 128], ident[:]),
                         reads=[ack, "ident"], writes=[ptk])
                    S.op("act", lambda e, pt=pt, xo=xo, blk=blk: e.activation(out=xo[:, blk * 128:(blk + 1) * 128], in_=pt[:, 0:128], func=AF.Copy),
                         reads=[ptk], writes=[xok + "_%d" % blk, ptk])
                    if blk == 2:
                        S.dma("sp", lambda e, xo=xo, t0=t0, sub=sub: e.dma_start(out=io["xtm"][t0 + sub * 128:t0 + (sub + 1) * 128, :], in_=xo[:]),
                              reads=[xok + "_0", xok + "_1", xok + "_2"], writes=["xtm"])
                elif blk == 3:
                    bo = btt[sub % 2]; bok = "btt%d" % (sub % 2)
                    S.op("pe", lambda e, xc=xc, sub=sub: e.transpose(psTb[:, 0:128], xc[:, sub * 128:(sub + 1) * 128], identb[:]),
                         reads=[xck, "identb3"], writes=["psTb"])
                    S.op("act", lambda e, bo=bo: e.activation(out=bo[:], in_=psTb[:, 0:128], func=AF.Copy), reads=["psTb"], writes=[bok, "psTb"])
                    S.dma("sp", lambda e, bo=bo, t0=t0, sub=sub: e.dma_start(out=io["Btm"][t0 + sub * 128:t0 + (sub + 1) * 128, :], in_=bo[:]),
                          reads=[bok], writes=["Btm"])

    st.close()
    S.barrier()
    st = st_outer
    xt = [T("xt%d" % i, [128, 384]) for i in range(2)]
    Bt = [T("Bt%d" % i, [128, 128], BF16) for i in range(2)]
    BTc = [T("BTc%d" % i, [128, 128], BF16) for i in range(2)]
    CTc = [T("CTc%d" % i, [128, 128], BF16) for i in range(2)]
    dtc = [T("dtc%d" % i, [128, 12]) for i in range(2)]
    zt = [T("zt%d" % i, [128, 384]) for i in range(2)]
    yft = [T("yft%d" % i, [128, 384]) for i in range(2)]
    da = T("da", [128, 6]); cumc = T("cumc", [128, 16]); ncum = T("ncum", [128, 6]); ecum = T("ecum", [128, 6])
    dsc = T("dsc", [128, 6]); dch = T("dch", [128, 6])
    GmT = T("GmT", [128, 128]); Ex = [T("Ex%d" % i, [128, 128]) for i in range(2)]
    MT = [T("MT%d" % i, [128, 128], BF16) for i in range(6)]
    xdt = T("xdt", [128, 384], BF16); xw = T("xw", [128, 384], BF16)
    yo = T("yo", [128, 384]); ydir = [T("ydir%d" % i, [128, 384]) for i in range(2)]
    Sst = T("Sst", [128, 384]); Sb = T("Sb", [128, 384], BF16)
    ysq = T("ysq", [128, 384]); rs = T("rs", [128, 4]); ynb = T("ynb", [128, 384], BF16)
    oT = [T("oTs%d" % i, [128, 3 * 128], BF16) for i in range(2)]
    psC = PS("psC"); psCB = [PS("psCB0"), PS("psCB1")]; psG = PS("psG"); psYd = PS("psYd"); psYo = PS("psYo"); psSt = PS("psSt")
    chunks = list(range(NKB))
    nld = 0
    nyo = 0
    for d in range(2):
        order = chunks if d == 0 else [1, 0] + chunks[:1:-1]
        trd = tri[:, d * 128:(d + 1) * 128]
        S.op("dve", lambda e: e.memset(Sst[:], 0.0), writes=["Sst"])
        S.op("dve", lambda e: e.memset(Sb[:], 0.0), writes=["Sb"])
        for c in order:
            t0 = c * 128
            is_ctx = c < 2
            i = nld % 2
            nld += 1
            lk = ["xt%d" % i, "Bt%d" % i, "BTc%d" % i, "CTc%d" % i, "dtc%d" % i]
            S.dma("sp", lambda e, i=i, t0=t0: e.dma_start(out=xt[i][:], in_=io["xtm"][t0:t0 + 128, :]), reads=["xtm"], writes=[lk[0]])
            S.dma("sp", lambda e, i=i, t0=t0: e.dma_start(out=Bt[i][:], in_=io["Btm"][t0:t0 + 128, :]), reads=["Btm"], writes=[lk[1]])
            S.dma("sp", lambda e, i=i, t0=t0: e.dma_start(out=dtc[i][:], in_=io["dtsp"][t0:t0 + 128, :]), reads=["dtsp"], writes=[lk[4]])
            if not is_ctx:
                S.dma("sp", lambda e, i=i, t0=t0: e.dma_start(out=BTc[i][:], in_=io["BTd"][:, t0:t0 + 128]), reads=["BTd"], writes=[lk[2]])
                S.dma("sp", lambda e, i=i, t0=t0: e.dma_start(out=CTc[i][:], in_=io["CTd"][:, t0:t0 + 128]), reads=["CTd"], writes=[lk[3]])
            dth = dtc[i][:, d * 6:(d + 1) * 6]
            S.op("dve", lambda e, dth=dth, d=d: e.tensor_tensor(out=da[:], in0=dth, in1=abc[:, d * 6:(d + 1) * 6], op=ALU.mult),
                 reads=[lk[4], "abc"], writes=["da"])

            def mmc(e, trd=trd):
                e.matmul(psC[:, 0:6], trd, da[:], start=True, stop=True)
                return e.matmul(psC[:, 8:14], onesf[:], da[:], start=True, stop=True)
            S.op("pe", mmc, reads=["da", "tri", "onesf"], writes=["psC"])
            S.op("dve", lambda e: e.tensor_copy(cumc[:, 0:14], psC[:, 0:14]), reads=["psC"], writes=["cumc", "psC"])
            S.op("dve", lambda e: e.tensor_tensor(out=dsc[:], in0=cumc[:, 8:14], in1=cumc[:, 0:6], op=ALU.subtract), reads=["cumc"], writes=["dsc"])
            S.op("act", lambda e: e.activation(out=dsc[:], in_=dsc[:], func=AF.Exp), reads=["dsc"], writes=["dsc"])
            S.op("act", lambda e: e.activation(out=dch[:], in_=cumc[:, 8:14], func=AF.Exp), reads=["cumc"], writes=["dch"])
            xv6 = xt[i][:].rearrange("p (h q) -> p h q", q=64)
            S.op("dve", lambda e, xv6=xv6, dth=dth: e.tensor_tensor(out=xdt[:].rearrange("p (h q) -> p h q", q=64), in0=xv6,
                                                                     in1=dth.unsqueeze(2).to_broadcast([128, 6, 64]), op=ALU.mult),
                 reads=[lk[0], lk[4]], writes=["xdt"])
            if not is_ctx:
                S.op("dve", lambda e: e.tensor_scalar_mul(ncum[:], cumc[:, 0:6], -1.0), reads=["cumc"], writes=["ncum"])
                S.op("act", lambda e: e.activation(out=ecum[:], in_=cumc[:, 0:6], func=AF.Exp), reads=["cumc"], writes=["ecum"])

                def mmcb(e, trd=trd):
                    ins = None
                    for h in range(6):
                        ins = e.matmul(psCB[h // 4][:, (h % 4) * 128:(h % 4 + 1) * 128], da[:, h:h + 1].to_broadcast([128, 128]), trd,
                                       start=True, stop=True)
                    return ins
                S.op("pe", mmcb, reads=["da", "tri"], writes=["psCB0", "psCB1"])
                S.op("pe", lambda e, i=i: e.matmul(psG[:, 0:128], BTc[i][:], CTc[i][:], start=True, stop=True), reads=[lk[2], lk[3]], writes=["psG"])
                S.op("dve", lambda e, trd=trd: e.tensor_tensor(out=GmT[:], in0=psG[:, 0:128], in1=trd, op=ALU.mult), reads=["psG", "tri"], writes=["GmT", "psG"])
                S.op("pe", lambda e, i=i: e.matmul(psYo[:, 0:384], CTc[i][:], Sb[:], start=True, stop=True), reads=[lk[3], "Sb"], writes=["psYo"])
                S.op("act", lambda e: e.activation(out=yo[:], in_=psYo[:, 0:384], func=AF.Copy), reads=["psYo"], writes=["yo", "psYo"])
                for h in range(6):
                    ex = Ex[h % 2]; exk = "Ex%d" % (h % 2)
                    cbk = "psCB%d" % (h // 4)
                    S.op("act", lambda e, h=h, ex=ex: e.activation(out=ex[:], in_=psCB[h // 4][:, (h % 4) * 128:(h % 4 + 1) * 128], func=AF.Exp,
                                                                   bias=ncum[:, h:h + 1]), reads=[cbk, "ncum"], writes=[exk, cbk])
                    S.op("dve", lambda e, h=h, ex=ex: e.scalar_tensor_tensor(out=MT[h][:], in0=ex[:], scalar=1.0, in1=GmT[:], op0=ALU.min, op1=ALU.mult),
                         reads=[exk, "GmT"], writes=["MT%d" % h])
                    S.op("pe", lambda e, h=h: e.matmul(psYd[:, h * 64:(h + 1) * 64], MT[h][:], xdt[:, h * 64:(h + 1) * 64], start=True, stop=True),
                         reads=["MT%d" % h, "xdt"], writes=["psYd"])
                yd = ydir[nyo % 2]; ydk = "ydir%d" % (nyo % 2)
                nyo += 1
                S.op("dve", lambda e: e.tensor_tensor(out=yo[:].rearrange("p (h q) -> p h q", q=64), in0=yo[:].rearrange("p (h q) -> p h q", q=64),
                                                      in1=ecum[:].unsqueeze(2).to_broadcast([128, 6, 64]), op=ALU.mult), reads=["yo", "ecum"], writes=["yo"])
                S.op("dve", lambda e, yd=yd: e.tensor_tensor(out=yd[:], in0=psYd[:, 0:384], in1=yo[:], op=ALU.add), reads=["psYd", "yo"], writes=[ydk, "psYd"])
            S.op("dve", lambda e: e.tensor_tensor(out=xw[:].rearrange("p (h q) -> p h q", q=64), in0=xdt[:].rearrange("p (h q) -> p h q", q=64),
                                                  in1=dsc[:].unsqueeze(2).to_broadcast([128, 6, 64]), op=ALU.mult), reads=["xdt", "dsc"], writes=["xw"])
            S.op("pe", lambda e, i=i: e.matmul(psSt[:, 0:384], Bt[i][:], xw[:], start=True, stop=True), reads=[lk[1], "xw"], writes=["psSt"])
            S.op("dve", lambda e: e.tensor_tensor(out=Sst[:].rearrange("p (h q) -> p h q", q=64), in0=Sst[:].rearrange("p (h q) -> p h q", q=64),
                                                  in1=dch[:].unsqueeze(2).to_broadcast([128, 6, 64]), op=ALU.mult), reads=["Sst", "dch"], writes=["Sst"])
            S.op("dve", lambda e: e.tensor_tensor(out=Sst[:], in0=Sst[:], in1=psSt[:, 0:384], op=ALU.add), reads=["Sst", "psSt"], writes=["Sst", "psSt"])
            S.op("act", lambda e: e.activation(out=Sb[:], in_=Sst[:], func=AF.Copy), reads=["Sst"], writes=["Sb"])
            if is_ctx:
                continue
            l0 = t0 - CTX
            if d == 0:
                S.dma("sp", lambda e, yd=yd, l0=l0: e.dma_start(out=io["Yf"][l0:l0 + 128, :], in_=yd[:]), reads=[ydk], writes=["Yf"])
            else:
                j = nyo % 2
                S.dma("sp", lambda e, j=j, l0=l0: e.dma_start(out=yft[j][:], in_=io["Yf"][l0:l0 + 128, :]), reads=["Yf"], writes=["yft%d" % j])
                S.dma("sp", lambda e, j=j, t0=t0: e.dma_start(out=zt[j][:], in_=zdtd[t0:t0 + 128, 0:384]), reads=["zdtd"], writes=["zt%d" % j])
                S.op("dve", lambda e, yd=yd, j=j: e.tensor_tensor(out=yd[:], in0=yd[:], in1=yft[j][:], op=ALU.add), reads=[ydk, "yft%d" % j], writes=[ydk])
                S.op("dve", lambda e, xv6=xv6: e.tensor_tensor(out=ysq[:].rearrange("p (h q) -> p h q", q=64), in0=xv6,
                                                                in1=dsk[:].unsqueeze(2).to_broadcast([128, 6, 64]), op=ALU.mult), reads=[lk[0], "dsk"], writes=["ysq"])
                S.op("dve", lambda e, yd=yd: e.tensor_tensor(out=yd[:], in0=yd[:], in1=ysq[:], op=ALU.add), reads=[ydk, "ysq"], writes=[ydk])
                S.op("act", lambda e, j=j: e.activation(out=zt[j][:], in_=zt[j][:], func=AF.Silu), reads=["zt%d" % j], writes=["zt%d" % j])
                S.op("dve", lambda e, yd=yd, j=j: e.tensor_tensor(out=yd[:], in0=yd[:], in1=zt[j][:], op=ALU.mult), reads=[ydk, "zt%d" % j], writes=[ydk])
                S.op("dve", lambda e, yd=yd: e.tensor_tensor(out=ysq[:], in0=yd[:], in1=yd[:], op=ALU.mult), reads=[ydk, "ysq"], writes=["ysq"])
                S.op("dve", lambda e: e.reduce_sum(out=rs[:, 0:1], in_=ysq[:], axis=AX.X), reads=["ysq"], writes=["rs"])
                S.op("dve", lambda e: e.tensor_scalar(out=rs[:, 1:2], in0=rs[:, 0:1], scalar1=1.0 / 384, scalar2=RMS_EPS, op0=ALU.mult, op1=ALU.add),
                     reads=["rs"], writes=["rs"])
                S.op("act", lambda e: e.activation(out=rs[:, 2:3], in_=rs[:, 1:2], func=AF.Sqrt), reads=["rs"], writes=["rs"])
                S.op("dve", lambda e: e.reciprocal(out=rs[:, 3:4], in_=rs[:, 2:3]), reads=["rs"], writes=["rs"])
                S.op("dve", lambda e, yd=yd: e.scalar_tensor_tensor(out=ynb[:], in0=yd[:], scalar=rs[:, 3:4], in1=nw[:], op0=ALU.mult, op1=ALU.mult),
                     reads=[ydk, "rs", "nw"], writes=["ynb"])

                def tr3(e):
                    ins = None
                    for k in range(3):
                        ins = e.transpose(psTb[:, k * 128:(k + 1) * 128], ynb[:, k * 128:(k + 1) * 128], identb[:])
                    return ins
                S.op("pe", tr3, reads=["ynb", "identb3"], writes=["psTb"])
                o = oT[j]; ok = "oTs%d" % j
                S.op("act", lambda e, o=o: e.activation(out=o[:], in_=psTb[:, 0:384], func=AF.Copy), reads=["psTb"], writes=[ok, "psTb"])
                S.dma("sp", lambda e, o=o, l0=l0: e.dma_start(out=io["ssdT"].rearrange("(k p) t -> p k t", p=128)[:, :, l0:l0 + 128],
                                                              in_=o[:].rearrange("p (k n) -> p k n", n=128)), reads=[ok])
    st.close()


def ssd_inputs(inp, j):
    ch = np.concatenate([j * 384 + np.arange(384), 3072 + j * 128 + np.arange(128), 4096 + j * 128 + np.arange(128)])
    cwj = inp["m2_conv_w"][0][:, ch]
    cw = np.ascontiguousarray(cwj.reshape(5, 5, 128).transpose(2, 1, 0)).reshape(128, 25)
    cb = np.ascontiguousarray(inp["m2_conv_b"][0][ch].reshape(5, 128).T)
    hs = 6 * j + np.arange(6)
    tri = np.ascontiguousarray(np.concatenate([np.triu(np.ones((128, 128), np.float32)), np.tril(np.ones((128, 128), np.float32))], 1))
    return {"m2cw": cw, "m2cb": cb, "m2alog": np.ascontiguousarray(inp["m2_a_log"][0][:, hs].reshape(12)),
            "m2dtb": np.ascontiguousarray(inp["m2_dt_bias"][0][:, hs].reshape(12)), "m2d": np.ascontiguousarray(inp["m2_d"][0][hs]),
            "m2nw": np.ascontiguousarray(inp["m2_norm_w"][0][j * 384:(j + 1) * 384]), "tri": tri}


def s5_tables(inp, j):
    gs = np.arange(8 * j, 8 * j + 8)
    def st(a):
        a = a.reshape(2, 4, 2, 64)
        return np.ascontiguousarray(a.transpose(2, 3, 0, 1).reshape(128, 8))
    lre = st(inp["s5_lam_re"][0][:, gs]); lim = st(inp["s5_lam_im"][0][:, gs])
    ls = st(np.repeat(inp["s5_log_step"][0][:, gs][:, :, None], 64, 2))
    s5p = np.ascontiguousarray(np.concatenate([lre, lim, ls], 1))
    def bt(a):
        a = a.reshape(4, 2, 64, 16)
        return a.transpose(1, 2, 0, 3).reshape(128, 64)
    s5b = np.ascontiguousarray(np.concatenate([bt(inp["s5_b_re"][0][gs]), bt(inp["s5_b_im"][0][gs])], 1))
    def ct(a):
        a = a.reshape(2, 4, 2, 16, 64)
        return a.transpose(2, 4, 0, 1, 3).reshape(128, 128)
    s5c = np.ascontiguousarray(np.concatenate([ct(inp["s5_c_re"][0][:, gs]), ct(inp["s5_c_im"][0][:, gs])], 1))
    s5d = np.ascontiguousarray(inp["s5_d"][0][j * 128:(j + 1) * 128].reshape(128, 1))
    return s5p, s5b, s5c, s5d


def e_cols(j):
    return np.concatenate([np.arange(j * 128, (j + 1) * 128), 1024 + np.arange(j * 384, (j + 1) * 384),
                           1024 + 3072 + np.arange(j * 128, (j + 1) * 128), 1024 + 4096 + np.arange(j * 128, (j + 1) * 128),
                           6240 + np.arange(j * 384, (j + 1) * 384), 6144 + 6 * j + np.arange(6), 6144 + 48 + 6 * j + np.arange(6)])


def stage_E(inp, aT_all, phases=(1, 2, 3)):
    w = inp["od_w_in"][0]
    ident = np.eye(128, dtype=np.float32)
    tvec = np.ascontiguousarray(np.tile(np.arange(1, 513, dtype=np.float32)[None, :], (128, 1)))
    maps = []
    for j in range(NCORES):
        s5p, s5b, s5c, s5d = s5_tables(inp, j)
        m = {"aT": aT_all, "w": np.ascontiguousarray(w[:, e_cols(j)]), "ident": ident, "tvec": tvec,
             "s5p": s5p, "s5b": s5b, "s5c": s5c, "s5d": s5d}
        m.update(ssd_inputs(inp, j))
        maps.append(m)
    res = run(build_E(phases), maps)
    return res


def stage_F(inp, mod, hT_lat_parts, s5T, ssdT):
    modt = mod_tables(mod[1])
    epsv = np.ascontiguousarray(np.tile(np.array([[RMS_EPS, LN_EPS]], np.float32), (128, 1)))
    ident = np.eye(128, dtype=np.float32)
    maps = []
    for j in range(NCORES):
        sl = slice(j * LAT_PC, (j + 1) * LAT_PC)
        maps.append({"hTin": np.ascontiguousarray(hT_lat_parts[j]), "modT": modt, "nw2T": tab(inp["norm_w"][1, 1]),
                     "w_out": inp["od_w_out"][0], "w1": inp["ffn_w1"][1], "w3": inp["ffn_w3"][1], "w2": inp["ffn_w2"][1],
                     "epsv": epsv, "s5T": np.ascontiguousarray(s5T[:, sl]), "ssdT": np.ascontiguousarray(ssdT[:, sl]),
                     "gluW": inp["s5_glu_w"][0], "glubT": tab(inp["s5_glu_b"][0]), "nwN": tab(inp["final_norm_w"]), "ident": ident})
    res = run(build_T2(1), maps)
    return np.concatenate([r["out"] for r in res], 0)


def kernel(**inp):
    inp = {k: np.asarray(v) for k, v in inp.items()}
    mod = stage_A(inp)
    hT0, aT0 = stage_B(inp, mod)
    aT0_all = gather_T(aT0)
    attnT, convT = stage_C(inp, aT0_all)
    hT1, aT1 = stage_D(inp, mod, hT0, attnT, convT)
    aT1_all = gather_T(aT1)
    resE = stage_E(inp, aT1_all)
    s5T = np.concatenate([r["s5T"] for r in resE], 0)
    ssdT = np.concatenate([r["ssdT"] for r in resE], 0)
    out = stage_F(inp, mod, [h[:, :LAT_PC] for h in hT1], s5T, ssdT)
    return out.reshape(1, SEQ, D).astype(np.float32)
```

```python
import contextlib
import math
import numpy as np
import ml_dtypes
import concourse.bass as bass
import concourse.mybir as mybir
from concourse.bass_utils import run_bass_kernel_spmd

F32 = mybir.dt.float32
BF16 = mybir.dt.bfloat16
I32 = mybir.dt.int32
AF = mybir.ActivationFunctionType
ALU = mybir.AluOpType
AX = mybir.AxisListType

NCORES = 8
D = 4096
DC = 32
SEQ = 8192
CTX = 256
NTOK = SEQ + CTX
LAT_PC = SEQ // NCORES
CTX_PC = CTX // NCORES
FFN_H = 11008
HC = FFN_H // 128
RMS_EPS = 1e-6
LN_EPS = 1e-5

ENGS = ("pe", "act", "dve", "pool", "sp")
EPOCH = 30000
DMA_K = 8


class Sched:
    def __init__(self, nc):
        self.nc = nc
        self.ops = {e: [] for e in ENGS}
        self.cnt = {e: 0 for e in ENGS}
        self.dcnt = {e: 0 for e in ENGS}
        self.last_w = {}
        self.readers = {}
        self.waited = {e: {} for e in ENGS}
        self.dma_tokens = []
        self.bar_deps = []
        self.bar_pending = set()
        self.last_dma = {}

    def barrier(self):
        deps = list(self.last_dma.values())
        for e in ENGS:
            if self.cnt[e] > 0:
                idx = self.cnt[e] - 1
                deps.append((("c", e, idx // EPOCH), idx % EPOCH + 1, e))
        self.bar_deps = deps
        self.bar_pending = set(ENGS)

    def _deps(self, reads, writes):
        deps = []
        for k in reads:
            t = self.last_w.get(k)
            if t is not None:
                deps.append(t)
        for k in writes:
            t = self.last_w.get(k)
            if t is not None:
                deps.append(t)
            deps.extend(self.readers.get(k, ()))
        return deps

    def _commit(self, tok, reads, writes):
        for k in reads:
            if k in writes:
                continue
            self.readers.setdefault(k, []).append(tok)
        for k in writes:
            self.last_w[k] = tok
            self.readers[k] = []

    def _waits(self, eng, deps):
        best = {}
        for (sk, val, src) in deps:
            if best.get(sk, 0) < val:
                best[sk] = val
        out = []
        for sk, val in best.items():
            if sk[0] == "c":
                lin = sk[2] * EPOCH + val
                key = ("clin", sk[1])
                if self.waited[eng].get(key, 0) >= lin:
                    continue
                self.waited[eng][key] = lin
                out.append((sk, val))
            else:
                if self.waited[eng].get(sk, 0) >= val:
                    continue
                self.waited[eng][sk] = val
                out.append((sk, val))
        return out

    def op(self, eng, fn, reads=(), writes=()):
        reads = tuple(reads)
        writes = tuple(writes)
        deps = self._deps(reads, writes)
        if eng in self.bar_pending:
            deps = deps + self.bar_deps
            self.bar_pending.discard(eng)
        waits = self._waits(eng, deps)
        idx = self.cnt[eng]
        self.cnt[eng] += 1
        sk = ("c", eng, idx // EPOCH)
        tok = (sk, idx % EPOCH + 1, eng)
        self.ops[eng].append((fn, waits, (sk, 1)))
        self._commit(tok, reads, writes)
        return tok

    def dma(self, eng, fn, reads=(), writes=()):
        reads = tuple(reads)
        writes = tuple(writes)
        deps = self._deps(reads, writes)
        n = self.dcnt[eng]
        self.dcnt[eng] += 1
        slot = n % DMA_K
        rnd = n // DMA_K
        sk = ("d", eng, slot)
        if rnd > 0:
            deps.append((sk, 16 * rnd, eng))
        if eng in self.bar_pending:
            deps = deps + self.bar_deps
            self.bar_pending.discard(eng)
        waits = self._waits(eng, deps)
        tok = (sk, 16 * (rnd + 1), eng)
        self.last_dma[sk] = tok
        self.ops[eng].append((fn, waits, (sk, 16)))
        self._commit(tok, reads, writes)
        self.dma_tokens.append(tok)
        return tok

    def emit(self):
        nc = self.nc
        final_deps = list(self.dma_tokens)
        for e in ENGS:
            if self.cnt[e] > 0:
                idx = self.cnt[e] - 1
                final_deps.append((("c", e, idx // EPOCH), idx % EPOCH + 1, e))
        fw = self._waits("sp", final_deps)
        self.ops["sp"].append((None, fw, None))
        semkeys = set()
        for e in ENGS:
            for (fn, waits, inc) in self.ops[e]:
                for (sk, v) in waits:
                    semkeys.add(sk)
                if inc is not None:
                    semkeys.add(inc[0])
        semkeys = sorted(semkeys)
        with contextlib.ExitStack() as st:
            sems = {}
            for sk in semkeys:
                sems[sk] = st.enter_context(nc.semaphore("s_%s_%s_%d" % sk))
            block = st.enter_context(nc.Block())
            handles = {"pe": block.tensor, "act": block.scalar, "dve": block.vector,
                       "pool": block.gpsimd, "sp": block.sync}
            for e in ENGS:
                ops = self.ops[e]
                if not ops:
                    continue

                def body(eng, ops=ops):
                    for (fn, waits, inc) in ops:
                        for (sk, v) in waits:
                            eng.wait_ge(sems[sk], v)
                        if fn is not None:
                            ins = fn(eng)
                            ins.then_inc(sems[inc[0]], inc[1])
                handles[e](body)


class B:
    def __init__(self):
        self.nc = bass.Bass("TRN2", target_bir_lowering=False)
        self.S = Sched(self.nc)
        self.st = contextlib.ExitStack()
        self.nps = 0

    def din(self, name, shape, dt=F32):
        return self.nc.dram_tensor(name, list(shape), dt, kind="ExternalInput").ap()

    def dout(self, name, shape, dt=F32):
        return self.nc.dram_tensor(name, list(shape), dt, kind="ExternalOutput").ap()

    def dint(self, name, shape, dt=F32):
        return self.nc.dram_tensor(name, list(shape), dt, kind="Internal").ap()

    def sb(self, name, shape, dt=F32):
        return self.st.enter_context(self.nc.sbuf_tensor(name, list(shape), dt))

    def ps(self, name, shape=(128, 512), dt=F32):
        self.nps += 1
        return self.st.enter_context(self.nc.psum_tensor(name, list(shape), dt))

    def finish(self):
        self.S.emit()
        self.st.close()
        return self.nc


TRACE = False
DEBUG = False
DBG = {}
LAST_NS = []


RUN_CORES = NCORES


def run(nc, in_maps):
    if RUN_CORES < NCORES:
        res = run_bass_kernel_spmd(nc, in_maps[:RUN_CORES], core_ids=list(range(RUN_CORES)), trace=TRACE)
        if TRACE:
            print("exec_time_ns", res.exec_time_ns, flush=True)
        return list(res.results) + [res.results[0]] * (NCORES - RUN_CORES)
    if TRACE:
        res = run_bass_kernel_spmd(nc, in_maps, core_ids=list(range(NCORES)), trace=True)
        LAST_NS.append(res.exec_time_ns)
        print("exec_time_ns", res.exec_time_ns, flush=True)
    else:
        res = run_bass_kernel_spmd(nc, in_maps, core_ids=list(range(NCORES)))
    return res.results


def tab(v):
    v = np.asarray(v, np.float32).reshape(-1, 128)
    return np.ascontiguousarray(v.T)


A_COLS = 6 * D // NCORES


def build_A():
    b = B()
    S = b.S
    cT = b.din("cT", [128, DC * 2])
    W = b.din("W", [2, D, A_COLS])
    bias = b.din("bias", [2, 2 * A_COLS])
    out = b.dout("out", [2, 2 * A_COLS])
    ct = b.sb("ct", [128, DC * 2])
    s = b.sb("s", [128, DC * 2])
    bt = b.sb("bt", [2, 2 * A_COLS])
    ot = b.sb("ot", [2, 2 * A_COLS])
    wts = [b.sb("wt%d" % i, [128, DC * 512]) for i in range(2)]
    pss = [b.ps("ps%d" % i) for i in range(2)]
    S.dma("sp", lambda e: e.dma_start(out=ct[:], in_=cT), writes=["ct"])
    S.dma("sp", lambda e: e.dma_start(out=bt[:], in_=bias), writes=["bt"])
    S.op("act", lambda e: e.activation(out=s[:], in_=ct[:], func=AF.Silu), reads=["ct"], writes=["s"])
    items = [(i, n) for i in range(2) for n in range(A_COLS // 512)]
    for it, (i, n) in enumerate(items):
        wt = wts[it % 2]
        ps = pss[it % 2]
        wk = "wt%d" % (it % 2)
        pk = "ps%d" % (it % 2)
        src = W[i].rearrange("(kc p) n -> p kc n", p=128)[:, :, n * 512:(n + 1) * 512]
        S.dma("sp", lambda e, wt=wt, src=src: e.dma_start(
            out=wt[:].rearrange("p (kc n) -> p kc n", n=512), in_=src), writes=[wk])

        def mm(e, wt=wt, ps=ps):
            ins = None
            for kc in range(DC):
                ins = e.matmul(ps[0:2, :], s[:, kc * 2:(kc + 1) * 2], wt[:, kc * 512:(kc + 1) * 512],
                               start=(kc == 0), stop=(kc == DC - 1))
            return ins
        S.op("pe", mm, reads=[wk, "s"], writes=[pk])
        c0 = i * A_COLS + n * 512
        S.op("dve", lambda e, ps=ps, c0=c0: e.tensor_tensor(out=ot[:, c0:c0 + 512], in0=ps[0:2, :],
                                                            in1=bt[:, c0:c0 + 512], op=ALU.add),
             reads=[pk, "bt"], writes=["ot"])
    S.dma("sp", lambda e: e.dma_start(out=out, in_=ot[:]), reads=["ot"])
    return b.finish()


def stage_A(inp):
    c2 = np.concatenate([inp["c"].reshape(1, D), inp["c_ctx"].reshape(1, D)], 0)
    cT = np.ascontiguousarray(c2.reshape(2, DC, 128).transpose(2, 1, 0)).reshape(128, DC * 2)
    maps = []
    for j in range(NCORES):
        sl = slice(j * A_COLS, (j + 1) * A_COLS)
        Wj = np.ascontiguousarray(inp["ada_w"][:, :, sl])
        bj = np.ascontiguousarray(inp["ada_b"][:, sl]).reshape(1, 2 * A_COLS)
        maps.append({"cT": cT, "W": Wj, "bias": np.ascontiguousarray(np.repeat(bj, 2, 0))})
    res = run(build_A(), maps)
    mod = np.zeros((2, 2, 6 * D), np.float32)
    for j in range(NCORES):
        o = res[j]["out"].reshape(2, 2, A_COLS)
        for i in range(2):
            mod[i, :, j * A_COLS:(j + 1) * A_COLS] = o[:, i, :]
    return mod


def mod_tables(mod_i):
    return np.ascontiguousarray(np.concatenate([tab(mod_i[r, v * D:(v + 1) * D]) for r in range(2) for v in range(6)], 1))


def mcol(modt, r, v, c0=0, c1=DC):
    base = (r * 6 + v) * DC
    return modt[:, base + c0:base + c1]


def emit_weff(b, modt, nwt, mk, nk, v_scale, name):
    S = b.S
    weff = b.sb(name, [128, 2 * DC])
    for r in range(2):
        S.op("dve", lambda e, r=r: e.scalar_tensor_tensor(
            out=weff[:, r * DC:(r + 1) * DC], in0=mcol(modt, r, v_scale), scalar=1.0, in1=nwt[:],
            op0=ALU.add, op1=ALU.mult), reads=[mk, nk], writes=[name])
    return weff


def emit_rstd(b, src, srck, C, N, ones, sq, sqk, ps, psk, rstd, rstdk, eps, ncols_feat):
    S = b.S
    S.op("act", lambda e: e.activation(out=sq[:, 0:C * N], in_=src, func=AF.Square) if False else
         e.activation(out=sq[:, 0:C * N].rearrange("p (c n) -> p c n", n=N), in_=src, func=AF.Square),
         reads=[srck], writes=[sqk])

    def mm(e):
        ins = None
        for c in range(C):
            ins = e.matmul(ps[:, 0:N], ones[:], sq[:, c * N:(c + 1) * N], start=(c == 0), stop=(c == C - 1))
        return ins
    S.op("pe", mm, reads=[sqk], writes=[psk])
    S.op("act", lambda e: e.activation(out=rstd[:, 0:N], in_=ps[:, 0:N], func=AF.Sqrt, scale=1.0 / ncols_feat, bias=eps),
         reads=[psk], writes=[rstdk])
    S.op("dve", lambda e: e.reciprocal(out=rstd[:, 0:N], in_=rstd[:, 0:N]), reads=[rstdk], writes=[rstdk])


TOK_PC = LAT_PC + CTX_PC


def build_B():
    b = B()
    S = b.S
    xt = b.din("xt", [TOK_PC, D])
    modT = b.din("modT", [128, 12 * DC])
    nwT = b.din("nwT", [128, DC])
    identD = b.din("ident", [128, 128])
    epsD = b.din("epsv", [128, 1])
    hT = b.dout("hT", [D, TOK_PC])
    aT = b.dout("aT", [D, TOK_PC], BF16)
    modt = b.sb("modt", [128, 12 * DC])
    nwt = b.sb("nwt", [128, DC])
    ident = b.sb("identt", [128, 128])
    epst = b.sb("epst", [128, 1])
    ones = b.sb("ones", [128, 128], BF16)
    S.dma("sp", lambda e: e.dma_start(out=modt[:], in_=modT), writes=["modt"])
    S.dma("sp", lambda e: e.dma_start(out=nwt[:], in_=nwT), writes=["nwt"])
    S.dma("sp", lambda e: e.dma_start(out=ident[:], in_=identD), writes=["ident"])
    S.dma("sp", lambda e: e.dma_start(out=epst[:], in_=epsD), writes=["epst"])
    S.op("dve", lambda e: e.memset(ones[:], 1.0), writes=["ones"])
    weff = emit_weff(b, modt, nwt, "modt", "nwt", 1, "weff")
    xin = [b.sb("xin%d" % i, [128, D]) for i in range(2)]
    hTt = [b.sb("hTt%d" % i, [128, DC * 128]) for i in range(2)]
    xn = b.sb("xn", [128, DC * 128])
    aTt = [b.sb("aTt%d" % i, [128, DC * 128], BF16) for i in range(2)]
    sq = b.sb("sq", [128, DC * 128], BF16)
    rstd = b.sb("rstd", [128, 128])
    pst = [b.ps("pst%d" % i) for i in range(4)]
    pss = b.ps("pss")
    tiles = [(i * 128, 128, 0) for i in range(8)] + [(LAT_PC, CTX_PC, 1)]
    for ti, (t0, T, r) in enumerate(tiles):
        xi = xin[ti % 2]; xk = "xin%d" % (ti % 2)
        ht = hTt[ti % 2]; hk = "hTt%d" % (ti % 2)
        at = aTt[ti % 2]; ak = "aTt%d" % (ti % 2)
        S.dma("sp", lambda e, xi=xi, t0=t0, T=T: e.dma_start(out=xi[0:T, :], in_=xt[t0:t0 + T, :]), writes=[xk])
        for cg in range(8):
            ps = pst[cg % 4]; pk = "pst%d" % (cg % 4)

            def tr(e, ps=ps, xi=xi, cg=cg, T=T):
                ins = None
                for k in range(4):
                    c = cg * 4 + k
                    ins = e.transpose(ps[:, k * T:(k + 1) * T], xi[0:T, c * 128:(c + 1) * 128], ident[0:T, 0:T])
                return ins
            S.op("pe", tr, reads=[xk, "ident"], writes=[pk])
            eng = "dve" if cg % 2 == 0 else "act"
            if eng == "dve":
                S.op("dve", lambda e, ps=ps, ht=ht, cg=cg, T=T: e.tensor_copy(ht[:, cg * 4 * T:(cg + 1) * 4 * T], ps[:, 0:4 * T]),
                     reads=[pk], writes=[hk + "_%d" % cg])
            else:
                S.op("act", lambda e, ps=ps, ht=ht, cg=cg, T=T: e.activation(out=ht[:, cg * 4 * T:(cg + 1) * 4 * T], in_=ps[:, 0:4 * T], func=AF.Copy),
                     reads=[pk], writes=[hk + "_%d" % cg])
        hks = [hk + "_%d" % cg for cg in range(8)]
        hv = ht[:, 0:DC * T].rearrange("p (c n) -> p c n", n=T)
        S.dma("sp", lambda e, hv=hv, t0=t0, T=T: e.dma_start(
            out=hT.rearrange("(c p) t -> p c t", p=128)[:, :, t0:t0 + T], in_=hv), reads=hks)
        S.op("act", lambda e, hv=hv, T=T: e.activation(out=sq[:, 0:DC * T].rearrange("p (c n) -> p c n", n=T), in_=hv, func=AF.Square),
             reads=hks, writes=["sq"])

        def mm(e, T=T):
            ins = None
            for c in range(DC):
                ins = e.matmul(pss[:, 0:T], ones[:], sq[:, c * T:(c + 1) * T], start=(c == 0), stop=(c == DC - 1))
            return ins
        S.op("pe", mm, reads=["sq", "ones"], writes=["pss"])
        S.op("act", lambda e, T=T: e.activation(out=rstd[:, 0:T], in_=pss[:, 0:T], func=AF.Sqrt, scale=1.0 / D, bias=epst[:, 0:1]),
             reads=["pss", "epst"], writes=["rstd"])
        S.op("dve", lambda e, T=T: e.reciprocal(out=rstd[:, 0:T], in_=rstd[:, 0:T]), reads=["rstd"], writes=["rstd"])
        xnv = xn[:, 0:DC * T].rearrange("p (c n) -> p c n", n=T)
        S.op("dve", lambda e, hv=hv, xnv=xnv, T=T: e.tensor_tensor(
            out=xnv, in0=hv, in1=rstd[:, 0:T].unsqueeze(1).to_broadcast([128, DC, T]), op=ALU.mult),
            reads=hks + ["rstd"], writes=["xn"])
        S.op("dve", lambda e, xnv=xnv, T=T, r=r: e.tensor_tensor(
            out=xnv, in0=xnv, in1=weff[:, r * DC:(r + 1) * DC].unsqueeze(2).to_broadcast([128, DC, T]), op=ALU.mult),
            reads=["xn", "weff"], writes=["xn"])
        atv = at[:, 0:DC * T].rearrange("p (c n) -> p c n", n=T)
        S.op("dve", lambda e, xnv=xnv, atv=atv, T=T, r=r: e.tensor_tensor(
            out=atv, in0=xnv, in1=mcol(modt, r, 0).unsqueeze(2).to_broadcast([128, DC, T]), op=ALU.add),
            reads=["xn", "modt"], writes=[ak])
        S.dma("sp", lambda e, atv=atv, t0=t0, T=T: e.dma_start(
            out=aT.rearrange("(c p) t -> p c t", p=128)[:, :, t0:t0 + T], in_=atv), reads=[ak])
    return b.finish()


def tok_shard(lat, ctx, j):
    return np.ascontiguousarray(np.concatenate([lat[j * LAT_PC:(j + 1) * LAT_PC], ctx[j * CTX_PC:(j + 1) * CTX_PC]], 0))


def gather_T(parts):
    return np.ascontiguousarray(np.concatenate([p[:, LAT_PC:] for p in parts] + [p[:, :LAT_PC] for p in parts], 1))


def stage_B(inp, mod):
    modt = mod_tables(mod[0])
    nwt = tab(inp["norm_w"][0, 0])
    ident = np.eye(128, dtype=np.float32)
    epsv = np.full((128, 1), RMS_EPS, np.float32)
    x = inp["x"].reshape(SEQ, D)
    ctx = inp["ctx"].reshape(CTX, D)
    maps = [{"xt": tok_shard(x, ctx, j), "modT": modt, "nwT": nwt, "ident": ident, "epsv": epsv} for j in range(NCORES)]
    res = run(build_B(), maps)
    hT = [r["hT"] for r in res]
    aT = [r["aT"] for r in res]
    return hT, aT


EV_A = 2048
CW = 31
CPAD = 15
U_CTX0 = CPAD
U_LAT0 = CPAD + CTX + CPAD
U_LEN = U_LAT0 + SEQ + CPAD
NKB = NTOK // 128
LAM_INIT0 = 0.8 - 0.6 * math.exp(-0.3 * 0)


def supertiles():
    return [(0, CTX)] + [(CTX + 512 * i, 512) for i in range(SEQ // 512)]


def build_C(phases=(1, 2, 3)):
    b = B()
    S = b.S
    aTd = b.din("aT", [D, NTOK], BF16)
    wd = b.din("w", [D, 1280])
    ropeD = b.din("rope", [NTOK, 512])
    lamD = b.din("lam", [256])
    swD = b.din("subw", [128])
    cwD = b.din("cw", [128, 2 * CW])
    cbD = b.din("cb", [128, 2])
    identD = b.din("ident", [128, 128])
    attnT = b.dout("attnT", [256, NTOK], BF16)
    convT = b.dout("convT", [256, NTOK])
    KTd = b.dint("KTd", [2, 128, NTOK], BF16)
    QTd = b.dint("QTd", [2, 128, NTOK], BF16)
    Vd = b.dint("Vd", [NTOK, 256], BF16)
    UTd = b.dint("UTd", [2, 128, NTOK])

    ident = b.sb("identt", [128, 128])
    identb = b.sb("identb", [128, 128], BF16)
    S.dma("sp", lambda e: e.dma_start(out=ident[:], in_=identD), writes=["ident"])
    S.op("dve", lambda e: e.tensor_copy(identb[:], ident[:]), reads=["ident"], writes=["identb"])

    st1 = contextlib.ExitStack()
    Wt = st1.enter_context(b.nc.sbuf_tensor("Wt", [128, DC * 1280], BF16))
    Wv = Wt[:].rearrange("p (kc n) -> p kc n", n=1280)
    for g in range(8):
        S.dma("pool", lambda e, g=g: e.dma_start(
            out=Wv[:, g * 4:(g + 1) * 4, :], in_=wd.rearrange("(kc p) n -> p kc n", p=128)[:, g * 4:(g + 1) * 4, :]),
            writes=["W%d" % g])
    Wk = ["W%d" % g for g in range(8)]
    aTs = [st1.enter_context(b.nc.sbuf_tensor("aTs%d" % i, [128, DC * 512], BF16)) for i in range(2)]
    cs = [st1.enter_context(b.nc.sbuf_tensor("cs%d" % i, [128, 512], F32)) for i in range(2)]
    t1 = st1.enter_context(b.nc.sbuf_tensor("t1", [128, 256], F32))
    t2 = st1.enter_context(b.nc.sbuf_tensor("t2", [128, 256], F32))
    rot = [st1.enter_context(b.nc.sbuf_tensor("rot%d" % i, [128, 512], BF16)) for i in range(2)]
    vt = [st1.enter_context(b.nc.sbuf_tensor("vt%d" % i, [128, 256], BF16)) for i in range(2)]
    kqT = [st1.enter_context(b.nc.sbuf_tensor("kqT%d" % i, [128, 4 * 512], BF16)) for i in range(2)]
    sig = st1.enter_context(b.nc.sbuf_tensor("sig", [128, 512], F32))
    ut = [st1.enter_context(b.nc.sbuf_tensor("ut%d" % i, [128, 512], F32)) for i in range(2)]
    ps_kv = [st1.enter_context(b.nc.psum_tensor("ps_kv%d" % i, [128, 512], F32)) for i in range(2)]
    ps_q = [st1.enter_context(b.nc.psum_tensor("ps_q%d" % i, [128, 512], F32)) for i in range(2)]
    ps_T = st1.enter_context(b.nc.psum_tensor("ps_T", [128, 1024], BF16))
    ps_c = [st1.enter_context(b.nc.psum_tensor("ps_c%d" % i, [128, 512], F32)) for i in range(2)]
    sts = supertiles()
    aTv = aTd.rearrange("(c p) t -> p c t", p=128)

    def load_aT(si):
        t0, N = sts[si]
        buf = aTs[si % 2]
        S.dma("sp", lambda e: e.dma_start(out=buf[:, 0:DC * N].rearrange("p (c n) -> p c n", n=N), in_=aTv[:, :, t0:t0 + N]),
              writes=["aTs%d" % (si % 2)])
    if 1 in phases:
        load_aT(0)
    nsub_total = 0
    for si, (t0, N) in enumerate(sts if 1 in phases else []):
        if si + 1 < len(sts):
            load_aT(si + 1)
        a = aTs[si % 2][:, 0:DC * N].rearrange("p (c n) -> p c n", n=N)
        ak = "aTs%d" % (si % 2)
        kq = kqT[si % 2]; kqk = "kqT%d" % (si % 2)
        kqv = kq[:, 0:4 * N].rearrange("p (a n) -> p a n", n=N)
        for sub in range(N // 128):
            i2 = nsub_total % 2
            nsub_total += 1
            tt0 = t0 + sub * 128
            pk = ps_kv[i2]; pq = ps_q[i2]
            pkk = "ps_kv%d" % i2; pqk = "ps_q%d" % i2
            S.dma("sp", lambda e, i2=i2, tt0=tt0: e.dma_start(out=cs[i2][:], in_=ropeD[tt0:tt0 + 128, :]), writes=["cs%d" % i2])

            def mmkv(e, a=a, sub=sub, pk=pk):
                ins = None
                for kc in range(DC):
                    ins = e.matmul(pk[:, 0:512], a[:, kc, sub * 128:(sub + 1) * 128], Wv[:, kc, 0:512], start=(kc == 0), stop=(kc == DC - 1))
                return ins
            S.op("pe", mmkv, reads=[ak] + Wk, writes=[pkk])

            def mmq(e, a=a, sub=sub, pq=pq):
                ins = None
                for kc in range(DC):
                    ins = e.matmul(pq[:, 0:256], a[:, kc, sub * 128:(sub + 1) * 128], Wv[:, kc, 512:768], start=(kc == 0), stop=(kc == DC - 1))
                return ins
            S.op("pe", mmq, reads=[ak] + Wk, writes=[pqk])
            csk = "cs%d" % i2
            rt = rot[i2]; rk = "rot%d" % i2
            for which, (src, srck, off) in enumerate(((pk[:, 0:256], pkk, 0), (pq[:, 0:256], pqk, 256))):
                swp = src.rearrange("p (g s i) -> p g s i", s=2, i=16)[:, :, ::-1, :]
                S.op("dve", lambda e, src=src, i2=i2: e.tensor_tensor(out=t1[:], in0=src, in1=cs[i2][:, 0:256], op=ALU.mult),
                     reads=[srck, csk], writes=["t1"])
                S.op("dve", lambda e, swp=swp, i2=i2: e.tensor_tensor(
                    out=t2[:].rearrange("p (g s i) -> p g s i", s=2, i=16), in0=swp,
                    in1=cs[i2][:, 256:512].rearrange("p (g s i) -> p g s i", s=2, i=16), op=ALU.mult),
                    reads=[srck, csk], writes=["t2"])
                S.op("dve", lambda e, rt=rt, off=off: e.tensor_tensor(out=rt[:, off:off + 256], in0=t1[:], in1=t2[:], op=ALU.add),
                     reads=["t1", "t2"], writes=[rk + "_%d" % which])
            S.op("act", lambda e, i2=i2, pk=pk: e.activation(out=vt[i2][:], in_=pk[:, 256:512], func=AF.Copy),
                 reads=[pkk], writes=["vt%d" % i2, pkk])
            S.dma("sp", lambda e, i2=i2, tt0=tt0: e.dma_start(out=Vd[tt0:tt0 + 128, :], in_=vt[i2][:]), reads=["vt%d" % i2], writes=["Vd"])

            def trs(e, rt=rt):
                ins = None
                for blk in range(4):
                    ins = e.transpose(ps_T[:, blk * 128:(blk + 1) * 128], rt[:, blk * 128:(blk + 1) * 128], identb[:])
                return ins
            S.op("pe", trs, reads=[rk + "_0", rk + "_1", "identb"], writes=["ps_T"])
            S.op("act", lambda e, kqv=kqv, sub=sub: e.activation(
                out=kqv[:, :, sub * 128:(sub + 1) * 128], in_=ps_T[:, 0:512].rearrange("p (a n) -> p a n", n=128), func=AF.Copy),
                reads=["ps_T"], writes=[kqk])
        S.dma("sp", lambda e, kqv=kqv, t0=t0, N=N: e.dma_start(out=KTd[:, :, t0:t0 + N].rearrange("a p n -> p a n"), in_=kqv[:, 0:2, :]),
              reads=[kqk], writes=["KTd"])
        S.dma("sp", lambda e, kqv=kqv, t0=t0, N=N: e.dma_start(out=QTd[:, :, t0:t0 + N].rearrange("a p n -> p a n"), in_=kqv[:, 2:4, :]),
              reads=[kqk], writes=["QTd"])
        for blk in range(2):
            for w2 in range(2):
                col = 768 + w2 * 256 + blk * 128

                def mmc(e, a=a, col=col, w2=w2, N=N):
                    ins = None
                    for kc in range(DC):
                        ins = e.matmul(ps_c[w2][:, 0:N], Wv[:, kc, col:col + 128], a[:, kc, :], start=(kc == 0), stop=(kc == DC - 1))
                    return ins
                S.op("pe", mmc, reads=[ak] + Wk, writes=["ps_c%d" % w2])
            S.op("act", lambda e, N=N: e.activation(out=sig[:, 0:N], in_=ps_c[1][:, 0:N], func=AF.Sigmoid), reads=["ps_c1"], writes=["sig"])
            S.op("dve", lambda e, N=N, blk=blk: e.tensor_tensor(out=ut[blk][:, 0:N], in0=ps_c[0][:, 0:N], in1=sig[:, 0:N], op=ALU.mult),
                 reads=["ps_c0", "sig"], writes=["ut%d" % blk])
            S.dma("sp", lambda e, N=N, blk=blk, t0=t0: e.dma_start(out=UTd[blk, :, t0:t0 + N], in_=ut[blk][:, 0:N]),
                  reads=["ut%d" % blk], writes=["UTd"])
    st1.close()
    S.barrier()

    st2 = contextlib.ExitStack()
    uT = st2.enter_context(b.nc.sbuf_tensor("uT", [128, 2 * U_LEN], F32))
    cwt = st2.enter_context(b.nc.sbuf_tensor("cwt", [128, 2 * CW], F32))
    cbt = st2.enter_context(b.nc.sbuf_tensor("cbt", [128, 2], F32))
    acc = [st2.enter_context(b.nc.sbuf_tensor("acc%d" % i, [128, 512], F32)) for i in range(2)]
    uv = uT[:].rearrange("p (b n) -> p b n", n=U_LEN)
    S.dma("sp", lambda e: e.dma_start(out=cwt[:], in_=cwD), writes=["cwt"])
    S.dma("sp", lambda e: e.dma_start(out=cbt[:], in_=cbD), writes=["cbt"])
    S.op("dve", lambda e: e.memset(uT[:], 0.0), writes=["uT"])
    S.dma("sp", lambda e: e.dma_start(out=uv[:, :, U_CTX0:U_CTX0 + CTX], in_=UTd[:, :, 0:CTX].rearrange("b p n -> p b n")),
          reads=["UTd"], writes=["uT"])
    S.dma("sp", lambda e: e.dma_start(out=uv[:, :, U_LAT0:U_LAT0 + SEQ], in_=UTd[:, :, CTX:NTOK].rearrange("b p n -> p b n")),
          reads=["UTd"], writes=["uT"])
    it = 0
    for (ubase, slen, tok0) in (((U_CTX0, CTX, 0), (U_LAT0, SEQ, CTX)) if 2 in phases else ()):
        for c0 in range(0, slen, 512):
            n = min(512, slen - c0)
            for blk in range(2):
                ac = acc[it % 2]; ack = "acc%d" % (it % 2)
                it += 1
                u0 = ubase + c0 - CPAD

                def conv(e, ac=ac, blk=blk, u0=u0, n=n):
                    ins = e.tensor_scalar(out=ac[:, 0:n], in0=uv[:, blk, u0:u0 + n], scalar1=cwt[:, blk * CW:blk * CW + 1],
                                          scalar2=cbt[:, blk:blk + 1], op0=ALU.mult, op1=ALU.add)
                    for k in range(1, CW):
                        ins = e.scalar_tensor_tensor(out=ac[:, 0:n], in0=uv[:, blk, u0 + k:u0 + k + n],
                                                     scalar=cwt[:, blk * CW + k:blk * CW + k + 1], in1=ac[:, 0:n],
                                                     op0=ALU.mult, op1=ALU.add)
                    return ins
                S.op("dve", conv, reads=["uT", "cwt", "cbt"], writes=[ack])
                S.dma("sp", lambda e, ac=ac, blk=blk, tok0=tok0, c0=c0, n=n: e.dma_start(
                    out=convT[blk * 128:(blk + 1) * 128, tok0 + c0:tok0 + c0 + n], in_=ac[:, 0:n]), reads=[ack])
    st2.close()
    S.barrier()

    st3 = contextlib.ExitStack()

    def T3(name, shape, dt=F32):
        return st3.enter_context(b.nc.sbuf_tensor(name + "_p3", list(shape), dt))
    KT = T3("KT", [128, 4 * NTOK], BF16)
    KTv = KT[:].rearrange("p (m a n) -> p m a n", m=2, n=NTOK)
    Vt = T3("Vt", [128, NKB * 256], BF16)
    Vtv = Vt[:].rearrange("p (k d) -> p k d", d=256)
    QTs = [T3("QTs%d" % i, [128, 2 * 512], BF16) for i in range(2)]
    PT = [T3("PT%d" % i, [128, 512], BF16) for i in range(6)]
    pacc = [T3("pacc%d" % i, [128, 512]) for i in range(4)]
    lamt = T3("lamt", [128, 256]); lp = T3("lp", [128, 128]); lsum = T3("lsum", [128, 4]); nlam = T3("nlam", [128, 1])
    swc = T3("swc", [128, 1]); eps5 = T3("eps5", [128, 1]); onesf = T3("onesf", [128, 128])
    r1 = T3("r1", [128, 512]); r2 = T3("r2", [128, 512]); o1 = T3("o1", [128, 512]); o2 = T3("o2", [128, 512])
    obf = [T3("obf%d" % i, [128, 512], BF16) for i in range(2)]
    psS = [st3.enter_context(b.nc.psum_tensor("psS%d" % i, [128, 512], F32)) for i in range(4)]
    psO = [st3.enter_context(b.nc.psum_tensor("psO%d" % i, [128, 512], F32)) for i in range(2)]
    psD = [st3.enter_context(b.nc.psum_tensor("psD%d" % i, [128, 512], F32)) for i in range(2)]
    S.op("dve", lambda e: e.memset(KT[:], 0.0), writes=["KT"])
    for mi_ in range(2):
        S.dma("sp", lambda e, mi_=mi_: e.dma_start(out=KTv[mi_ * 64:(mi_ + 1) * 64, mi_, :, :],
                                                     in_=KTd[:, mi_ * 64:(mi_ + 1) * 64, :].rearrange("a p n -> p a n")),
              reads=["KTd"], writes=["KT"])
    S.dma("sp", lambda e: e.dma_start(out=Vtv, in_=Vd.rearrange("(k p) d -> p k d", p=128)), reads=["Vd"], writes=["Vt"])
    S.dma("sp", lambda e: e.dma_start(out=lamt[:], in_=lamD.partition_broadcast(128)), writes=["lamt"])
    S.dma("sp", lambda e: e.dma_start(out=swc[:], in_=swD.rearrange("(p o) -> p o", o=1)), writes=["swc"])
    S.op("dve", lambda e: e.memset(eps5[:], 1e-5), writes=["eps5"])
    S.op("dve", lambda e: e.memset(onesf[:], 1.0), writes=["onesf"])
    S.op("dve", lambda e: e.tensor_tensor(out=lp[:].rearrange("p (a n) -> p a n", n=64),
                                          in0=lamt[:].rearrange("p (a b n) -> p a b n", b=2, n=64)[:, :, 0, :],
                                          in1=lamt[:].rearrange("p (a b n) -> p a b n", b=2, n=64)[:, :, 1, :], op=ALU.mult),
         reads=["lamt"], writes=["lp"])
    S.op("dve", lambda e: e.reduce_sum(out=lsum[:, 0:2], in_=lp[:].rearrange("p (a n) -> p a n", n=64), axis=AX.X), reads=["lp"], writes=["lsum"])
    S.op("act", lambda e: e.activation(out=lsum[:, 2:4], in_=lsum[:, 0:2], func=AF.Exp), reads=["lsum"], writes=["lsum"])
    S.op("dve", lambda e: e.tensor_tensor(out=nlam[:], in0=lsum[:, 3:4], in1=lsum[:, 2:3], op=ALU.subtract), reads=["lsum"], writes=["nlam"])
    S.op("dve", lambda e: e.tensor_scalar_add(nlam[:], nlam[:], -LAM_INIT0), reads=["nlam"], writes=["nlam"])
    S.op("dve", lambda e: e.tensor_scalar_mul(swc[:], swc[:], 1.0 - LAM_INIT0), reads=["swc"], writes=["swc"])

    qsts = [(0, CTX, 0, 2)] + [(CTX + 512 * i, 512, 0, NKB) for i in range(SEQ // 512)]
    QTdv = QTd.rearrange("a p n -> p a n")

    def load_q(qi):
        q0, NQ, _, _ = qsts[qi]
        S.dma("sp", lambda e: e.dma_start(out=QTs[qi % 2][:, 0:2 * NQ].rearrange("p (a n) -> p a n", n=NQ), in_=QTdv[:, :, q0:q0 + NQ]),
              reads=["QTd"], writes=["QTs%d" % (qi % 2)])
    if 3 not in phases:
        qsts = []
    else:
        load_q(0)
    nS = [0, 0]
    nH = 0
    for qi, (q0, NQ, kb0, kb1) in enumerate(qsts):
        if qi + 1 < len(qsts):
            load_q(qi + 1)
        Q = QTs[qi % 2][:, 0:2 * NQ].rearrange("p (a n) -> p a n", n=NQ)
        Qk = "QTs%d" % (qi % 2)
        for hh in range(2):
            par = 0
            nH += 1

            def emit_S(mi, kb, n):
                p0 = mi * 64
                sS = psS[mi * 2 + n % 2]; sSk = "psS%d" % (mi * 2 + n % 2)
                S.op("pe", lambda e, sS=sS, mi=mi, kb=kb, hh=hh, NQ=NQ, Q=Q: e.matmul(
                    sS[:, 0:NQ], KTv[:, mi, hh, kb * 128:(kb + 1) * 128], Q[:, hh, :], start=True, stop=True),
                    reads=["KT", Qk], writes=[sSk])
            for mi in range(2):
                emit_S(mi, kb0, nS[mi])
            for kb in range(kb0, kb1):
                for mi in range(2):
                    oi = mi
                    pO = psO[oi]; pOk = "psO%d" % oi
                    pa = pacc[oi]; pak = "pacc%d" % oi
                    n = nS[mi]
                    sS = psS[mi * 2 + n % 2]; sSk = "psS%d" % (mi * 2 + n % 2)
                    pt = PT[mi * 3 + n % 3]; ptk = "PT%d" % (mi * 3 + n % 3)
                    nS[mi] += 1
                    S.op("act", lambda e, sS=sS, pt=pt, NQ=NQ: e.activation(out=pt[:, 0:NQ], in_=sS[:, 0:NQ], func=AF.Exp, scale=0.125),
                         reads=[sSk], writes=[ptk, sSk])
                    if kb + 1 < kb1:
                        emit_S(mi, kb + 1, nS[mi])
                    S.op("pe", lambda e, pO=pO, pt=pt, kb=kb, hh=hh, NQ=NQ, kb0=kb0, kb1=kb1: e.matmul(
                        pO[:, 0:NQ], Vtv[:, kb, hh * 128:(hh + 1) * 128], pt[:, 0:NQ], start=(kb == kb0), stop=(kb == kb1 - 1)),
                        reads=[ptk, "Vt"], writes=[pOk])
                    if kb == kb0:
                        S.op("dve", lambda e, pa=pa, pt=pt, NQ=NQ: e.tensor_copy(pa[:, 0:NQ], pt[:, 0:NQ]), reads=[ptk], writes=[pak])
                    else:
                        S.op("dve", lambda e, pa=pa, pt=pt, NQ=NQ: e.tensor_tensor(out=pa[:, 0:NQ], in0=pt[:, 0:NQ], in1=pa[:, 0:NQ], op=ALU.add),
                             reads=[ptk, pak], writes=[pak])
            oa = par * 2
            for mi in range(2):
                S.op("pe", lambda e, mi=mi, oa=oa, NQ=NQ: e.matmul(psD[mi][:, 0:NQ], onesf[:], pacc[oa + mi][:, 0:NQ], start=True, stop=True),
                     reads=["pacc%d" % (oa + mi), "onesf"], writes=["psD%d" % mi])
            S.op("dve", lambda e, NQ=NQ: e.reciprocal(out=r1[:, 0:NQ], in_=psD[0][:, 0:NQ]), reads=["psD0"], writes=["r1", "psD0"])
            S.op("dve", lambda e, NQ=NQ: e.reciprocal(out=r2[:, 0:NQ], in_=psD[1][:, 0:NQ]), reads=["psD1"], writes=["r2", "psD1"])
            S.op("dve", lambda e, NQ=NQ, oa=oa: e.tensor_tensor(out=o1[:, 0:NQ], in0=psO[oa][:, 0:NQ], in1=r1[:, 0:NQ], op=ALU.mult),
                 reads=["psO%d" % oa, "r1"], writes=["o1", "psO%d" % oa])
            S.op("dve", lambda e, NQ=NQ, oa=oa: e.tensor_tensor(out=o2[:, 0:NQ], in0=psO[oa + 1][:, 0:NQ], in1=r2[:, 0:NQ], op=ALU.mult),
                 reads=["psO%d" % (oa + 1), "r2"], writes=["o2", "psO%d" % (oa + 1)])
            S.op("dve", lambda e, NQ=NQ: e.scalar_tensor_tensor(out=o1[:, 0:NQ], in0=o2[:, 0:NQ], scalar=nlam[:, 0:1], in1=o1[:, 0:NQ],
                                                                op0=ALU.mult, op1=ALU.add), reads=["o1", "o2", "nlam"], writes=["o1"])
            S.op("dve", lambda e, NQ=NQ: e.tensor_tensor(out=o2[:, 0:NQ], in0=o1[:, 0:NQ], in1=o1[:, 0:NQ], op=ALU.mult), reads=["o1", "o2"], writes=["o2"])
            S.op("pe", lambda e, NQ=NQ: e.matmul(psD[0][:, 0:NQ], onesf[:], o2[:, 0:NQ], start=True, stop=True), reads=["o2", "onesf"], writes=["psD0"])
            S.op("act", lambda e, NQ=NQ: e.activation(out=r1[:, 0:NQ], in_=psD[0][:, 0:NQ], func=AF.Sqrt, scale=1.0 / 128, bias=eps5[:, 0:1]),
                 reads=["psD0", "eps5"], writes=["r1", "psD0"])
            S.op("dve", lambda e, NQ=NQ: e.reciprocal(out=r1[:, 0:NQ], in_=r1[:, 0:NQ]), reads=["r1"], writes=["r1"])
            ob_ = obf[par]; obk = "obf%d" % par
            S.op("dve", lambda e, NQ=NQ, ob_=ob_: e.scalar_tensor_tensor(out=ob_[:, 0:NQ], in0=o1[:, 0:NQ], scalar=swc[:, 0:1], in1=r1[:, 0:NQ],
                                                                         op0=ALU.mult, op1=ALU.mult), reads=["o1", "swc", "r1"], writes=[obk])
            S.dma("sp", lambda e, ob_=ob_, hh=hh, q0=q0, NQ=NQ: e.dma_start(out=attnT[hh * 128:(hh + 1) * 128, q0:q0 + NQ], in_=ob_[:, 0:NQ]),
                  reads=[obk])
    st3.close()
    S.barrier()
    return b.finish()


def rope_tables():
    n_rows = SEQ // 64
    rows = np.repeat(np.arange(n_rows), 64).astype(np.float32)
    cols = np.tile(np.arange(64), n_rows).astype(np.float32)
    inv = np.power(np.float32(10000.0), -np.arange(0, 32, 2, dtype=np.float32) / np.float32(32)).astype(np.float32)
    ang = [rows[:, None] * inv, cols[:, None] * inv]
    cosf = np.zeros((NTOK, 4, 2, 2, 16), np.float32)
    sinf = np.zeros((NTOK, 4, 2, 2, 16), np.float32)
    cosf[:CTX] = 1.0
    for a in range(2):
        c = np.cos(ang[a]).astype(np.float32)
        s = np.sin(ang[a]).astype(np.float32)
        cosf[CTX:, :, a, 0, :] = c[:, None, :]
        cosf[CTX:, :, a, 1, :] = c[:, None, :]
        sinf[CTX:, :, a, 0, :] = -s[:, None, :]
        sinf[CTX:, :, a, 1, :] = s[:, None, :]
    return np.ascontiguousarray(np.concatenate([cosf.reshape(NTOK, 256), sinf.reshape(NTOK, 256)], 1))


def stage_C(inp, aT_all, phases=(1, 2, 3)):
    w = inp["ev_w_in"][0]
    rope = rope_tables()
    ident = np.eye(128, dtype=np.float32)
    maps = []
    for j in range(NCORES):
        cols = np.concatenate([np.arange(j * 256, (j + 1) * 256), EV_A + np.arange(j * 256, (j + 1) * 256),
                               2 * EV_A + np.arange(j * 256, (j + 1) * 256), 3 * EV_A + np.arange(j * 256, (j + 1) * 256),
                               3 * EV_A + 2048 + np.arange(j * 256, (j + 1) * 256)])
        cwj = inp["ev_conv_w"][0][:, j * 256:(j + 1) * 256]
        cw = np.ascontiguousarray(cwj.reshape(CW, 2, 128).transpose(2, 1, 0)).reshape(128, 2 * CW)
        cb = np.ascontiguousarray(inp["ev_conv_b"][0][j * 256:(j + 1) * 256].reshape(2, 128).T)
        maps.append({"aT": aT_all, "w": np.ascontiguousarray(w[:, cols]), "rope": rope,
                     "lam": np.ascontiguousarray(inp["ev_lambda"][0].reshape(256)), "subw": np.ascontiguousarray(inp["ev_subln_w"][0]),
                     "cw": cw, "cb": cb, "ident": ident})
    res = run(build_C(phases), maps)
    attnT = np.concatenate([r["attnT"] for r in res], 0)
    convT = np.concatenate([r["convT"] for r in res], 0)
    return attnT, convT


def group_cols(layer):
    if layer == 0:
        return np.concatenate([np.arange(0, 512), LAT_PC + np.arange(0, 16), np.arange(512, 1024), LAT_PC + np.arange(16, 32)])
    return np.arange(0, LAT_PC)


def build_T2(layer):
    b = B()
    S = b.S
    nc = b.nc
    NT = 528 if layer == 0 else 512
    NTOT = 2 * NT
    H = NT // 2
    halves = [(0, H), (H, NT)]
    if layer == 0:
        rngs = [(0, H, 0), (H, 512, 0), (512, NT, 1)]
    else:
        rngs = [(0, H, 0), (H, NT, 0)]
    hTin = b.din("hTin", [D, NTOT])
    modT = b.din("modT", [128, 12 * DC])
    nw2T = b.din("nw2T", [128, DC])
    w_out = b.din("w_out", [D, D])
    w1 = b.din("w1", [D, FFN_H])
    w3 = b.din("w3", [D, FFN_H])
    w2 = b.din("w2", [FFN_H, D])
    epsD = b.din("epsv", [128, 2])
    if layer == 0:
        attnTin = b.din("attnTin", [2048, NTOT], BF16)
        convTin = b.din("convTin", [2048, NTOT])
        lnT = b.din("lnT", [128, 32])
        modN = b.din("modN", [128, 12 * DC])
        nwN = b.din("nwN", [128, DC])
        aTout = b.dout("aTout", [D, NTOT], BF16)
        hTout = b.dout("hTout", [D, NTOT])
    else:
        s5T = b.din("s5T", [1024, NTOT], BF16)
        ssdT = b.din("ssdT", [3072, NTOT], BF16)
        gluW = b.din("gluW", [1024, 1024])
        glubT = b.din("glubT", [128, 8])
        nwN = b.din("nwN", [128, DC])
        identD = b.din("ident", [128, 128])
        outD = b.dout("out", [NTOT, D])
    hT1 = b.dout("hT1", [D, NTOT]) if DEBUG else b.dint("hT1", [D, NTOT])
    hT2 = hTout if layer == 0 else b.dint("hT2", [D, NTOT])

    modt = b.sb("modt", [128, 12 * DC])
    nw2t = b.sb("nw2t", [128, DC])
    nwNt = b.sb("nwNt", [128, DC])
    epst = b.sb("epst", [128, 2])
    ones = b.sb("ones", [128, 128], BF16)
    S.dma("sp", lambda e: e.dma_start(out=modt[:], in_=modT), writes=["modt"])
    S.dma("sp", lambda e: e.dma_start(out=nw2t[:], in_=nw2T), writes=["nw2t"])
    S.dma("sp", lambda e: e.dma_start(out=nwNt[:], in_=nwN), writes=["nwNt"])
    S.dma("sp", lambda e: e.dma_start(out=epst[:], in_=epsD), writes=["epst"])
    S.op("dve", lambda e: e.memset(ones[:], 1.0), writes=["ones"])
    weff2 = emit_weff(b, modt, nw2t, "modt", "nw2t", 4, "weff2")
    if layer == 0:
        lnt = b.sb("lnt", [128, 32])
        modn = b.sb("modn", [128, 12 * DC])
        onesf = b.sb("onesf", [128, 128])
        S.dma("sp", lambda e: e.dma_start(out=lnt[:], in_=lnT), writes=["lnt"])
        S.dma("sp", lambda e: e.dma_start(out=modn[:], in_=modN), writes=["modn"])
        S.op("dve", lambda e: e.memset(onesf[:], 1.0), writes=["onesf"])
        weffN = emit_weff(b, modn, nwNt, "modn", "nwNt", 1, "weffN")
    else:
        glubt = b.sb("glubt", [128, 8])
        ident = b.sb("identt", [128, 128])
        S.dma("sp", lambda e: e.dma_start(out=glubt[:], in_=glubT), writes=["glubt"])
        S.dma("sp", lambda e: e.dma_start(out=ident[:], in_=identD), writes=["ident"])

    NWB = 4
    wb = [b.sb("wb%d" % i, [128, 32 * 256], BF16) for i in range(NWB)]
    actT = b.sb("actT", [128, DC * NT], BF16)
    actv = actT[:].rearrange("p (c n) -> p c n", n=NT)
    hch = [b.sb("hch%d" % i, [128, NT]) for i in range(2)]
    hnew = [b.sb("hnew%d" % i, [128, NT]) for i in range(2)]
    sqb = [b.sb("sqb%d" % i, [128, NT], BF16) for i in range(2)]
    tmpf = [b.sb("tmpf%d" % i, [128, NT]) for i in range(2)]
    rstd = b.sb("rstd", [128, NT])
    pb = [b.ps("pb%d" % i) for i in range(8)]
    cnt = {"w": 0, "h": 0, "hn": 0, "sq": 0, "tmp": 0}

    def wload(src_ap, kcn, ncols):
        i = cnt["w"] % NWB
        cnt["w"] += 1
        buf = wb[i]
        view = buf[:, 0:kcn * ncols].rearrange("p (k n) -> p k n", n=ncols)
        S.dma("pool", lambda e: e.dma_start(out=view, in_=src_ap.rearrange("(k p) n -> p k n", p=128)), writes=["wb%d" % i])
        return view, "wb%d" % i

    def stats_begin():
        return {"first": True}

    def stats_add(st, src, srck, last):
        i = cnt["sq"] % 2
        cnt["sq"] += 1
        sq = sqb[i]
        S.op("act", lambda e: e.activation(out=sq[:], in_=src, func=AF.Square), reads=[srck], writes=["sqb%d" % i])
        first = st["first"]
        st["first"] = False

        def mm(e):
            ins = None
            for hi, (a0, a1) in enumerate(halves):
                ins = e.matmul(pb[4 + hi][:, 0:a1 - a0], ones[:], sq[:, a0:a1], start=first, stop=last)
            return ins
        S.op("pe", mm, reads=["sqb%d" % i, "ones"], writes=["pb4", "pb5"])

    def stats_finish(eps_col, nfeat):
        for hi, (a0, a1) in enumerate(halves):
            S.op("act", lambda e, hi=hi, a0=a0, a1=a1: e.activation(out=rstd[:, a0:a1], in_=pb[4 + hi][:, 0:a1 - a0], func=AF.Sqrt,
                                                                    scale=1.0 / nfeat, bias=epst[:, eps_col:eps_col + 1]),
                 reads=["pb4", "pb5", "epst"], writes=["rstd", "pb4", "pb5"])
        S.op("dve", lambda e: e.reciprocal(out=rstd[:], in_=rstd[:]), reads=["rstd"], writes=["rstd"])

    def load_h(src, c, g0):
        i = cnt["h"] % 2
        cnt["h"] += 1
        S.dma("sp", lambda e: e.dma_start(out=hch[i][:], in_=src[c * 128:(c + 1) * 128, g0:g0 + NT]), reads=[id(src)], writes=["hch%d" % i])
        return hch[i], "hch%d" % i

    def gemm_resid(Wd, KC, xk, gate_v, src_h, dst_h, g0, st):
        npan = D // 256
        segs = [(k0, min(k0 + 32, KC)) for k0 in range(0, KC, 32)]
        xv = gemm_resid.xv
        for pn in range(npan):
            for si, (k0, k1) in enumerate(segs):
                wv, wk = wload(Wd[k0 * 128:k1 * 128, pn * 256:(pn + 1) * 256], k1 - k0, 256)
                for o2 in range(2):
                    bi = (o2 * 2) if (o2 == 1 or pn % 2 == 0) else 6
                    bks = [pb[bi], pb[bi + 1]]
                    bkk = ["pb%d" % bi, "pb%d" % (bi + 1)]

                    def mm(e, o2=o2, bks=bks, wv=wv, k0=k0, k1=k1, xv=xv):
                        ins = None
                        for k in range(k0, k1):
                            for hi, (a0, a1) in enumerate(halves):
                                ins = e.matmul(bks[hi][:, 0:a1 - a0], wv[:, k - k0, o2 * 128:(o2 + 1) * 128], xv[:, k, a0:a1],
                                               start=(k == 0), stop=(k == KC - 1))
                        return ins
                    S.op("pe", mm, reads=[wk] + list(xk[k0:k1]), writes=bkk)
            for o2 in range(2):
                oc = pn * 2 + o2
                bi = (o2 * 2) if (o2 == 1 or pn % 2 == 0) else 6
                bks = [pb[bi], pb[bi + 1]]
                bkk = ["pb%d" % bi, "pb%d" % (bi + 1)]
                ht, hk = load_h(src_h, oc, g0)
                j = cnt["hn"] % 2
                cnt["hn"] += 1
                hn = hnew[j]; hnk = "hnew%d" % j
                for (c0, c1, r) in rngs:
                    hi = 0 if c1 <= H else 1
                    a0 = halves[hi][0]
                    S.op("dve", lambda e, hi=hi, a0=a0, c0=c0, c1=c1, r=r, oc=oc, ht=ht, hn=hn, bks=bks: e.scalar_tensor_tensor(
                        out=hn[:, c0:c1], in0=bks[hi][:, c0 - a0:c1 - a0], scalar=mcol(modt, r, gate_v, oc, oc + 1), in1=ht[:, c0:c1],
                        op0=ALU.mult, op1=ALU.add), reads=[bkk[hi], hk, "modt"], writes=[hnk, bkk[hi]])
                S.dma("sp", lambda e, hn=hn, oc=oc: e.dma_start(out=dst_h[oc * 128:(oc + 1) * 128, g0:g0 + NT], in_=hn[:]),
                      reads=[hnk], writes=[id(dst_h)])
                stats_add(st, hn[:], hnk, oc == DC - 1)

    def norm_mod(src_h, g0, weff, shift_tab, shift_v, dst_view_fn, dstk_fn, after=None):
        for c in range(DC):
            ht, hk = load_h(src_h, c, g0)
            i = cnt["tmp"] % 2
            cnt["tmp"] += 1
            tf = tmpf[i]; tk = "tmpf%d" % i
            S.op("dve", lambda e, ht=ht, tf=tf: e.tensor_tensor(out=tf[:], in0=ht[:], in1=rstd[:], op=ALU.mult), reads=[hk, "rstd"], writes=[tk])
            dv = dst_view_fn(c)
            for (c0, c1, r) in rngs:
                if weff is not None and shift_tab is not None:
                    S.op("act", lambda e, tf=tf, dv=dv, c0=c0, c1=c1, r=r, c=c: e.activation(
                        out=dv[:, c0:c1], in_=tf[:, c0:c1], func=AF.Identity,
                        scale=weff[:, r * DC + c:r * DC + c + 1], bias=mcol(shift_tab, r, shift_v, c, c + 1)),
                        reads=[tk], writes=[dstk_fn(c)])
                else:
                    S.op("act", lambda e, tf=tf, dv=dv, c0=c0, c1=c1, c=c: e.activation(
                        out=dv[:, c0:c1], in_=tf[:, c0:c1], func=AF.Copy, scale=weff[:, c:c + 1]) if False else
                        e.activation(out=dv[:, c0:c1], in_=tf[:, c0:c1], func=AF.Identity, scale=nwNt[:, c:c + 1], bias=0.0),
                        reads=[tk], writes=[dstk_fn(c)])
            if after is not None:
                after(c)

    def do_group(g):
        g0 = g * NT
        if layer == 0:
            st1 = contextlib.ExitStack()
            cv = st1.enter_context(nc.sbuf_tensor("cv" + "_g%d" % g, [128, 16 * NT], F32))
            cvv = cv[:].rearrange("p (c n) -> p c n", n=NT)
            S.dma("sp", lambda e: e.dma_start(out=cvv, in_=convTin.rearrange("(c p) t -> p c t", p=128)[:, :, g0:g0 + NT]), writes=["cv"])
            S.dma("sp", lambda e: e.dma_start(out=actv[:, 0:16, :], in_=attnTin.rearrange("(c p) t -> p c t", p=128)[:, :, g0:g0 + NT]),
                  writes=["act_%d" % c for c in range(16)])
            for c in range(16):
                i = cnt["tmp"] % 2
                cnt["tmp"] += 1
                tf = tmpf[i]; tk = "tmpf%d" % i
                S.op("act", lambda e, c=c, tf=tf: e.activation(out=tf[:], in_=cvv[:, c, :], func=AF.Square), reads=["cv"], writes=[tk])

                def mm(e, c=c, tf=tf):
                    ins = None
                    for hi, (a0, a1) in enumerate(halves):
                        ins = e.matmul(pb[hi][:, 0:a1 - a0], onesf[:], cvv[:, c, a0:a1], start=(c == 0), stop=(c == 15))
                        ins = e.matmul(pb[2 + hi][:, 0:a1 - a0], onesf[:], tf[:, a0:a1], start=(c == 0), stop=(c == 15))
                    return ins
                S.op("pe", mm, reads=["cv", tk, "onesf"], writes=["pb0", "pb1", "pb2", "pb3"])
            mean = st1.enter_context(nc.sbuf_tensor("lnmean" + "_g%d" % g, [128, NT], F32))
            msq = st1.enter_context(nc.sbuf_tensor("lnmsq" + "_g%d" % g, [128, NT], F32))
            nmr = st1.enter_context(nc.sbuf_tensor("lnnmr" + "_g%d" % g, [128, NT], F32))
            for hi, (a0, a1) in enumerate(halves):
                S.op("dve", lambda e, hi=hi, a0=a0, a1=a1: e.tensor_scalar(out=mean[:, a0:a1], in0=pb[hi][:, 0:a1 - a0], scalar1=1.0 / 2048, scalar2=None, op0=ALU.mult),
                     reads=["pb%d" % hi], writes=["lnmean", "pb%d" % hi])
            S.op("dve", lambda e: e.tensor_tensor(out=msq[:], in0=mean[:], in1=mean[:], op=ALU.mult), reads=["lnmean"], writes=["lnmsq"])
            for hi, (a0, a1) in enumerate(halves):
                S.op("dve", lambda e, hi=hi, a0=a0, a1=a1: e.scalar_tensor_tensor(
                    out=rstd[:, a0:a1], in0=pb[2 + hi][:, 0:a1 - a0], scalar=1.0 / 2048, in1=msq[:, a0:a1], op0=ALU.mult, op1=ALU.subtract),
                    reads=["pb%d" % (2 + hi), "lnmsq"], writes=["rstd", "pb%d" % (2 + hi)])
            S.op("act", lambda e: e.activation(out=rstd[:], in_=rstd[:], func=AF.Sqrt, bias=epst[:, 1:2]), reads=["rstd", "epst"], writes=["rstd"])
            S.op("dve", lambda e: e.reciprocal(out=rstd[:], in_=rstd[:]), reads=["rstd"], writes=["rstd"])
            S.op("dve", lambda e: e.scalar_tensor_tensor(out=nmr[:], in0=mean[:], scalar=-1.0, in1=rstd[:], op0=ALU.mult, op1=ALU.mult),
                 reads=["lnmean", "rstd"], writes=["lnnmr"])
            for c in range(16):
                i = cnt["tmp"] % 2
                cnt["tmp"] += 1
                tf = tmpf[i]; tk = "tmpf%d" % i
                S.op("dve", lambda e, c=c, tf=tf: e.tensor_tensor(out=tf[:], in0=cvv[:, c, :], in1=rstd[:], op=ALU.mult), reads=["cv", "rstd"], writes=[tk])
                S.op("dve", lambda e, tf=tf: e.tensor_tensor(out=tf[:], in0=tf[:], in1=nmr[:], op=ALU.add), reads=[tk, "lnnmr"], writes=[tk])
                S.op("act", lambda e, c=c, tf=tf: e.activation(out=actv[:, 16 + c, :], in_=tf[:], func=AF.Silu,
                                                               scale=lnt[:, c:c + 1], bias=lnt[:, 16 + c:17 + c]),
                     reads=[tk, "lnt"], writes=["act_%d" % (16 + c)])
            st1.close()
            S.barrier()
        else:
            st1 = contextlib.ExitStack()
            gT = st1.enter_context(nc.sbuf_tensor("gT5" + "_g%d" % g, [128, 8 * NT], BF16))
            gv = gT[:].rearrange("p (c n) -> p c n", n=NT)
            S.dma("sp", lambda e: e.dma_start(out=gv, in_=s5T.rearrange("(c p) t -> p c t", p=128)[:, :, g0:g0 + NT]), writes=["gT5"])
            S.dma("sp", lambda e: e.dma_start(out=actv[:, 8:32, :], in_=ssdT.rearrange("(c p) t -> p c t", p=128)[:, :, g0:g0 + NT]),
                  writes=["act_%d" % c for c in range(8, 32)])
            for pn in range(4):
                wv, wk = wload(gluW[:, pn * 256:(pn + 1) * 256], 8, 256)
                for o2 in range(2):
                    oc = pn * 2 + o2
                    bks = [pb[o2 * 2], pb[o2 * 2 + 1]]
                    bkk = ["pb%d" % (o2 * 2), "pb%d" % (o2 * 2 + 1)]

                    def mm(e, wv=wv, o2=o2, bks=bks):
                        ins = None
                        for k in range(8):
                            for hi, (a0, a1) in enumerate(halves):
                                ins = e.matmul(bks[hi][:, 0:a1 - a0], wv[:, k, o2 * 128:(o2 + 1) * 128], gv[:, k, a0:a1], start=(k == 0), stop=(k == 7))
                        return ins
                    S.op("pe", mm, reads=[wk, "gT5"], writes=bkk)
                    i = cnt["tmp"] % 2
                    cnt["tmp"] += 1
                    tf = tmpf[i]; tk = "tmpf%d" % i
                    for hi, (a0, a1) in enumerate(halves):
                        S.op("act", lambda e, hi=hi, a0=a0, a1=a1, tf=tf, oc=oc, bks=bks: e.activation(
                            out=tf[:, a0:a1], in_=bks[hi][:, 0:a1 - a0], func=AF.Sigmoid, bias=glubt[:, oc:oc + 1]),
                            reads=[bkk[hi], "glubt"], writes=[tk, bkk[hi]])
                    S.op("dve", lambda e, tf=tf, oc=oc: e.tensor_tensor(out=actv[:, oc, :], in0=gv[:, oc, :], in1=tf[:], op=ALU.mult),
                         reads=[tk, "gT5"], writes=["act_%d" % oc])
            st1.close()
            S.barrier()
        allact = ["act_%d" % c for c in range(DC)]
        gemm_resid.xv = actv
        st = stats_begin()
        gemm_resid(w_out, DC, allact, 2, hTin, hT1, g0, st)
        stats_finish(0, D)
        norm_mod(hT1, g0, weff2, modt, 3, lambda c: actv[:, c, :], lambda c: "act_%d" % c)
        st4 = contextlib.ExitStack()
        gT = st4.enter_context(nc.sbuf_tensor("gT" + "_g%d" % g, [128, HC * NT], BF16))
        gTv = gT[:].rearrange("p (c n) -> p c n", n=NT)
        for pn in range(HC // 2):
            w1v, w1k = wload(w1[:, pn * 256:(pn + 1) * 256], DC, 256)
            w3v, w3k = wload(w3[:, pn * 256:(pn + 1) * 256], DC, 256)
            for o2 in range(2):
                hc = pn * 2 + o2
                bks = [pb[o2 * 4 + i] for i in range(4)]
                bkk = ["pb%d" % (o2 * 4 + i) for i in range(4)]

                def mm(e, w1v=w1v, w3v=w3v, o2=o2, bks=bks):
                    ins = None
                    for wi, wv in enumerate((w1v, w3v)):
                        for k in range(DC):
                            for hi, (a0, a1) in enumerate(halves):
                                ins = e.matmul(bks[wi * 2 + hi][:, 0:a1 - a0], wv[:, k, o2 * 128:(o2 + 1) * 128], actv[:, k, a0:a1],
                                               start=(k == 0), stop=(k == DC - 1))
                    return ins
                S.op("pe", mm, reads=[w1k, w3k] + allact, writes=bkk)
                i = cnt["tmp"] % 2
                cnt["tmp"] += 1
                tf = tmpf[i]; tk = "tmpf%d" % i
                for hi, (a0, a1) in enumerate(halves):
                    S.op("act", lambda e, hi=hi, a0=a0, a1=a1, tf=tf, bks=bks: e.activation(out=tf[:, a0:a1], in_=bks[hi][:, 0:a1 - a0], func=AF.Silu),
                         reads=[bkk[hi]], writes=[tk + "_%d" % hi, bkk[hi]])
                    S.op("dve", lambda e, hi=hi, a0=a0, a1=a1, tf=tf, hc=hc, bks=bks: e.tensor_tensor(
                        out=gTv[:, hc, a0:a1], in0=bks[2 + hi][:, 0:a1 - a0], in1=tf[:, a0:a1], op=ALU.mult),
                        reads=[bkk[2 + hi], tk + "_%d" % hi], writes=["gT_%d" % hc, bkk[2 + hi]])
        gemm_resid.xv = gTv
        st = stats_begin()
        gemm_resid(w2, HC, ["gT_%d" % c for c in range(HC)], 5, hT1, hT2, g0, st)
        stats_finish(0, D)
        st4.close()
        S.barrier()
        if layer == 0:
            def store_a(c):
                pass
            norm_mod(hT2, g0, weffN, modn, 0, lambda c: actv[:, c, :], lambda c: "act_%d" % c)
            S.dma("sp", lambda e: e.dma_start(out=aTout.rearrange("(c p) t -> p c t", p=128)[:, :, g0:g0 + NT], in_=actv), reads=allact)
        else:
            st6 = contextlib.ExitStack()
            yT = st6.enter_context(nc.sbuf_tensor("yT" + "_g%d" % g, [128, DC * NT], F32))
            yv = yT[:].rearrange("p (c n) -> p c n", n=NT)
            norm_mod(hT2, g0, None, None, 0, lambda c: yv[:, c, :], lambda c: "yT_%d" % c)
            ot = [st6.enter_context(nc.sbuf_tensor("otk%d_g%d" % (i, g), [128, 2048], F32)) for i in range(2)]
            for tt in range(NT // 128):
                for hf in range(2):
                    o = ot[hf]; ok = "otk%d" % hf
                    for cq in range(4):
                        cg = hf * 4 + cq
                        bk = pb[cg % 4]; bkn = "pb%d" % (cg % 4)

                        def tr(e, cg=cg, bk=bk, tt=tt):
                            ins = None
                            for k in range(4):
                                c = cg * 4 + k
                                ins = e.transpose(bk[:, k * 128:(k + 1) * 128], yv[:, c, tt * 128:(tt + 1) * 128], ident[:])
                            return ins
                        S.op("pe", tr, reads=["yT_%d" % (cg * 4 + k) for k in range(4)] + ["ident"], writes=[bkn])
                        if cg % 2 == 0:
                            S.op("dve", lambda e, o=o, cq=cq, bk=bk: e.tensor_copy(o[:, cq * 512:(cq + 1) * 512], bk[:, 0:512]), reads=[bkn], writes=[ok, bkn])
                        else:
                            S.op("act", lambda e, o=o, cq=cq, bk=bk: e.activation(out=o[:, cq * 512:(cq + 1) * 512], in_=bk[:, 0:512], func=AF.Copy),
                                 reads=[bkn], writes=[ok, bkn])
                    S.dma("sp", lambda e, o=o, tt=tt, hf=hf: e.dma_start(out=outD[g0 + tt * 128:g0 + (tt + 1) * 128, hf * 2048:(hf + 1) * 2048], in_=o[:]), reads=[ok])
            st6.close()
            S.barrier()
    for g in range(2):
        do_group(g)
    return b.finish()


def core_cols_all(j, layer):
    gc = group_cols(layer)
    loc = np.where(gc < LAT_PC, CTX + j * LAT_PC + gc, j * CTX_PC + (gc - LAT_PC))
    return loc


def stage_D(inp, mod, hT_parts, attnT, convT):
    modt = mod_tables(mod[0])
    modn = mod_tables(mod[1])
    lnT = np.ascontiguousarray(np.concatenate([tab(inp["ev_ln_w"][0]), tab(inp["ev_ln_b"][0])], 1))
    epsv = np.ascontiguousarray(np.tile(np.array([[RMS_EPS, LN_EPS]], np.float32), (128, 1)))
    gc = group_cols(0)
    maps = []
    for j in range(NCORES):
        cols = core_cols_all(j, 0)
        maps.append({"hTin": np.ascontiguousarray(hT_parts[j][:, gc]), "modT": modt, "nw2T": tab(inp["norm_w"][0, 1]),
                     "w_out": inp["ev_w_out"][0], "w1": inp["ffn_w1"][0], "w3": inp["ffn_w3"][0], "w2": inp["ffn_w2"][0],
                     "epsv": epsv, "attnTin": np.ascontiguousarray(attnT[:, cols]), "convTin": np.ascontiguousarray(convT[:, cols]),
                     "lnT": lnT, "modN": modn, "nwN": tab(inp["norm_w"][1, 0])})
    res = run(build_T2(0), maps)
    inv = np.argsort(gc)
    if DEBUG:
        DBG["hT1"] = [np.ascontiguousarray(r["hT1"][:, inv]) for r in res]
    aT = [np.ascontiguousarray(r["aTout"][:, inv]) for r in res]
    hT = [np.ascontiguousarray(r["hTout"][:, inv]) for r in res]
    return hT, aT


E_COLS = 1164
TWO_PI = 2.0 * math.pi


def emit_sincos(b, ang, angk, n, sin_out, cos_out, outk, tag, scratch=None):
    S = b.S
    if scratch is None:
        scratch = (b.sb("sc_yi_" + tag, [128, n], I32), b.sb("sc_yf_" + tag, [128, n]), b.sb("sc_r_" + tag, [128, n]), b.sb("sc_mk_" + tag, [128, n]))
    yi, yf, r, mk = scratch
    tag = "x" if len(tag) > 1 else tag
    S.op("dve", lambda e: e.tensor_scalar(out=yf[:], in0=ang, scalar1=1.0 / TWO_PI, scalar2=None, op0=ALU.mult), reads=[angk], writes=["sc_yf" + tag])
    S.op("dve", lambda e: e.tensor_copy(yi[:], yf[:]), reads=["sc_yf" + tag], writes=["sc_yi" + tag])
    S.op("dve", lambda e: e.tensor_copy(yf[:], yi[:]), reads=["sc_yi" + tag], writes=["sc_yf" + tag])
    S.op("dve", lambda e: e.scalar_tensor_tensor(out=r[:], in0=yf[:], scalar=-TWO_PI, in1=ang, op0=ALU.mult, op1=ALU.add),
         reads=["sc_yf" + tag, angk], writes=["sc_r" + tag])
    S.op("dve", lambda e: e.tensor_single_scalar(out=mk[:], in_=r[:], scalar=math.pi, op=ALU.is_gt), reads=["sc_r" + tag], writes=["sc_mk" + tag])
    S.op("dve", lambda e: e.scalar_tensor_tensor(out=r[:], in0=mk[:], scalar=-TWO_PI, in1=r[:], op0=ALU.mult, op1=ALU.add),
         reads=["sc_mk" + tag, "sc_r" + tag], writes=["sc_r" + tag])
    S.op("dve", lambda e: e.tensor_single_scalar(out=mk[:], in_=r[:], scalar=-math.pi, op=ALU.is_lt), reads=["sc_r" + tag], writes=["sc_mk" + tag])
    S.op("dve", lambda e: e.scalar_tensor_tensor(out=r[:], in0=mk[:], scalar=TWO_PI, in1=r[:], op0=ALU.mult, op1=ALU.add),
         reads=["sc_mk" + tag, "sc_r" + tag], writes=["sc_r" + tag])
    S.op("dve", lambda e: e.tensor_scalar(out=r[:], in0=r[:], scalar1=math.pi, scalar2=-math.pi, op0=ALU.min, op1=ALU.max),
         reads=["sc_r" + tag], writes=["sc_r" + tag])
    S.op("act", lambda e: e.activation(out=sin_out, in_=r[:], func=AF.Sin), reads=["sc_r" + tag], writes=[outk + "_s"])
    S.op("act", lambda e: e.activation(out=yf[:], in_=r[:], func=AF.Sin, scale=0.5), reads=["sc_r" + tag], writes=["sc_yf" + tag])
    S.op("dve", lambda e: e.tensor_tensor(out=yf[:], in0=yf[:], in1=yf[:], op=ALU.mult), reads=["sc_yf" + tag], writes=["sc_yf" + tag])
    S.op("dve", lambda e: e.tensor_scalar(out=cos_out, in0=yf[:], scalar1=-2.0, scalar2=1.0, op0=ALU.mult, op1=ALU.add),
         reads=["sc_yf" + tag], writes=[outk + "_c"])
    return r, "sc_r" + tag


def build_E(phases=(1, 2, 3)):
    b = B()
    S = b.S
    nc = b.nc
    aTd = b.din("aT", [D, NTOK], BF16)
    wd = b.din("w", [D, E_COLS])
    identD = b.din("ident", [128, 128])
    tvecD = b.din("tvec", [128, 512])
    s5pD = b.din("s5p", [128, 3 * 8])
    s5bD = b.din("s5b", [128, 2 * 4 * 16])
    s5cD = b.din("s5c", [128, 2 * 8 * 16])
    s5dD = b.din("s5d", [128, 1])
    s5T = b.dout("s5T", [128, SEQ], BF16)
    uTd = b.dint("uTd", [128, NTOK])
    xbcTd = b.dint("xbcTd", [5, 128, NTOK])
    zdtd = b.dint("zdtd", [NTOK, 396])
    ssd_io = build_E_ssd_decl(b)

    ident = b.sb("identt", [128, 128])
    S.dma("sp", lambda e: e.dma_start(out=ident[:], in_=identD), writes=["ident"])

    if 1 in phases:
        st1 = contextlib.ExitStack()
        Wt = st1.enter_context(nc.sbuf_tensor("Wt", [128, DC * E_COLS], BF16))
        Wv = Wt[:].rearrange("p (kc n) -> p kc n", n=E_COLS)
        for g in range(8):
            S.dma("pool", lambda e, g=g: e.dma_start(out=Wv[:, g * 4:(g + 1) * 4, :],
                                                     in_=wd.rearrange("(kc p) n -> p kc n", p=128)[:, g * 4:(g + 1) * 4, :]), writes=["W%d" % g])
        Wk = ["W%d" % g for g in range(8)]
        aTs = [st1.enter_context(nc.sbuf_tensor("aTs%d" % i, [128, DC * 512], BF16)) for i in range(2)]
        ob = [st1.enter_context(nc.sbuf_tensor("ob%d" % i, [128, 512], F32)) for i in range(3)]
        zb = [st1.enter_context(nc.sbuf_tensor("zb%d" % i, [128, 396], F32)) for i in range(2)]
        psA = [st1.enter_context(nc.psum_tensor("psA%d" % i, [128, 512], F32)) for i in range(3)]
        psZ = [st1.enter_context(nc.psum_tensor("psZ%d" % i, [128, 512], F32)) for i in range(2)]
        sts = supertiles()
        aTv = aTd.rearrange("(c p) t -> p c t", p=128)

        def load_aT(si):
            t0, N = sts[si]
            buf = aTs[si % 2]
            S.dma("sp", lambda e: e.dma_start(out=buf[:, 0:DC * N].rearrange("p (c n) -> p c n", n=N), in_=aTv[:, :, t0:t0 + N]),
                  writes=["aTs%d" % (si % 2)])
        load_aT(0)
        nA = 0
        nZ = 0
        for si, (t0, N) in enumerate(sts):
            if si + 1 < len(sts):
                load_aT(si + 1)
            a = aTs[si % 2][:, 0:DC * N].rearrange("p (c n) -> p c n", n=N)
            ak = "aTs%d" % (si % 2)
            for blk in range(6):
                i = nA % 3
                nA += 1

                def mm(e, a=a, blk=blk, i=i, N=N):
                    ins = None
                    for kc in range(DC):
                        ins = e.matmul(psA[i][:, 0:N], Wv[:, kc, blk * 128:(blk + 1) * 128], a[:, kc, :], start=(kc == 0), stop=(kc == DC - 1))
                    return ins
                S.op("pe", mm, reads=[ak] + Wk, writes=["psA%d" % i])
                if blk % 2 == 0:
                    S.op("dve", lambda e, i=i, N=N: e.tensor_copy(ob[i][:, 0:N], psA[i][:, 0:N]), reads=["psA%d" % i], writes=["ob%d" % i, "psA%d" % i])
                else:
                    S.op("act", lambda e, i=i, N=N: e.activation(out=ob[i][:, 0:N], in_=psA[i][:, 0:N], func=AF.Copy), reads=["psA%d" % i], writes=["ob%d" % i, "psA%d" % i])
                dst = uTd[:, t0:t0 + N] if blk == 0 else xbcTd[blk - 1, :, t0:t0 + N]
                S.dma("sp", lambda e, i=i, N=N, dst=dst: e.dma_start(out=dst, in_=ob[i][:, 0:N]), reads=["ob%d" % i], writes=["uTd" if blk == 0 else "xbcTd"])
            for sub in range(N // 128):
                i = nZ % 2
                nZ += 1

                def mmz(e, a=a, sub=sub, i=i):
                    ins = None
                    for kc in range(DC):
                        ins = e.matmul(psZ[i][:, 0:396], a[:, kc, sub * 128:(sub + 1) * 128], Wv[:, kc, 768:1164], start=(kc == 0), stop=(kc == DC - 1))
                    return ins
                S.op("pe", mmz, reads=[ak] + Wk, writes=["psZ%d" % i])
                S.op("dve", lambda e, i=i: e.tensor_copy(zb[i][:], psZ[i][:, 0:396]), reads=["psZ%d" % i], writes=["zb%d" % i, "psZ%d" % i])
                S.dma("sp", lambda e, i=i, t0=t0, sub=sub: e.dma_start(out=zdtd[t0 + sub * 128:t0 + (sub + 1) * 128, :], in_=zb[i][:]),
                      reads=["zb%d" % i], writes=["zdtd"])
        st1.close()
        S.barrier()

    if 2 in phases:
        st2 = contextlib.ExitStack()

        def T(name, shape, dt=F32):
            return st2.enter_context(nc.sbuf_tensor(name + "_s2", list(shape), dt))
        bsave = b.sb
        b.sb = T
        uT = T("uT", [128, NTOK])
        yf = T("yf", [128, SEQ])
        tvec = T("tvec", [128, 512])
        s5p = T("s5p", [128, 24])
        s5b = T("s5b", [128, 128])
        s5c = T("s5c", [128, 256])
        s5d = T("s5d", [128, 1])
        S.dma("sp", lambda e: e.dma_start(out=uT[:], in_=uTd), reads=["uTd"], writes=["uT"])
        S.dma("sp", lambda e: e.dma_start(out=tvec[:], in_=tvecD), writes=["tvec"])
        S.dma("sp", lambda e: e.dma_start(out=s5p[:], in_=s5pD), writes=["s5p"])
        S.dma("sp", lambda e: e.dma_start(out=s5b[:], in_=s5bD), writes=["s5b"])
        S.dma("sp", lambda e: e.dma_start(out=s5c[:], in_=s5cD), writes=["s5c"])
        S.dma("sp", lambda e: e.dma_start(out=s5d[:], in_=s5dD), writes=["s5d"])
        lre = s5p[:, 0:8]; lim = s5p[:, 8:16]
        dl = T("dl", [128, 8]); th = T("th", [128, 8]); mg = T("mg", [128, 8])
        sn = T("sn", [128, 8]); cs = T("cs", [128, 8])
        S.op("act", lambda e: e.activation(out=dl[:], in_=s5p[:, 16:24], func=AF.Exp), reads=["s5p"], writes=["dl"])
        S.op("dve", lambda e: e.tensor_tensor(out=th[:], in0=lim, in1=dl[:], op=ALU.mult), reads=["s5p", "dl"], writes=["th"])
        S.op("dve", lambda e: e.tensor_tensor(out=mg[:], in0=lre, in1=dl[:], op=ALU.mult), reads=["s5p", "dl"], writes=["mg"])
        S.op("act", lambda e: e.activation(out=mg[:], in_=mg[:], func=AF.Exp), reads=["mg"], writes=["mg"])
        thred, thredk = emit_sincos(b, th[:], "th", 8, sn[:], cs[:], "sc8", "p")
        ar = T("ar", [128, 8]); ai = T("ai", [128, 8]); den = T("den", [128, 8]); kr = T("kr", [128, 8]); ki = T("ki", [128, 8]); tq = T("tq", [128, 8])
        S.op("dve", lambda e: e.tensor_tensor(out=ar[:], in0=mg[:], in1=cs[:], op=ALU.mult), reads=["mg", "sc8_c"], writes=["ar"])
        S.op("dve", lambda e: e.tensor_scalar_add(ar[:], ar[:], -1.0), reads=["ar"], writes=["ar"])
        S.op("dve", lambda e: e.tensor_tensor(out=ai[:], in0=mg[:], in1=sn[:], op=ALU.mult), reads=["mg", "sc8_s"], writes=["ai"])
        S.op("dve", lambda e: e.tensor_tensor(out=den[:], in0=lre, in1=lre, op=ALU.mult), reads=["s5p"], writes=["den"])
        S.op("dve", lambda e: e.tensor_tensor(out=tq[:], in0=lim, in1=lim, op=ALU.mult), reads=["s5p"], writes=["tq"])
        S.op("dve", lambda e: e.tensor_tensor(out=den[:], in0=den[:], in1=tq[:], op=ALU.add), reads=["den", "tq"], writes=["den"])
        S.op("dve", lambda e: e.reciprocal(out=den[:], in_=den[:]), reads=["den"], writes=["den"])
        S.op("dve", lambda e: e.tensor_tensor(out=kr[:], in0=ar[:], in1=lre, op=ALU.mult), reads=["ar", "s5p"], writes=["kr"])
        S.op("dve", lambda e: e.tensor_tensor(out=tq[:], in0=ai[:], in1=lim, op=ALU.mult), reads=["ai", "s5p"], writes=["tq"])
        S.op("dve", lambda e: e.tensor_tensor(out=kr[:], in0=kr[:], in1=tq[:], op=ALU.add), reads=["kr", "tq"], writes=["kr"])
        S.op("dve", lambda e: e.tensor_tensor(out=kr[:], in0=kr[:], in1=den[:], op=ALU.mult), reads=["kr", "den"], writes=["kr"])
        S.op("dve", lambda e: e.tensor_tensor(out=ki[:], in0=ai[:], in1=lre, op=ALU.mult), reads=["ai", "s5p"], writes=["ki"])
        S.op("dve", lambda e: e.tensor_tensor(out=tq[:], in0=ar[:], in1=lim, op=ALU.mult), reads=["ar", "s5p"], writes=["tq"])
        S.op("dve", lambda e: e.tensor_tensor(out=ki[:], in0=ki[:], in1=tq[:], op=ALU.subtract), reads=["ki", "tq"], writes=["ki"])
        S.op("dve", lambda e: e.tensor_tensor(out=ki[:], in0=ki[:], in1=den[:], op=ALU.mult), reads=["ki", "den"], writes=["ki"])
        bbr = T("bbr", [128, 128]); bbi = T("bbi", [128, 128]); tb = T("tb", [128, 128])
        bre = s5b[:, 0:64].rearrange("p (s h) -> p s h", h=16)
        bim = s5b[:, 64:128].rearrange("p (s h) -> p s h", h=16)
        for d in range(2):
            krb = kr[:, d * 4:(d + 1) * 4].unsqueeze(2).to_broadcast([128, 4, 16])
            kib = ki[:, d * 4:(d + 1) * 4].unsqueeze(2).to_broadcast([128, 4, 16])
            ovr = bbr[:, d * 64:(d + 1) * 64].rearrange("p (s h) -> p s h", h=16)
            ovi = bbi[:, d * 64:(d + 1) * 64].rearrange("p (s h) -> p s h", h=16)
            tv = tb[:, d * 64:(d + 1) * 64].rearrange("p (s h) -> p s h", h=16)
            S.op("dve", lambda e, ovr=ovr, krb=krb: e.tensor_tensor(out=ovr, in0=bre, in1=krb, op=ALU.mult), reads=["s5b", "kr"], writes=["bbr"])
            S.op("dve", lambda e, tv=tv, kib=kib: e.tensor_tensor(out=tv, in0=bim, in1=kib, op=ALU.mult), reads=["s5b", "ki"], writes=["tb"])
            S.op("dve", lambda e, ovr=ovr, tv=tv: e.tensor_tensor(out=ovr, in0=ovr, in1=tv, op=ALU.subtract), reads=["bbr", "tb"], writes=["bbr"])
            S.op("dve", lambda e, ovi=ovi, krb=krb: e.tensor_tensor(out=ovi, in0=bim, in1=krb, op=ALU.mult), reads=["s5b", "kr"], writes=["bbi"])
            S.op("dve", lambda e, tv=tv, kib=kib: e.tensor_tensor(out=tv, in0=bre, in1=kib, op=ALU.mult), reads=["s5b", "ki"], writes=["tb"])
            S.op("dve", lambda e, ovi=ovi, tv=tv: e.tensor_tensor(out=ovi, in0=ovi, in1=tv, op=ALU.add), reads=["bbi", "tb"], writes=["bbi"])
        BD = T("BD", [128, 128])
        BL = T("BL", [128, 16 * 128])
        CD = T("CD", [128, 16 * 128])
        psB = st2.enter_context(nc.psum_tensor("psB5", [128, 512], F32))
        S.op("dve", lambda e: e.memset(CD[:], 0.0), writes=["CD"])
        for ri in range(2):
            src = bbr if ri == 0 else bbi
            srck = "bbr" if ri == 0 else "bbi"
            for k in range(8):
                sb_ = k % 4
                S.op("dve", lambda e: e.memset(BD[:], 0.0), writes=["BD"])
                for gl in range(2):
                    g8 = 2 * sb_ + gl
                    S.op("dve", lambda e, src=src, k=k, gl=gl, g8=g8: e.tensor_copy(
                        BD[gl * 64:(gl + 1) * 64, g8 * 16:(g8 + 1) * 16], src[gl * 64:(gl + 1) * 64, k * 16:(k + 1) * 16]),
                        reads=[srck], writes=["BD"])
                    col = (ri * 8 + k) * 128 + g8 * 16
                    sgn = 1.0 if ri == 0 else -1.0
                    S.op("dve", lambda e, ri=ri, k=k, gl=gl, col=col, sgn=sgn: e.tensor_scalar(
                        out=CD[gl * 64:(gl + 1) * 64, col:col + 16], in0=s5c[gl * 64:(gl + 1) * 64, (ri * 8 + k) * 16:(ri * 8 + k + 1) * 16],
                        scalar1=sgn, scalar2=None, op0=ALU.mult), reads=["s5c"], writes=["CD"])
                S.op("pe", lambda e: e.transpose(psB[:, 0:128], BD[:], ident[:]), reads=["BD", "ident"], writes=["psB5"])
                S.op("dve", lambda e, ri=ri, k=k: e.tensor_copy(BL[:, (ri * 8 + k) * 128:(ri * 8 + k + 1) * 128], psB[:, 0:128]),
                     reads=["psB5"], writes=["BL", "psB5"])
        thr = T("thr", [128, 8])
        S.op("dve", lambda e: e.tensor_copy(thr[:], thred[:]), reads=[thredk], writes=["thr"])
        COS = T("COS", [128, 8 * 512]); SIN = T("SIN", [128, 8 * 512])
        ang = T("ang", [128, 512])
        scr = (T("scyi", [128, 512], I32), T("scyf", [128, 512]), T("scr", [128, 512]), T("scmk", [128, 512]))
        for k in range(8):
            S.op("dve", lambda e, k=k: e.tensor_scalar(out=ang[:], in0=tvec[:], scalar1=thr[:, k:k + 1], scalar2=None, op0=ALU.mult),
                 reads=["tvec", "thr"], writes=["ang"])
            emit_sincos(b, ang[:], "ang", 512, SIN[:, k * 512:(k + 1) * 512], COS[:, k * 512:(k + 1) * 512], "tab%d" % k, "t%d" % k, scr)
        NW = 2
        wr = [T("wr%d" % i, [128, 512]) for i in range(NW)]; wi = [T("wi%d" % i, [128, 512]) for i in range(NW)]
        zr = [T("zr%d" % i, [128, 512]) for i in range(NW)]; zi = [T("zi%d" % i, [128, 512]) for i in range(NW)]
        sr = [T("sr%d" % i, [128, 512]) for i in range(NW)]; si_ = [T("si%d" % i, [128, 512]) for i in range(NW)]
        t1 = [T("tt1%d" % i, [128, 512]) for i in range(NW)]
        carry = T("carry", [128, 2 * 8])
        gy = T("gy", [128, 512]); g2 = T("g2", [128, 512]); gob = [T("gob%d" % i, [128, 512], BF16) for i in range(2)]
        psU = [st2.enter_context(nc.psum_tensor("psU%d" % i, [128, 512], F32)) for i in range(4)]
        psY = [st2.enter_context(nc.psum_tensor("psY%d" % i, [128, 512], F32)) for i in range(2)]
        S.op("dve", lambda e: e.memset(carry[:], 0.0), writes=["carry"])
        chunks = [(0, CTX, True)] + [(CTX + 512 * i, 512, False) for i in range(SEQ // 512)]
        nit = 0
        ny = 0
        for d in range(2):
            order = chunks if d == 0 else [chunks[0]] + chunks[:0:-1]
            for (t0, L, is_ctx) in order:
                yps = psY[ny % 2]; ypk = "psY%d" % (ny % 2)
                ny += 1
                for sb_ in range(4):
                    k = d * 4 + sb_
                    i = nit % NW
                    pu = [psU[(nit % 2) * 2], psU[(nit % 2) * 2 + 1]]
                    puk = ["psU%d" % ((nit % 2) * 2), "psU%d" % ((nit % 2) * 2 + 1)]
                    nit += 1

                    def mmb(e, k=k, t0=t0, L=L, pu=pu):
                        e.matmul(pu[0][:, 0:L], BL[:, k * 128:(k + 1) * 128], uT[:, t0:t0 + L], start=True, stop=True)
                        return e.matmul(pu[1][:, 0:L], BL[:, (8 + k) * 128:(9 + k) * 128], uT[:, t0:t0 + L], start=True, stop=True)
                    S.op("pe", mmb, reads=["BL", "uT"], writes=puk)
                    co = COS[:, k * 512:k * 512 + L]; so = SIN[:, k * 512:k * 512 + L]
                    if d == 0:
                        bur = pu[0][:, 0:L]; bui = pu[1][:, 0:L]
                        srv = sr[i][:, 0:L]; siv = si_[i][:, 0:L]
                    else:
                        bur = pu[0][:, 0:L][:, ::-1]; bui = pu[1][:, 0:L][:, ::-1]
                        srv = sr[i][:, 0:L][:, ::-1]; siv = si_[i][:, 0:L][:, ::-1]
                    W_r = wr[i][:, 0:L]; W_i = wi[i][:, 0:L]; Z_r = zr[i][:, 0:L]; Z_i = zi[i][:, 0:L]; T1 = t1[i][:, 0:L]
                    ks = ["wr%d" % i, "wi%d" % i, "zr%d" % i, "zi%d" % i, "sr%d" % i, "si%d" % i, "tt1%d" % i]
                    S.op("dve", lambda e, T1=T1, so=so, bui=bui: e.tensor_tensor(out=T1, in0=bui, in1=so, op=ALU.mult),
                         reads=[puk[1], "tab%d_s" % k, "tab%d_c" % k], writes=[ks[6], puk[1]])
                    S.op("dve", lambda e, W_r=W_r, co=co, bur=bur: e.tensor_tensor(out=W_r, in0=bur, in1=co, op=ALU.mult), reads=[puk[0]], writes=[ks[0], puk[0]])
                    S.op("dve", lambda e, W_r=W_r, T1=T1: e.tensor_tensor(out=W_r, in0=W_r, in1=T1, op=ALU.add), reads=[ks[0], ks[6]], writes=[ks[0]])
                    S.op("dve", lambda e, T1=T1, so=so, bur=bur: e.tensor_tensor(out=T1, in0=bur, in1=so, op=ALU.mult), reads=[puk[0], ks[0]], writes=[ks[6], puk[0]])
                    S.op("dve", lambda e, W_i=W_i, co=co, bui=bui: e.tensor_tensor(out=W_i, in0=bui, in1=co, op=ALU.mult), reads=[puk[1]], writes=[ks[1], puk[1]])
                    S.op("dve", lambda e, W_i=W_i, T1=T1: e.tensor_tensor(out=W_i, in0=W_i, in1=T1, op=ALU.subtract), reads=[ks[1], ks[6]], writes=[ks[1]])
                    S.op("dve", lambda e, Z_r=Z_r, W_r=W_r, k=k, L=L: e.tensor_tensor_scan(
                        out=Z_r, data0=mg[:, k:k + 1].to_broadcast([128, L]), data1=W_r, initial=carry[:, k:k + 1], op0=ALU.mult, op1=ALU.add),
                        reads=[ks[0], "mg", "carry"], writes=[ks[2]])
                    S.op("dve", lambda e, Z_i=Z_i, W_i=W_i, k=k, L=L: e.tensor_tensor_scan(
                        out=Z_i, data0=mg[:, k:k + 1].to_broadcast([128, L]), data1=W_i, initial=carry[:, 8 + k:9 + k], op0=ALU.mult, op1=ALU.add),
                        reads=[ks[1], "mg", "carry"], writes=[ks[3]])
                    S.op("dve", lambda e, T1=T1, so=so, Z_i=Z_i: e.tensor_tensor(out=T1, in0=Z_i, in1=so, op=ALU.mult), reads=[ks[3], ks[1]], writes=[ks[6]])
                    S.op("dve", lambda e, srv=srv, co=co, Z_r=Z_r: e.tensor_tensor(out=srv, in0=Z_r, in1=co, op=ALU.mult), reads=[ks[2]], writes=[ks[4]])
                    S.op("dve", lambda e, srv=srv, T1=T1: e.tensor_tensor(out=srv, in0=srv, in1=T1, op=ALU.subtract), reads=[ks[4], ks[6]], writes=[ks[4]])
                    S.op("dve", lambda e, T1=T1, so=so, Z_r=Z_r: e.tensor_tensor(out=T1, in0=Z_r, in1=so, op=ALU.mult), reads=[ks[2], ks[4]], writes=[ks[6]])
                    S.op("dve", lambda e, siv=siv, co=co, Z_i=Z_i: e.tensor_tensor(out=siv, in0=Z_i, in1=co, op=ALU.mult), reads=[ks[3]], writes=[ks[5]])
                    S.op("dve", lambda e, siv=siv, T1=T1: e.tensor_tensor(out=siv, in0=siv, in1=T1, op=ALU.add), reads=[ks[5], ks[6]], writes=[ks[5]])
                    last = (L - 1) if d == 0 else 0
                    S.op("act", lambda e, i=i, k=k, last=last: e.activation(out=carry[:, k:k + 1], in_=sr[i][:, last:last + 1], func=AF.Copy),
                         reads=[ks[4]], writes=["carry"])
                    S.op("act", lambda e, i=i, k=k, last=last: e.activation(out=carry[:, 8 + k:9 + k], in_=si_[i][:, last:last + 1], func=AF.Copy),
                         reads=[ks[5]], writes=["carry"])
                    if not is_ctx:
                        def mmy(e, i=i, k=k, L=L, sb_=sb_, yps=yps):
                            e.matmul(yps[:, 0:L], CD[:, k * 128:(k + 1) * 128], sr[i][:, 0:L], start=(sb_ == 0), stop=False)
                            return e.matmul(yps[:, 0:L], CD[:, (8 + k) * 128:(9 + k) * 128], si_[i][:, 0:L], start=False, stop=(sb_ == 3))
                        S.op("pe", mmy, reads=["CD", ks[4], ks[5]], writes=[ypk])
                if is_ctx:
                    continue
                l0 = t0 - CTX
                if d == 0:
                    S.op("act", lambda e, l0=l0, L=L, yps=yps: e.activation(out=yf[:, l0:l0 + L], in_=yps[:, 0:L], func=AF.Copy),
                         reads=[ypk], writes=["yf", ypk])
                else:
                    go = gob[ny % 2]; gok = "gob%d" % (ny % 2)
                    S.op("dve", lambda e, l0=l0, L=L, yps=yps: e.tensor_tensor(out=gy[:, 0:L], in0=yps[:, 0:L], in1=yf[:, l0:l0 + L], op=ALU.add),
                         reads=[ypk, "yf"], writes=["gy", ypk])
                    S.op("dve", lambda e, t0=t0, L=L: e.scalar_tensor_tensor(out=gy[:, 0:L], in0=uT[:, t0:t0 + L], scalar=s5d[:, 0:1], in1=gy[:, 0:L],
                                                                             op0=ALU.mult, op1=ALU.add), reads=["uT", "s5d", "gy"], writes=["gy"])
                    S.op("dve", lambda e, L=L: e.tensor_tensor(out=g2[:, 0:L], in0=gy[:, 0:L], in1=gy[:, 0:L], op=ALU.mult), reads=["gy"], writes=["g2"])
                    S.op("dve", lambda e, L=L: e.tensor_scalar(out=g2[:, 0:L], in0=g2[:, 0:L], scalar1=0.044715, scalar2=1.0, op0=ALU.mult, op1=ALU.add),
                         reads=["g2"], writes=["g2"])
                    S.op("dve", lambda e, L=L: e.tensor_tensor(out=g2[:, 0:L], in0=g2[:, 0:L], in1=gy[:, 0:L], op=ALU.mult), reads=["g2", "gy"], writes=["g2"])
                    S.op("act", lambda e, L=L: e.activation(out=g2[:, 0:L], in_=g2[:, 0:L], func=AF.Sigmoid, scale=2.0 * math.sqrt(2.0 / math.pi)),
                         reads=["g2"], writes=["g2"])
                    S.op("dve", lambda e, L=L, go=go: e.tensor_tensor(out=go[:, 0:L], in0=g2[:, 0:L], in1=gy[:, 0:L], op=ALU.mult), reads=["g2", "gy"], writes=[gok])
                    S.dma("sp", lambda e, l0=l0, L=L, go=go: e.dma_start(out=s5T[:, l0:l0 + L], in_=go[:, 0:L]), reads=[gok])
        b.sb = bsave
        st2.close()
        S.barrier()
    if 3 in phases:
        build_E_ssd(b, ssd_io, xbcTd, zdtd, ident)
    return b.finish()


def build_E_ssd_decl(b):
    io = {}
    io["cw"] = b.din("m2cw", [128, 25])
    io["cb"] = b.din("m2cb", [128, 5])
    io["alog"] = b.din("m2alog", [12])
    io["dtb"] = b.din("m2dtb", [12])
    io["dsk"] = b.din("m2d", [6])
    io["nw"] = b.din("m2nw", [384])
    io["tri"] = b.din("tri", [128, 256])
    io["ssdT"] = b.dout("ssdT", [384, SEQ], BF16)
    mk = b.dout if DEBUG else b.dint
    io["xtm"] = mk("xtm", [NTOK, 384])
    io["Btm"] = mk("Btm", [NTOK, 128], BF16)
    io["BTd"] = mk("BTd", [128, NTOK], BF16)
    io["CTd"] = mk("CTd", [128, NTOK], BF16)
    io["dtsp"] = mk("dtsp", [NTOK, 12])
    io["Yf"] = mk("Yf", [SEQ, 384])
    return io


def build_E_ssd(b, io, xbcTd, zdtd, ident):
    S = b.S
    nc = b.nc
    st = contextlib.ExitStack()

    def T(name, shape, dt=F32):
        return st.enter_context(nc.sbuf_tensor(name + "_s3", list(shape), dt))

    def PS(name, dt=F32):
        return st.enter_context(nc.psum_tensor(name + "_s3", [128, 512 if dt == F32 else 1024], dt))
    cw = T("cw", [128, 25]); cb = T("cb", [128, 5]); abc = T("abc", [128, 12]); dtb = T("dtb", [128, 12])
    dsk = T("dsk", [128, 6]); nw = T("nw", [128, 384]); tri = T("tri", [128, 256]); onesf = T("onesf", [128, 128])
    identb = T("identb", [128, 128], BF16)
    S.dma("sp", lambda e: e.dma_start(out=cw[:], in_=io["cw"]), writes=["cw"])
    S.dma("sp", lambda e: e.dma_start(out=cb[:], in_=io["cb"]), writes=["cb"])
    S.dma("sp", lambda e: e.dma_start(out=abc[:], in_=io["alog"].partition_broadcast(128)), writes=["abc"])
    S.dma("sp", lambda e: e.dma_start(out=dtb[:], in_=io["dtb"].partition_broadcast(128)), writes=["dtb"])
    S.dma("sp", lambda e: e.dma_start(out=dsk[:], in_=io["dsk"].partition_broadcast(128)), writes=["dsk"])
    S.dma("sp", lambda e: e.dma_start(out=nw[:], in_=io["nw"].partition_broadcast(128)), writes=["nw"])
    S.dma("sp", lambda e: e.dma_start(out=tri[:], in_=io["tri"]), writes=["tri"])
    S.op("dve", lambda e: e.memset(onesf[:], 1.0), writes=["onesf"])
    S.op("dve", lambda e: e.tensor_copy(identb[:], ident[:]), reads=["ident"], writes=["identb3"])
    S.op("act", lambda e: e.activation(out=abc[:], in_=abc[:], func=AF.Exp), reads=["abc"], writes=["abc"])
    S.op("dve", lambda e: e.tensor_scalar_mul(abc[:], abc[:], -1.0), reads=["abc"], writes=["abc"])

    psTb = PS("psTb", BF16)
    st_outer = st
    st = contextlib.ExitStack()
    xin = [T("xin%d" % i, [128, 5 * 516]) for i in range(2)]
    acc = [T("cacc%d" % i, [128, 512]) for i in range(2)]
    xcb = [T("xcb%d" % i, [128, 512], BF16) for i in range(2)]
    xtt = [T("xtt%d" % i, [128, 384]) for i in range(4)]
    btt = [T("btt%d" % i, [128, 128], BF16) for i in range(2)]
    dtt = [T("dtt%d" % i, [128, 4 * 12]) for i in range(2)]
    psT = [PS("psT%d" % i) for i in range(2)]
    xbv = xbcTd.rearrange("b p t -> p b t")
    sts = supertiles()
    ntr = 0
    for si, (t0, N) in enumerate(sts):
        seq0, seq1 = (0, CTX) if si == 0 else (CTX, NTOK)
        lo = max(seq0, t0 - 2); hi = min(seq1, t0 + N + 2)
        xi = xin[si % 2]; xk = "xin%d" % (si % 2)
        xiv = xi[:].rearrange("p (b n) -> p b n", n=516)
        S.op("dve", lambda e, xi=xi: e.memset(xi[:], 0.0), writes=[xk])
        S.dma("sp", lambda e, xiv=xiv, lo=lo, hi=hi, t0=t0: e.dma_start(out=xiv[:, :, lo - (t0 - 2):hi - (t0 - 2)], in_=xbv[:, :, lo:hi]),
              reads=["xbcTd"], writes=[xk])
        nsub = N // 128
        dt_ = dtt[si % 2]; dk = "dtt%d" % (si % 2)
        dv = dt_[:, 0:nsub * 12].rearrange("p (s c) -> p s c", c=12)
        S.dma("sp", lambda e, dv=dv, t0=t0, N=N: e.dma_start(out=dv, in_=zdtd[t0:t0 + N, 384:396].rearrange("(s p) c -> p s c", p=128)),
              reads=["zdtd"], writes=[dk])
        S.op("dve", lambda e, dv=dv, nsub=nsub: e.tensor_tensor(out=dv, in0=dv, in1=dtb[:].unsqueeze(1).to_broadcast([128, nsub, 12]), op=ALU.add),
             reads=[dk, "dtb"], writes=[dk])
        S.op("act", lambda e, dv=dv: e.activation(out=dv, in_=dv, func=AF.Exp), reads=[dk], writes=[dk])
        S.op("act", lambda e, dv=dv: e.activation(out=dv, in_=dv, func=AF.Ln, bias=1.0), reads=[dk], writes=[dk])
        S.dma("sp", lambda e, dv=dv, t0=t0, N=N: e.dma_start(out=io["dtsp"][t0:t0 + N, :].rearrange("(s p) c -> p s c", p=128), in_=dv),
              reads=[dk], writes=["dtsp"])
        for blk in range(5):
            ai = (si * 5 + blk) % 2
            ac = acc[ai]; ack = "cacc%d" % ai
            xc = xcb[ai]; xck = "xcb%d" % ai

            def conv(e, ac=ac, xiv=xiv, blk=blk, N=N):
                ins = e.tensor_scalar(out=ac[:, 0:N], in0=xiv[:, blk, 0:N], scalar1=cw[:, blk * 5:blk * 5 + 1], scalar2=cb[:, blk:blk + 1],
                                      op0=ALU.mult, op1=ALU.add)
                for k in range(1, 5):
                    ins = e.scalar_tensor_tensor(out=ac[:, 0:N], in0=xiv[:, blk, k:k + N], scalar=cw[:, blk * 5 + k:blk * 5 + k + 1],
                                                 in1=ac[:, 0:N], op0=ALU.mult, op1=ALU.add)
                return ins
            S.op("dve", conv, reads=[xk, "cw", "cb"], writes=[ack])
            if blk < 3:
                S.op("act", lambda e, ac=ac, N=N: e.activation(out=ac[:, 0:N], in_=ac[:, 0:N], func=AF.Silu), reads=[ack], writes=[ack])
            else:
                S.op("act", lambda e, ac=ac, xc=xc, N=N: e.activation(out=xc[:, 0:N], in_=ac[:, 0:N], func=AF.Silu), reads=[ack], writes=[xck])
                dstT = io["BTd"] if blk == 3 else io["CTd"]
                S.dma("sp", lambda e, xc=xc, N=N, t0=t0, dstT=dstT: e.dma_start(out=dstT[:, t0:t0 + N], in_=xc[:, 0:N]), reads=[xck],
                      writes=["BTd" if blk == 3 else "CTd"])
            for sub in range(nsub):
                if blk < 3:
                    pt = psT[ntr % 2]; ptk = "psT%d" % (ntr % 2)
                    xo = xtt[sub]; xok = "xtt%d" % sub
                    ntr += 1
                    S.op("pe", lambda e, pt=pt, ac=ac, sub=sub: e.transpose(pt[:, 0:128], ac[:, sub * 128:(sub + 1) * 128], ident[:]),
                         reads=[ack, "ident"], writes=[ptk])
                    S.op("act", lambda e, pt=pt, xo=xo, blk=blk: e.activation(out=xo[:, blk * 128:(blk + 1) * 128], in_=pt[:, 0:128], func=AF.Copy),
                         reads=[ptk], writes=[xok + "_%d" % blk, ptk])
                    if blk == 2:
                        S.dma("sp", lambda e, xo=xo, t0=t0, sub=sub: e.dma_start(out=io["xtm"][t0 + sub * 128:t0 + (sub + 1) * 128, :], in_=xo[:]),
                              reads=[xok + "_0", xok + "_1", xok + "_2"], writes=["xtm"])
                elif blk == 3:
                    bo = btt[sub % 2]; bok = "btt%d" % (sub % 2)
                    S.op("pe", lambda e, xc=xc, sub=sub: e.transpose(psTb[:, 0:128], xc[:, sub * 128:(sub + 1) * 128], identb[:]),
                         reads=[xck, "identb3"], writes=["psTb"])
                    S.op("act", lambda e, bo=bo: e.activation(out=bo[:], in_=psTb[:, 0:128], func=AF.Copy), reads=["psTb"], writes=[bok, "psTb"])
                    S.dma("sp", lambda e, bo=bo, t0=t0, sub=sub: e.dma_start(out=io["Btm"][t0 + sub * 128:t0 + (sub + 1) * 128, :], in_=bo[:]),
                          reads=[bok], writes=["Btm"])

    st.close()
    S.barrier()
    st = st_outer
    xt = [T("xt%d" % i, [128, 384]) for i in range(2)]
    Bt = [T("Bt%d" % i, [128, 128], BF16) for i in range(2)]
    BTc = [T("BTc%d" % i, [128, 128], BF16) for i in range(2)]
    CTc = [T("CTc%d" % i, [128, 128], BF16) for i in range(2)]
    dtc = [T("dtc%d" % i, [128, 12]) for i in range(2)]
    zt = [T("zt%d" % i, [128, 384]) for i in range(2)]
    yft = [T("yft%d" % i, [128, 384]) for i in range(2)]
    da = T("da", [128, 6]); cumc = T("cumc", [128, 16]); ncum = T("ncum", [128, 6]); ecum = T("ecum", [128, 6])
    dsc = T("dsc", [128, 6]); dch = T("dch", [128, 6])
    GmT = T("GmT", [128, 128]); Ex = [T("Ex%d" % i, [128, 128]) for i in range(2)]
    MT = [T("MT%d" % i, [128, 128], BF16) for i in range(6)]
    xdt = T("xdt", [128, 384], BF16); xw = T("xw", [128, 384], BF16)
    yo = T("yo", [128, 384]); ydir = [T("ydir%d" % i, [128, 384]) for i in range(2)]
    Sst = T("Sst", [128, 384]); Sb = T("Sb", [128, 384], BF16)
    ysq = T("ysq", [128, 384]); rs = T("rs", [128, 4]); ynb = T("ynb", [128, 384], BF16)
    oT = [T("oTs%d" % i, [128, 3 * 128], BF16) for i in range(2)]
    psC = PS("psC"); psCB = [PS("psCB0"), PS("psCB1")]; psG = PS("psG"); psYd = PS("psYd"); psYo = PS("psYo"); psSt = PS("psSt")
    chunks = list(range(NKB))
    nld = 0
    nyo = 0
    for d in range(2):
        order = chunks if d == 0 else [1, 0] + chunks[:1:-1]
        trd = tri[:, d * 128:(d + 1) * 128]
        S.op("dve", lambda e: e.memset(Sst[:], 0.0), writes=["Sst"])
        S.op("dve", lambda e: e.memset(Sb[:], 0.0), writes=["Sb"])
        for c in order:
            t0 = c * 128
            is_ctx = c < 2
            i = nld % 2
            nld += 1
            lk = ["xt%d" % i, "Bt%d" % i, "BTc%d" % i, "CTc%d" % i, "dtc%d" % i]
            S.dma("sp", lambda e, i=i, t0=t0: e.dma_start(out=xt[i][:], in_=io["xtm"][t0:t0 + 128, :]), reads=["xtm"], writes=[lk[0]])
            S.dma("sp", lambda e, i=i, t0=t0: e.dma_start(out=Bt[i][:], in_=io["Btm"][t0:t0 + 128, :]), reads=["Btm"], writes=[lk[1]])
            S.dma("sp", lambda e, i=i, t0=t0: e.dma_start(out=dtc[i][:], in_=io["dtsp"][t0:t0 + 128, :]), reads=["dtsp"], writes=[lk[4]])
            if not is_ctx:
                S.dma("sp", lambda e, i=i, t0=t0: e.dma_start(out=BTc[i][:], in_=io["BTd"][:, t0:t0 + 128]), reads=["BTd"], writes=[lk[2]])
                S.dma("sp", lambda e, i=i, t0=t0: e.dma_start(out=CTc[i][:], in_=io["CTd"][:, t0:t0 + 128]), reads=["CTd"], writes=[lk[3]])
            dth = dtc[i][:, d * 6:(d + 1) * 6]
            S.op("dve", lambda e, dth=dth, d=d: e.tensor_tensor(out=da[:], in0=dth, in1=abc[:, d * 6:(d + 1) * 6], op=ALU.mult),
                 reads=[lk[4], "abc"], writes=["da"])

            def mmc(e, trd=trd):
                e.matmul(psC[:, 0:6], trd, da[:], start=True, stop=True)
                return e.matmul(psC[:, 8:14], onesf[:], da[:], start=True, stop=True)
            S.op("pe", mmc, reads=["da", "tri", "onesf"], writes=["psC"])
            S.op("dve", lambda e: e.tensor_copy(cumc[:, 0:14], psC[:, 0:14]), reads=["psC"], writes=["cumc", "psC"])
            S.op("dve", lambda e: e.tensor_tensor(out=dsc[:], in0=cumc[:, 8:14], in1=cumc[:, 0:6], op=ALU.subtract), reads=["cumc"], writes=["dsc"])
            S.op("act", lambda e: e.activation(out=dsc[:], in_=dsc[:], func=AF.Exp), reads=["dsc"], writes=["dsc"])
            S.op("act", lambda e: e.activation(out=dch[:], in_=cumc[:, 8:14], func=AF.Exp), reads=["cumc"], writes=["dch"])
            xv6 = xt[i][:].rearrange("p (h q) -> p h q", q=64)
            S.op("dve", lambda e, xv6=xv6, dth=dth: e.tensor_tensor(out=xdt[:].rearrange("p (h q) -> p h q", q=64), in0=xv6,
                                                                     in1=dth.unsqueeze(2).to_broadcast([128, 6, 64]), op=ALU.mult),
                 reads=[lk[0], lk[4]], writes=["xdt"])
            if not is_ctx:
                S.op("dve", lambda e: e.tensor_scalar_mul(ncum[:], cumc[:, 0:6], -1.0), reads=["cumc"], writes=["ncum"])
                S.op("act", lambda e: e.activation(out=ecum[:], in_=cumc[:, 0:6], func=AF.Exp), reads=["cumc"], writes=["ecum"])

                def mmcb(e, trd=trd):
                    ins = None
                    for h in range(6):
                        ins = e.matmul(psCB[h // 4][:, (h % 4) * 128:(h % 4 + 1) * 128], da[:, h:h + 1].to_broadcast([128, 128]), trd,
                                       start=True, stop=True)
                    return ins
                S.op("pe", mmcb, reads=["da", "tri"], writes=["psCB0", "psCB1"])
                S.op("pe", lambda e, i=i: e.matmul(psG[:, 0:128], BTc[i][:], CTc[i][:], start=True, stop=True), reads=[lk[2], lk[3]], writes=["psG"])
                S.op("dve", lambda e, trd=trd: e.tensor_tensor(out=GmT[:], in0=psG[:, 0:128], in1=trd, op=ALU.mult), reads=["psG", "tri"], writes=["GmT", "psG"])
                S.op("pe", lambda e, i=i: e.matmul(psYo[:, 0:384], CTc[i][:], Sb[:], start=True, stop=True), reads=[lk[3], "Sb"], writes=["psYo"])
                S.op("act", lambda e: e.activation(out=yo[:], in_=psYo[:, 0:384], func=AF.Copy), reads=["psYo"], writes=["yo", "psYo"])
                for h in range(6):
                    ex = Ex[h % 2]; exk = "Ex%d" % (h % 2)
                    cbk = "psCB%d" % (h // 4)
                    S.op("act", lambda e, h=h, ex=ex: e.activation(out=ex[:], in_=psCB[h // 4][:, (h % 4) * 128:(h % 4 + 1) * 128], func=AF.Exp,
                                                                   bias=ncum[:, h:h + 1]), reads=[cbk, "ncum"], writes=[exk, cbk])
                    S.op("dve", lambda e, h=h, ex=ex: e.scalar_tensor_tensor(out=MT[h][:], in0=ex[:], scalar=1.0, in1=GmT[:], op0=ALU.min, op1=ALU.mult),
                         reads=[exk, "GmT"], writes=["MT%d" % h])
                    S.op("pe", lambda e, h=h: e.matmul(psYd[:, h * 64:(h + 1) * 64], MT[h][:], xdt[:, h * 64:(h + 1) * 64], start=True, stop=True),
                         reads=["MT%d" % h, "xdt"], writes=["psYd"])
                yd = ydir[nyo % 2]; ydk = "ydir%d" % (nyo % 2)
                nyo += 1
                S.op("dve", lambda e: e.tensor_tensor(out=yo[:].rearrange("p (h q) -> p h q", q=64), in0=yo[:].rearrange("p (h q) -> p h q", q=64),
                                                      in1=ecum[:].unsqueeze(2).to_broadcast([128, 6, 64]), op=ALU.mult), reads=["yo", "ecum"], writes=["yo"])
                S.op("dve", lambda e, yd=yd: e.tensor_tensor(out=yd[:], in0=psYd[:, 0:384], in1=yo[:], op=ALU.add), reads=["psYd", "yo"], writes=[ydk, "psYd"])
            S.op("dve", lambda e: e.tensor_tensor(out=xw[:].rearrange("p (h q) -> p h q", q=64), in0=xdt[:].rearrange("p (h q) -> p h q", q=64),
                                                  in1=dsc[:].unsqueeze(2).to_broadcast([128, 6, 64]), op=ALU.mult), reads=["xdt", "dsc"], writes=["xw"])
            S.op("pe", lambda e, i=i: e.matmul(psSt[:, 0:384], Bt[i][:], xw[:], start=True, stop=True), reads=[lk[1], "xw"], writes=["psSt"])
            S.op("dve", lambda e: e.tensor_tensor(out=Sst[:].rearrange("p (h q) -> p h q", q=64), in0=Sst[:].rearrange("p (h q) -> p h q", q=64),
                                                  in1=dch[:].unsqueeze(2).to_broadcast([128, 6, 64]), op=ALU.mult), reads=["Sst", "dch"], writes=["Sst"])
            S.op("dve", lambda e: e.tensor_tensor(out=Sst[:], in0=Sst[:], in1=psSt[:, 0:384], op=ALU.add), reads=["Sst", "psSt"], writes=["Sst", "psSt"])
            S.op("act", lambda e: e.activation(out=Sb[:], in_=Sst[:], func=AF.Copy), reads=["Sst"], writes=["Sb"])
            if is_ctx:
                continue
            l0 = t0 - CTX
            if d == 0:
                S.dma("sp", lambda e, yd=yd, l0=l0: e.dma_start(out=io["Yf"][l0:l0 + 128, :], in_=yd[:]), reads=[ydk], writes=["Yf"])
            else:
                j = nyo % 2
                S.dma("sp", lambda e, j=j, l0=l0: e.dma_start(out=yft[j][:], in_=io["Yf"][l0:l0 + 128, :]), reads=["Yf"], writes=["yft%d" % j])
                S.dma("sp", lambda e, j=j, t0=t0: e.dma_start(out=zt[j][:], in_=zdtd[t0:t0 + 128, 0:384]), reads=["zdtd"], writes=["zt%d" % j])
                S.op("dve", lambda e, yd=yd, j=j: e.tensor_tensor(out=yd[:], in0=yd[:], in1=yft[j][:], op=ALU.add), reads=[ydk, "yft%d" % j], writes=[ydk])
                S.op("dve", lambda e, xv6=xv6: e.tensor_tensor(out=ysq[:].rearrange("p (h q) -> p h q", q=64), in0=xv6,
                                                                in1=dsk[:].unsqueeze(2).to_broadcast([128, 6, 64]), op=ALU.mult), reads=[lk[0], "dsk"], writes=["ysq"])
                S.op("dve", lambda e, yd=yd: e.tensor_tensor(out=yd[:], in0=yd[:], in1=ysq[:], op=ALU.add), reads=[ydk, "ysq"], writes=[ydk])
                S.op("act", lambda e, j=j: e.activation(out=zt[j][:], in_=zt[j][:], func=AF.Silu), reads=["zt%d" % j], writes=["zt%d" % j])
                S.op("dve", lambda e, yd=yd, j=j: e.tensor_tensor(out=yd[:], in0=yd[:], in1=zt[j][:], op=ALU.mult), reads=[ydk, "zt%d" % j], writes=[ydk])
                S.op("dve", lambda e, yd=yd: e.tensor_tensor(out=ysq[:], in0=yd[:], in1=yd[:], op=ALU.mult), reads=[ydk, "ysq"], writes=["ysq"])
                S.op("dve", lambda e: e.reduce_sum(out=rs[:, 0:1], in_=ysq[:], axis=AX.X), reads=["ysq"], writes=["rs"])
                S.op("dve", lambda e: e.tensor_scalar(out=rs[:, 1:2], in0=rs[:, 0:1], scalar1=1.0 / 384, scalar2=RMS_EPS, op0=ALU.mult, op1=ALU.add),
                     reads=["rs"], writes=["rs"])
                S.op("act", lambda e: e.activation(out=rs[:, 2:3], in_=rs[:, 1:2], func=AF.Sqrt), reads=["rs"], writes=["rs"])
                S.op("dve", lambda e: e.reciprocal(out=rs[:, 3:4], in_=rs[:, 2:3]), reads=["rs"], writes=["rs"])
                S.op("dve", lambda e, yd=yd: e.scalar_tensor_tensor(out=ynb[:], in0=yd[:], scalar=rs[:, 3:4], in1=nw[:], op0=ALU.mult, op1=ALU.mult),
                     reads=[ydk, "rs", "nw"], writes=["ynb"])

                def tr3(e):
                    ins = None
                    for k in range(3):
                        ins = e.transpose(psTb[:, k * 128:(k + 1) * 128], ynb[:, k * 128:(k + 1) * 128], identb[:])
                    return ins
                S.op("pe", tr3, reads=["ynb", "identb3"], writes=["psTb"])
                o = oT[j]; ok = "oTs%d" % j
                S.op("act", lambda e, o=o: e.activation(out=o[:], in_=psTb[:, 0:384], func=AF.Copy), reads=["psTb"], writes=[ok, "psTb"])
                S.dma("sp", lambda e, o=o, l0=l0: e.dma_start(out=io["ssdT"].rearrange("(k p) t -> p k t", p=128)[:, :, l0:l0 + 128],
                                                              in_=o[:].rearrange("p (k n) -> p k n", n=128)), reads=[ok])
    st.close()


def ssd_inputs(inp, j):
    ch = np.concatenate([j * 384 + np.arange(384), 3072 + j * 128 + np.arange(128), 4096 + j * 128 + np.arange(128)])
    cwj = inp["m2_conv_w"][0][:, ch]
    cw = np.ascontiguousarray(cwj.reshape(5, 5, 128).transpose(2, 1, 0)).reshape(128, 25)
    cb = np.ascontiguousarray(inp["m2_conv_b"][0][ch].reshape(5, 128).T)
    hs = 6 * j + np.arange(6)
    tri = np.ascontiguousarray(np.concatenate([np.triu(np.ones((128, 128), np.float32)), np.tril(np.ones((128, 128), np.float32))], 1))
    return {"m2cw": cw, "m2cb": cb, "m2alog": np.ascontiguousarray(inp["m2_a_log"][0][:, hs].reshape(12)),
            "m2dtb": np.ascontiguousarray(inp["m2_dt_bias"][0][:, hs].reshape(12)), "m2d": np.ascontiguousarray(inp["m2_d"][0][hs]),
            "m2nw": np.ascontiguousarray(inp["m2_norm_w"][0][j * 384:(j + 1) * 384]), "tri": tri}


def s5_tables(inp, j):
    gs = np.arange(8 * j, 8 * j + 8)
    def st(a):
        a = a.reshape(2, 4, 2, 64)
        return np.ascontiguousarray(a.transpose(2, 3, 0, 1).reshape(128, 8))
    lre = st(inp["s5_lam_re"][0][:, gs]); lim = st(inp["s5_lam_im"][0][:, gs])
    ls = st(np.repeat(inp["s5_log_step"][0][:, gs][:, :, None], 64, 2))
    s5p = np.ascontiguousarray(np.concatenate([lre, lim, ls], 1))
    def bt(a):
        a = a.reshape(4, 2, 64, 16)
        return a.transpose(1, 2, 0, 3).reshape(128, 64)
    s5b = np.ascontiguousarray(np.concatenate([bt(inp["s5_b_re"][0][gs]), bt(inp["s5_b_im"][0][gs])], 1))
    def ct(a):
        a = a.reshape(2, 4, 2, 16, 64)
        return a.transpose(2, 4, 0, 1, 3).reshape(128, 128)
    s5c = np.ascontiguousarray(np.concatenate([ct(inp["s5_c_re"][0][:, gs]), ct(inp["s5_c_im"][0][:, gs])], 1))
    s5d = np.ascontiguousarray(inp["s5_d"][0][j * 128:(j + 1) * 128].reshape(128, 1))
    return s5p, s5b, s5c, s5d


def e_cols(j):
    return np.concatenate([np.arange(j * 128, (j + 1) * 128), 1024 + np.arange(j * 384, (j + 1) * 384),
                           1024 + 3072 + np.arange(j * 128, (j + 1) * 128), 1024 + 4096 + np.arange(j * 128, (j + 1) * 128),
                           6240 + np.arange(j * 384, (j + 1) * 384), 6144 + 6 * j + np.arange(6), 6144 + 48 + 6 * j + np.arange(6)])


def stage_E(inp, aT_all, phases=(1, 2, 3)):
    w = inp["od_w_in"][0]
    ident = np.eye(128, dtype=np.float32)
    tvec = np.ascontiguousarray(np.tile(np.arange(1, 513, dtype=np.float32)[None, :], (128, 1)))
    maps = []
    for j in range(NCORES):
        s5p, s5b, s5c, s5d = s5_tables(inp, j)
        m = {"aT": aT_all, "w": np.ascontiguousarray(w[:, e_cols(j)]), "ident": ident, "tvec": tvec,
             "s5p": s5p, "s5b": s5b, "s5c": s5c, "s5d": s5d}
        m.update(ssd_inputs(inp, j))
        maps.append(m)
    res = run(build_E(phases), maps)
    return res


def stage_F(inp, mod, hT_lat_parts, s5T, ssdT):
    modt = mod_tables(mod[1])
    epsv = np.ascontiguousarray(np.tile(np.array([[RMS_EPS, LN_EPS]], np.float32), (128, 1)))
    ident = np.eye(128, dtype=np.float32)
    maps = []
    for j in range(NCORES):
        sl = slice(j * LAT_PC, (j + 1) * LAT_PC)
        maps.append({"hTin": np.ascontiguousarray(hT_lat_parts[j]), "modT": modt, "nw2T": tab(inp["norm_w"][1, 1]),
                     "w_out": inp["od_w_out"][0], "w1": inp["ffn_w1"][1], "w3": inp["ffn_w3"][1], "w2": inp["ffn_w2"][1],
                     "epsv": epsv, "s5T": np.ascontiguousarray(s5T[:, sl]), "ssdT": np.ascontiguousarray(ssdT[:, sl]),
                     "gluW": inp["s5_glu_w"][0], "glubT": tab(inp["s5_glu_b"][0]), "nwN": tab(inp["final_norm_w"]), "ident": ident})
    res = run(build_T2(1), maps)
    return np.concatenate([r["out"] for r in res], 0)


def kernel(**inp):
    inp = {k: np.asarray(v) for k, v in inp.items()}
    mod = stage_A(inp)
    hT0, aT0 = stage_B(inp, mod)
    aT0_all = gather_T(aT0)
    attnT, convT = stage_C(inp, aT0_all)
    hT1, aT1 = stage_D(inp, mod, hT0, attnT, convT)
    aT1_all = gather_T(aT1)
    resE = stage_E(inp, aT1_all)
    s5T = np.concatenate([r["s5T"] for r in resE], 0)
    ssdT = np.concatenate([r["ssdT"] for r in resE], 0)
    out = stage_F(inp, mod, [h[:, :LAT_PC] for h in hT1], s5T, ssdT)
    return out.reshape(1, SEQ, D).astype(np.float32)
```
